# Optimizing a Trainium2 kernel written in Bass

```python
import jax
import jax.numpy as jnp
from jax import lax
import numpy as np


D_MODEL = 1024
BATCH = 8
SEQ = 4096
DEPTH = 2

CHUNK = 64
HEAD_DIM = 64
F32 = jnp.float32
RMS_EPS = 1e-6
GN_EPS = 64e-5

A_HEADS = 8
A_WIDTH = A_HEADS * HEAD_DIM
DECAY_LORA = 64
AAA_LORA = 64
A_SPLITS = (A_WIDTH, DECAY_LORA, A_WIDTH, A_WIDTH, AAA_LORA, A_WIDTH)
A_COLS = A_WIDTH * 4 + DECAY_LORA + AAA_LORA

B_HEADS = 8
B_WIDTH = B_HEADS * HEAD_DIM
PAST_CHUNKS = 8
BAND = (PAST_CHUNKS + 1) * CHUNK
REL_CLIP = 256
N_REL = (CHUNK - 1) + REL_CLIP + 1
B_SPLITS = (B_WIDTH, B_WIDTH, B_WIDTH, B_WIDTH)
B_COLS = B_WIDTH * 4

C_HEADS = 8
C_KEY_DIM = 64
C_VAL_DIM = 64
C_FWIDTH = C_HEADS * C_KEY_DIM
C_WIDTH = C_HEADS * C_VAL_DIM
C_SPLITS = (C_FWIDTH, C_FWIDTH, C_WIDTH, C_WIDTH)
C_COLS = C_FWIDTH * 2 + C_WIDTH * 2

N_BRANCH = 3
MERGE_COLS = N_BRANCH * D_MODEL
IN_COLS = A_COLS + B_COLS + C_COLS + MERGE_COLS

kernel_name = 'hybrid_rwkv7_chunkattn_hgrn2_gated'


def _split(z, sizes):
    out, start = [], 0
    for s in sizes:
        out.append(z[..., start:start + s])
        start += s
    return out


def _rmsnorm(x, g, eps=RMS_EPS):
    xf = x.astype(F32)
    y = xf * lax.rsqrt(jnp.mean(xf * xf, axis=-1, keepdims=True) + eps)
    return (y * g.astype(F32)).astype(x.dtype)


def _rwkv7_scan(r, decay, k, v, kk, b):
    Bsz, T, H, N = r.shape
    seq = tuple(jnp.moveaxis(t, 1, 0) for t in (r, decay, k, v, kk, b))

    def step(S, inp):
        r_t, w_t, k_t, v_t, kk_t, b_t = inp
        sa = jnp.einsum('bhij,bhj->bhi', S, -kk_t)
        S = S * w_t[:, :, None, :] + sa[..., None] * b_t[:, :, None, :] + v_t[..., None] * k_t[:, :, None, :]
        return S, jnp.einsum('bhij,bhj->bhi', S, r_t)

    S0 = jnp.zeros((Bsz, H, N, N), F32)
    _, y = lax.scan(step, S0, seq)
    return jnp.moveaxis(y, 0, 1)


def _rwkv7_mixer(za, mu, w0, w2, a0, a2, k_k, k_a, r_k, ln_w, ln_b):
    Bsz, T, _ = za.shape
    za_prev = jnp.pad(za, ((0, 0), (1, 0), (0, 0)))[:, :T]
    za = za + (za_prev - za) * mu
    r, wl, k, v, al, gate = _split(za, A_SPLITS)
    w_log = -jax.nn.softplus(-(w0 + jnp.tanh(wl) @ w2).astype(F32)) - 0.5
    decay = jnp.exp(-jnp.exp(w_log))
    a = jax.nn.sigmoid((a0 + al @ a2).astype(F32))

    def heads(t):
        return t.astype(F32).reshape(t.shape[:-1] + (A_HEADS, HEAD_DIM))

    r, k, v, decay, a = heads(r), heads(k), heads(v), heads(decay), heads(a)
    kk = k * heads(k_k)
    kk = kk / jnp.maximum(jnp.sqrt(jnp.sum(kk * kk, axis=-1, keepdims=True)), 1e-12)
    k = k * (1.0 + (a - 1.0) * heads(k_a))
    y = _rwkv7_scan(r, decay, k, v, kk, kk * a)
    mean = jnp.mean(y, axis=-1, keepdims=True)
    var = jnp.mean(jnp.square(y - mean), axis=-1, keepdims=True)
    y = (y - mean) * lax.rsqrt(var + GN_EPS) * heads(ln_w) + heads(ln_b)
    y = y + jnp.sum(r * k * r_k.astype(F32), axis=-1, keepdims=True) * v
    y = y.reshape(Bsz, T, A_WIDTH).astype(gate.dtype)
    return y * jax.nn.silu(gate)


def _chunk_attention(q, k, v, gate, q_g, k_g, rel_bias):
    Bsz, T, _ = q.shape
    n_chunks = T // CHUNK

    def heads(t):
        return t.reshape(Bsz, T, B_HEADS, HEAD_DIM)

    q = _rmsnorm(heads(q), q_g).astype(F32) * (HEAD_DIM ** -0.5)
    k = _rmsnorm(heads(k), k_g).astype(F32)
    v = heads(v).astype(F32)
    pad = ((0, 0), (PAST_CHUNKS * CHUNK, 0), (0, 0), (0, 0))
    k_pad = jnp.pad(k, pad)
    v_pad = jnp.pad(v, pad)
    qi = np.arange(CHUNK)[:, None]
    kj = np.arange(BAND)[None, :]
    rel = np.clip(PAST_CHUNKS * CHUNK + qi - kj, -(CHUNK - 1), REL_CLIP) + (CHUNK - 1)
    bias = rel_bias.astype(F32)[:, rel]
    key_slot = jnp.arange(BAND)
    q_chunks = jnp.moveaxis(q.reshape(Bsz, n_chunks, CHUNK, B_HEADS, HEAD_DIM), 1, 0)

    def one_chunk(args):
        n, q_c = args
        start = n * CHUNK
        k_b = lax.dynamic_slice_in_dim(k_pad, start, BAND, axis=1)
        v_b = lax.dynamic_slice_in_dim(v_pad, start, BAND, axis=1)
        s = jnp.einsum('bqhd,bkhd->bhqk', q_c, k_b) + bias
        valid = key_slot >= (PAST_CHUNKS - n) * CHUNK
        s = jnp.where(valid, s, -jnp.inf)
        p = jax.nn.softmax(s, axis=-1)
        return jnp.einsum('bhqk,bkhd->bqhd', p, v_b)

    o = lax.map(one_chunk, (jnp.arange(n_chunks), q_chunks))
    o = jnp.moveaxis(o, 0, 1).reshape(Bsz, T, B_WIDTH).astype(gate.dtype)
    return o * jax.nn.silu(gate)


def _hgrn2_chunk_scan(q, k, g, v):
    Bsz, T, H, DK = q.shape
    DV = v.shape[-1]
    n = T // CHUNK

    def to_chunks(t):
        return t.reshape(Bsz, n, CHUNK, H, t.shape[-1]).transpose(1, 0, 3, 2, 4)

    causal = jnp.asarray(np.tril(np.ones((CHUNK, CHUNK), dtype=bool)))

    def step(S, inp):
        q_c, k_c, g_c, v_c = inp
        b = jnp.cumsum(g_c, axis=2)
        o_inter = jnp.einsum('bhtd,bhde->bhte', q_c * jnp.exp(b), S)
        diff = b[:, :, :, None, :] - b[:, :, None, :, :]
        dec = jnp.exp(jnp.where(causal[:, :, None], diff, -jnp.inf))
        att = jnp.einsum('bhtd,bhsd,bhtsd->bhts', q_c, k_c, dec)
        o = o_inter + jnp.einsum('bhts,bhse->bhte', att, v_c)
        b_end = b[:, :, -1:, :]
        S = jnp.exp(b_end[:, :, 0, :])[..., None] * S + jnp.einsum('bhsd,bhse->bhde', k_c * jnp.exp(b_end - b), v_c)
        return S, o

    S0 = jnp.zeros((Bsz, H, DK, DV), F32)
    _, o = lax.scan(step, S0, (to_chunks(q), to_chunks(k), to_chunks(g), to_chunks(v)))
    return o.transpose(1, 0, 3, 2, 4).reshape(Bsz, T, H, DV)


def _hgrn2_mixer(q, f, i, gate, lb, norm_g):
    Bsz, T, _ = q.shape
    lbf = lb.astype(F32)
    fg = lbf + (1.0 - lbf) * jax.nn.sigmoid(f.astype(F32))
    key = 1.0 - fg
    logf = jnp.log(fg)
    qh = jax.nn.silu(q.astype(F32)).reshape(Bsz, T, C_HEADS, C_KEY_DIM)
    o = _hgrn2_chunk_scan(qh, key.reshape(Bsz, T, C_HEADS, C_KEY_DIM), logf.reshape(Bsz, T, C_HEADS, C_KEY_DIM),
                          i.astype(F32).reshape(Bsz, T, C_HEADS, C_VAL_DIM))
    o = _rmsnorm(o, norm_g).reshape(Bsz, T, C_WIDTH).astype(gate.dtype)
    return o * jax.nn.silu(gate)


def setup_inputs(seed: int = 0) -> dict:
    key = jax.random.key(seed)
    ks = jax.random.split(key, 24)
    L = DEPTH

    def nrm(k, shape, s):
        return jax.random.normal(k, shape, F32) * s

    frac = jnp.arange(A_WIDTH, dtype=F32) / (A_WIDTH - 1)
    return {
        'x': nrm(ks[0], (BATCH, SEQ, D_MODEL), 1.0),
        'norm_g': 1.0 + nrm(ks[1], (L, D_MODEL), 0.02),
        'w_in': nrm(ks[2], (L, D_MODEL, IN_COLS), D_MODEL ** -0.5),
        'rwkv_mu': jax.random.uniform(ks[3], (L, A_COLS), F32),
        'rwkv_w0': -5.5 + 5.0 * frac ** 0.85 + nrm(ks[4], (L, A_WIDTH), 0.1),
        'rwkv_w2': nrm(ks[5], (L, DECAY_LORA, A_WIDTH), 0.1),
        'rwkv_a0': nrm(ks[6], (L, A_WIDTH), 0.1),
        'rwkv_a2': nrm(ks[7], (L, AAA_LORA, A_WIDTH), 0.1),
        'rwkv_k_k': 0.85 + nrm(ks[8], (L, A_WIDTH), 0.02),
        'rwkv_k_a': 1.0 + nrm(ks[9], (L, A_WIDTH), 0.02),
        'rwkv_r_k': nrm(ks[10], (L, A_HEADS, HEAD_DIM), 0.1),
        'rwkv_ln_w': 1.0 + nrm(ks[11], (L, A_WIDTH), 0.02),
        'rwkv_ln_b': nrm(ks[12], (L, A_WIDTH), 0.02),
        'attn_q_norm': 1.0 + nrm(ks[13], (L, HEAD_DIM), 0.02),
        'attn_k_norm': 1.0 + nrm(ks[14], (L, HEAD_DIM), 0.02),
        'attn_rel_bias': nrm(ks[15], (L, B_HEADS, N_REL), 0.1),
        'hgrn_lb': nrm(ks[16], (L, C_FWIDTH), 0.1),
        'hgrn_norm': 1.0 + nrm(ks[17], (L, C_VAL_DIM), 0.02),
        'proj_a': nrm(ks[18], (L, A_WIDTH, D_MODEL), A_WIDTH ** -0.5),
        'proj_b': nrm(ks[19], (L, B_WIDTH, D_MODEL), B_WIDTH ** -0.5),
        'proj_c': nrm(ks[20], (L, C_WIDTH, D_MODEL), C_WIDTH ** -0.5),
        'w_out': nrm(ks[21], (L, D_MODEL, D_MODEL), D_MODEL ** -0.5),
    }


def reference(x, norm_g, w_in, rwkv_mu, rwkv_w0, rwkv_w2, rwkv_a0, rwkv_a2, rwkv_k_k, rwkv_k_a, rwkv_r_k,
              rwkv_ln_w, rwkv_ln_b, attn_q_norm, attn_k_norm, attn_rel_bias, hgrn_lb, hgrn_norm,
              proj_a, proj_b, proj_c, w_out):
    lb_all = jax.nn.softmax(hgrn_lb.astype(F32), axis=0)
    lb_all = jnp.cumsum(lb_all, axis=0) - lb_all[0]
    for l in range(DEPTH):
        h = _rmsnorm(x, norm_g[l])
        z = h @ w_in[l]
        za, zb, zc, zg = _split(z, (A_COLS, B_COLS, C_COLS, MERGE_COLS))
        y_a = _rwkv7_mixer(za, rwkv_mu[l], rwkv_w0[l], rwkv_w2[l], rwkv_a0[l], rwkv_a2[l], rwkv_k_k[l],
                           rwkv_k_a[l], rwkv_r_k[l], rwkv_ln_w[l], rwkv_ln_b[l])
        qb, kb, vb, gb = _split(zb, B_SPLITS)
        y_b = _chunk_attention(qb, kb, vb, gb, attn_q_norm[l], attn_k_norm[l], attn_rel_bias[l])
        qc, fc, ic, gc = _split(zc, C_SPLITS)
        y_c = _hgrn2_mixer(qc, fc, ic, gc, lb_all[l], hgrn_norm[l])
        g_a, g_b, g_c = _split(jax.nn.sigmoid(zg), (D_MODEL, D_MODEL, D_MODEL))
        m = g_a * (y_a @ proj_a[l]) + g_b * (y_b @ proj_b[l]) + g_c * (y_c @ proj_c[l])
        x = x + (m @ w_out[l]).astype(x.dtype)
    return x
```

```python
import numpy as np
from contextlib import ExitStack
import concourse.bass as bass
import concourse.mybir as mybir
from concourse.bass_utils import run_bass_kernel_spmd

F32 = mybir.dt.float32
BF16 = mybir.dt.bfloat16
AF = mybir.ActivationFunctionType
ALU = mybir.AluOpType
AX = mybir.AxisListType

D_MODEL = 1024
A_COLS = 2176
B_COLS = 2048
C_COLS = 2048
IN_COLS = 9344
N_REL = 320


class _Rec:
    def __init__(self):
        self.call = None

    def __getattr__(self, name):
        def f(*a, **kw):
            self.call = (name, a, kw)
            return self
        return f


def _bind(fn):
    rec = _Rec()
    fn(rec)
    name, a, kw = rec.call
    return lambda eng: getattr(eng, name)(*a, **kw)


class Op:
    __slots__ = ("eng", "fn", "deps", "signal", "sigval", "dma_sem", "dma_val", "is_dma", "pre_wait")

    def __init__(self, eng, fn):
        self.eng = eng
        self.fn = fn
        self.deps = []
        self.signal = False
        self.sigval = 0
        self.is_dma = False
        self.dma_sem = None
        self.dma_val = 0
        self.pre_wait = None


class Prog:
    ENGS = ("tensor", "vector", "scalar", "gpsimd", "sync")
    NDMA = 12

    def __init__(self, nc, es):
        self.nc = nc
        self.ops = {e: [] for e in self.ENGS}
        self.last_write = {}
        self.readers = {}
        self.sem = {e: es.enter_context(nc.semaphore("s_" + e)) for e in ("tensor", "vector", "scalar", "gpsimd")}
        self.dma_sems = {q: [es.enter_context(nc.semaphore("d_%s_%d" % (q, i))) for i in range(self.NDMA)]
                         for q in ("sync", "gpsimd")}
        self.dma_count = {"sync": 0, "gpsimd": 0}
        self.dma_hist = {"sync": [], "gpsimd": []}

    def _deps(self, op, reads, writes):
        deps = []
        for k in reads:
            w = self.last_write.get(k)
            if w is not None:
                deps.append(w)
        for k in writes:
            w = self.last_write.get(k)
            if w is not None:
                deps.append(w)
            deps.extend(self.readers.get(k, ()))
        seen = set()
        for d in deps:
            if id(d) in seen or d is op:
                continue
            seen.add(id(d))
            if d.eng == op.eng and op.eng == "tensor" and not d.is_dma:
                continue
            op.deps.append(d)
            if not d.is_dma:
                d.signal = True
        for k in reads:
            self.readers.setdefault(k, []).append(op)
        for k in writes:
            self.last_write[k] = op
            self.readers[k] = []

    def op(self, eng, fn, reads=(), writes=()):
        o = Op(eng, _bind(fn))
        self._deps(o, reads, writes)
        self._apply_bar(o)
        self.ops[eng].append(o)
        return o

    def dma(self, q, fn, reads=(), writes=()):
        o = Op(q, _bind(fn))
        o.is_dma = True
        i = self.dma_count[q]
        self.dma_count[q] += 1
        o.dma_sem = self.dma_sems[q][i % self.NDMA]
        o.dma_val = 16 * (i // self.NDMA + 1)
        if i >= self.NDMA:
            o.pre_wait = self.dma_hist[q][i - self.NDMA]
        self.dma_hist[q].append(o)
        self._deps(o, reads, writes)
        self._apply_bar(o)
        self.ops[q].append(o)
        return o

    def emit(self, block):
        for e in ("tensor", "vector", "scalar", "gpsimd"):
            c = 0
            for o in self.ops[e]:
                if o.is_dma:
                    continue
                if o.signal:
                    c += 1
                o.sigval = c
        all_dmas = self.dma_hist["sync"] + self.dma_hist["gpsimd"]

        def run(eng_name):
            def body(eng):
                water = {}

                def wait(sem, val):
                    key = id(sem)
                    if water.get(key, 0) >= val:
                        return
                    water[key] = val
                    eng.wait_ge(sem, val)

                for o in self.ops[eng_name]:
                    if o.pre_wait is not None:
                        wait(o.pre_wait.dma_sem, o.pre_wait.dma_val)
                    for d in o.deps:
                        if d.is_dma:
                            wait(d.dma_sem, d.dma_val)
                        else:
                            wait(self.sem[d.eng], d.sigval)
                    ins = o.fn(eng)
                    if o.is_dma:
                        ins.then_inc(o.dma_sem, 16)
                    elif o.signal:
                        ins.then_inc(self.sem[eng_name], 1)
                if eng_name in ("sync", "gpsimd"):
                    for o in self.dma_hist[eng_name][-self.NDMA:]:
                        wait(o.dma_sem, o.dma_val)
            return body

        block.tensor(run("tensor"))
        block.vector(run("vector"))
        block.scalar(run("scalar"))
        block.gpsimd(run("gpsimd"))
        block.sync(run("sync"))

    def barrier(self):
        lasts = []
        for e in ("tensor", "vector", "scalar", "gpsimd"):
            cs = [o for o in self.ops[e] if not o.is_dma]
            if cs:
                lasts.append(cs[-1])
        dmas = self.dma_hist["sync"][-self.NDMA:] + self.dma_hist["gpsimd"][-self.NDMA:]
        self.last_write = {"__bar__": None}
        self.readers = {}
        self._bar = lasts + dmas
        self._bar_pending = set(self.ENGS)

    def _apply_bar(self, o):
        if getattr(self, "_bar_pending", None) and o.eng in self._bar_pending:
            self._bar_pending.discard(o.eng)
            for d in self._bar:
                if d is o:
                    continue
                if d.eng == o.eng and not d.is_dma and o.eng == "tensor":
                    continue
                o.deps.append(d)
                if not d.is_dma:
                    d.signal = True


class Arena:
    def __init__(self, tens, size):
        self.t = tens
        self.size = size
        self.off = 0
        self.mark = 0

    def reset(self):
        self.off = self.mark

    def alloc(self, n, dt):
        nf = n if dt == F32 else (n + 1) // 2
        nf_al = (nf + 7) // 8 * 8
        assert self.off + nf_al <= self.size, ("arena overflow", self.off, nf_al, self.size)
        ap = self.t[:, self.off:self.off + nf]
        self.off += nf_al
        if dt != F32:
            ap = ap.bitcast(dt)[:, 0:n]
        return ap


class _AView:
    def __init__(self, arena, dt):
        self.a, self.dt = arena, dt

    def alloc(self, n):
        return self.a.alloc(n, self.dt)

    def reset(self):
        self.a.reset()

    @property
    def off(self):
        return self.a.off

    @property
    def mark(self):
        return self.a.mark

    @mark.setter
    def mark(self, v):
        self.a.mark = v


def _consts():
    s = np.arange(128)[:, None]
    t = np.arange(128)[None, :]
    same = (s // 64) == (t // 64)
    c = {}
    c["ident"] = np.eye(128)
    c["tri"] = same & (s <= t)
    c["ch"] = same
    c["midm"] = same & ((s % 64) <= 31)
    c["msu"] = same & (s < t)
    c["msl"] = same & (s > t)
    c["miu"] = same & (s <= t)
    names = ["ident", "tri", "ch", "midm", "msu", "msl", "miu"]
    arr = np.stack([c[k].astype(np.float32) for k in names], axis=1)
    chsel = np.zeros((128, 2), np.float32)
    chsel[:64, 0] = 1
    chsel[64:, 1] = 1
    negm = np.zeros((128, 5, 128), np.float32)
    negm[:64, 0, 64:] = -30000.0
    negm[64:, 4, :64] = -30000.0
    return names, arr, chsel, negm


CNAMES, CARR, CHSEL, NEGM = _consts()


def _bias_gather(rel_bias):
    k = np.arange(128)[:, None, None]
    r = np.arange(5)[None, :, None]
    q = np.arange(128)[None, None, :]
    idx = np.clip(512 + q - (r * 128 + k), -63, 256) + 63
    g = rel_bias[:, :, idx]
    return np.ascontiguousarray(np.transpose(g, (0, 2, 1, 3, 4)))


class K:
    pass


def build_nc(T=4096, L=2, branches=("a", "b", "c"), debug=False):
    NT = T // 128
    nc = bass.Bass("TRN2", target_bir_lowering=False)
    k = K()
    k.nc, k.T, k.L, k.NT, k.branches, k.debug = nc, T, L, NT, branches, debug

    def din(name, shape, dt=F32):
        return nc.dram_tensor(name, list(shape), dt, kind="ExternalInput").ap()

    k.x = din("x", [T, 1024])
    k.norm_g = din("norm_g", [L, 1024])
    k.w_in = din("w_in", [L, 1024, IN_COLS])
    k.rwkv_mu = din("rwkv_mu", [L, A_COLS])
    k.rwkv_w0 = din("rwkv_w0", [L, 512])
    k.rwkv_w2 = din("rwkv_w2", [L, 64, 512])
    k.rwkv_a0 = din("rwkv_a0", [L, 512])
    k.rwkv_a2 = din("rwkv_a2", [L, 64, 512])
    k.rwkv_k_k = din("rwkv_k_k", [L, 512])
    k.rwkv_k_a = din("rwkv_k_a", [L, 512])
    k.rwkv_r_k = din("rwkv_r_k", [L, 512])
    k.rwkv_ln_w = din("rwkv_ln_w", [L, 512])
    k.rwkv_ln_b = din("rwkv_ln_b", [L, 512])
    k.attn_q_norm = din("attn_q_norm", [L, 64])
    k.attn_k_norm = din("attn_k_norm", [L, 64])
    k.attn_bias = din("attn_bias", [L, 128, 8 * 5 * 128])
    k.hgrn_lb = din("hgrn_lb", [L, 512])
    k.hgrn_norm = din("hgrn_norm", [L, 64])
    k.proj_a = din("proj_a", [L, 512, 1024])
    k.proj_b = din("proj_b", [L, 512, 1024])
    k.proj_c = din("proj_c", [L, 512, 1024])
    k.w_out = din("w_out", [L, 1024, 1024])
    k.cmat = din("cmat", [128, 7 * 128])
    k.chsel = din("chsel", [128, 2])
    k.negm = din("negm", [128, 5 * 128])
    k.out = nc.dram_tensor("out", [T, 1024], F32, kind="ExternalOutput").ap()
    k.x1 = nc.dram_tensor("x1", [T, 1024], F32).ap()
    k.hT_d = nc.dram_tensor("hT_d", [128, 8, T + 16], BF16).ap()
    k.yT_d = {b: nc.dram_tensor("yT_" + b, [128, 4, T], BF16).ap() for b in "abc"}
    if debug:
        k.dbg = {b: nc.dram_tensor("dbg_" + b, [T, 512], F32, kind="ExternalOutput").ap() for b in "abc"}

    with ExitStack() as es:
        FA = 42 * 1024
        arena = Arena(es.enter_context(nc.sbuf_tensor("arena", [128, FA], F32))[:], FA)
        k.fa = _AView(arena, F32)
        k.ba = _AView(arena, BF16)
        k.pf = [es.enter_context(nc.psum_tensor("pf%d" % i, [128, 512], F32))[:] for i in range(8)]
        k.P = Prog(nc, es)
        block = es.enter_context(nc.Block())
        P = k.P
        k.cm_f = k.fa.alloc(7 * 128)
        k.cm_b = k.ba.alloc(7 * 128)
        k.chs = k.fa.alloc(2)
        P.dma("sync", lambda e: e.dma_start(out=k.cm_f, in_=k.cmat[:, :]), writes=["cm_f"])
        P.dma("gpsimd", lambda e: e.dma_start(out=k.cm_b, in_=k.cmat[:, :]), writes=["cm_b"])
        P.dma("sync", lambda e: e.dma_start(out=k.chs, in_=k.chsel[:, :]), writes=["chs"])
        k.fa.mark = k.fa.off
        k.ba.mark = k.ba.off
        for l in range(L):
            xin = k.x if l == 0 else k.x1
            xout = k.out if l == L - 1 else k.x1
            phase0(k, l, xin)
            if STOP == "0":
                phaseCopy(k, xin, xout)
                continue
            if "a" in branches:
                phaseA(k, l)
            if "b" in branches:
                phaseB(k, l)
            if "c" in branches:
                phaseC(k, l)
            if STOP == "B":
                phaseCopy(k, xin, xout)
                continue
            phaseM(k, l, xin, xout)
        P.emit(block)
    return nc


import os
STOP = os.environ.get("STOP", "")
LVL = float(os.environ.get("LVL", "9"))


def phaseCopy(k, xin, xout):
    P = k.P
    new_phase(k)
    t = k.fa.alloc(1024)
    for n in range(k.NT):
        P.dma("sync", lambda e: e.dma_start(out=t, in_=xin[n * 128:(n + 1) * 128, :]), writes=["t"])
        P.dma("sync", lambda e: e.dma_start(out=xout[n * 128:(n + 1) * 128, :], in_=t), reads=["t"])


def cview(k, name, bf=True):
    i = CNAMES.index(name)
    t = k.cm_b if bf else k.cm_f
    return t[:, i * 128:(i + 1) * 128]


def new_phase(k):
    k.P.barrier()
    k.fa.reset()
    k.ba.reset()


def phase0(k, l, xin):
    P, nc = k.P, k.nc
    new_phase(k)
    fa, ba = k.fa, k.ba
    gb = fa.alloc(1024)
    xt = [fa.alloc(1024) for _ in range(2)]
    junk = fa.alloc(1024)
    ss = [fa.alloc(1) for _ in range(2)]
    rs = [fa.alloc(1) for _ in range(2)]
    hb = [ba.alloc(1024) for _ in range(2)]
    hs = [ba.alloc(1024) for _ in range(2)]
    zc = ba.alloc(8 * 16)
    idb = cview(k, "ident")
    P.dma("sync", lambda e: e.dma_start(out=gb, in_=k.norm_g[l:l + 1, :].partition_broadcast(128)), writes=["gb"])
    if l == 0:
        P.op("gpsimd", lambda e: e.memset(zc, 0.0), writes=["zc"])
        P.dma("sync", lambda e: e.dma_start(out=k.hT_d[:, :, 0:16], in_=zc.rearrange("p (c t) -> p c t", c=8)), reads=["zc"])
    for n in range(k.NT):
        b = n % 2
        X, HB, HS, SS, RS = "xt%d" % b, "hb%d" % b, "hs%d" % b, "ss%d" % b, "rs%d" % b
        P.dma("sync", lambda e, n=n, b=b: e.dma_start(out=xt[b], in_=xin[n * 128:(n + 1) * 128, :]), writes=[X])
        P.op("scalar", lambda e, b=b: e.activation(out=junk, in_=xt[b], func=AF.Square, accum_out=ss[b]), reads=[X], writes=["junk", SS])
        P.op("scalar", lambda e, b=b: e.activation(out=rs[b], in_=ss[b], func=AF.Sqrt, scale=1.0 / 1024, bias=1e-6), reads=[SS], writes=[RS])
        P.op("vector", lambda e, b=b: e.reciprocal(out=rs[b], in_=rs[b]), reads=[RS], writes=[RS])
        P.op("vector", lambda e, b=b: e.scalar_tensor_tensor(out=hb[b], in0=xt[b], scalar=rs[b][:, 0:1], in1=gb, op0=ALU.mult, op1=ALU.mult),
             reads=[X, RS, "gb"], writes=[HB])
        pt = k.pf[n % 2].bitcast(BF16)
        PT = "pf%d" % (n % 2)
        for c in range(8):
            P.op("tensor", lambda e, c=c, b=b, pt=pt: e.transpose(out=pt[:, c * 128:(c + 1) * 128], in_=hb[b][:, c * 128:(c + 1) * 128], identity=idb),
                 reads=[HB, "cm_b"], writes=[PT])
        P.op("scalar", lambda e, b=b, pt=pt: e.copy(out=hs[b], in_=pt[:, 0:1024]), reads=[PT], writes=[HS])
        P.dma("sync", lambda e, n=n, b=b: e.dma_start(out=k.hT_d[:, :, 16 + n * 128:16 + (n + 1) * 128], in_=hs[b].rearrange("p (c t) -> p c t", c=8)),
              reads=[HS], writes=["hT_d"])


def pbf(k, i):
    return k.pf[i].bitcast(BF16)


def load_w_cast(k, dst3, src2d, key, nsplit=8):
    C = dst3.shape[1]
    N = dst3.shape[2]
    step = max(1, 2048 // 1)
    for c in range(C):
        for n0 in range(0, N, 2048):
            n1 = min(N, n0 + 2048)
            k.P.dma("gpsimd", lambda e, c=c, n0=n0, n1=n1: e.dma_start(out=dst3[:, c, n0:n1], in_=src2d[c * 128:(c + 1) * 128, n0:n1]),
                    writes=[key])


def proj_block(k, pbank, pkey, hT, hkey, W, wkey, c0, ncols, shift=0):
    for c in range(8):
        k.P.op("tensor", lambda e, c=c: e.matmul(pbank[:, 0:ncols], lhsT=hT[:, c, shift:shift + 128], rhs=W[:, c, c0:c0 + ncols],
                                                start=(c == 0), stop=(c == 7)),
               reads=[hkey, wkey], writes=[pkey])


def store_yT(k, br, n, ysrc_key, ysrc_bf, tp_bank, l):
    P = k.P
    b = n % 2
    pt = pbf(k, tp_bank)
    PT = "pf%d" % tp_bank
    idb = cview(k, "ident")
    for c in range(4):
        P.op("tensor", lambda e, c=c: e.transpose(out=pt[:, c * 128:(c + 1) * 128], in_=ysrc_bf[:, c * 128:(c + 1) * 128], identity=idb),
             reads=[ysrc_key, "cm_b"], writes=[PT])
    ys = k.ystage[b]
    YS = "ystage%d" % (b if k.ystage[0] is not k.ystage[1] else 0)
    P.op("scalar", lambda e: e.copy(out=ys, in_=pt[:, 0:512]), reads=[PT], writes=[YS])
    P.dma("sync", lambda e: e.dma_start(out=k.yT_d[br][:, :, n * 128:(n + 1) * 128], in_=ys.rearrange("p (c t) -> p c t", c=4)),
          reads=[YS], writes=["yT_d" + br])


def v3(ap, a):
    return ap.rearrange("p (a b) -> p a b", a=a)


def phaseB(k, l):
    P, nc = k.P, k.nc
    new_phase(k)
    fa, ba = k.fa, k.ba
    NT = k.NT
    idb = cview(k, "ident")
    W = v3(ba.alloc(8 * 2048), 8)
    load_w_cast(k, W, k.w_in[l, :, A_COLS:A_COLS + B_COLS], "WB")
    gq = fa.alloc(64)
    gk = fa.alloc(64)
    P.dma("sync", lambda e: e.dma_start(out=gq, in_=k.attn_q_norm[l:l + 1, :].partition_broadcast(128)), writes=["gq"])
    P.dma("sync", lambda e: e.dma_start(out=gk, in_=k.attn_k_norm[l:l + 1, :].partition_broadcast(128)), writes=["gk"])
    P.op("vector", lambda e: e.scalar_tensor_tensor(out=gq, in0=gq, scalar=0.125, in1=gk, op0=ALU.mult, op1=ALU.mult), reads=["gq", "gk"], writes=["gq"])
    bstage = fa.alloc(5120)
    nm = fa.alloc(640)
    biasT = ba.alloc(5120)
    P.dma("sync", lambda e: e.dma_start(out=bstage, in_=k.attn_bias[l, :, :]), writes=["bstage"])
    P.dma("sync", lambda e: e.dma_start(out=nm, in_=k.negm[:, :]), writes=["nm"])
    P.op("vector", lambda e: e.tensor_tensor(out=v3(biasT, 8), in0=v3(bstage, 8), in1=nm.unsqueeze(1).to_broadcast([128, 8, 640]), op=ALU.add),
         reads=["bstage", "nm"], writes=["biasT"])
    bias4 = biasT.rearrange("p (h r q) -> p h r q", h=8, r=5)
    Vr = ba.alloc(8 * 8 * 80).rearrange("p (s h d) -> p s h d", s=8, h=8)
    P.op("gpsimd", lambda e: e.memset(Vr, 1.0), writes=["Vr"])
    kT = ba.alloc(4 * 8 * 128).rearrange("p (c s t) -> p c s t", c=4, s=8)
    qTm = [ba.alloc(8 * 128) for _ in range(2)]
    for b in range(2):
        P.op("gpsimd", lambda e, b=b: e.memset(qTm[b], 0.0), writes=["qTm%d" % b])
    hTt = [v3(ba.alloc(1024), 8) for _ in range(2)]
    sqt = fa.alloc(1024)
    ss16 = fa.alloc(16)
    rs16 = fa.alloc(16)
    qn32 = fa.alloc(512)
    qb = ba.alloc(512)
    kb = ba.alloc(512)
    sg = fa.alloc(512)
    PTb = [ba.alloc(512) for _ in range(3)]
    rinv = fa.alloc(8)
    y32 = fa.alloc(512)
    ygb = ba.alloc(512)
    k.ystage = [ba.alloc(512) for _ in range(2)]
    pf = k.pf
    for n in range(NT):
        b = n % 2
        slot = n % 8
        H = "hTt%d" % b
        P.dma("sync", lambda e, n=n, b=b: e.dma_start(out=hTt[b], in_=k.hT_d[:, :, 16 + n * 128:16 + (n + 1) * 128]), writes=[H])
        for blk in range(4):
            proj_block(k, pf[blk], "pf%d" % blk, hTt[b], H, W, "WB", blk * 512, 512)
        if LVL < 1.1:
            continue
        P.op("scalar", lambda e: e.activation(out=sqt[:, 0:512], in_=pf[0], func=AF.Square), reads=["pf0"], writes=["sqt"])
        P.op("scalar", lambda e: e.activation(out=sqt[:, 512:1024], in_=pf[1], func=AF.Square), reads=["pf1"], writes=["sqt"])
        P.op("vector", lambda e: e.tensor_reduce(out=ss16, in_=v3(sqt, 16), op=ALU.add, axis=AX.X), reads=["sqt"], writes=["ss16"])
        P.op("scalar", lambda e: e.activation(out=rs16, in_=ss16, func=AF.Sqrt, scale=1.0 / 64, bias=1e-6), reads=["ss16"], writes=["rs16"])
        P.op("vector", lambda e: e.reciprocal(out=rs16, in_=rs16), reads=["rs16"], writes=["rs16"])
        if LVL < 1.2:
            continue
        P.op("vector", lambda e: e.tensor_tensor(out=v3(qn32, 8), in0=v3(pf[0], 8), in1=rs16[:, 0:8].unsqueeze(2).to_broadcast([128, 8, 64]), op=ALU.mult),
             reads=["pf0", "rs16"], writes=["qn32"])
        P.op("vector", lambda e: e.tensor_tensor(out=v3(qb, 8), in0=v3(qn32, 8), in1=gq.unsqueeze(1).to_broadcast([128, 8, 64]), op=ALU.mult),
             reads=["qn32", "gq"], writes=["qb"])
        P.op("vector", lambda e: e.tensor_tensor(out=v3(kb, 8), in0=v3(pf[1], 8), in1=rs16[:, 8:16].unsqueeze(2).to_broadcast([128, 8, 64]), op=ALU.mult),
             reads=["pf1", "rs16"], writes=["kb"])
        if LVL < 1.4:
            continue
        pt = pbf(k, 0)
        for c in range(4):
            P.op("tensor", lambda e, c=c: e.transpose(out=pt[:, c * 128:(c + 1) * 128], in_=qb[:, c * 128:(c + 1) * 128], identity=idb),
                 reads=["qb", "cm_b"], writes=["pf0"])
        for c in range(4):
            P.op("tensor", lambda e, c=c: e.transpose(out=pt[:, 512 + c * 128:512 + (c + 1) * 128], in_=kb[:, c * 128:(c + 1) * 128], identity=idb),
                 reads=["kb", "cm_b"], writes=["pf0"])
        if LVL < 1.6:
            continue
        Q = "qTm%d" % b
        q4 = qTm[b].rearrange("p (c u t) -> p c u t", c=4, u=2)
        for u in range(2):
            P.op("scalar", lambda e, u=u, q4=q4: e.copy(out=q4[u * 64:(u + 1) * 64, :, u, :], in_=v3(pt[u * 64:(u + 1) * 64, 0:512], 4)),
                 reads=["pf0"], writes=[Q])
        if LVL < 1.8:
            continue
        P.op("scalar", lambda e, slot=slot: e.copy(out=kT[:, :, slot, :], in_=v3(pt[:, 512:1024], 4)), reads=["pf0"], writes=["kT"])
        if LVL < 1.9:
            continue
        P.op("scalar", lambda e, slot=slot: e.copy(out=Vr[:, slot, :, 0:64], in_=v3(pf[2], 8)), reads=["pf2"], writes=["Vr"])
        if LVL < 1.95:
            continue
        VAR = os.environ.get("VAR", "")
        if VAR == "copy3":
            P.op("scalar", lambda e: e.copy(out=sg, in_=pf[3]), reads=["pf3"], writes=["sg"])
        elif VAR == "sig2":
            P.op("scalar", lambda e: e.activation(out=sg, in_=pf[2], func=AF.Sigmoid), reads=["pf2"], writes=["sg"])
        elif VAR == "dve3":
            P.op("vector", lambda e: e.tensor_copy(out=sg, in_=pf[3]), reads=["pf3"], writes=["sg"])
        else:
            P.op("scalar", lambda e: e.activation(out=sg, in_=pf[3], func=AF.Silu), reads=["pf3"], writes=["sg"])
        if LVL < 3:
            continue
        blocks = [(h, r) for h in range(8) for r in range(5) if n - 4 + r >= 0]
        groups = [blocks[i:i + 4] for i in range(0, len(blocks), 4)]
        first_r = max(0, 4 - n)
        for gi, grp in enumerate(groups):
            bank = 4 + gi % 2
            BK = "pf%d" % bank
            for j, (h, r) in enumerate(grp):
                kslot = (n - 4 + r) % 8
                P.op("tensor", lambda e, j=j, h=h, kslot=kslot, bank=bank: e.matmul(pf[bank][:, j * 128:(j + 1) * 128], lhsT=kT[:, h // 2, kslot, :],
                                                                                   rhs=qTm[b][:, h * 128:(h + 1) * 128], start=True, stop=False),
                     reads=["kT", Q], writes=[BK])
                P.op("tensor", lambda e, j=j, h=h, r=r, bank=bank: e.matmul(pf[bank][:, j * 128:(j + 1) * 128], lhsT=idb, rhs=bias4[:, h, r, :],
                                                                           start=False, stop=True),
                     reads=["biasT", "cm_b"], writes=[BK])
            pb = gi % 3
            PTK = "PT%d" % pb
            ncol = len(grp) * 128
            P.op("scalar", lambda e, bank=bank, pb=pb, ncol=ncol: e.activation(out=PTb[pb][:, 0:ncol], in_=pf[bank][:, 0:ncol], func=AF.Exp),
                 reads=[BK], writes=[PTK])
            for j, (h, r) in enumerate(grp):
                kslot = (n - 4 + r) % 8
                ob = 6 + h // 4
                P.op("tensor", lambda e, j=j, h=h, r=r, kslot=kslot, ob=ob, pb=pb: e.matmul(
                    pf[ob][:, (h % 4) * 65:(h % 4) * 65 + 65], lhsT=PTb[pb][:, j * 128:(j + 1) * 128], rhs=Vr[:, kslot, h, 0:65],
                    start=(r == first_r), stop=(r == 4)), reads=[PTK, "Vr"], writes=["pf%d" % ob])
        if LVL < 4:
            continue
        for hb_ in range(2):
            o3 = v3(pf[6 + hb_][:, 0:260], 4)
            P.op("vector", lambda e, o3=o3, hb_=hb_: e.reciprocal(out=rinv[:, hb_ * 4:(hb_ + 1) * 4].unsqueeze(2), in_=o3[:, :, 64:65]),
                 reads=["pf%d" % (6 + hb_)], writes=["rinv"])
            P.op("vector", lambda e, o3=o3, hb_=hb_: e.tensor_tensor(out=v3(y32[:, hb_ * 256:(hb_ + 1) * 256], 4), in0=o3[:, :, 0:64],
                                                                    in1=rinv[:, hb_ * 4:(hb_ + 1) * 4].unsqueeze(2).to_broadcast([128, 4, 64]), op=ALU.mult),
                 reads=["pf%d" % (6 + hb_), "rinv"], writes=["y32"])
        if k.debug:
            P.dma("sync", lambda e, n=n: e.dma_start(out=k.dbg["b"][n * 128:(n + 1) * 128, :], in_=y32), reads=["y32"])
        P.op("vector", lambda e: e.tensor_tensor(out=ygb, in0=y32, in1=sg, op=ALU.mult), reads=["y32", "sg"], writes=["ygb"])
        store_yT(k, "b", n, "ygb", ygb, 3, l)


def phaseM(k, l, xin, xout):
    P, nc = k.P, k.nc
    new_phase(k)
    fa, ba = k.fa, k.ba
    brs = [b for b in "abc" if b in k.branches]
    projs = {"a": k.proj_a, "b": k.proj_b, "c": k.proj_c}
    Wz, Wp = {}, {}
    for bi, br in enumerate("abc"):
        if br not in brs:
            continue
        Wz[br] = v3(ba.alloc(8 * 1024), 8)
        load_w_cast(k, Wz[br], k.w_in[l, :, 6272 + bi * 1024:6272 + (bi + 1) * 1024], "Wz" + br)
        Wp[br] = v3(ba.alloc(4 * 1024), 4)
        load_w_cast(k, Wp[br], projs[br][l, :, :], "Wp" + br)
    Wo = v3(ba.alloc(8 * 1024), 8)
    load_w_cast(k, Wo, k.w_out[l, :, :], "Wo")
    TB = 512
    NB = k.T // TB
    hTb = [v3(ba.alloc(8 * TB), 8) for _ in range(2)]
    yTb = {br: [v3(ba.alloc(4 * TB), 4) for _ in range(2)] for br in brs}
    mT = v3(ba.alloc(8 * TB), 8)
    gsb = {br: fa.alloc(TB) for br in brs}
    acc = fa.alloc(TB)
    tmp = fa.alloc(TB)
    xt = [fa.alloc(1024) for _ in range(2)]
    ot = [fa.alloc(1024) for _ in range(2)]
    pf = k.pf
    for tb in range(NB):
        b = tb % 2
        H = "hTb%d" % b
        P.dma("sync", lambda e, tb=tb, b=b: e.dma_start(out=hTb[b], in_=k.hT_d[:, :, 16 + tb * TB:16 + (tb + 1) * TB]), writes=[H])
        for br in brs:
            P.dma("sync", lambda e, tb=tb, b=b, br=br: e.dma_start(out=yTb[br][b], in_=k.yT_d[br][:, :, tb * TB:(tb + 1) * TB]),
                  writes=["yTb%s%d" % (br, b)])
        for fc in range(8):
            for bi, br in enumerate(brs):
                zb, pb_ = bi, 3 + bi
                for c in range(8):
                    P.op("tensor", lambda e, c=c, br=br, zb=zb: e.matmul(pf[zb], lhsT=Wz[br][:, c, fc * 128:(fc + 1) * 128], rhs=hTb[b][:, c, :],
                                                                        start=(c == 0), stop=(c == 7)), reads=["Wz" + br, H], writes=["pf%d" % zb])
                P.op("scalar", lambda e, br=br, zb=zb: e.activation(out=gsb[br], in_=pf[zb], func=AF.Sigmoid), reads=["pf%d" % zb], writes=["gsb" + br])
                for c in range(4):
                    P.op("tensor", lambda e, c=c, br=br, pb_=pb_: e.matmul(pf[pb_], lhsT=Wp[br][:, c, fc * 128:(fc + 1) * 128], rhs=yTb[br][b][:, c, :],
                                                                          start=(c == 0), stop=(c == 3)),
                         reads=["Wp" + br, "yTb%s%d" % (br, b)], writes=["pf%d" % pb_])
            for bi, br in enumerate(brs):
                pb_ = 3 + bi
                last = bi == len(brs) - 1
                if bi == 0:
                    dst = mT[:, fc, :] if last else acc
                    P.op("vector", lambda e, br=br, pb_=pb_, dst=dst: e.tensor_tensor(out=dst, in0=pf[pb_], in1=gsb[br], op=ALU.mult),
                         reads=["pf%d" % pb_, "gsb" + br], writes=["mT" if last else "acc"])
                else:
                    P.op("vector", lambda e, br=br, pb_=pb_: e.tensor_tensor(out=tmp, in0=pf[pb_], in1=gsb[br], op=ALU.mult),
                         reads=["pf%d" % pb_, "gsb" + br], writes=["tmp"])
                    dst = mT[:, fc, :] if last else acc
                    P.op("vector", lambda e, dst=dst: e.tensor_tensor(out=dst, in0=acc, in1=tmp, op=ALU.add),
                         reads=["acc", "tmp"], writes=["mT" if last else "acc"])
        for tt in range(TB // 128):
            n = tb * (TB // 128) + tt
            xb = n % 2
            X, O = "xm%d" % xb, "om%d" % xb
            P.dma("sync", lambda e, n=n, xb=xb: e.dma_start(out=xt[xb], in_=xin[n * 128:(n + 1) * 128, :]), writes=[X])
            for cb in range(2):
                bank = 6 + cb
                for c in range(8):
                    P.op("tensor", lambda e, c=c, cb=cb, bank=bank, tt=tt: e.matmul(pf[bank], lhsT=mT[:, c, tt * 128:(tt + 1) * 128],
                                                                                   rhs=Wo[:, c, cb * 512:(cb + 1) * 512], start=(c == 0), stop=(c == 7)),
                         reads=["mT", "Wo"], writes=["pf%d" % bank])
                P.op("vector", lambda e, cb=cb, bank=bank, xb=xb: e.tensor_tensor(out=ot[xb][:, cb * 512:(cb + 1) * 512], in0=pf[bank],
                                                                                 in1=xt[xb][:, cb * 512:(cb + 1) * 512], op=ALU.add),
                     reads=["pf%d" % bank, X], writes=[O])
            P.dma("sync", lambda e, n=n, xb=xb: e.dma_start(out=xout[n * 128:(n + 1) * 128, :], in_=ot[xb]), reads=[O], writes=["xout"])


_NC_CACHE = {}


def make_in_maps(inputs, T, L, nb):
    f = lambda a: np.ascontiguousarray(np.asarray(a, dtype=np.float32))
    shared = {
        "norm_g": f(inputs["norm_g"])[:L], "w_in": f(inputs["w_in"])[:L], "rwkv_mu": f(inputs["rwkv_mu"])[:L],
        "rwkv_w0": f(inputs["rwkv_w0"])[:L], "rwkv_w2": f(inputs["rwkv_w2"])[:L], "rwkv_a0": f(inputs["rwkv_a0"])[:L],
        "rwkv_a2": f(inputs["rwkv_a2"])[:L], "rwkv_k_k": f(inputs["rwkv_k_k"])[:L], "rwkv_k_a": f(inputs["rwkv_k_a"])[:L],
        "rwkv_r_k": f(inputs["rwkv_r_k"])[:L].reshape(L, 512), "rwkv_ln_w": f(inputs["rwkv_ln_w"])[:L],
        "rwkv_ln_b": f(inputs["rwkv_ln_b"])[:L], "attn_q_norm": f(inputs["attn_q_norm"])[:L],
        "attn_k_norm": f(inputs["attn_k_norm"])[:L],
        "attn_bias": _bias_gather(f(inputs["attn_rel_bias"])[:L]).reshape(L, 128, 8 * 5 * 128),
        "hgrn_lb": f(inputs["hgrn_lb"])[:L], "hgrn_norm": f(inputs["hgrn_norm"])[:L],
        "proj_a": f(inputs["proj_a"])[:L], "proj_b": f(inputs["proj_b"])[:L], "proj_c": f(inputs["proj_c"])[:L],
        "w_out": f(inputs["w_out"])[:L],
        "cmat": np.ascontiguousarray(CARR.reshape(128, 7 * 128)), "chsel": CHSEL,
        "negm": np.ascontiguousarray(NEGM.reshape(128, 640)),
    }
    x = f(inputs["x"])
    maps = []
    for b in range(nb):
        m = dict(shared)
        m["x"] = np.ascontiguousarray(x[b, :T])
        maps.append(m)
    return maps


def kernel(**inputs):
    T, L = 4096, 2
    key = (T, L)
    if key not in _NC_CACHE:
        _NC_CACHE[key] = build_nc(T, L)
    nc = _NC_CACHE[key]
    maps = make_in_maps(inputs, T, L, 8)
    res = run_bass_kernel_spmd(nc, maps, core_ids=list(range(8)))
    return np.stack([r["out"] for r in res.results], axis=0).astype(np.float32)


def phaseC(k, l):
    P, nc = k.P, k.nc
    new_phase(k)
    fa, ba = k.fa, k.ba
    NT, L = k.NT, k.L
    pf = k.pf
    idb = cview(k, "ident")
    V_ = lambda fn, r, w: P.op("vector", fn, r, w)
    S_ = lambda fn, r, w: P.op("scalar", fn, r, w)
    G_ = lambda fn, r, w: P.op("gpsimd", fn, r, w)
    T_ = lambda fn, r, w: P.op("tensor", fn, r, w)
    W = v3(ba.alloc(8 * 2048), 8)
    load_w_cast(k, W, k.w_in[l, :, A_COLS + B_COLS:A_COLS + B_COLS + C_COLS], "WC")
    lbb = fa.alloc(512)
    oml = fa.alloc(512)
    if l == 0:
        V_(lambda e: e.memset(lbb, 0.0), [], ["lbb"])
    else:
        er = fa.alloc(L * 512)
        P.dma("sync", lambda e: e.dma_start(out=er, in_=k.hgrn_lb.rearrange("l c -> (l c)").unsqueeze(0).partition_broadcast(128).squeeze(1)
                                            if False else k.hgrn_lb.rearrange("(o l) c -> o (l c)", o=1).partition_broadcast(128)), writes=["er"])
        S_(lambda e: e.activation(out=er, in_=er, func=AF.Exp), ["er"], ["er"])
        ssum = fa.alloc(512)
        V_(lambda e: e.tensor_tensor(out=ssum, in0=er[:, 0:512], in1=er[:, 512:1024], op=ALU.add), ["er"], ["ssum"])
        for j in range(2, L):
            V_(lambda e: e.tensor_tensor(out=ssum, in0=ssum, in1=er[:, j * 512:(j + 1) * 512], op=ALU.add), ["er", "ssum"], ["ssum"])
        V_(lambda e: e.tensor_copy(out=lbb, in_=er[:, 512:1024]), ["er"], ["lbb"])
        for j in range(2, l + 1):
            V_(lambda e: e.tensor_tensor(out=lbb, in0=lbb, in1=er[:, j * 512:(j + 1) * 512], op=ALU.add), ["er", "lbb"], ["lbb"])
        V_(lambda e: e.reciprocal(out=ssum, in_=ssum), ["ssum"], ["ssum"])
        V_(lambda e: e.tensor_tensor(out=lbb, in0=lbb, in1=ssum, op=ALU.mult), ["lbb", "ssum"], ["lbb"])
    V_(lambda e: e.tensor_scalar(out=oml, in0=lbb, scalar1=-1.0, scalar2=1.0, op0=ALU.mult, op1=ALU.add), ["lbb"], ["oml"])
    gn = fa.alloc(64)
    P.dma("sync", lambda e: e.dma_start(out=gn, in_=k.hgrn_norm[l:l + 1, :].partition_broadcast(128)), writes=["gn"])
    tri, chm, midm, miu = cview(k, "tri", False), cview(k, "ch", False), cview(k, "midm", False), cview(k, "miu", False)
    Sf = fa.alloc(256)
    V_(lambda e: e.memset(Sf, 0.0), [], ["Sf"])
    Sb = [ba.alloc(256) for _ in range(3)]
    G_(lambda e: e.memset(Sb[0], 0.0), [], ["Sb0"])
    QpTm = [ba.alloc(2048) for _ in range(2)]
    QTm = [ba.alloc(1024) for _ in range(2)]
    Kh = [ba.alloc(2048) for _ in range(2)]
    for b in range(2):
        G_(lambda e: e.memset(QpTm[b], 0.0), [], ["QpTm%d" % b])
        G_(lambda e: e.memset(QTm[b], 0.0), [], ["QTm%d" % b])
        G_(lambda e: e.memset(Kh[b], 0.0), [], ["Kh%d" % b])
    hTt = [v3(ba.alloc(1024), 8) for _ in range(2)]
    sgm, sgn, fg, key, logf, qh, eb = [fa.alloc(512) for _ in range(7)]
    bm, be, d1, e1 = [fa.alloc(512) for _ in range(4)]
    Qp, Qt, Kt, Vb = [ba.alloc(512) for _ in range(4)]
    KT = ba.alloc(512)
    attm = ba.alloc(1024)
    gS = fa.alloc(8)
    sq = fa.alloc(512)
    ss8 = fa.alloc(8)
    rs8 = fa.alloc(8)
    o32 = fa.alloc(512)
    gate = fa.alloc(512)
    ygb = ba.alloc(512)
    k.ystage = [ba.alloc(512) for _ in range(2)]
    sbi = 0
    for n in range(NT):
        b = n % 2
        H = "hTt%d" % b
        P.dma("sync", lambda e: e.dma_start(out=hTt[b], in_=k.hT_d[:, :, 16 + n * 128:16 + (n + 1) * 128]), writes=[H])
        for blk in range(4):
            proj_block(k, pf[blk], "pf%d" % blk, hTt[b], H, W, "WC", blk * 512, 512)
        S_(lambda e: e.activation(out=sgm, in_=pf[1], func=AF.Sigmoid), ["pf1"], ["sgm"])
        S_(lambda e: e.activation(out=sgn, in_=pf[1], func=AF.Sigmoid, scale=-1.0), ["pf1"], ["sgn"])
        V_(lambda e: e.tensor_tensor(out=fg, in0=sgm, in1=oml, op=ALU.mult), ["sgm", "oml"], ["fg"])
        V_(lambda e: e.tensor_tensor(out=fg, in0=fg, in1=lbb, op=ALU.add), ["fg", "lbb"], ["fg"])
        G_(lambda e: e.tensor_tensor(out=key, in0=sgn, in1=oml, op=ALU.mult), ["sgn", "oml"], ["key"])
        S_(lambda e: e.activation(out=logf, in_=fg, func=AF.Ln), ["fg"], ["logf"])
        for bank, m in ((4, tri), (5, midm), (6, chm)):
            T_(lambda e: e.matmul(pf[bank], lhsT=m, rhs=logf, start=True, stop=True), ["logf", "cm_f"], ["pf%d" % bank])
        S_(lambda e: e.copy(out=Vb, in_=pf[2]), ["pf2"], ["Vb"])
        S_(lambda e: e.activation(out=qh, in_=pf[0], func=AF.Silu), ["pf0"], ["qh"])
        S_(lambda e: e.activation(out=gate, in_=pf[3], func=AF.Silu), ["pf3"], ["gate"])
        S_(lambda e: e.activation(out=eb, in_=pf[4], func=AF.Exp), ["pf4"], ["eb"])
        S_(lambda e: e.copy(out=bm, in_=pf[5]), ["pf5"], ["bm"])
        S_(lambda e: e.copy(out=be, in_=pf[6]), ["pf6"], ["be"])
        for p in range(4):
            T_(lambda e: e.matmul(pf[5][:, 256 + p * 2:256 + p * 2 + 2], lhsT=logf[:, p * 128:(p + 1) * 128], rhs=k.chs, start=True, stop=True),
               ["logf", "chs"], ["pf5"])
        S_(lambda e: e.activation(out=gS, in_=pf[5][:, 256:264], func=AF.Exp), ["pf5"], ["gS"])
        V_(lambda e: e.tensor_tensor(out=Qp, in0=qh, in1=eb, op=ALU.mult), ["qh", "eb"], ["Qp"])
        V_(lambda e: e.tensor_tensor(out=d1, in0=pf[4], in1=bm, op=ALU.subtract), ["pf4", "bm"], ["d1"])
        S_(lambda e: e.activation(out=e1, in_=d1, func=AF.Exp), ["d1"], ["e1"])
        V_(lambda e: e.tensor_tensor(out=Qt, in0=qh, in1=e1, op=ALU.mult), ["qh", "e1"], ["Qt"])
        S_(lambda e: e.activation(out=e1, in_=d1, func=AF.Exp, scale=-1.0), ["d1", "Qt"], ["e1"])
        V_(lambda e: e.tensor_tensor(out=Kt, in0=key, in1=e1, op=ALU.mult), ["key", "e1"], ["Kt"])
        V_(lambda e: e.tensor_tensor(out=d1, in0=be, in1=pf[4], op=ALU.subtract), ["pf4", "be", "e1"], ["d1"])
        S_(lambda e: e.activation(out=e1, in_=d1, func=AF.Exp), ["d1", "Kt"], ["e1"])
        KH = "Kh%d" % b
        kh5 = Kh[b].rearrange("q (c u p d) -> q c u p d", c=2, u=2, p=4)
        for c in range(2):
            for u in range(2):
                rows = slice(c * 64, (c + 1) * 64)
                V_(lambda e: e.tensor_tensor(out=kh5[rows, c, u, :, u * 64:(u + 1) * 64] if False else
                                             Kh[b].rearrange("q (c u p w) -> q c u p w", c=2, u=2, p=4)[rows, c, u, :, u * 64:(u + 1) * 64],
                                             in0=v3(key, 4)[rows, :, u * 64:(u + 1) * 64], in1=v3(e1, 4)[rows, :, u * 64:(u + 1) * 64], op=ALU.mult),
                   ["key", "e1"], [KH])
        p7 = pbf(k, 7)
        p5 = pbf(k, 5)
        for c in range(4):
            T_(lambda e: e.transpose(out=p7[:, c * 128:(c + 1) * 128], in_=Qp[:, c * 128:(c + 1) * 128], identity=idb), ["Qp", "cm_b"], ["pf7"])
        for c in range(4):
            T_(lambda e: e.transpose(out=p7[:, 512 + c * 128:512 + (c + 1) * 128], in_=Qt[:, c * 128:(c + 1) * 128], identity=idb), ["Qt", "cm_b"], ["pf7"])
        for c in range(4):
            T_(lambda e: e.transpose(out=p5[:, c * 128:(c + 1) * 128], in_=Kt[:, c * 128:(c + 1) * 128], identity=idb), ["Kt", "cm_b"], ["pf5"])
        QP, QT = "QpTm%d" % b, "QTm%d" % b
        qp6 = QpTm[b].rearrange("q (p u c t) -> q p u c t", p=4, u=2, c=2)
        qt4 = QTm[b].rearrange("q (p u t) -> q p u t", p=4, u=2)
        for u in range(2):
            rows = slice(u * 64, (u + 1) * 64)
            for c in range(2):
                S_(lambda e: e.copy(out=qp6[rows, :, u, c, c * 64:(c + 1) * 64], in_=v3(p7[rows, 0:512], 4)[:, :, c * 64:(c + 1) * 64]), ["pf7"], [QP])
            S_(lambda e: e.copy(out=qt4[rows, :, u, :], in_=v3(p7[rows, 512:1024], 4)), ["pf7"], [QT])
        S_(lambda e: e.copy(out=KT, in_=p5[:, 0:512]), ["pf5"], ["KT"])
        for h in range(8):
            bank = 4 if h < 4 else 6
            T_(lambda e: e.matmul(pf[bank][:, (h % 4) * 128:(h % 4 + 1) * 128], lhsT=KT[:, (h // 2) * 128:(h // 2 + 1) * 128],
                                  rhs=QTm[b][:, h * 128:(h + 1) * 128], start=True, stop=True), ["KT", QT], ["pf%d" % bank])
        for hb_ in range(2):
            bank = 4 if hb_ == 0 else 6
            V_(lambda e: e.tensor_tensor(out=v3(attm[:, hb_ * 512:(hb_ + 1) * 512], 4), in0=v3(pf[bank], 4),
                                         in1=miu.unsqueeze(1).to_broadcast([128, 4, 128]), op=ALU.mult), ["pf%d" % bank, "cm_f"], ["attm"])
        for c in range(2):
            for p in range(4):
                for u in range(2):
                    h = 2 * p + u
                    T_(lambda e: e.matmul(pf[1][:, c * 256 + p * 64:c * 256 + (p + 1) * 64],
                                          lhsT=Kh[b][:, (c * 2 + u) * 512 + p * 128:(c * 2 + u) * 512 + (p + 1) * 128],
                                          rhs=Vb[:, h * 64:(h + 1) * 64], start=(u == 0), stop=(u == 1)), [KH, "Vb"], ["pf1"])
        sb_in = sbi
        for c in range(2):
            V_(lambda e: e.tensor_tensor(out=v3(Sf, 4), in0=v3(Sf, 4), in1=gS[:, c:8:2].unsqueeze(2).to_broadcast([128, 4, 64]), op=ALU.mult),
               ["Sf", "gS"], ["Sf"])
            V_(lambda e: e.tensor_tensor(out=Sf, in0=Sf, in1=pf[1][:, c * 256:(c + 1) * 256], op=ALU.add), ["Sf", "pf1"], ["Sf"])
            sbi = (sbi + 1) % 3
            S_(lambda e: e.copy(out=Sb[sbi], in_=Sf), ["Sf"], ["Sb%d" % sbi])
        sb0, sb1 = sb_in, (sb_in + 1) % 3
        for h in range(8):
            p = h // 2
            for c, sbx in ((0, sb0), (1, sb1)):
                T_(lambda e: e.matmul(pf[0][:, h * 64:(h + 1) * 64], lhsT=QpTm[b][:, ((p * 2 + h % 2) * 2 + c) * 128:((p * 2 + h % 2) * 2 + c + 1) * 128],
                                      rhs=Sb[sbx][:, p * 64:(p + 1) * 64], start=(c == 0), stop=False), [QP, "Sb%d" % sbx], ["pf0"])
            T_(lambda e: e.matmul(pf[0][:, h * 64:(h + 1) * 64], lhsT=attm[:, h * 128:(h + 1) * 128], rhs=Vb[:, h * 64:(h + 1) * 64],
                                  start=False, stop=True), ["attm", "Vb"], ["pf0"])
        S_(lambda e: e.activation(out=sq, in_=pf[0], func=AF.Square), ["pf0"], ["sq"])
        V_(lambda e: e.tensor_reduce(out=ss8, in_=v3(sq, 8), op=ALU.add, axis=AX.X), ["sq"], ["ss8"])
        S_(lambda e: e.activation(out=rs8, in_=ss8, func=AF.Sqrt, scale=1.0 / 64, bias=1e-6), ["ss8"], ["rs8"])
        V_(lambda e: e.reciprocal(out=rs8, in_=rs8), ["rs8"], ["rs8"])
        V_(lambda e: e.tensor_tensor(out=v3(o32, 8), in0=v3(pf[0], 8), in1=rs8.unsqueeze(2).to_broadcast([128, 8, 64]), op=ALU.mult), ["pf0", "rs8"], ["o32"])
        V_(lambda e: e.tensor_tensor(out=v3(o32, 8), in0=v3(o32, 8), in1=gn.unsqueeze(1).to_broadcast([128, 8, 64]), op=ALU.mult), ["o32", "gn"], ["o32"])
        if k.debug:
            P.dma("sync", lambda e: e.dma_start(out=k.dbg["c"][n * 128:(n + 1) * 128, :], in_=o32), reads=["o32"])
        V_(lambda e: e.tensor_tensor(out=ygb, in0=o32, in1=gate, op=ALU.mult), ["o32", "gate"], ["ygb"])
        store_yT(k, "c", n, "ygb", ygb, 3, l)


def phaseA(k, l):
    P, nc = k.P, k.nc
    new_phase(k)
    fa, ba = k.fa, k.ba
    NT = k.NT
    pf = k.pf
    idb = cview(k, "ident")
    C0 = float(np.exp(-0.5))
    V_ = lambda fn, r, w: P.op("vector", fn, r, w)
    S_ = lambda fn, r, w: P.op("scalar", fn, r, w)
    G_ = lambda fn, r, w: P.op("gpsimd", fn, r, w)
    T_ = lambda fn, r, w: P.op("tensor", fn, r, w)
    bc = lambda src: src.partition_broadcast(128)
    W1 = v3(ba.alloc(8 * A_COLS), 8)
    W2 = v3(ba.alloc(8 * A_COLS), 8)
    w2a2 = ba.alloc(1024)
    vecs = {}
    for nm, src in (("w0b", k.rwkv_w0), ("a0b", k.rwkv_a0), ("kkb", k.rwkv_k_k), ("kab", k.rwkv_k_a), ("rkb", k.rwkv_r_k),
                    ("lnw", k.rwkv_ln_w), ("lnb", k.rwkv_ln_b)):
        vecs[nm] = fa.alloc(512)
        P.dma("sync", lambda e: e.dma_start(out=vecs[nm], in_=bc(src[l:l + 1, :])), writes=[nm])
    G_(lambda e: e.memset(w2a2, 0.0), [], ["w2a2"])
    P.dma("gpsimd", lambda e: e.dma_start(out=w2a2[0:64, 0:512], in_=k.rwkv_w2[l, :, :]), writes=["w2a2"])
    P.dma("gpsimd", lambda e: e.dma_start(out=w2a2[64:128, 512:1024], in_=k.rwkv_a2[l, :, :]), writes=["w2a2"])
    keep = k.fa.a.off
    mu_b = fa.alloc(A_COLS)
    omu = fa.alloc(A_COLS)
    stage = fa.alloc(A_COLS)
    P.dma("sync", lambda e: e.dma_start(out=mu_b, in_=bc(k.rwkv_mu[l:l + 1, :])), writes=["mu_b"])
    V_(lambda e: e.tensor_scalar(out=omu, in0=mu_b, scalar1=-1.0, scalar2=1.0, op0=ALU.mult, op1=ALU.add), ["mu_b"], ["omu"])
    for c in range(8):
        P.dma("sync", lambda e: e.dma_start(out=stage, in_=k.w_in[l, c * 128:(c + 1) * 128, 0:A_COLS]), writes=["stage"])
        V_(lambda e: e.tensor_tensor(out=W1[:, c, :], in0=stage, in1=omu, op=ALU.mult), ["stage", "omu"], ["W1"])
        G_(lambda e: e.tensor_tensor(out=W2[:, c, :], in0=stage, in1=mu_b, op=ALU.mult), ["stage", "mu_b"], ["W2"])
    P.barrier()
    k.fa.a.off = keep
    tri, chm = cview(k, "tri", False), cview(k, "ch", False)
    msu, msl, miu = cview(k, "msu", False), cview(k, "msl", False), cview(k, "miu", False)
    idf = cview(k, "ident", False)
    STf = fa.alloc(256)
    V_(lambda e: e.memset(STf, 0.0), [], ["STf"])
    STb = [ba.alloc(256) for _ in range(3)]
    G_(lambda e: e.memset(STb[0], 0.0), [], ["STb0"])
    Uall = ba.alloc(512)
    G_(lambda e: e.memset(Uall, 0.0), [], ["Uall"])
    msk = {}
    for nm, sz in (("ATm", 1024), ("BTm", 1024), ("KTm", 1024), ("RTc", 2048), ("M1m", 2048), ("Atm", 1024), ("Bh", 2048), ("Kh", 2048)):
        msk[nm] = ba.alloc(sz)
        G_(lambda e: e.memset(msk[nm], 0.0), [], [nm])
    hTc = v3(ba.alloc(1024), 8)
    hTp = v3(ba.alloc(1024), 8)
    F = [fa.alloc(512) for _ in range(11)]
    FK = ["F%d" % i for i in range(11)]
    lor = ba.alloc(128)
    lorT = ba.alloc(128)
    Rt, At, Bt, Kt, Vb = [ba.alloc(512) for _ in range(5)]
    PA, QA, PB, QB, XI, Aak, ArbT, ArkT = [ba.alloc(1024) for _ in range(8)]
    gS, ss8, rk8, m8, q8, r8 = [fa.alloc(8) for _ in range(6)]
    k.ystage = [ba.alloc(512)] * 2
    cols = {"r": 0, "wl": 512, "k": 576, "v": 1088, "al": 1600, "g": 1664}
    h3 = lambda ap: v3(ap, 8)
    b864 = lambda ap: ap.unsqueeze(2).to_broadcast([128, 8, 64])
    sbi = 0

    def evac_masked(dst, dkey, src_bf, skey, chunked):
        if chunked:
            d6 = dst.rearrange("q (p u c t) -> q p u c t", p=4, u=2, c=2)
        else:
            d4 = dst.rearrange("q (p u t) -> q p u t", p=4, u=2)
        for u in range(2):
            rows = slice(u * 64, (u + 1) * 64)
            if chunked:
                for c in range(2):
                    S_(lambda e: e.copy(out=d6[rows, :, u, c, c * 64:(c + 1) * 64], in_=v3(src_bf[rows, :], 4)[:, :, c * 64:(c + 1) * 64]), [skey], [dkey])
            else:
                S_(lambda e: e.copy(out=d4[rows, :, u, :], in_=v3(src_bf[rows, :], 4)), [skey], [dkey])

    def headmm(bankA, bankB, lhs, lkey, rhs, rkey):
        for h in range(8):
            bank = bankA if h < 4 else bankB
            T_(lambda e: e.matmul(pf[bank][:, (h % 4) * 128:(h % 4 + 1) * 128], lhsT=lhs[:, h * 128:(h + 1) * 128], rhs=rhs[:, h * 128:(h + 1) * 128],
                                  start=True, stop=True), [lkey, rkey], ["pf%d" % bank])

    def headmm_rc(bankA, bankB, lhs, lkey):
        for h in range(8):
            bank = bankA if h < 4 else bankB
            for c in range(2):
                o0 = (h % 4) * 128 + c * 64
                r0 = (h * 2 + c) * 128 + c * 64
                T_(lambda e: e.matmul(pf[bank][:, o0:o0 + 64], lhsT=lhs[:, h * 128:(h + 1) * 128], rhs=msk["RTc"][:, r0:r0 + 64],
                                      start=True, stop=True), [lkey, "RTc"], ["pf%d" % bank])

    def evac_mask(dst, dkey, bankA, bankB, mask):
        for i, bank in enumerate((bankA, bankB)):
            if mask is None:
                S_(lambda e: e.copy(out=dst[:, i * 512:(i + 1) * 512], in_=pf[bank]), ["pf%d" % bank], [dkey])
            else:
                V_(lambda e: e.tensor_tensor(out=v3(dst[:, i * 512:(i + 1) * 512], 4), in0=v3(pf[bank], 4), in1=mask.unsqueeze(1).to_broadcast([128, 4, 128]),
                                             op=ALU.mult), ["pf%d" % bank, "cm_f"], [dkey])

    for n in range(NT):
        P.dma("sync", lambda e: e.dma_start(out=hTc, in_=k.hT_d[:, :, 16 + n * 128:16 + (n + 1) * 128]), writes=["hTc"])
        P.dma("sync", lambda e: e.dma_start(out=hTp, in_=k.hT_d[:, :, 15 + n * 128:15 + (n + 1) * 128]), writes=["hTp"])

        def proj(out_ap, okey, c0, ncols):
            for c in range(8):
                T_(lambda e: e.matmul(out_ap, lhsT=hTc[:, c, :], rhs=W1[:, c, c0:c0 + ncols], start=(c == 0), stop=False), ["hTc", "W1"], [okey])
                T_(lambda e: e.matmul(out_ap, lhsT=hTp[:, c, :], rhs=W2[:, c, c0:c0 + ncols], start=False, stop=(c == 7)), ["hTp", "W2"], [okey])
        proj(pf[0], "pf0", cols["r"], 512)
        proj(pf[1], "pf1", cols["k"], 512)
        proj(pf[2], "pf2", cols["v"], 512)
        proj(pf[3], "pf3", cols["g"], 512)
        proj(pf[4][:, 0:64], "pf4", cols["wl"], 64)
        proj(pf[4][:, 64:128], "pf4", cols["al"], 64)
        p5b, p6b = pbf(k, 5), pbf(k, 6)
        S_(lambda e: e.activation(out=lor[:, 0:64], in_=pf[4][:, 0:64], func=AF.Tanh), ["pf4"], ["lor"])
        S_(lambda e: e.copy(out=lor[:, 64:128], in_=pf[4][:, 64:128]), ["pf4"], ["lor"])
        T_(lambda e: e.transpose(out=p5b[:, 0:128], in_=lor, identity=idb), ["lor", "cm_b"], ["pf5"])
        S_(lambda e: e.copy(out=lorT, in_=p5b[:, 0:128]), ["pf5"], ["lorT"])
        T_(lambda e: e.matmul(pf[5], lhsT=lorT, rhs=w2a2[:, 0:512], start=True, stop=True), ["lorT", "w2a2"], ["pf5"])
        T_(lambda e: e.matmul(pf[6], lhsT=lorT, rhs=w2a2[:, 512:1024], start=True, stop=True), ["lorT", "w2a2"], ["pf6"])
        V_(lambda e: e.tensor_tensor(out=F[1], in0=pf[5], in1=vecs["w0b"], op=ALU.add), ["pf5", "w0b"], ["F1"])
        S_(lambda e: e.activation(out=F[1], in_=F[1], func=AF.Sigmoid), ["F1"], ["F1"])
        V_(lambda e: e.tensor_tensor(out=F[4], in0=pf[6], in1=vecs["a0b"], op=ALU.add), ["pf6", "a0b"], ["F4"])
        S_(lambda e: e.activation(out=F[2], in_=F[4], func=AF.Sigmoid), ["F4"], ["F2"])
        S_(lambda e: e.copy(out=F[0], in_=pf[2]), ["pf2"], ["F0"])
        S_(lambda e: e.copy(out=Vb, in_=pf[2]), ["pf2"], ["Vb"])
        S_(lambda e: e.activation(out=F[10], in_=pf[3], func=AF.Silu), ["pf3"], ["F10"])
        T_(lambda e: e.matmul(pf[5], lhsT=tri, rhs=F[1], start=True, stop=True), ["F1", "cm_f"], ["pf5"])
        T_(lambda e: e.matmul(pf[6], lhsT=chm, rhs=F[1], start=True, stop=True), ["F1", "cm_f"], ["pf6"])
        for p in range(4):
            T_(lambda e: e.matmul(pf[7][:, p * 2:p * 2 + 2], lhsT=F[1][:, p * 128:(p + 1) * 128], rhs=k.chs, start=True, stop=True), ["F1", "chs"], ["pf7"])
        S_(lambda e: e.activation(out=gS, in_=pf[7][:, 0:8], func=AF.Exp, scale=-C0), ["pf7"], ["gS"])
        S_(lambda e: e.copy(out=F[3], in_=pf[5]), ["pf5"], ["F3"])
        V_(lambda e: e.tensor_tensor(out=F[4], in0=pf[5], in1=F[1], op=ALU.subtract), ["pf5", "F1"], ["F4"])
        S_(lambda e: e.activation(out=F[5], in_=F[3], func=AF.Exp, scale=-C0), ["F3"], ["F5"])
        V_(lambda e: e.tensor_tensor(out=Rt, in0=pf[0], in1=F[5], op=ALU.mult), ["pf0", "F5"], ["Rt"])
        S_(lambda e: e.activation(out=F[6], in_=F[4], func=AF.Exp, scale=-C0), ["F4"], ["F6"])
        V_(lambda e: e.tensor_tensor(out=F[7], in0=pf[1], in1=vecs["kkb"], op=ALU.mult), ["pf1", "kkb"], ["F7"])
        S_(lambda e: e.activation(out=F[9], in_=F[7], func=AF.Square), ["F7"], ["F9"])
        V_(lambda e: e.tensor_reduce(out=ss8, in_=h3(F[9]), op=ALU.add, axis=AX.X), ["F9"], ["ss8"])
        S_(lambda e: e.activation(out=ss8, in_=ss8, func=AF.Sqrt), ["ss8"], ["ss8"])
        V_(lambda e: e.tensor_scalar(out=ss8, in0=ss8, scalar1=1e-12, scalar2=0.0, op0=ALU.max, op1=ALU.add), ["ss8"], ["ss8"])
        V_(lambda e: e.reciprocal(out=ss8, in_=ss8), ["ss8"], ["ss8"])
        V_(lambda e: e.tensor_tensor(out=h3(F[7]), in0=h3(F[7]), in1=b864(ss8), op=ALU.mult), ["F7", "ss8"], ["F7"])
        V_(lambda e: e.scalar_tensor_tensor(out=At, in0=F[7], scalar=-1.0, in1=F[6], op0=ALU.mult, op1=ALU.mult), ["F7", "F6"], ["At"])
        atm4 = msk["Atm"].rearrange("q (u p w) -> q u p w", u=2, p=4)
        for u in range(2):
            G_(lambda e: e.tensor_copy(out=atm4[:, u, :, u * 64:(u + 1) * 64], in_=v3(At, 4)[:, :, u * 64:(u + 1) * 64]), ["At"], ["Atm"])
        V_(lambda e: e.tensor_tensor(out=F[4], in0=pf[6], in1=F[3], op=ALU.subtract), ["pf6", "F3", "F6"], ["F4"])
        S_(lambda e: e.activation(out=F[5], in_=F[3], func=AF.Exp, scale=C0), ["F3", "Rt"], ["F5"])
        S_(lambda e: e.activation(out=F[6], in_=F[4], func=AF.Exp, scale=-C0), ["F4", "At"], ["F6"])
        V_(lambda e: e.scalar_tensor_tensor(out=F[9], in0=F[2], scalar=-1.0, in1=vecs["kab"], op0=ALU.add, op1=ALU.mult), ["F2", "kab"], ["F9"])
        V_(lambda e: e.scalar_tensor_tensor(out=F[8], in0=F[9], scalar=1.0, in1=pf[1], op0=ALU.add, op1=ALU.mult), ["F9", "pf1"], ["F8"])
        V_(lambda e: e.tensor_tensor(out=F[9], in0=F[7], in1=F[2], op=ALU.mult), ["F7", "F2", "F8"], ["F9"])
        G_(lambda e: e.tensor_tensor(out=Bt, in0=F[9], in1=F[5], op=ALU.mult), ["F9", "F5"], ["Bt"])
        G_(lambda e: e.tensor_tensor(out=Kt, in0=F[8], in1=F[5], op=ALU.mult), ["F8", "F5"], ["Kt"])
        for nm, src, skey in (("Bh", F[9], "F9"), ("Kh", F[8], "F8")):
            d5 = msk[nm].rearrange("q (c u p w) -> q c u p w", c=2, u=2, p=4)
            for c in range(2):
                for u in range(2):
                    rows = slice(c * 64, (c + 1) * 64)
                    V_(lambda e: e.tensor_tensor(out=d5[rows, c, u, :, u * 64:(u + 1) * 64], in0=v3(src, 4)[rows, :, u * 64:(u + 1) * 64],
                                                 in1=v3(F[6], 4)[rows, :, u * 64:(u + 1) * 64], op=ALU.mult), [skey, "F6"], [nm])
        V_(lambda e: e.tensor_tensor(out=F[4], in0=pf[0], in1=F[8], op=ALU.mult), ["pf0", "F8", "F6"], ["F4"])
        V_(lambda e: e.tensor_tensor(out=F[4], in0=F[4], in1=vecs["rkb"], op=ALU.mult), ["F4", "rkb"], ["F4"])
        V_(lambda e: e.tensor_reduce(out=rk8, in_=h3(F[4]), op=ALU.add, axis=AX.X), ["F4"], ["rk8"])
        for j, (src, skey) in enumerate(((Rt, "Rt"), (At, "At"))):
            for c in range(4):
                T_(lambda e: e.transpose(out=p5b[:, j * 512 + c * 128:j * 512 + (c + 1) * 128], in_=src[:, c * 128:(c + 1) * 128], identity=idb),
                   [skey, "cm_b"], ["pf5"])
        for j, (src, skey) in enumerate(((Bt, "Bt"), (Kt, "Kt"))):
            for c in range(4):
                T_(lambda e: e.transpose(out=p6b[:, j * 512 + c * 128:j * 512 + (c + 1) * 128], in_=src[:, c * 128:(c + 1) * 128], identity=idb),
                   [skey, "cm_b"], ["pf6"])
        evac_masked(msk["RTc"], "RTc", p5b[:, 0:512], "pf5", True)
        evac_masked(msk["ATm"], "ATm", p5b[:, 512:1024], "pf5", False)
        evac_masked(msk["BTm"], "BTm", p6b[:, 0:512], "pf6", False)
        evac_masked(msk["KTm"], "KTm", p6b[:, 512:1024], "pf6", False)
        headmm(0, 1, msk["BTm"], "BTm", msk["ATm"], "ATm"); evac_mask(PA, "PA", 0, 1, msu)
        headmm(2, 3, msk["ATm"], "ATm", msk["BTm"], "BTm"); evac_mask(QA, "QA", 2, 3, msl)
        headmm(4, 7, msk["ATm"], "ATm", msk["KTm"], "KTm"); evac_mask(Aak, "Aak", 4, 7, msl)
        headmm_rc(5, 6, msk["BTm"], "BTm"); evac_mask(ArbT, "ArbT", 5, 6, miu)
        headmm_rc(0, 1, msk["KTm"], "KTm"); evac_mask(ArkT, "ArkT", 0, 1, miu)
        for i in range(2):
            V_(lambda e: e.tensor_tensor(out=v3(XI[:, i * 512:(i + 1) * 512], 4), in0=v3(PA[:, i * 512:(i + 1) * 512], 4),
                                         in1=idf.unsqueeze(1).to_broadcast([128, 4, 128]), op=ALU.add), ["PA", "cm_f"], ["XI"])
        cur, nxt = (PA, QA, "PA", "QA"), (PB, QB, "PB", "QB")
        for lev in range(5):
            Pc, Qc, Pk, Qk = cur
            Pn, Qn, Pnk, Qnk = nxt
            if lev < 4:
                headmm(2, 3, Qc, Qk, Pc, Pk)
                evac_mask(Pn, Pnk, 2, 3, None)
            headmm(4, 7, Pc, Pk, Qc, Qk)
            evac_mask(Qn, Qnk, 4, 7, None)
            for h in range(8):
                bank = 5 if h < 4 else 6
                o = pf[bank][:, (h % 4) * 128:(h % 4 + 1) * 128]
                T_(lambda e: e.matmul(o, lhsT=idb, rhs=XI[:, h * 128:(h + 1) * 128], start=True, stop=False), ["XI", "cm_b"], ["pf%d" % bank])
                T_(lambda e: e.matmul(o, lhsT=Qn[:, h * 128:(h + 1) * 128], rhs=XI[:, h * 128:(h + 1) * 128], start=False, stop=True),
                   ["XI", Qnk], ["pf%d" % bank])
            evac_mask(XI, "XI", 5, 6, None)
            cur, nxt = nxt, cur
        for p in range(4):
            for u in range(2):
                h = 2 * p + u
                T_(lambda e: e.matmul(pf[2][:, p * 128:(p + 1) * 128], lhsT=msk["Atm"][:, u * 512 + p * 128:u * 512 + (p + 1) * 128],
                                      rhs=XI[:, h * 128:(h + 1) * 128], start=(u == 0), stop=(u == 1)), ["Atm", "XI"], ["pf2"])
        S_(lambda e: e.copy(out=Bt, in_=pf[2]), ["pf2"], ["Bt"])
        m6 = msk["M1m"].rearrange("q (p u c t) -> q p u c t", p=4, u=2, c=2)
        for u in range(2):
            rows = slice(u * 64, (u + 1) * 64)
            for c in range(2):
                G_(lambda e: e.tensor_copy(out=m6[rows, :, u, c, c * 64:(c + 1) * 64], in_=v3(Bt[rows, :], 4)[:, :, c * 64:(c + 1) * 64]), ["Bt"], ["M1m"])
        M2 = PA
        headmm(3, 4, Aak, "Aak", XI, "XI"); evac_mask(M2, "PA", 3, 4, None)
        sb_in = sbi
        for c in range(2):
            ub = 0 if c == 0 else 2
            UB = "pf%d" % ub
            sbc = (sb_in + c) % 3
            for h in range(8):
                p = h // 2
                o = pf[ub][:, h * 64:(h + 1) * 64]
                T_(lambda e: e.matmul(o, lhsT=msk["M1m"][:, (h * 2 + c) * 128:(h * 2 + c + 1) * 128], rhs=STb[sbc][:, p * 64:(p + 1) * 64],
                                      start=True, stop=False), ["M1m", "STb%d" % sbc], [UB])
                T_(lambda e: e.matmul(o, lhsT=M2[:, h * 128:(h + 1) * 128], rhs=Vb[:, h * 64:(h + 1) * 64], start=False, stop=True), ["PA", "Vb"], [UB])
            rows = slice(c * 64, (c + 1) * 64)
            S_(lambda e: e.copy(out=Uall[rows, :], in_=pf[ub][rows, :]), [UB], ["Uall"])
            for p in range(4):
                for u in range(2):
                    h = 2 * p + u
                    o = pf[1][:, p * 64:(p + 1) * 64]
                    T_(lambda e: e.matmul(o, lhsT=msk["Bh"][:, (c * 2 + u) * 512 + p * 128:(c * 2 + u) * 512 + (p + 1) * 128],
                                          rhs=Uall[:, h * 64:(h + 1) * 64], start=(u == 0), stop=False), ["Bh", "Uall"], ["pf1"])
                    T_(lambda e: e.matmul(o, lhsT=msk["Kh"][:, (c * 2 + u) * 512 + p * 128:(c * 2 + u) * 512 + (p + 1) * 128],
                                          rhs=Vb[:, h * 64:(h + 1) * 64], start=False, stop=(u == 1)), ["Kh", "Vb"], ["pf1"])
            V_(lambda e: e.tensor_tensor(out=v3(STf, 4), in0=v3(STf, 4), in1=gS[:, c:8:2].unsqueeze(2).to_broadcast([128, 4, 64]), op=ALU.mult),
               ["STf", "gS"], ["STf"])
            V_(lambda e: e.tensor_tensor(out=STf, in0=STf, in1=pf[1][:, 0:256], op=ALU.add), ["STf", "pf1"], ["STf"])
            sbi = (sbi + 1) % 3
            S_(lambda e: e.copy(out=STb[sbi], in_=STf), ["STf"], ["STb%d" % sbi])
        sb0, sb1 = sb_in, (sb_in + 1) % 3
        for h in range(8):
            p = h // 2
            o = pf[3][:, h * 64:(h + 1) * 64]
            for c, sbx in ((0, sb0), (1, sb1)):
                T_(lambda e: e.matmul(o, lhsT=msk["RTc"][:, (h * 2 + c) * 128:(h * 2 + c + 1) * 128], rhs=STb[sbx][:, p * 64:(p + 1) * 64],
                                      start=(c == 0), stop=False), ["RTc", "STb%d" % sbx], ["pf3"])
            T_(lambda e: e.matmul(o, lhsT=ArbT[:, h * 128:(h + 1) * 128], rhs=Uall[:, h * 64:(h + 1) * 64], start=False, stop=False), ["ArbT", "Uall"], ["pf3"])
            T_(lambda e: e.matmul(o, lhsT=ArkT[:, h * 128:(h + 1) * 128], rhs=Vb[:, h * 64:(h + 1) * 64], start=False, stop=True), ["ArkT", "Vb"], ["pf3"])
        S_(lambda e: e.copy(out=F[4], in_=pf[3]), ["pf3"], ["F4"])
        V_(lambda e: e.tensor_reduce(out=m8, in_=h3(F[4]), op=ALU.add, axis=AX.X), ["F4"], ["m8"])
        S_(lambda e: e.activation(out=F[9], in_=F[4], func=AF.Square), ["F4"], ["F9"])
        V_(lambda e: e.tensor_reduce(out=q8, in_=h3(F[9]), op=ALU.add, axis=AX.X), ["F9"], ["q8"])
        V_(lambda e: e.tensor_scalar(out=m8, in0=m8, scalar1=1.0 / 64, scalar2=0.0, op0=ALU.mult, op1=ALU.add), ["m8"], ["m8"])
        V_(lambda e: e.tensor_tensor(out=r8, in0=m8, in1=m8, op=ALU.mult), ["m8"], ["r8"])
        V_(lambda e: e.scalar_tensor_tensor(out=q8, in0=q8, scalar=1.0 / 64, in1=r8, op0=ALU.mult, op1=ALU.subtract), ["q8", "r8"], ["q8"])
        S_(lambda e: e.activation(out=r8, in_=q8, func=AF.Sqrt, bias=64e-5), ["q8"], ["r8"])
        V_(lambda e: e.reciprocal(out=r8, in_=r8), ["r8"], ["r8"])
        V_(lambda e: e.tensor_tensor(out=h3(F[4]), in0=h3(F[4]), in1=b864(m8), op=ALU.subtract), ["F4", "m8"], ["F4"])
        V_(lambda e: e.tensor_tensor(out=h3(F[4]), in0=h3(F[4]), in1=b864(r8), op=ALU.mult), ["F4", "r8"], ["F4"])
        V_(lambda e: e.tensor_tensor(out=F[4], in0=F[4], in1=vecs["lnw"], op=ALU.mult), ["F4", "lnw"], ["F4"])
        V_(lambda e: e.tensor_tensor(out=F[4], in0=F[4], in1=vecs["lnb"], op=ALU.add), ["F4", "lnb"], ["F4"])
        V_(lambda e: e.tensor_tensor(out=h3(F[9]), in0=h3(F[0]), in1=b864(rk8), op=ALU.mult), ["F0", "rk8"], ["F9"])
        V_(lambda e: e.tensor_tensor(out=F[4], in0=F[4], in1=F[9], op=ALU.add), ["F4", "F9"], ["F4"])
        if k.debug:
            P.dma("sync", lambda e: e.dma_start(out=k.dbg["a"][n * 128:(n + 1) * 128, :], in_=F[4]), reads=["F4"])
        V_(lambda e: e.tensor_tensor(out=Rt, in0=F[4], in1=F[10], op=ALU.mult), ["F4", "F10"], ["Rt"])
        store_yT(k, "a", n, "Rt", Rt, 3, l)
```

```python
import numpy as np
from contextlib import ExitStack
import concourse.bass as bass
import concourse.mybir as mybir
from concourse.bass_utils import run_bass_kernel_spmd

F32 = mybir.dt.float32
BF16 = mybir.dt.bfloat16
AF = mybir.ActivationFunctionType
ALU = mybir.AluOpType
AX = mybir.AxisListType

D_MODEL = 1024
A_COLS = 2176
B_COLS = 2048
C_COLS = 2048
IN_COLS = 9344
N_REL = 320


class _Rec:
    def __init__(self):
        self.call = None

    def __getattr__(self, name):
        def f(*a, **kw):
            self.call = (name, a, kw)
            return self
        return f


def _bind(fn):
    rec = _Rec()
    fn(rec)
    name, a, kw = rec.call
    return lambda eng: getattr(eng, name)(*a, **kw)


class Op:
    __slots__ = ("eng", "fn", "deps", "signal", "sigval", "dma_sem", "dma_val", "is_dma", "pre_wait")

    def __init__(self, eng, fn):
        self.eng = eng
        self.fn = fn
        self.deps = []
        self.signal = False
        self.sigval = 0
        self.is_dma = False
        self.dma_sem = None
        self.dma_val = 0
        self.pre_wait = None


class Prog:
    ENGS = ("tensor", "vector", "scalar", "gpsimd", "sync")
    NDMA = 12

    def __init__(self, nc, es):
        self.nc = nc
        self.ops = {e: [] for e in self.ENGS}
        self.last_write = {}
        self.readers = {}
        self.sem = {e: es.enter_context(nc.semaphore("s_" + e)) for e in ("tensor", "vector", "scalar", "gpsimd")}
        self.dma_sems = {q: [es.enter_context(nc.semaphore("d_%s_%d" % (q, i))) for i in range(self.NDMA)]
                         for q in ("sync", "gpsimd")}
        self.dma_count = {"sync": 0, "gpsimd": 0}
        self.dma_hist = {"sync": [], "gpsimd": []}

    def _deps(self, op, reads, writes):
        deps = []
        for k in reads:
            w = self.last_write.get(k)
            if w is not None:
                deps.append(w)
        for k in writes:
            w = self.last_write.get(k)
            if w is not None:
                deps.append(w)
            deps.extend(self.readers.get(k, ()))
        seen = set()
        for d in deps:
            if id(d) in seen or d is op:
                continue
            seen.add(id(d))
            if d.eng == op.eng and op.eng == "tensor" and not d.is_dma:
                continue
            op.deps.append(d)
            if not d.is_dma:
                d.signal = True
        for k in reads:
            self.readers.setdefault(k, []).append(op)
        for k in writes:
            self.last_write[k] = op
            self.readers[k] = []

    def op(self, eng, fn, reads=(), writes=()):
        o = Op(eng, _bind(fn))
        self._deps(o, reads, writes)
        self._apply_bar(o)
        self.ops[eng].append(o)
        return o

    def dma(self, q, fn, reads=(), writes=()):
        o = Op(q, _bind(fn))
        o.is_dma = True
        i = self.dma_count[q]
        self.dma_count[q] += 1
        o.dma_sem = self.dma_sems[q][i % self.NDMA]
        o.dma_val = 16 * (i // self.NDMA + 1)
        if i >= self.NDMA:
            o.pre_wait = self.dma_hist[q][i - self.NDMA]
        self.dma_hist[q].append(o)
        self._deps(o, reads, writes)
        self._apply_bar(o)
        self.ops[q].append(o)
        return o

    def emit(self, block):
        for e in ("tensor", "vector", "scalar", "gpsimd"):
            c = 0
            for o in self.ops[e]:
                if o.is_dma:
                    continue
                if o.signal:
                    c += 1
                o.sigval = c
        all_dmas = self.dma_hist["sync"] + self.dma_hist["gpsimd"]

        def run(eng_name):
            def body(eng):
                water = {}

                def wait(sem, val):
                    key = id(sem)
                    if water.get(key, 0) >= val:
                        return
                    water[key] = val
                    eng.wait_ge(sem, val)

                for o in self.ops[eng_name]:
                    if o.pre_wait is not None:
                        wait(o.pre_wait.dma_sem, o.pre_wait.dma_val)
                    for d in o.deps:
                        if d.is_dma:
                            wait(d.dma_sem, d.dma_val)
                        else:
                            wait(self.sem[d.eng], d.sigval)
                    ins = o.fn(eng)
                    if o.is_dma:
                        ins.then_inc(o.dma_sem, 16)
                    elif o.signal:
                        ins.then_inc(self.sem[eng_name], 1)
                if eng_name in ("sync", "gpsimd"):
                    for o in self.dma_hist[eng_name][-self.NDMA:]:
                        wait(o.dma_sem, o.dma_val)
            return body

        block.tensor(run("tensor"))
        block.vector(run("vector"))
        block.scalar(run("scalar"))
        block.gpsimd(run("gpsimd"))
        block.sync(run("sync"))

    def barrier(self):
        lasts = []
        for e in ("tensor", "vector", "scalar", "gpsimd"):
            cs = [o for o in self.ops[e] if not o.is_dma]
            if cs:
                lasts.append(cs[-1])
        dmas = self.dma_hist["sync"][-self.NDMA:] + self.dma_hist["gpsimd"][-self.NDMA:]
        self.last_write = {"__bar__": None}
        self.readers = {}
        self._bar = lasts + dmas
        self._bar_pending = set(self.ENGS)

    def _apply_bar(self, o):
        if getattr(self, "_bar_pending", None) and o.eng in self._bar_pending:
            self._bar_pending.discard(o.eng)
            for d in self._bar:
                if d is o:
                    continue
                if d.eng == o.eng and not d.is_dma and o.eng == "tensor":
                    continue
                o.deps.append(d)
                if not d.is_dma:
                    d.signal = True


class Arena:
    def __init__(self, tens, size):
        self.t = tens
        self.size = size
        self.off = 0
        self.mark = 0

    def reset(self):
        self.off = self.mark

    def alloc(self, n, dt):
        nf = n if dt == F32 else (n + 1) // 2
        nf_al = (nf + 7) // 8 * 8
        assert self.off + nf_al <= self.size, ("arena overflow", self.off, nf_al, self.size)
        ap = self.t[:, self.off:self.off + nf]
        self.off += nf_al
        if dt != F32:
            ap = ap.bitcast(dt)[:, 0:n]
        return ap


class _AView:
    def __init__(self, arena, dt):
        self.a, self.dt = arena, dt

    def alloc(self, n):
        return self.a.alloc(n, self.dt)

    def reset(self):
        self.a.reset()

    @property
    def off(self):
        return self.a.off

    @property
    def mark(self):
        return self.a.mark

    @mark.setter
    def mark(self, v):
        self.a.mark = v


def _consts():
    s = np.arange(128)[:, None]
    t = np.arange(128)[None, :]
    same = (s // 64) == (t // 64)
    c = {}
    c["ident"] = np.eye(128)
    c["tri"] = same & (s <= t)
    c["ch"] = same
    c["midm"] = same & ((s % 64) <= 31)
    c["msu"] = same & (s < t)
    c["msl"] = same & (s > t)
    c["miu"] = same & (s <= t)
    names = ["ident", "tri", "ch", "midm", "msu", "msl", "miu"]
    arr = np.stack([c[k].astype(np.float32) for k in names], axis=1)
    chsel = np.zeros((128, 2), np.float32)
    chsel[:64, 0] = 1
    chsel[64:, 1] = 1
    negm = np.zeros((128, 5, 128), np.float32)
    negm[:64, 0, 64:] = -30000.0
    negm[64:, 4, :64] = -30000.0
    return names, arr, chsel, negm


CNAMES, CARR, CHSEL, NEGM = _consts()


def _bias_gather(rel_bias):
    k = np.arange(128)[:, None, None]
    r = np.arange(5)[None, :, None]
    q = np.arange(128)[None, None, :]
    idx = np.clip(512 + q - (r * 128 + k), -63, 256) + 63
    g = rel_bias[:, :, idx]
    return np.ascontiguousarray(np.transpose(g, (0, 2, 1, 3, 4)))


class K:
    pass


def build_nc(T=4096, L=2, branches=("a", "b", "c"), debug=False):
    NT = T // 128
    nc = bass.Bass("TRN2", target_bir_lowering=False)
    k = K()
    k.nc, k.T, k.L, k.NT, k.branches, k.debug = nc, T, L, NT, branches, debug

    def din(name, shape, dt=F32):
        return nc.dram_tensor(name, list(shape), dt, kind="ExternalInput").ap()

    k.x = din("x", [T, 1024])
    k.norm_g = din("norm_g", [L, 1024])
    k.w_in = din("w_in", [L, 1024, IN_COLS])
    k.rwkv_mu = din("rwkv_mu", [L, A_COLS])
    k.rwkv_w0 = din("rwkv_w0", [L, 512])
    k.rwkv_w2 = din("rwkv_w2", [L, 64, 512])
    k.rwkv_a0 = din("rwkv_a0", [L, 512])
    k.rwkv_a2 = din("rwkv_a2", [L, 64, 512])
    k.rwkv_k_k = din("rwkv_k_k", [L, 512])
    k.rwkv_k_a = din("rwkv_k_a", [L, 512])
    k.rwkv_r_k = din("rwkv_r_k", [L, 512])
    k.rwkv_ln_w = din("rwkv_ln_w", [L, 512])
    k.rwkv_ln_b = din("rwkv_ln_b", [L, 512])
    k.attn_q_norm = din("attn_q_norm", [L, 64])
    k.attn_k_norm = din("attn_k_norm", [L, 64])
    k.attn_bias = din("attn_bias", [L, 128, 8 * 5 * 128])
    k.hgrn_lb = din("hgrn_lb", [L, 512])
    k.hgrn_norm = din("hgrn_norm", [L, 64])
    k.proj_a = din("proj_a", [L, 512, 1024])
    k.proj_b = din("proj_b", [L, 512, 1024])
    k.proj_c = din("proj_c", [L, 512, 1024])
    k.w_out = din("w_out", [L, 1024, 1024])
    k.cmat = din("cmat", [128, 7 * 128])
    k.chsel = din("chsel", [128, 2])
    k.negm = din("negm", [128, 5 * 128])
    k.out = nc.dram_tensor("out", [T, 1024], F32, kind="ExternalOutput").ap()
    k.x1 = nc.dram_tensor("x1", [T, 1024], F32).ap()
    k.hT_d = nc.dram_tensor("hT_d", [128, 8, T + 16], BF16).ap()
    k.yT_d = {b: nc.dram_tensor("yT_" + b, [128, 4, T], BF16).ap() for b in "abc"}
    if debug:
        k.dbg = {b: nc.dram_tensor("dbg_" + b, [T, 512], F32, kind="ExternalOutput").ap() for b in "abc"}

    with ExitStack() as es:
        FA = 42 * 1024
        arena = Arena(es.enter_context(nc.sbuf_tensor("arena", [128, FA], F32))[:], FA)
        k.fa = _AView(arena, F32)
        k.ba = _AView(arena, BF16)
        k.pf = [es.enter_context(nc.psum_tensor("pf%d" % i, [128, 512], F32))[:] for i in range(8)]
        k.P = Prog(nc, es)
        block = es.enter_context(nc.Block())
        P = k.P
        k.cm_f = k.fa.alloc(7 * 128)
        k.cm_b = k.ba.alloc(7 * 128)
        k.chs = k.fa.alloc(2)
        P.dma("sync", lambda e: e.dma_start(out=k.cm_f, in_=k.cmat[:, :]), writes=["cm_f"])
        P.dma("gpsimd", lambda e: e.dma_start(out=k.cm_b, in_=k.cmat[:, :]), writes=["cm_b"])
        P.dma("sync", lambda e: e.dma_start(out=k.chs, in_=k.chsel[:, :]), writes=["chs"])
        k.fa.mark = k.fa.off
        k.ba.mark = k.ba.off
        for l in range(L):
            xin = k.x if l == 0 else k.x1
            xout = k.out if l == L - 1 else k.x1
            phase0(k, l, xin)
            if STOP == "0":
                phaseCopy(k, xin, xout)
                continue
            if "a" in branches:
                phaseA(k, l)
            if "b" in branches:
                phaseB(k, l)
            if "c" in branches:
                phaseC(k, l)
            if STOP == "B":
                phaseCopy(k, xin, xout)
                continue
            phaseM(k, l, xin, xout)
        P.emit(block)
    return nc


import os
STOP = os.environ.get("STOP", "")
LVL = float(os.environ.get("LVL", "9"))


def phaseCopy(k, xin, xout):
    P = k.P
    new_phase(k)
    t = k.fa.alloc(1024)
    for n in range(k.NT):
        P.dma("sync", lambda e: e.dma_start(out=t, in_=xin[n * 128:(n + 1) * 128, :]), writes=["t"])
        P.dma("sync", lambda e: e.dma_start(out=xout[n * 128:(n + 1) * 128, :], in_=t), reads=["t"])


def cview(k, name, bf=True):
    i = CNAMES.index(name)
    t = k.cm_b if bf else k.cm_f
    return t[:, i * 128:(i + 1) * 128]


def new_phase(k):
    k.P.barrier()
    k.fa.reset()
    k.ba.reset()


def phase0(k, l, xin):
    P, nc = k.P, k.nc
    new_phase(k)
    fa, ba = k.fa, k.ba
    gb = fa.alloc(1024)
    xt = [fa.alloc(1024) for _ in range(2)]
    junk = fa.alloc(1024)
    ss = [fa.alloc(1) for _ in range(2)]
    rs = [fa.alloc(1) for _ in range(2)]
    hb = [ba.alloc(1024) for _ in range(2)]
    hs = [ba.alloc(1024) for _ in range(2)]
    zc = ba.alloc(8 * 16)
    idb = cview(k, "ident")
    P.dma("sync", lambda e: e.dma_start(out=gb, in_=k.norm_g[l:l + 1, :].partition_broadcast(128)), writes=["gb"])
    if l == 0:
        P.op("gpsimd", lambda e: e.memset(zc, 0.0), writes=["zc"])
        P.dma("sync", lambda e: e.dma_start(out=k.hT_d[:, :, 0:16], in_=zc.rearrange("p (c t) -> p c t", c=8)), reads=["zc"])
    for n in range(k.NT):
        b = n % 2
        X, HB, HS, SS, RS = "xt%d" % b, "hb%d" % b, "hs%d" % b, "ss%d" % b, "rs%d" % b
        P.dma("sync", lambda e, n=n, b=b: e.dma_start(out=xt[b], in_=xin[n * 128:(n + 1) * 128, :]), writes=[X])
        P.op("scalar", lambda e, b=b: e.activation(out=junk, in_=xt[b], func=AF.Square, accum_out=ss[b]), reads=[X], writes=["junk", SS])
        P.op("scalar", lambda e, b=b: e.activation(out=rs[b], in_=ss[b], func=AF.Sqrt, scale=1.0 / 1024, bias=1e-6), reads=[SS], writes=[RS])
        P.op("vector", lambda e, b=b: e.reciprocal(out=rs[b], in_=rs[b]), reads=[RS], writes=[RS])
        P.op("vector", lambda e, b=b: e.scalar_tensor_tensor(out=hb[b], in0=xt[b], scalar=rs[b][:, 0:1], in1=gb, op0=ALU.mult, op1=ALU.mult),
             reads=[X, RS, "gb"], writes=[HB])
        pt = k.pf[n % 2].bitcast(BF16)
        PT = "pf%d" % (n % 2)
        for c in range(8):
            P.op("tensor", lambda e, c=c, b=b, pt=pt: e.transpose(out=pt[:, c * 128:(c + 1) * 128], in_=hb[b][:, c * 128:(c + 1) * 128], identity=idb),
                 reads=[HB, "cm_b"], writes=[PT])
        P.op("scalar", lambda e, b=b, pt=pt: e.copy(out=hs[b], in_=pt[:, 0:1024]), reads=[PT], writes=[HS])
        P.dma("sync", lambda e, n=n, b=b: e.dma_start(out=k.hT_d[:, :, 16 + n * 128:16 + (n + 1) * 128], in_=hs[b].rearrange("p (c t) -> p c t", c=8)),
              reads=[HS], writes=["hT_d"])


def pbf(k, i):
    return k.pf[i].bitcast(BF16)


def load_w_cast(k, dst3, src2d, key, nsplit=8):
    C = dst3.shape[1]
    N = dst3.shape[2]
    step = max(1, 2048 // 1)
    for c in range(C):
        for n0 in range(0, N, 2048):
            n1 = min(N, n0 + 2048)
            k.P.dma("gpsimd", lambda e, c=c, n0=n0, n1=n1: e.dma_start(out=dst3[:, c, n0:n1], in_=src2d[c * 128:(c + 1) * 128, n0:n1]),
                    writes=[key])


def proj_block(k, pbank, pkey, hT, hkey, W, wkey, c0, ncols, shift=0):
    for c in range(8):
        k.P.op("tensor", lambda e, c=c: e.matmul(pbank[:, 0:ncols], lhsT=hT[:, c, shift:shift + 128], rhs=W[:, c, c0:c0 + ncols],
                                                start=(c == 0), stop=(c == 7)),
               reads=[hkey, wkey], writes=[pkey])


def store_yT(k, br, n, ysrc_key, ysrc_bf, tp_bank, l):
    P = k.P
    b = n % 2
    pt = pbf(k, tp_bank)
    PT = "pf%d" % tp_bank
    idb = cview(k, "ident")
    for c in range(4):
        P.op("tensor", lambda e, c=c: e.transpose(out=pt[:, c * 128:(c + 1) * 128], in_=ysrc_bf[:, c * 128:(c + 1) * 128], identity=idb),
             reads=[ysrc_key, "cm_b"], writes=[PT])
    ys = k.ystage[b]
    YS = "ystage%d" % (b if k.ystage[0] is not k.ystage[1] else 0)
    P.op("scalar", lambda e: e.copy(out=ys, in_=pt[:, 0:512]), reads=[PT], writes=[YS])
    P.dma("sync", lambda e: e.dma_start(out=k.yT_d[br][:, :, n * 128:(n + 1) * 128], in_=ys.rearrange("p (c t) -> p c t", c=4)),
          reads=[YS], writes=["yT_d" + br])


def v3(ap, a):
    return ap.rearrange("p (a b) -> p a b", a=a)


def phaseB(k, l):
    P, nc = k.P, k.nc
    new_phase(k)
    fa, ba = k.fa, k.ba
    NT = k.NT
    idb = cview(k, "ident")
    W = v3(ba.alloc(8 * 2048), 8)
    load_w_cast(k, W, k.w_in[l, :, A_COLS:A_COLS + B_COLS], "WB")
    gq = fa.alloc(64)
    gk = fa.alloc(64)
    P.dma("sync", lambda e: e.dma_start(out=gq, in_=k.attn_q_norm[l:l + 1, :].partition_broadcast(128)), writes=["gq"])
    P.dma("sync", lambda e: e.dma_start(out=gk, in_=k.attn_k_norm[l:l + 1, :].partition_broadcast(128)), writes=["gk"])
    P.op("vector", lambda e: e.scalar_tensor_tensor(out=gq, in0=gq, scalar=0.125, in1=gk, op0=ALU.mult, op1=ALU.mult), reads=["gq", "gk"], writes=["gq"])
    bstage = fa.alloc(5120)
    nm = fa.alloc(640)
    biasT = ba.alloc(5120)
    P.dma("sync", lambda e: e.dma_start(out=bstage, in_=k.attn_bias[l, :, :]), writes=["bstage"])
    P.dma("sync", lambda e: e.dma_start(out=nm, in_=k.negm[:, :]), writes=["nm"])
    P.op("vector", lambda e: e.tensor_tensor(out=v3(biasT, 8), in0=v3(bstage, 8), in1=nm.unsqueeze(1).to_broadcast([128, 8, 640]), op=ALU.add),
         reads=["bstage", "nm"], writes=["biasT"])
    bias4 = biasT.rearrange("p (h r q) -> p h r q", h=8, r=5)
    Vr = ba.alloc(8 * 8 * 80).rearrange("p (s h d) -> p s h d", s=8, h=8)
    P.op("gpsimd", lambda e: e.memset(Vr, 1.0), writes=["Vr"])
    kT = ba.alloc(4 * 8 * 128).rearrange("p (c s t) -> p c s t", c=4, s=8)
    qTm = [ba.alloc(8 * 128) for _ in range(2)]
    for b in range(2):
        P.op("gpsimd", lambda e, b=b: e.memset(qTm[b], 0.0), writes=["qTm%d" % b])
    hTt = [v3(ba.alloc(1024), 8) for _ in range(2)]
    sqt = fa.alloc(1024)
    ss16 = fa.alloc(16)
    rs16 = fa.alloc(16)
    qn32 = fa.alloc(512)
    qb = ba.alloc(512)
    kb = ba.alloc(512)
    sg = fa.alloc(512)
    PTb = [ba.alloc(512) for _ in range(3)]
    rinv = fa.alloc(8)
    y32 = fa.alloc(512)
    ygb = ba.alloc(512)
    k.ystage = [ba.alloc(512) for _ in range(2)]
    pf = k.pf
    for n in range(NT):
        b = n % 2
        slot = n % 8
        H = "hTt%d" % b
        P.dma("sync", lambda e, n=n, b=b: e.dma_start(out=hTt[b], in_=k.hT_d[:, :, 16 + n * 128:16 + (n + 1) * 128]), writes=[H])
        for blk in range(4):
            proj_block(k, pf[blk], "pf%d" % blk, hTt[b], H, W, "WB", blk * 512, 512)
        if LVL < 1.1:
            continue
        P.op("scalar", lambda e: e.activation(out=sqt[:, 0:512], in_=pf[0], func=AF.Square), reads=["pf0"], writes=["sqt"])
        P.op("scalar", lambda e: e.activation(out=sqt[:, 512:1024], in_=pf[1], func=AF.Square), reads=["pf1"], writes=["sqt"])
        P.op("vector", lambda e: e.tensor_reduce(out=ss16, in_=v3(sqt, 16), op=ALU.add, axis=AX.X), reads=["sqt"], writes=["ss16"])
        P.op("scalar", lambda e: e.activation(out=rs16, in_=ss16, func=AF.Sqrt, scale=1.0 / 64, bias=1e-6), reads=["ss16"], writes=["rs16"])
        P.op("vector", lambda e: e.reciprocal(out=rs16, in_=rs16), reads=["rs16"], writes=["rs16"])
        if LVL < 1.2:
            continue
        P.op("vector", lambda e: e.tensor_tensor(out=v3(qn32, 8), in0=v3(pf[0], 8), in1=rs16[:, 0:8].unsqueeze(2).to_broadcast([128, 8, 64]), op=ALU.mult),
             reads=["pf0", "rs16"], writes=["qn32"])
        P.op("vector", lambda e: e.tensor_tensor(out=v3(qb, 8), in0=v3(qn32, 8), in1=gq.unsqueeze(1).to_broadcast([128, 8, 64]), op=ALU.mult),
             reads=["qn32", "gq"], writes=["qb"])
        P.op("vector", lambda e: e.tensor_tensor(out=v3(kb, 8), in0=v3(pf[1], 8), in1=rs16[:, 8:16].unsqueeze(2).to_broadcast([128, 8, 64]), op=ALU.mult),
             reads=["pf1", "rs16"], writes=["kb"])
        if LVL < 1.4:
            continue
        pt = pbf(k, 0)
        for c in range(4):
            P.op("tensor", lambda e, c=c: e.transpose(out=pt[:, c * 128:(c + 1) * 128], in_=qb[:, c * 128:(c + 1) * 128], identity=idb),
                 reads=["qb", "cm_b"], writes=["pf0"])
        for c in range(4):
            P.op("tensor", lambda e, c=c: e.transpose(out=pt[:, 512 + c * 128:512 + (c + 1) * 128], in_=kb[:, c * 128:(c + 1) * 128], identity=idb),
                 reads=["kb", "cm_b"], writes=["pf0"])
        if LVL < 1.6:
            continue
        Q = "qTm%d" % b
        q4 = qTm[b].rearrange("p (c u t) -> p c u t", c=4, u=2)
        for u in range(2):
            P.op("scalar", lambda e, u=u, q4=q4: e.copy(out=q4[u * 64:(u + 1) * 64, :, u, :], in_=v3(pt[u * 64:(u + 1) * 64, 0:512], 4)),
                 reads=["pf0"], writes=[Q])
        if LVL < 1.8:
            continue
        P.op("scalar", lambda e, slot=slot: e.copy(out=kT[:, :, slot, :], in_=v3(pt[:, 512:1024], 4)), reads=["pf0"], writes=["kT"])
        if LVL < 1.9:
            continue
        P.op("scalar", lambda e, slot=slot: e.copy(out=Vr[:, slot, :, 0:64], in_=v3(pf[2], 8)), reads=["pf2"], writes=["Vr"])
        if LVL < 1.95:
            continue
        VAR = os.environ.get("VAR", "")
        if VAR == "copy3":
            P.op("scalar", lambda e: e.copy(out=sg, in_=pf[3]), reads=["pf3"], writes=["sg"])
        elif VAR == "sig2":
            P.op("scalar", lambda e: e.activation(out=sg, in_=pf[2], func=AF.Sigmoid), reads=["pf2"], writes=["sg"])
        elif VAR == "dve3":
            P.op("vector", lambda e: e.tensor_copy(out=sg, in_=pf[3]), reads=["pf3"], writes=["sg"])
        else:
            P.op("scalar", lambda e: e.activation(out=sg, in_=pf[3], func=AF.Silu), reads=["pf3"], writes=["sg"])
        if LVL < 3:
            continue
        blocks = [(h, r) for h in range(8) for r in range(5) if n - 4 + r >= 0]
        groups = [blocks[i:i + 4] for i in range(0, len(blocks), 4)]
        first_r = max(0, 4 - n)
        def emit_pv(grp, pb):
            PTK = "PT%d" % pb
            for j, (h, r) in enumerate(grp):
                kslot = (n - 4 + r) % 8
                ob = 6 + h // 4
                P.op("tensor", lambda e: e.matmul(pf[ob][:, (h % 4) * 65:(h % 4) * 65 + 65], lhsT=PTb[pb][:, j * 128:(j + 1) * 128],
                                                  rhs=Vr[:, kslot, h, 0:65], start=(r == first_r), stop=(r == 4)), reads=[PTK, "Vr"], writes=["pf%d" % ob])

        prev = None
        for gi, grp in enumerate(groups):
            bank = 4 + gi % 2
            BK = "pf%d" % bank
            for j, (h, r) in enumerate(grp):
                kslot = (n - 4 + r) % 8
                P.op("tensor", lambda e: e.matmul(pf[bank][:, j * 128:(j + 1) * 128], lhsT=kT[:, h // 2, kslot, :],
                                                  rhs=qTm[b][:, h * 128:(h + 1) * 128], start=True, stop=False), reads=["kT", Q], writes=[BK])
                P.op("tensor", lambda e: e.matmul(pf[bank][:, j * 128:(j + 1) * 128], lhsT=idb, rhs=bias4[:, h, r, :], start=False, stop=True),
                     reads=["biasT", "cm_b"], writes=[BK])
            if prev is not None:
                emit_pv(*prev)
            pb = gi % 3
            ncol = len(grp) * 128
            P.op("scalar", lambda e: e.activation(out=PTb[pb][:, 0:ncol], in_=pf[bank][:, 0:ncol], func=AF.Exp), reads=[BK], writes=["PT%d" % pb])
            prev = (grp, pb)
        if prev is not None:
            emit_pv(*prev)
        if LVL < 4:
            continue
        for hb_ in range(2):
            o3 = v3(pf[6 + hb_][:, 0:260], 4)
            P.op("vector", lambda e, o3=o3, hb_=hb_: e.reciprocal(out=rinv[:, hb_ * 4:(hb_ + 1) * 4].unsqueeze(2), in_=o3[:, :, 64:65]),
                 reads=["pf%d" % (6 + hb_)], writes=["rinv"])
            P.op("vector", lambda e, o3=o3, hb_=hb_: e.tensor_tensor(out=v3(y32[:, hb_ * 256:(hb_ + 1) * 256], 4), in0=o3[:, :, 0:64],
                                                                    in1=rinv[:, hb_ * 4:(hb_ + 1) * 4].unsqueeze(2).to_broadcast([128, 4, 64]), op=ALU.mult),
                 reads=["pf%d" % (6 + hb_), "rinv"], writes=["y32"])
        if k.debug:
            P.dma("sync", lambda e, n=n: e.dma_start(out=k.dbg["b"][n * 128:(n + 1) * 128, :], in_=y32), reads=["y32"])
        P.op("vector", lambda e: e.tensor_tensor(out=ygb, in0=y32, in1=sg, op=ALU.mult), reads=["y32", "sg"], writes=["ygb"])
        store_yT(k, "b", n, "ygb", ygb, 3, l)


def phaseM(k, l, xin, xout):
    P, nc = k.P, k.nc
    new_phase(k)
    fa, ba = k.fa, k.ba
    brs = [b for b in "abc" if b in k.branches]
    if not brs:
        return phaseCopy(k, xin, xout)
    projs = {"a": k.proj_a, "b": k.proj_b, "c": k.proj_c}
    Wz, Wp = {}, {}
    for bi, br in enumerate("abc"):
        if br not in brs:
            continue
        Wz[br] = v3(ba.alloc(8 * 1024), 8)
        load_w_cast(k, Wz[br], k.w_in[l, :, 6272 + bi * 1024:6272 + (bi + 1) * 1024], "Wz" + br)
        Wp[br] = v3(ba.alloc(4 * 1024), 4)
        load_w_cast(k, Wp[br], projs[br][l, :, :], "Wp" + br)
    Wo = v3(ba.alloc(8 * 1024), 8)
    load_w_cast(k, Wo, k.w_out[l, :, :], "Wo")
    TB = 512
    NB = k.T // TB
    hTb = [v3(ba.alloc(8 * TB), 8) for _ in range(2)]
    yTb = {br: [v3(ba.alloc(4 * TB), 4) for _ in range(2)] for br in brs}
    mT = v3(ba.alloc(8 * TB), 8)
    gsb = {br: fa.alloc(TB) for br in brs}
    acc = fa.alloc(TB)
    tmp = fa.alloc(TB)
    xt = [fa.alloc(1024) for _ in range(2)]
    ot = [fa.alloc(1024) for _ in range(2)]
    pf = k.pf
    for tb in range(NB):
        b = tb % 2
        H = "hTb%d" % b
        P.dma("sync", lambda e, tb=tb, b=b: e.dma_start(out=hTb[b], in_=k.hT_d[:, :, 16 + tb * TB:16 + (tb + 1) * TB]), writes=[H])
        for br in brs:
            P.dma("sync", lambda e, tb=tb, b=b, br=br: e.dma_start(out=yTb[br][b], in_=k.yT_d[br][:, :, tb * TB:(tb + 1) * TB]),
                  writes=["yTb%s%d" % (br, b)])
        for fc in range(8):
            for bi, br in enumerate(brs):
                zb, pb_ = (fc * len(brs) + bi) % 4, 4 + (fc * len(brs) + bi) % 4
                for c in range(8):
                    P.op("tensor", lambda e, c=c, br=br, zb=zb: e.matmul(pf[zb], lhsT=Wz[br][:, c, fc * 128:(fc + 1) * 128], rhs=hTb[b][:, c, :],
                                                                        start=(c == 0), stop=(c == 7)), reads=["Wz" + br, H], writes=["pf%d" % zb])
                P.op("scalar", lambda e, br=br, zb=zb: e.activation(out=gsb[br], in_=pf[zb], func=AF.Sigmoid), reads=["pf%d" % zb], writes=["gsb" + br])
                for c in range(4):
                    P.op("tensor", lambda e, c=c, br=br, pb_=pb_: e.matmul(pf[pb_], lhsT=Wp[br][:, c, fc * 128:(fc + 1) * 128], rhs=yTb[br][b][:, c, :],
                                                                          start=(c == 0), stop=(c == 3)),
                         reads=["Wp" + br, "yTb%s%d" % (br, b)], writes=["pf%d" % pb_])
            for bi, br in enumerate(brs):
                pb_ = 4 + (fc * len(brs) + bi) % 4
                last = bi == len(brs) - 1
                if bi == 0:
                    dst = mT[:, fc, :] if last else acc
                    P.op("vector", lambda e, br=br, pb_=pb_, dst=dst: e.tensor_tensor(out=dst, in0=pf[pb_], in1=gsb[br], op=ALU.mult),
                         reads=["pf%d" % pb_, "gsb" + br], writes=["mT" if last else "acc"])
                else:
                    P.op("vector", lambda e, br=br, pb_=pb_: e.tensor_tensor(out=tmp, in0=pf[pb_], in1=gsb[br], op=ALU.mult),
                         reads=["pf%d" % pb_, "gsb" + br], writes=["tmp"])
                    dst = mT[:, fc, :] if last else acc
                    P.op("vector", lambda e, dst=dst: e.tensor_tensor(out=dst, in0=acc, in1=tmp, op=ALU.add),
                         reads=["acc", "tmp"], writes=["mT" if last else "acc"])
        for tt in range(TB // 128):
            n = tb * (TB // 128) + tt
            xb = n % 2
            X, O = "xm%d" % xb, "om%d" % xb
            P.dma("sync", lambda e, n=n, xb=xb: e.dma_start(out=xt[xb], in_=xin[n * 128:(n + 1) * 128, :]), writes=[X])
            for cb in range(2):
                bank = cb
                for c in range(8):
                    P.op("tensor", lambda e, c=c, cb=cb, bank=bank, tt=tt: e.matmul(pf[bank], lhsT=mT[:, c, tt * 128:(tt + 1) * 128],
                                                                                   rhs=Wo[:, c, cb * 512:(cb + 1) * 512], start=(c == 0), stop=(c == 7)),
                         reads=["mT", "Wo"], writes=["pf%d" % bank])
                P.op("vector", lambda e, cb=cb, bank=bank, xb=xb: e.tensor_tensor(out=ot[xb][:, cb * 512:(cb + 1) * 512], in0=pf[bank],
                                                                                 in1=xt[xb][:, cb * 512:(cb + 1) * 512], op=ALU.add),
                     reads=["pf%d" % bank, X], writes=[O])
            P.dma("sync", lambda e, n=n, xb=xb: e.dma_start(out=xout[n * 128:(n + 1) * 128, :], in_=ot[xb]), reads=[O], writes=["xout"])


_NC_CACHE = {}


def make_in_maps(inputs, T, L, nb):
    f = lambda a: np.ascontiguousarray(np.asarray(a, dtype=np.float32))
    shared = {
        "norm_g": f(inputs["norm_g"])[:L], "w_in": f(inputs["w_in"])[:L], "rwkv_mu": f(inputs["rwkv_mu"])[:L],
        "rwkv_w0": f(inputs["rwkv_w0"])[:L], "rwkv_w2": f(inputs["rwkv_w2"])[:L], "rwkv_a0": f(inputs["rwkv_a0"])[:L],
        "rwkv_a2": f(inputs["rwkv_a2"])[:L], "rwkv_k_k": f(inputs["rwkv_k_k"])[:L], "rwkv_k_a": f(inputs["rwkv_k_a"])[:L],
        "rwkv_r_k": f(inputs["rwkv_r_k"])[:L].reshape(L, 512), "rwkv_ln_w": f(inputs["rwkv_ln_w"])[:L],
        "rwkv_ln_b": f(inputs["rwkv_ln_b"])[:L], "attn_q_norm": f(inputs["attn_q_norm"])[:L],
        "attn_k_norm": f(inputs["attn_k_norm"])[:L],
        "attn_bias": _bias_gather(f(inputs["attn_rel_bias"])[:L]).reshape(L, 128, 8 * 5 * 128),
        "hgrn_lb": f(inputs["hgrn_lb"])[:L], "hgrn_norm": f(inputs["hgrn_norm"])[:L],
        "proj_a": f(inputs["proj_a"])[:L], "proj_b": f(inputs["proj_b"])[:L], "proj_c": f(inputs["proj_c"])[:L],
        "w_out": f(inputs["w_out"])[:L],
        "cmat": np.ascontiguousarray(CARR.reshape(128, 7 * 128)), "chsel": CHSEL,
        "negm": np.ascontiguousarray(NEGM.reshape(128, 640)),
    }
    x = f(inputs["x"])
    maps = []
    for b in range(nb):
        m = dict(shared)
        m["x"] = np.ascontiguousarray(x[b, :T])
        maps.append(m)
    return maps


def kernel(**inputs):
    T, L = 4096, 2
    key = (T, L)
    if key not in _NC_CACHE:
        _NC_CACHE[key] = build_nc(T, L, branches=tuple(os.environ.get("KBR", "abc")))
    nc = _NC_CACHE[key]
    maps = make_in_maps(inputs, T, L, 8)
    res = run_bass_kernel_spmd(nc, maps, core_ids=list(range(8)))
    return np.stack([r["out"] for r in res.results], axis=0).astype(np.float32)


def phaseC(k, l):
    P, nc = k.P, k.nc
    new_phase(k)
    fa, ba = k.fa, k.ba
    NT, L = k.NT, k.L
    pf = k.pf
    idb = cview(k, "ident")
    V_ = lambda fn, r, w: P.op("vector", fn, r, w)
    S_ = lambda fn, r, w: P.op("scalar", fn, r, w)
    G_ = lambda fn, r, w: P.op("gpsimd", fn, r, w)
    T_ = lambda fn, r, w: P.op("tensor", fn, r, w)
    W = v3(ba.alloc(8 * 2048), 8)
    load_w_cast(k, W, k.w_in[l, :, A_COLS + B_COLS:A_COLS + B_COLS + C_COLS], "WC")
    lbb = fa.alloc(512)
    oml = fa.alloc(512)
    if l == 0:
        V_(lambda e: e.memset(lbb, 0.0), [], ["lbb"])
    else:
        er = fa.alloc(L * 512)
        P.dma("sync", lambda e: e.dma_start(out=er, in_=k.hgrn_lb.rearrange("l c -> (l c)").unsqueeze(0).partition_broadcast(128).squeeze(1)
                                            if False else k.hgrn_lb.rearrange("(o l) c -> o (l c)", o=1).partition_broadcast(128)), writes=["er"])
        S_(lambda e: e.activation(out=er, in_=er, func=AF.Exp), ["er"], ["er"])
        ssum = fa.alloc(512)
        V_(lambda e: e.tensor_tensor(out=ssum, in0=er[:, 0:512], in1=er[:, 512:1024], op=ALU.add), ["er"], ["ssum"])
        for j in range(2, L):
            V_(lambda e: e.tensor_tensor(out=ssum, in0=ssum, in1=er[:, j * 512:(j + 1) * 512], op=ALU.add), ["er", "ssum"], ["ssum"])
        V_(lambda e: e.tensor_copy(out=lbb, in_=er[:, 512:1024]), ["er"], ["lbb"])
        for j in range(2, l + 1):
            V_(lambda e: e.tensor_tensor(out=lbb, in0=lbb, in1=er[:, j * 512:(j + 1) * 512], op=ALU.add), ["er", "lbb"], ["lbb"])
        V_(lambda e: e.reciprocal(out=ssum, in_=ssum), ["ssum"], ["ssum"])
        V_(lambda e: e.tensor_tensor(out=lbb, in0=lbb, in1=ssum, op=ALU.mult), ["lbb", "ssum"], ["lbb"])
    V_(lambda e: e.tensor_scalar(out=oml, in0=lbb, scalar1=-1.0, scalar2=1.0, op0=ALU.mult, op1=ALU.add), ["lbb"], ["oml"])
    gn = fa.alloc(64)
    P.dma("sync", lambda e: e.dma_start(out=gn, in_=k.hgrn_norm[l:l + 1, :].partition_broadcast(128)), writes=["gn"])
    tri, chm, midm, miu = cview(k, "tri", False), cview(k, "ch", False), cview(k, "midm", False), cview(k, "miu", False)
    Sf = fa.alloc(256)
    V_(lambda e: e.memset(Sf, 0.0), [], ["Sf"])
    Sb = [ba.alloc(256) for _ in range(3)]
    G_(lambda e: e.memset(Sb[0], 0.0), [], ["Sb0"])
    QpTm = [ba.alloc(2048) for _ in range(2)]
    QTm = [ba.alloc(1024) for _ in range(2)]
    Kh = [ba.alloc(2048) for _ in range(2)]
    for b in range(2):
        G_(lambda e: e.memset(QpTm[b], 0.0), [], ["QpTm%d" % b])
        G_(lambda e: e.memset(QTm[b], 0.0), [], ["QTm%d" % b])
        G_(lambda e: e.memset(Kh[b], 0.0), [], ["Kh%d" % b])
    hTt = [v3(ba.alloc(1024), 8) for _ in range(2)]
    sgm, sgn, fg, key, logf, qh, eb = [fa.alloc(512) for _ in range(7)]
    bm, be, d1, e1 = [fa.alloc(512) for _ in range(4)]
    Qp, Qt, Kt, Vb = [ba.alloc(512) for _ in range(4)]
    KT = ba.alloc(512)
    attm = ba.alloc(1024)
    gS = fa.alloc(8)
    sq = fa.alloc(512)
    ss8 = fa.alloc(8)
    rs8 = fa.alloc(8)
    o32 = fa.alloc(512)
    gate = fa.alloc(512)
    ygb = ba.alloc(512)
    k.ystage = [ba.alloc(512) for _ in range(2)]
    sbi = 0
    for n in range(NT):
        b = n % 2
        H = "hTt%d" % b
        P.dma("sync", lambda e: e.dma_start(out=hTt[b], in_=k.hT_d[:, :, 16 + n * 128:16 + (n + 1) * 128]), writes=[H])
        for blk in range(4):
            proj_block(k, pf[blk], "pf%d" % blk, hTt[b], H, W, "WC", blk * 512, 512)
        S_(lambda e: e.activation(out=sgm, in_=pf[1], func=AF.Sigmoid), ["pf1"], ["sgm"])
        S_(lambda e: e.activation(out=sgn, in_=pf[1], func=AF.Sigmoid, scale=-1.0), ["pf1"], ["sgn"])
        V_(lambda e: e.tensor_tensor(out=fg, in0=sgm, in1=oml, op=ALU.mult), ["sgm", "oml"], ["fg"])
        V_(lambda e: e.tensor_tensor(out=fg, in0=fg, in1=lbb, op=ALU.add), ["fg", "lbb"], ["fg"])
        G_(lambda e: e.tensor_tensor(out=key, in0=sgn, in1=oml, op=ALU.mult), ["sgn", "oml"], ["key"])
        S_(lambda e: e.activation(out=logf, in_=fg, func=AF.Ln), ["fg"], ["logf"])
        for bank, m in ((4, tri), (5, midm), (6, chm)):
            T_(lambda e: e.matmul(pf[bank], lhsT=m, rhs=logf, start=True, stop=True), ["logf", "cm_f"], ["pf%d" % bank])
        S_(lambda e: e.copy(out=Vb, in_=pf[2]), ["pf2"], ["Vb"])
        S_(lambda e: e.activation(out=qh, in_=pf[0], func=AF.Silu), ["pf0"], ["qh"])
        S_(lambda e: e.activation(out=gate, in_=pf[3], func=AF.Silu), ["pf3"], ["gate"])
        S_(lambda e: e.activation(out=eb, in_=pf[4], func=AF.Exp), ["pf4"], ["eb"])
        S_(lambda e: e.copy(out=bm, in_=pf[5]), ["pf5"], ["bm"])
        S_(lambda e: e.copy(out=be, in_=pf[6]), ["pf6"], ["be"])
        for p in range(4):
            T_(lambda e: e.matmul(pf[5][:, 256 + p * 2:256 + p * 2 + 2], lhsT=logf[:, p * 128:(p + 1) * 128], rhs=k.chs, start=True, stop=True),
               ["logf", "chs"], ["pf5"])
        S_(lambda e: e.activation(out=gS, in_=pf[5][:, 256:264], func=AF.Exp), ["pf5"], ["gS"])
        V_(lambda e: e.tensor_tensor(out=Qp, in0=qh, in1=eb, op=ALU.mult), ["qh", "eb"], ["Qp"])
        V_(lambda e: e.tensor_tensor(out=d1, in0=pf[4], in1=bm, op=ALU.subtract), ["pf4", "bm"], ["d1"])
        S_(lambda e: e.activation(out=e1, in_=d1, func=AF.Exp), ["d1"], ["e1"])
        V_(lambda e: e.tensor_tensor(out=Qt, in0=qh, in1=e1, op=ALU.mult), ["qh", "e1"], ["Qt"])
        S_(lambda e: e.activation(out=e1, in_=d1, func=AF.Exp, scale=-1.0), ["d1", "Qt"], ["e1"])
        V_(lambda e: e.tensor_tensor(out=Kt, in0=key, in1=e1, op=ALU.mult), ["key", "e1"], ["Kt"])
        V_(lambda e: e.tensor_tensor(out=d1, in0=be, in1=pf[4], op=ALU.subtract), ["pf4", "be", "e1"], ["d1"])
        S_(lambda e: e.activation(out=e1, in_=d1, func=AF.Exp), ["d1", "Kt"], ["e1"])
        KH = "Kh%d" % b
        kh5 = Kh[b].rearrange("q (c u p d) -> q c u p d", c=2, u=2, p=4)
        for c in range(2):
            for u in range(2):
                rows = slice(c * 64, (c + 1) * 64)
                V_(lambda e: e.tensor_tensor(out=kh5[rows, c, u, :, u * 64:(u + 1) * 64] if False else
                                             Kh[b].rearrange("q (c u p w) -> q c u p w", c=2, u=2, p=4)[rows, c, u, :, u * 64:(u + 1) * 64],
                                             in0=v3(key, 4)[rows, :, u * 64:(u + 1) * 64], in1=v3(e1, 4)[rows, :, u * 64:(u + 1) * 64], op=ALU.mult),
                   ["key", "e1"], [KH])
        p7 = pbf(k, 7)
        p5 = pbf(k, 5)
        for c in range(4):
            T_(lambda e: e.transpose(out=p7[:, c * 128:(c + 1) * 128], in_=Qp[:, c * 128:(c + 1) * 128], identity=idb), ["Qp", "cm_b"], ["pf7"])
        for c in range(4):
            T_(lambda e: e.transpose(out=p7[:, 512 + c * 128:512 + (c + 1) * 128], in_=Qt[:, c * 128:(c + 1) * 128], identity=idb), ["Qt", "cm_b"], ["pf7"])
        for c in range(4):
            T_(lambda e: e.transpose(out=p5[:, c * 128:(c + 1) * 128], in_=Kt[:, c * 128:(c + 1) * 128], identity=idb), ["Kt", "cm_b"], ["pf5"])
        QP, QT = "QpTm%d" % b, "QTm%d" % b
        qp6 = QpTm[b].rearrange("q (p u c t) -> q p u c t", p=4, u=2, c=2)
        qt4 = QTm[b].rearrange("q (p u t) -> q p u t", p=4, u=2)
        for u in range(2):
            rows = slice(u * 64, (u + 1) * 64)
            for c in range(2):
                S_(lambda e: e.copy(out=qp6[rows, :, u, c, c * 64:(c + 1) * 64], in_=v3(p7[rows, 0:512], 4)[:, :, c * 64:(c + 1) * 64]), ["pf7"], [QP])
            S_(lambda e: e.copy(out=qt4[rows, :, u, :], in_=v3(p7[rows, 512:1024], 4)), ["pf7"], [QT])
        S_(lambda e: e.copy(out=KT, in_=p5[:, 0:512]), ["pf5"], ["KT"])
        for h in range(8):
            bank = 4 if h < 4 else 6
            T_(lambda e: e.matmul(pf[bank][:, (h % 4) * 128:(h % 4 + 1) * 128], lhsT=KT[:, (h // 2) * 128:(h // 2 + 1) * 128],
                                  rhs=QTm[b][:, h * 128:(h + 1) * 128], start=True, stop=True), ["KT", QT], ["pf%d" % bank])
        for hb_ in range(2):
            bank = 4 if hb_ == 0 else 6
            V_(lambda e: e.tensor_tensor(out=v3(attm[:, hb_ * 512:(hb_ + 1) * 512], 4), in0=v3(pf[bank], 4),
                                         in1=miu.unsqueeze(1).to_broadcast([128, 4, 128]), op=ALU.mult), ["pf%d" % bank, "cm_f"], ["attm"])
        for c in range(2):
            for p in range(4):
                for u in range(2):
                    h = 2 * p + u
                    T_(lambda e: e.matmul(pf[1][:, c * 256 + p * 64:c * 256 + (p + 1) * 64],
                                          lhsT=Kh[b][:, (c * 2 + u) * 512 + p * 128:(c * 2 + u) * 512 + (p + 1) * 128],
                                          rhs=Vb[:, h * 64:(h + 1) * 64], start=(u == 0), stop=(u == 1)), [KH, "Vb"], ["pf1"])
        sb_in = sbi
        for c in range(2):
            V_(lambda e: e.tensor_tensor(out=v3(Sf, 4), in0=v3(Sf, 4), in1=gS[:, c:8:2].unsqueeze(2).to_broadcast([128, 4, 64]), op=ALU.mult),
               ["Sf", "gS"], ["Sf"])
            V_(lambda e: e.tensor_tensor(out=Sf, in0=Sf, in1=pf[1][:, c * 256:(c + 1) * 256], op=ALU.add), ["Sf", "pf1"], ["Sf"])
            sbi = (sbi + 1) % 3
            S_(lambda e: e.copy(out=Sb[sbi], in_=Sf), ["Sf"], ["Sb%d" % sbi])
        sb0, sb1 = sb_in, (sb_in + 1) % 3
        for h in range(8):
            p = h // 2
            for c, sbx in ((0, sb0), (1, sb1)):
                T_(lambda e: e.matmul(pf[0][:, h * 64:(h + 1) * 64], lhsT=QpTm[b][:, ((p * 2 + h % 2) * 2 + c) * 128:((p * 2 + h % 2) * 2 + c + 1) * 128],
                                      rhs=Sb[sbx][:, p * 64:(p + 1) * 64], start=(c == 0), stop=False), [QP, "Sb%d" % sbx], ["pf0"])
            T_(lambda e: e.matmul(pf[0][:, h * 64:(h + 1) * 64], lhsT=attm[:, h * 128:(h + 1) * 128], rhs=Vb[:, h * 64:(h + 1) * 64],
                                  start=False, stop=True), ["attm", "Vb"], ["pf0"])
        S_(lambda e: e.activation(out=sq, in_=pf[0], func=AF.Square), ["pf0"], ["sq"])
        V_(lambda e: e.tensor_reduce(out=ss8, in_=v3(sq, 8), op=ALU.add, axis=AX.X), ["sq"], ["ss8"])
        S_(lambda e: e.activation(out=rs8, in_=ss8, func=AF.Sqrt, scale=1.0 / 64, bias=1e-6), ["ss8"], ["rs8"])
        V_(lambda e: e.reciprocal(out=rs8, in_=rs8), ["rs8"], ["rs8"])
        V_(lambda e: e.tensor_tensor(out=v3(o32, 8), in0=v3(pf[0], 8), in1=rs8.unsqueeze(2).to_broadcast([128, 8, 64]), op=ALU.mult), ["pf0", "rs8"], ["o32"])
        V_(lambda e: e.tensor_tensor(out=v3(o32, 8), in0=v3(o32, 8), in1=gn.unsqueeze(1).to_broadcast([128, 8, 64]), op=ALU.mult), ["o32", "gn"], ["o32"])
        if k.debug:
            P.dma("sync", lambda e: e.dma_start(out=k.dbg["c"][n * 128:(n + 1) * 128, :], in_=o32), reads=["o32"])
        V_(lambda e: e.tensor_tensor(out=ygb, in0=o32, in1=gate, op=ALU.mult), ["o32", "gate"], ["ygb"])
        store_yT(k, "c", n, "ygb", ygb, 3, l)


def phaseA(k, l):
    P, nc = k.P, k.nc
    new_phase(k)
    fa, ba = k.fa, k.ba
    NT = k.NT
    pf = k.pf
    idb = cview(k, "ident")
    C0 = float(np.exp(-0.5))
    V_ = lambda fn, r, w: P.op("vector", fn, r, w)
    S_ = lambda fn, r, w: P.op("scalar", fn, r, w)
    G_ = lambda fn, r, w: P.op("gpsimd", fn, r, w)
    T_ = lambda fn, r, w: P.op("tensor", fn, r, w)
    bc = lambda src: src.partition_broadcast(128)
    W1 = v3(ba.alloc(8 * A_COLS), 8)
    W2 = v3(ba.alloc(8 * A_COLS), 8)
    w2a2 = ba.alloc(1024)
    vecs = {}
    for nm, src in (("w0b", k.rwkv_w0), ("a0b", k.rwkv_a0), ("kkb", k.rwkv_k_k), ("kab", k.rwkv_k_a), ("rkb", k.rwkv_r_k),
                    ("lnw", k.rwkv_ln_w), ("lnb", k.rwkv_ln_b)):
        vecs[nm] = fa.alloc(512)
        P.dma("sync", lambda e: e.dma_start(out=vecs[nm], in_=bc(src[l:l + 1, :])), writes=[nm])
    G_(lambda e: e.memset(w2a2, 0.0), [], ["w2a2"])
    P.dma("gpsimd", lambda e: e.dma_start(out=w2a2[0:64, 0:512], in_=k.rwkv_w2[l, :, :]), writes=["w2a2"])
    P.dma("gpsimd", lambda e: e.dma_start(out=w2a2[64:128, 512:1024], in_=k.rwkv_a2[l, :, :]), writes=["w2a2"])
    keep = k.fa.a.off
    mu_b = fa.alloc(A_COLS)
    omu = fa.alloc(A_COLS)
    stage = fa.alloc(A_COLS)
    P.dma("sync", lambda e: e.dma_start(out=mu_b, in_=bc(k.rwkv_mu[l:l + 1, :])), writes=["mu_b"])
    V_(lambda e: e.tensor_scalar(out=omu, in0=mu_b, scalar1=-1.0, scalar2=1.0, op0=ALU.mult, op1=ALU.add), ["mu_b"], ["omu"])
    for c in range(8):
        P.dma("sync", lambda e: e.dma_start(out=stage, in_=k.w_in[l, c * 128:(c + 1) * 128, 0:A_COLS]), writes=["stage"])
        V_(lambda e: e.tensor_tensor(out=W1[:, c, :], in0=stage, in1=omu, op=ALU.mult), ["stage", "omu"], ["W1"])
        G_(lambda e: e.tensor_tensor(out=W2[:, c, :], in0=stage, in1=mu_b, op=ALU.mult), ["stage", "mu_b"], ["W2"])
    P.barrier()
    k.fa.a.off = keep
    tri, chm = cview(k, "tri", False), cview(k, "ch", False)
    msu, msl, miu = cview(k, "msu", False), cview(k, "msl", False), cview(k, "miu", False)
    idf = cview(k, "ident", False)
    STf = fa.alloc(256)
    V_(lambda e: e.memset(STf, 0.0), [], ["STf"])
    STb = [ba.alloc(256) for _ in range(3)]
    G_(lambda e: e.memset(STb[0], 0.0), [], ["STb0"])
    Uall = ba.alloc(512)
    G_(lambda e: e.memset(Uall, 0.0), [], ["Uall"])
    msk = {}
    for nm, sz in (("ATm", 1024), ("BTm", 1024), ("KTm", 1024), ("RTc", 2048), ("M1m", 2048), ("Atm", 1024), ("Bh", 2048), ("Kh", 2048)):
        msk[nm] = ba.alloc(sz)
        G_(lambda e: e.memset(msk[nm], 0.0), [], [nm])
    hTc = v3(ba.alloc(1024), 8)
    hTp = v3(ba.alloc(1024), 8)
    F = [fa.alloc(512) for _ in range(11)]
    FK = ["F%d" % i for i in range(11)]
    lor = ba.alloc(128)
    lorT = ba.alloc(128)
    Rt, At, Bt, Kt, Vb = [ba.alloc(512) for _ in range(5)]
    PA, QA, PB, QB, XI, Aak, ArbT, ArkT = [ba.alloc(1024) for _ in range(8)]
    gS, ss8, rk8, m8, q8, r8 = [fa.alloc(8) for _ in range(6)]
    k.ystage = [ba.alloc(512)] * 2
    cols = {"r": 0, "wl": 512, "k": 576, "v": 1088, "al": 1600, "g": 1664}
    h3 = lambda ap: v3(ap, 8)
    b864 = lambda ap: ap.unsqueeze(2).to_broadcast([128, 8, 64])
    sbi = 0

    def evac_masked(dst, dkey, src_bf, skey, chunked):
        if chunked:
            d6 = dst.rearrange("q (p u c t) -> q p u c t", p=4, u=2, c=2)
        else:
            d4 = dst.rearrange("q (p u t) -> q p u t", p=4, u=2)
        for u in range(2):
            rows = slice(u * 64, (u + 1) * 64)
            if chunked:
                for c in range(2):
                    S_(lambda e: e.copy(out=d6[rows, :, u, c, c * 64:(c + 1) * 64], in_=v3(src_bf[rows, :], 4)[:, :, c * 64:(c + 1) * 64]), [skey], [dkey])
            else:
                S_(lambda e: e.copy(out=d4[rows, :, u, :], in_=v3(src_bf[rows, :], 4)), [skey], [dkey + "0", dkey + "1"])

    def headmm(bankA, bankB, lhs, lkey, rhs, rkey, halves=(0, 1)):
        for h in range(8):
            if h // 4 not in halves:
                continue
            bank = bankA if h < 4 else bankB
            hk = str(h // 4)
            T_(lambda e: e.matmul(pf[bank][:, (h % 4) * 128:(h % 4 + 1) * 128], lhsT=lhs[:, h * 128:(h + 1) * 128], rhs=rhs[:, h * 128:(h + 1) * 128],
                                  start=True, stop=True), [lkey + hk, rkey + hk], ["pf%d" % bank])

    def headmm_rc(bankA, bankB, lhs, lkey):
        for h in range(8):
            bank = bankA if h < 4 else bankB
            for c in range(2):
                o0 = (h % 4) * 128 + c * 64
                r0 = (h * 2 + c) * 128 + c * 64
                T_(lambda e: e.matmul(pf[bank][:, o0:o0 + 64], lhsT=lhs[:, h * 128:(h + 1) * 128], rhs=msk["RTc"][:, r0:r0 + 64],
                                      start=True, stop=True), [lkey + str(h // 4), "RTc"], ["pf%d" % bank])

    def evac_mask(dst, dkey, bankA, bankB, mask, halves=(0, 1), engs=("scalar", "vector")):
        for i, bank in enumerate((bankA, bankB)):
            if i not in halves:
                continue
            if mask is None:
                if engs[i] == "scalar":
                    S_(lambda e: e.copy(out=dst[:, i * 512:(i + 1) * 512], in_=pf[bank]), ["pf%d" % bank], [dkey + str(i)])
                else:
                    V_(lambda e: e.tensor_copy(out=dst[:, i * 512:(i + 1) * 512], in_=pf[bank]), ["pf%d" % bank], [dkey + str(i)])
            else:
                V_(lambda e: e.tensor_tensor(out=v3(dst[:, i * 512:(i + 1) * 512], 4), in0=v3(pf[bank], 4), in1=mask.unsqueeze(1).to_broadcast([128, 4, 128]),
                                             op=ALU.mult), ["pf%d" % bank, "cm_f"], [dkey + str(i)])

    for n in range(NT):
        P.dma("sync", lambda e: e.dma_start(out=hTc, in_=k.hT_d[:, :, 16 + n * 128:16 + (n + 1) * 128]), writes=["hTc"])
        P.dma("sync", lambda e: e.dma_start(out=hTp, in_=k.hT_d[:, :, 15 + n * 128:15 + (n + 1) * 128]), writes=["hTp"])

        def proj(out_ap, okey, c0, ncols):
            for c in range(8):
                T_(lambda e: e.matmul(out_ap, lhsT=hTc[:, c, :], rhs=W1[:, c, c0:c0 + ncols], start=(c == 0), stop=False), ["hTc", "W1"], [okey])
                T_(lambda e: e.matmul(out_ap, lhsT=hTp[:, c, :], rhs=W2[:, c, c0:c0 + ncols], start=False, stop=(c == 7)), ["hTp", "W2"], [okey])
        proj(pf[0], "pf0", cols["r"], 512)
        proj(pf[1], "pf1", cols["k"], 512)
        proj(pf[2], "pf2", cols["v"], 512)
        proj(pf[3], "pf3", cols["g"], 512)
        proj(pf[4][:, 0:64], "pf4", cols["wl"], 64)
        proj(pf[4][:, 64:128], "pf4", cols["al"], 64)
        p5b, p6b = pbf(k, 5), pbf(k, 6)
        S_(lambda e: e.activation(out=lor[:, 0:64], in_=pf[4][:, 0:64], func=AF.Tanh), ["pf4"], ["lor"])
        S_(lambda e: e.copy(out=lor[:, 64:128], in_=pf[4][:, 64:128]), ["pf4"], ["lor"])
        T_(lambda e: e.transpose(out=p5b[:, 0:128], in_=lor, identity=idb), ["lor", "cm_b"], ["pf5"])
        S_(lambda e: e.copy(out=lorT, in_=p5b[:, 0:128]), ["pf5"], ["lorT"])
        T_(lambda e: e.matmul(pf[5], lhsT=lorT, rhs=w2a2[:, 0:512], start=True, stop=True), ["lorT", "w2a2"], ["pf5"])
        T_(lambda e: e.matmul(pf[6], lhsT=lorT, rhs=w2a2[:, 512:1024], start=True, stop=True), ["lorT", "w2a2"], ["pf6"])
        V_(lambda e: e.tensor_tensor(out=F[1], in0=pf[5], in1=vecs["w0b"], op=ALU.add), ["pf5", "w0b"], ["F1"])
        S_(lambda e: e.activation(out=F[1], in_=F[1], func=AF.Sigmoid), ["F1"], ["F1"])
        V_(lambda e: e.tensor_tensor(out=F[4], in0=pf[6], in1=vecs["a0b"], op=ALU.add), ["pf6", "a0b"], ["F4"])
        S_(lambda e: e.activation(out=F[2], in_=F[4], func=AF.Sigmoid), ["F4"], ["F2"])
        S_(lambda e: e.copy(out=F[0], in_=pf[2]), ["pf2"], ["F0"])
        S_(lambda e: e.copy(out=Vb, in_=pf[2]), ["pf2"], ["Vb"])
        S_(lambda e: e.activation(out=F[10], in_=pf[3], func=AF.Silu), ["pf3"], ["F10"])
        T_(lambda e: e.matmul(pf[5], lhsT=tri, rhs=F[1], start=True, stop=True), ["F1", "cm_f"], ["pf5"])
        T_(lambda e: e.matmul(pf[6], lhsT=chm, rhs=F[1], start=True, stop=True), ["F1", "cm_f"], ["pf6"])
        for p in range(4):
            T_(lambda e: e.matmul(pf[7][:, p * 2:p * 2 + 2], lhsT=F[1][:, p * 128:(p + 1) * 128], rhs=k.chs, start=True, stop=True), ["F1", "chs"], ["pf7"])
        S_(lambda e: e.activation(out=gS, in_=pf[7][:, 0:8], func=AF.Exp, scale=-C0), ["pf7"], ["gS"])
        S_(lambda e: e.copy(out=F[3], in_=pf[5]), ["pf5"], ["F3"])
        V_(lambda e: e.tensor_tensor(out=F[4], in0=pf[5], in1=F[1], op=ALU.subtract), ["pf5", "F1"], ["F4"])
        S_(lambda e: e.activation(out=F[5], in_=F[3], func=AF.Exp, scale=-C0), ["F3"], ["F5"])
        V_(lambda e: e.tensor_tensor(out=Rt, in0=pf[0], in1=F[5], op=ALU.mult), ["pf0", "F5"], ["Rt"])
        S_(lambda e: e.activation(out=F[6], in_=F[4], func=AF.Exp, scale=-C0), ["F4"], ["F6"])
        V_(lambda e: e.tensor_tensor(out=F[7], in0=pf[1], in1=vecs["kkb"], op=ALU.mult), ["pf1", "kkb"], ["F7"])
        S_(lambda e: e.activation(out=F[9], in_=F[7], func=AF.Square), ["F7"], ["F9"])
        V_(lambda e: e.tensor_reduce(out=ss8, in_=h3(F[9]), op=ALU.add, axis=AX.X), ["F9"], ["ss8"])
        S_(lambda e: e.activation(out=ss8, in_=ss8, func=AF.Sqrt), ["ss8"], ["ss8"])
        V_(lambda e: e.tensor_scalar(out=ss8, in0=ss8, scalar1=1e-12, scalar2=0.0, op0=ALU.max, op1=ALU.add), ["ss8"], ["ss8"])
        V_(lambda e: e.reciprocal(out=ss8, in_=ss8), ["ss8"], ["ss8"])
        V_(lambda e: e.tensor_tensor(out=h3(F[7]), in0=h3(F[7]), in1=b864(ss8), op=ALU.mult), ["F7", "ss8"], ["F7"])
        V_(lambda e: e.scalar_tensor_tensor(out=At, in0=F[7], scalar=-1.0, in1=F[6], op0=ALU.mult, op1=ALU.mult), ["F7", "F6"], ["At"])
        atm4 = msk["Atm"].rearrange("q (u p w) -> q u p w", u=2, p=4)
        for u in range(2):
            G_(lambda e: e.tensor_copy(out=atm4[:, u, :, u * 64:(u + 1) * 64], in_=v3(At, 4)[:, :, u * 64:(u + 1) * 64]), ["At"], ["Atm"])
        V_(lambda e: e.tensor_tensor(out=F[4], in0=pf[6], in1=F[3], op=ALU.subtract), ["pf6", "F3", "F6"], ["F4"])
        S_(lambda e: e.activation(out=F[5], in_=F[3], func=AF.Exp, scale=C0), ["F3", "Rt"], ["F5"])
        S_(lambda e: e.activation(out=F[6], in_=F[4], func=AF.Exp, scale=-C0), ["F4", "At"], ["F6"])
        V_(lambda e: e.scalar_tensor_tensor(out=F[9], in0=F[2], scalar=-1.0, in1=vecs["kab"], op0=ALU.add, op1=ALU.mult), ["F2", "kab"], ["F9"])
        V_(lambda e: e.scalar_tensor_tensor(out=F[8], in0=F[9], scalar=1.0, in1=pf[1], op0=ALU.add, op1=ALU.mult), ["F9", "pf1"], ["F8"])
        V_(lambda e: e.tensor_tensor(out=F[9], in0=F[7], in1=F[2], op=ALU.mult), ["F7", "F2", "F8"], ["F9"])
        G_(lambda e: e.tensor_tensor(out=Bt, in0=F[9], in1=F[5], op=ALU.mult), ["F9", "F5"], ["Bt"])
        G_(lambda e: e.tensor_tensor(out=Kt, in0=F[8], in1=F[5], op=ALU.mult), ["F8", "F5"], ["Kt"])
        for nm, src, skey in (("Bh", F[9], "F9"), ("Kh", F[8], "F8")):
            d5 = msk[nm].rearrange("q (c u p w) -> q c u p w", c=2, u=2, p=4)
            for c in range(2):
                for u in range(2):
                    rows = slice(c * 64, (c + 1) * 64)
                    V_(lambda e: e.tensor_tensor(out=d5[rows, c, u, :, u * 64:(u + 1) * 64], in0=v3(src, 4)[rows, :, u * 64:(u + 1) * 64],
                                                 in1=v3(F[6], 4)[rows, :, u * 64:(u + 1) * 64], op=ALU.mult), [skey, "F6"], [nm])
        V_(lambda e: e.tensor_tensor(out=F[4], in0=pf[0], in1=F[8], op=ALU.mult), ["pf0", "F8", "F6"], ["F4"])
        V_(lambda e: e.tensor_tensor(out=F[4], in0=F[4], in1=vecs["rkb"], op=ALU.mult), ["F4", "rkb"], ["F4"])
        V_(lambda e: e.tensor_reduce(out=rk8, in_=h3(F[4]), op=ALU.add, axis=AX.X), ["F4"], ["rk8"])
        for j, (src, skey) in enumerate(((Rt, "Rt"), (At, "At"))):
            for c in range(4):
                T_(lambda e: e.transpose(out=p5b[:, j * 512 + c * 128:j * 512 + (c + 1) * 128], in_=src[:, c * 128:(c + 1) * 128], identity=idb),
                   [skey, "cm_b"], ["pf5"])
        for j, (src, skey) in enumerate(((Bt, "Bt"), (Kt, "Kt"))):
            for c in range(4):
                T_(lambda e: e.transpose(out=p6b[:, j * 512 + c * 128:j * 512 + (c + 1) * 128], in_=src[:, c * 128:(c + 1) * 128], identity=idb),
                   [skey, "cm_b"], ["pf6"])
        evac_masked(msk["RTc"], "RTc", p5b[:, 0:512], "pf5", True)
        evac_masked(msk["ATm"], "ATm", p5b[:, 512:1024], "pf5", False)
        evac_masked(msk["BTm"], "BTm", p6b[:, 0:512], "pf6", False)
        evac_masked(msk["KTm"], "KTm", p6b[:, 512:1024], "pf6", False)
        headmm(0, 1, msk["BTm"], "BTm", msk["ATm"], "ATm"); evac_mask(PA, "PA", 0, 1, msu)
        headmm(2, 3, msk["ATm"], "ATm", msk["BTm"], "BTm"); evac_mask(QA, "QA", 2, 3, msl)
        headmm(4, 7, msk["ATm"], "ATm", msk["KTm"], "KTm"); evac_mask(Aak, "Aak", 4, 7, msl)
        headmm_rc(5, 6, msk["BTm"], "BTm"); evac_mask(ArbT, "ArbT", 5, 6, miu)
        headmm_rc(0, 1, msk["KTm"], "KTm"); evac_mask(ArkT, "ArkT", 0, 1, miu)
        for i in range(2):
            V_(lambda e: e.tensor_tensor(out=v3(XI[:, i * 512:(i + 1) * 512], 4), in0=v3(PA[:, i * 512:(i + 1) * 512], 4),
                                         in1=idf.unsqueeze(1).to_broadcast([128, 4, 128]), op=ALU.add), ["PA%d" % i, "cm_f"], ["XI%d" % i])
        cur, nxt = (PA, QA, "PA", "QA"), (PB, QB, "PB", "QB")
        for lev in range(5):
            Pc, Qc, Pk, Qk = cur
            Pn, Qn, Pnk, Qnk = nxt
            for g in range(2):
                if lev < 4:
                    headmm(2, 3, Qc, Qk, Pc, Pk, halves=(g,))
                headmm(4, 7, Pc, Pk, Qc, Qk, halves=(g,))
            for g in range(2):
                evac_mask(Qn, Qnk, 4, 7, None, halves=(g,), engs=("vector", "vector"))
                if lev < 4:
                    evac_mask(Pn, Pnk, 2, 3, None, halves=(g,), engs=("scalar", "scalar"))
            for g in range(2):
                for h in range(4 * g, 4 * g + 4):
                    bank = 5 if h < 4 else 6
                    o = pf[bank][:, (h % 4) * 128:(h % 4 + 1) * 128]
                    T_(lambda e: e.matmul(o, lhsT=idb, rhs=XI[:, h * 128:(h + 1) * 128], start=True, stop=False), ["XI%d" % g, "cm_b"], ["pf%d" % bank])
                    T_(lambda e: e.matmul(o, lhsT=Qn[:, h * 128:(h + 1) * 128], rhs=XI[:, h * 128:(h + 1) * 128], start=False, stop=True),
                       ["XI%d" % g, Qnk + str(g)], ["pf%d" % bank])
            evac_mask(XI, "XI", 5, 6, None, engs=("scalar", "vector"))
            cur, nxt = nxt, cur
        for p in range(4):
            for u in range(2):
                h = 2 * p + u
                T_(lambda e: e.matmul(pf[2][:, p * 128:(p + 1) * 128], lhsT=msk["Atm"][:, u * 512 + p * 128:u * 512 + (p + 1) * 128],
                                      rhs=XI[:, h * 128:(h + 1) * 128], start=(u == 0), stop=(u == 1)), ["Atm", "XI%d" % (h // 4)], ["pf2"])
        S_(lambda e: e.copy(out=Bt, in_=pf[2]), ["pf2"], ["Bt"])
        m6 = msk["M1m"].rearrange("q (p u c t) -> q p u c t", p=4, u=2, c=2)
        for u in range(2):
            rows = slice(u * 64, (u + 1) * 64)
            for c in range(2):
                G_(lambda e: e.tensor_copy(out=m6[rows, :, u, c, c * 64:(c + 1) * 64], in_=v3(Bt[rows, :], 4)[:, :, c * 64:(c + 1) * 64]), ["Bt"], ["M1m"])
        M2 = PA
        headmm(3, 4, Aak, "Aak", XI, "XI"); evac_mask(M2, "PA", 3, 4, None)
        sb_in = sbi
        for c in range(2):
            ub = 0 if c == 0 else 2
            UB = "pf%d" % ub
            sbc = (sb_in + c) % 3
            for h in range(8):
                p = h // 2
                o = pf[ub][:, h * 64:(h + 1) * 64]
                T_(lambda e: e.matmul(o, lhsT=msk["M1m"][:, (h * 2 + c) * 128:(h * 2 + c + 1) * 128], rhs=STb[sbc][:, p * 64:(p + 1) * 64],
                                      start=True, stop=False), ["M1m", "STb%d" % sbc], [UB])
                T_(lambda e: e.matmul(o, lhsT=M2[:, h * 128:(h + 1) * 128], rhs=Vb[:, h * 64:(h + 1) * 64], start=False, stop=True), ["PA%d" % (h // 4), "Vb"], [UB])
            rows = slice(c * 64, (c + 1) * 64)
            S_(lambda e: e.copy(out=Uall[rows, :], in_=pf[ub][rows, :]), [UB], ["Uall"])
            for p in range(4):
                for u in range(2):
                    h = 2 * p + u
                    o = pf[1][:, p * 64:(p + 1) * 64]
                    T_(lambda e: e.matmul(o, lhsT=msk["Bh"][:, (c * 2 + u) * 512 + p * 128:(c * 2 + u) * 512 + (p + 1) * 128],
                                          rhs=Uall[:, h * 64:(h + 1) * 64], start=(u == 0), stop=False), ["Bh", "Uall"], ["pf1"])
                    T_(lambda e: e.matmul(o, lhsT=msk["Kh"][:, (c * 2 + u) * 512 + p * 128:(c * 2 + u) * 512 + (p + 1) * 128],
                                          rhs=Vb[:, h * 64:(h + 1) * 64], start=False, stop=(u == 1)), ["Kh", "Vb"], ["pf1"])
            V_(lambda e: e.tensor_tensor(out=v3(STf, 4), in0=v3(STf, 4), in1=gS[:, c:8:2].unsqueeze(2).to_broadcast([128, 4, 64]), op=ALU.mult),
               ["STf", "gS"], ["STf"])
            V_(lambda e: e.tensor_tensor(out=STf, in0=STf, in1=pf[1][:, 0:256], op=ALU.add), ["STf", "pf1"], ["STf"])
            sbi = (sbi + 1) % 3
            S_(lambda e: e.copy(out=STb[sbi], in_=STf), ["STf"], ["STb%d" % sbi])
        sb0, sb1 = sb_in, (sb_in + 1) % 3
        for h in range(8):
            p = h // 2
            o = pf[3][:, h * 64:(h + 1) * 64]
            for c, sbx in ((0, sb0), (1, sb1)):
                T_(lambda e: e.matmul(o, lhsT=msk["RTc"][:, (h * 2 + c) * 128:(h * 2 + c + 1) * 128], rhs=STb[sbx][:, p * 64:(p + 1) * 64],
                                      start=(c == 0), stop=False), ["RTc", "STb%d" % sbx], ["pf3"])
            T_(lambda e: e.matmul(o, lhsT=ArbT[:, h * 128:(h + 1) * 128], rhs=Uall[:, h * 64:(h + 1) * 64], start=False, stop=False), ["ArbT%d" % (h // 4), "Uall"], ["pf3"])
            T_(lambda e: e.matmul(o, lhsT=ArkT[:, h * 128:(h + 1) * 128], rhs=Vb[:, h * 64:(h + 1) * 64], start=False, stop=True), ["ArkT%d" % (h // 4), "Vb"], ["pf3"])
        S_(lambda e: e.copy(out=F[4], in_=pf[3]), ["pf3"], ["F4"])
        V_(lambda e: e.tensor_reduce(out=m8, in_=h3(F[4]), op=ALU.add, axis=AX.X), ["F4"], ["m8"])
        S_(lambda e: e.activation(out=F[9], in_=F[4], func=AF.Square), ["F4"], ["F9"])
        V_(lambda e: e.tensor_reduce(out=q8, in_=h3(F[9]), op=ALU.add, axis=AX.X), ["F9"], ["q8"])
        V_(lambda e: e.tensor_scalar(out=m8, in0=m8, scalar1=1.0 / 64, scalar2=0.0, op0=ALU.mult, op1=ALU.add), ["m8"], ["m8"])
        V_(lambda e: e.tensor_tensor(out=r8, in0=m8, in1=m8, op=ALU.mult), ["m8"], ["r8"])
        V_(lambda e: e.scalar_tensor_tensor(out=q8, in0=q8, scalar=1.0 / 64, in1=r8, op0=ALU.mult, op1=ALU.subtract), ["q8", "r8"], ["q8"])
        S_(lambda e: e.activation(out=r8, in_=q8, func=AF.Sqrt, bias=64e-5), ["q8"], ["r8"])
        V_(lambda e: e.reciprocal(out=r8, in_=r8), ["r8"], ["r8"])
        V_(lambda e: e.tensor_tensor(out=h3(F[4]), in0=h3(F[4]), in1=b864(m8), op=ALU.subtract), ["F4", "m8"], ["F4"])
        V_(lambda e: e.tensor_tensor(out=h3(F[4]), in0=h3(F[4]), in1=b864(r8), op=ALU.mult), ["F4", "r8"], ["F4"])
        V_(lambda e: e.tensor_tensor(out=F[4], in0=F[4], in1=vecs["lnw"], op=ALU.mult), ["F4", "lnw"], ["F4"])
        V_(lambda e: e.tensor_tensor(out=F[4], in0=F[4], in1=vecs["lnb"], op=ALU.add), ["F4", "lnb"], ["F4"])
        V_(lambda e: e.tensor_tensor(out=h3(F[9]), in0=h3(F[0]), in1=b864(rk8), op=ALU.mult), ["F0", "rk8"], ["F9"])
        V_(lambda e: e.tensor_tensor(out=F[4], in0=F[4], in1=F[9], op=ALU.add), ["F4", "F9"], ["F4"])
        if k.debug:
            P.dma("sync", lambda e: e.dma_start(out=k.dbg["a"][n * 128:(n + 1) * 128, :], in_=F[4]), reads=["F4"])
        V_(lambda e: e.tensor_tensor(out=Rt, in0=F[4], in1=F[10], op=ALU.mult), ["F4", "F10"], ["Rt"])
        store_yT(k, "a", n, "Rt", Rt, 3, l)
```

```python
import numpy as np
from contextlib import ExitStack
import concourse.bass as bass
import concourse.mybir as mybir
from concourse.bass_utils import run_bass_kernel_spmd

F32 = mybir.dt.float32
BF16 = mybir.dt.bfloat16
AF = mybir.ActivationFunctionType
ALU = mybir.AluOpType
AX = mybir.AxisListType

D_MODEL = 1024
A_COLS = 2176
B_COLS = 2048
C_COLS = 2048
IN_COLS = 9344
N_REL = 320


class _Rec:
    def __init__(self):
        self.call = None

    def __getattr__(self, name):
        def f(*a, **kw):
            self.call = (name, a, kw)
            return self
        return f


def _bind(fn):
    rec = _Rec()
    fn(rec)
    name, a, kw = rec.call
    return lambda eng: getattr(eng, name)(*a, **kw)


class Op:
    __slots__ = ("eng", "fn", "deps", "signal", "sigval", "dma_sem", "dma_val", "is_dma", "pre_wait")

    def __init__(self, eng, fn):
        self.eng = eng
        self.fn = fn
        self.deps = []
        self.signal = False
        self.sigval = 0
        self.is_dma = False
        self.dma_sem = None
        self.dma_val = 0
        self.pre_wait = None


class Prog:
    ENGS = ("tensor", "vector", "scalar", "gpsimd", "sync")
    NDMA = 12

    def __init__(self, nc, es):
        self.nc = nc
        self.ops = {e: [] for e in self.ENGS}
        self.last_write = {}
        self.readers = {}
        self.sem = {e: es.enter_context(nc.semaphore("s_" + e)) for e in ("tensor", "vector", "scalar", "gpsimd")}
        self.dma_sems = {q: [es.enter_context(nc.semaphore("d_%s_%d" % (q, i))) for i in range(self.NDMA)]
                         for q in ("sync", "gpsimd")}
        self.dma_count = {"sync": 0, "gpsimd": 0}
        self.dma_hist = {"sync": [], "gpsimd": []}

    def _deps(self, op, reads, writes):
        deps = []
        for k in reads:
            w = self.last_write.get(k)
            if w is not None:
                deps.append(w)
        for k in writes:
            w = self.last_write.get(k)
            if w is not None:
                deps.append(w)
            deps.extend(self.readers.get(k, ()))
        seen = set()
        for d in deps:
            if id(d) in seen or d is op:
                continue
            seen.add(id(d))
            if d.eng == op.eng and op.eng == "tensor" and not d.is_dma:
                continue
            op.deps.append(d)
            if not d.is_dma:
                d.signal = True
        for k in reads:
            self.readers.setdefault(k, []).append(op)
        for k in writes:
            self.last_write[k] = op
            self.readers[k] = []

    def op(self, eng, fn, reads=(), writes=()):
        o = Op(eng, _bind(fn))
        self._deps(o, reads, writes)
        self._apply_bar(o)
        self.ops[eng].append(o)
        return o

    def dma(self, q, fn, reads=(), writes=()):
        o = Op(q, _bind(fn))
        o.is_dma = True
        i = self.dma_count[q]
        self.dma_count[q] += 1
        o.dma_sem = self.dma_sems[q][i % self.NDMA]
        o.dma_val = 16 * (i // self.NDMA + 1)
        if i >= self.NDMA:
            o.pre_wait = self.dma_hist[q][i - self.NDMA]
        self.dma_hist[q].append(o)
        self._deps(o, reads, writes)
        self._apply_bar(o)
        self.ops[q].append(o)
        return o

    def emit(self, block):
        for e in ("tensor", "vector", "scalar", "gpsimd"):
            c = 0
            for o in self.ops[e]:
                if o.is_dma:
                    continue
                if o.signal:
                    c += 1
                o.sigval = c
        all_dmas = self.dma_hist["sync"] + self.dma_hist["gpsimd"]

        def run(eng_name):
            def body(eng):
                water = {}

                def wait(sem, val):
                    key = id(sem)
                    if water.get(key, 0) >= val:
                        return
                    water[key] = val
                    eng.wait_ge(sem, val)

                for o in self.ops[eng_name]:
                    if o.pre_wait is not None:
                        wait(o.pre_wait.dma_sem, o.pre_wait.dma_val)
                    for d in o.deps:
                        if d.is_dma:
                            wait(d.dma_sem, d.dma_val)
                        else:
                            wait(self.sem[d.eng], d.sigval)
                    ins = o.fn(eng)
                    if o.is_dma:
                        ins.then_inc(o.dma_sem, 16)
                    elif o.signal:
                        ins.then_inc(self.sem[eng_name], 1)
                if eng_name in ("sync", "gpsimd"):
                    for o in self.dma_hist[eng_name][-self.NDMA:]:
                        wait(o.dma_sem, o.dma_val)
            return body

        block.tensor(run("tensor"))
        block.vector(run("vector"))
        block.scalar(run("scalar"))
        block.gpsimd(run("gpsimd"))
        block.sync(run("sync"))

    def barrier(self):
        lasts = []
        for e in ("tensor", "vector", "scalar", "gpsimd"):
            cs = [o for o in self.ops[e] if not o.is_dma]
            if cs:
                lasts.append(cs[-1])
        dmas = self.dma_hist["sync"][-self.NDMA:] + self.dma_hist["gpsimd"][-self.NDMA:]
        self.last_write = {"__bar__": None}
        self.readers = {}
        self._bar = lasts + dmas
        self._bar_pending = set(self.ENGS)

    def _apply_bar(self, o):
        if getattr(self, "_bar_pending", None) and o.eng in self._bar_pending:
            self._bar_pending.discard(o.eng)
            for d in self._bar:
                if d is o:
                    continue
                if d.eng == o.eng and not d.is_dma and o.eng == "tensor":
                    continue
                o.deps.append(d)
                if not d.is_dma:
                    d.signal = True


class Arena:
    def __init__(self, tens, size):
        self.t = tens
        self.size = size
        self.off = 0
        self.mark = 0

    def reset(self):
        self.off = self.mark

    def alloc(self, n, dt):
        nf = n if dt == F32 else (n + 1) // 2
        nf_al = (nf + 7) // 8 * 8
        assert self.off + nf_al <= self.size, ("arena overflow", self.off, nf_al, self.size)
        ap = self.t[:, self.off:self.off + nf]
        self.off += nf_al
        if dt != F32:
            ap = ap.bitcast(dt)[:, 0:n]
        return ap


class _AView:
    def __init__(self, arena, dt):
        self.a, self.dt = arena, dt

    def alloc(self, n):
        return self.a.alloc(n, self.dt)

    def reset(self):
        self.a.reset()

    @property
    def off(self):
        return self.a.off

    @property
    def mark(self):
        return self.a.mark

    @mark.setter
    def mark(self, v):
        self.a.mark = v


def _consts():
    s = np.arange(128)[:, None]
    t = np.arange(128)[None, :]
    same = (s // 64) == (t // 64)
    c = {}
    c["ident"] = np.eye(128)
    c["tri"] = same & (s <= t)
    c["ch"] = same
    c["midm"] = same & ((s % 64) <= 31)
    c["msu"] = same & (s < t)
    c["msl"] = same & (s > t)
    c["miu"] = same & (s <= t)
    names = ["ident", "tri", "ch", "midm", "msu", "msl", "miu"]
    arr = np.stack([c[k].astype(np.float32) for k in names], axis=1)
    chsel = np.zeros((128, 2), np.float32)
    chsel[:64, 0] = 1
    chsel[64:, 1] = 1
    negm = np.zeros((128, 5, 128), np.float32)
    negm[:64, 0, 64:] = -30000.0
    negm[64:, 4, :64] = -30000.0
    return names, arr, chsel, negm


CNAMES, CARR, CHSEL, NEGM = _consts()


def _bias_gather(rel_bias):
    k = np.arange(128)[:, None, None]
    r = np.arange(5)[None, :, None]
    q = np.arange(128)[None, None, :]
    idx = np.clip(512 + q - (r * 128 + k), -63, 256) + 63
    g = rel_bias[:, :, idx]
    return np.ascontiguousarray(np.transpose(g, (0, 2, 1, 3, 4)))


class K:
    pass


def build_nc(T=4096, L=2, branches=("a", "b", "c"), debug=False):
    NT = T // 128
    nc = bass.Bass("TRN2", target_bir_lowering=False)
    k = K()
    k.nc, k.T, k.L, k.NT, k.branches, k.debug = nc, T, L, NT, branches, debug

    def din(name, shape, dt=F32):
        return nc.dram_tensor(name, list(shape), dt, kind="ExternalInput").ap()

    k.x = din("x", [T, 1024])
    k.norm_g = din("norm_g", [L, 1024])
    k.w_in = din("w_in", [L, 1024, IN_COLS])
    k.rwkv_mu = din("rwkv_mu", [L, A_COLS])
    k.rwkv_w0 = din("rwkv_w0", [L, 512])
    k.rwkv_w2 = din("rwkv_w2", [L, 64, 512])
    k.rwkv_a0 = din("rwkv_a0", [L, 512])
    k.rwkv_a2 = din("rwkv_a2", [L, 64, 512])
    k.rwkv_k_k = din("rwkv_k_k", [L, 512])
    k.rwkv_k_a = din("rwkv_k_a", [L, 512])
    k.rwkv_r_k = din("rwkv_r_k", [L, 512])
    k.rwkv_ln_w = din("rwkv_ln_w", [L, 512])
    k.rwkv_ln_b = din("rwkv_ln_b", [L, 512])
    k.attn_q_norm = din("attn_q_norm", [L, 64])
    k.attn_k_norm = din("attn_k_norm", [L, 64])
    k.attn_bias = din("attn_bias", [L, 128, 8 * 5 * 128])
    k.hgrn_lb = din("hgrn_lb", [L, 512])
    k.hgrn_norm = din("hgrn_norm", [L, 64])
    k.proj_a = din("proj_a", [L, 512, 1024])
    k.proj_b = din("proj_b", [L, 512, 1024])
    k.proj_c = din("proj_c", [L, 512, 1024])
    k.w_out = din("w_out", [L, 1024, 1024])
    k.cmat = din("cmat", [128, 7 * 128])
    k.chsel = din("chsel", [128, 2])
    k.negm = din("negm", [128, 5 * 128])
    k.out = nc.dram_tensor("out", [T, 1024], F32, kind="ExternalOutput").ap()
    k.x1 = nc.dram_tensor("x1", [T, 1024], F32).ap()
    k.hT_d = nc.dram_tensor("hT_d", [128, 8, T + 16], BF16).ap()
    k.yT_d = {b: nc.dram_tensor("yT_" + b, [128, 4, T], BF16).ap() for b in "abc"}
    if debug:
        k.dbg = {b: nc.dram_tensor("dbg_" + b, [T, 512], F32, kind="ExternalOutput").ap() for b in "abc"}

    with ExitStack() as es:
        FA = 42 * 1024
        arena = Arena(es.enter_context(nc.sbuf_tensor("arena", [128, FA], F32))[:], FA)
        k.fa = _AView(arena, F32)
        k.ba = _AView(arena, BF16)
        k.pf = [es.enter_context(nc.psum_tensor("pf%d" % i, [128, 512], F32))[:] for i in range(8)]
        k.P = Prog(nc, es)
        block = es.enter_context(nc.Block())
        P = k.P
        k.cm_f = k.fa.alloc(7 * 128)
        k.cm_b = k.ba.alloc(7 * 128)
        k.chs = k.fa.alloc(2)
        P.dma("sync", lambda e: e.dma_start(out=k.cm_f, in_=k.cmat[:, :]), writes=["cm_f"])
        P.dma("gpsimd", lambda e: e.dma_start(out=k.cm_b, in_=k.cmat[:, :]), writes=["cm_b"])
        P.dma("sync", lambda e: e.dma_start(out=k.chs, in_=k.chsel[:, :]), writes=["chs"])
        k.fa.mark = k.fa.off
        k.ba.mark = k.ba.off
        for l in range(L):
            xin = k.x if l == 0 else k.x1
            xout = k.out if l == L - 1 else k.x1
            phase0(k, l, xin)
            if STOP == "0":
                phaseCopy(k, xin, xout)
                continue
            if "a" in branches:
                phaseA(k, l)
            if "b" in branches:
                phaseB(k, l)
            if "c" in branches:
                phaseC(k, l)
            if STOP == "B":
                phaseCopy(k, xin, xout)
                continue
            phaseM(k, l, xin, xout)
        P.emit(block)
    return nc


import os
STOP = os.environ.get("STOP", "")
LVL = float(os.environ.get("LVL", "9"))


def phaseCopy(k, xin, xout):
    P = k.P
    new_phase(k)
    t = k.fa.alloc(1024)
    for n in range(k.NT):
        P.dma("sync", lambda e: e.dma_start(out=t, in_=xin[n * 128:(n + 1) * 128, :]), writes=["t"])
        P.dma("sync", lambda e: e.dma_start(out=xout[n * 128:(n + 1) * 128, :], in_=t), reads=["t"])


def cview(k, name, bf=True):
    i = CNAMES.index(name)
    t = k.cm_b if bf else k.cm_f
    return t[:, i * 128:(i + 1) * 128]


def new_phase(k):
    k.P.barrier()
    k.fa.reset()
    k.ba.reset()


def phase0(k, l, xin):
    P, nc = k.P, k.nc
    new_phase(k)
    fa, ba = k.fa, k.ba
    gb = fa.alloc(1024)
    xt = [fa.alloc(1024) for _ in range(2)]
    junk = fa.alloc(1024)
    ss = [fa.alloc(1) for _ in range(2)]
    rs = [fa.alloc(1) for _ in range(2)]
    hb = [ba.alloc(1024) for _ in range(2)]
    hs = [ba.alloc(1024) for _ in range(2)]
    zc = ba.alloc(8 * 16)
    idb = cview(k, "ident")
    P.dma("sync", lambda e: e.dma_start(out=gb, in_=k.norm_g[l:l + 1, :].partition_broadcast(128)), writes=["gb"])
    if l == 0:
        P.op("gpsimd", lambda e: e.memset(zc, 0.0), writes=["zc"])
        P.dma("sync", lambda e: e.dma_start(out=k.hT_d[:, :, 0:16], in_=zc.rearrange("p (c t) -> p c t", c=8)), reads=["zc"])
    for n in range(k.NT):
        b = n % 2
        X, HB, HS, SS, RS = "xt%d" % b, "hb%d" % b, "hs%d" % b, "ss%d" % b, "rs%d" % b
        if n == 0:
            P.dma("sync", lambda e: e.dma_start(out=xt[0], in_=xin[0:128, :]), writes=["xt0"])
        if n + 1 < k.NT:
            P.dma("sync", lambda e: e.dma_start(out=xt[(n + 1) % 2], in_=xin[(n + 1) * 128:(n + 2) * 128, :]), writes=["xt%d" % ((n + 1) % 2)])
        P.op("scalar", lambda e, b=b: e.activation(out=junk, in_=xt[b], func=AF.Square, accum_out=ss[b]), reads=[X], writes=["junk", SS])
        P.op("scalar", lambda e, b=b: e.activation(out=rs[b], in_=ss[b], func=AF.Sqrt, scale=1.0 / 1024, bias=1e-6), reads=[SS], writes=[RS])
        P.op("vector", lambda e, b=b: e.reciprocal(out=rs[b], in_=rs[b]), reads=[RS], writes=[RS])
        P.op("vector", lambda e, b=b: e.scalar_tensor_tensor(out=hb[b], in0=xt[b], scalar=rs[b][:, 0:1], in1=gb, op0=ALU.mult, op1=ALU.mult),
             reads=[X, RS, "gb"], writes=[HB])
        pt = k.pf[n % 2].bitcast(BF16)
        PT = "pf%d" % (n % 2)
        for c in range(8):
            P.op("tensor", lambda e, c=c, b=b, pt=pt: e.transpose(out=pt[:, c * 128:(c + 1) * 128], in_=hb[b][:, c * 128:(c + 1) * 128], identity=idb),
                 reads=[HB, "cm_b"], writes=[PT])
        P.op("scalar", lambda e, b=b, pt=pt: e.copy(out=hs[b], in_=pt[:, 0:1024]), reads=[PT], writes=[HS])
        P.dma("gpsimd", lambda e, n=n, b=b: e.dma_start(out=k.hT_d[:, :, 16 + n * 128:16 + (n + 1) * 128], in_=hs[b].rearrange("p (c t) -> p c t", c=8)),
              reads=[HS], writes=["hT_d"])


def pbf(k, i):
    return k.pf[i].bitcast(BF16)


def load_w_cast(k, dst3, src2d, key, nsplit=8):
    C = dst3.shape[1]
    N = dst3.shape[2]
    step = max(1, 2048 // 1)
    for c in range(C):
        for n0 in range(0, N, 2048):
            n1 = min(N, n0 + 2048)
            k.P.dma("gpsimd", lambda e, c=c, n0=n0, n1=n1: e.dma_start(out=dst3[:, c, n0:n1], in_=src2d[c * 128:(c + 1) * 128, n0:n1]),
                    writes=[key])


def proj_block(k, pbank, pkey, hT, hkey, W, wkey, c0, ncols, shift=0):
    for c in range(8):
        k.P.op("tensor", lambda e, c=c: e.matmul(pbank[:, 0:ncols], lhsT=hT[:, c, shift:shift + 128], rhs=W[:, c, c0:c0 + ncols],
                                                start=(c == 0), stop=(c == 7)),
               reads=[hkey, wkey], writes=[pkey])


def store_yT(k, br, n, ysrc_key, ysrc_bf, tp_bank, l):
    P = k.P
    b = n % 2
    pt = pbf(k, tp_bank)
    PT = "pf%d" % tp_bank
    idb = cview(k, "ident")
    for c in range(4):
        P.op("tensor", lambda e, c=c: e.transpose(out=pt[:, c * 128:(c + 1) * 128], in_=ysrc_bf[:, c * 128:(c + 1) * 128], identity=idb),
             reads=[ysrc_key, "cm_b"], writes=[PT])
    ys = k.ystage[b]
    YS = "ystage%d" % (b if k.ystage[0] is not k.ystage[1] else 0)
    P.op("scalar", lambda e: e.copy(out=ys, in_=pt[:, 0:512]), reads=[PT], writes=[YS])
    P.dma("gpsimd", lambda e: e.dma_start(out=k.yT_d[br][:, :, n * 128:(n + 1) * 128], in_=ys.rearrange("p (c t) -> p c t", c=4)),
          reads=[YS], writes=["yT_d" + br])


def v3(ap, a):
    return ap.rearrange("p (a b) -> p a b", a=a)


def phaseB(k, l):
    P, nc = k.P, k.nc
    new_phase(k)
    fa, ba = k.fa, k.ba
    NT = k.NT
    idb = cview(k, "ident")
    W = v3(ba.alloc(8 * 2048), 8)
    load_w_cast(k, W, k.w_in[l, :, A_COLS:A_COLS + B_COLS], "WB")
    gq = fa.alloc(64)
    gk = fa.alloc(64)
    P.dma("sync", lambda e: e.dma_start(out=gq, in_=k.attn_q_norm[l:l + 1, :].partition_broadcast(128)), writes=["gq"])
    P.dma("sync", lambda e: e.dma_start(out=gk, in_=k.attn_k_norm[l:l + 1, :].partition_broadcast(128)), writes=["gk"])
    P.op("vector", lambda e: e.scalar_tensor_tensor(out=gq, in0=gq, scalar=0.125, in1=gk, op0=ALU.mult, op1=ALU.mult), reads=["gq", "gk"], writes=["gq"])
    bstage = fa.alloc(5120)
    nm = fa.alloc(640)
    biasT = ba.alloc(5120)
    P.dma("sync", lambda e: e.dma_start(out=bstage, in_=k.attn_bias[l, :, :]), writes=["bstage"])
    P.dma("sync", lambda e: e.dma_start(out=nm, in_=k.negm[:, :]), writes=["nm"])
    P.op("vector", lambda e: e.tensor_tensor(out=v3(biasT, 8), in0=v3(bstage, 8), in1=nm.unsqueeze(1).to_broadcast([128, 8, 640]), op=ALU.add),
         reads=["bstage", "nm"], writes=["biasT"])
    bias4 = biasT.rearrange("p (h r q) -> p h r q", h=8, r=5)
    Vr = ba.alloc(8 * 8 * 80).rearrange("p (s h d) -> p s h d", s=8, h=8)
    P.op("gpsimd", lambda e: e.memset(Vr, 1.0), writes=["Vr"])
    kT = ba.alloc(4 * 8 * 128).rearrange("p (c s t) -> p c s t", c=4, s=8)
    qTm = [ba.alloc(8 * 128) for _ in range(2)]
    for b in range(2):
        P.op("gpsimd", lambda e, b=b: e.memset(qTm[b], 0.0), writes=["qTm%d" % b])
    hTt = [v3(ba.alloc(1024), 8) for _ in range(2)]
    sqt = fa.alloc(1024)
    ss16 = fa.alloc(16)
    rs16 = fa.alloc(16)
    qn32 = fa.alloc(512)
    qb = ba.alloc(512)
    kb = ba.alloc(512)
    sg = fa.alloc(512)
    PTb = [ba.alloc(512) for _ in range(3)]
    rinv = fa.alloc(8)
    y32 = fa.alloc(512)
    ygb = ba.alloc(512)
    k.ystage = [ba.alloc(512) for _ in range(2)]
    pf = k.pf
    for n in range(NT):
        b = n % 2
        slot = n % 8
        H = "hTt%d" % b
        if n == 0:
            P.dma("sync", lambda e: e.dma_start(out=hTt[0], in_=k.hT_d[:, :, 16:16 + 128]), writes=["hTt0"])
        if n + 1 < NT:
            P.dma("sync", lambda e: e.dma_start(out=hTt[(n + 1) % 2], in_=k.hT_d[:, :, 16 + (n + 1) * 128:16 + (n + 2) * 128]), writes=["hTt%d" % ((n + 1) % 2)])
        for blk in range(4):
            proj_block(k, pf[blk], "pf%d" % blk, hTt[b], H, W, "WB", blk * 512, 512)
        if LVL < 1.1:
            continue
        P.op("scalar", lambda e: e.activation(out=sqt[:, 0:512], in_=pf[0], func=AF.Square), reads=["pf0"], writes=["sqt"])
        P.op("scalar", lambda e: e.activation(out=sqt[:, 512:1024], in_=pf[1], func=AF.Square), reads=["pf1"], writes=["sqt"])
        P.op("vector", lambda e: e.tensor_reduce(out=ss16, in_=v3(sqt, 16), op=ALU.add, axis=AX.X), reads=["sqt"], writes=["ss16"])
        P.op("scalar", lambda e: e.activation(out=rs16, in_=ss16, func=AF.Sqrt, scale=1.0 / 64, bias=1e-6), reads=["ss16"], writes=["rs16"])
        P.op("vector", lambda e: e.reciprocal(out=rs16, in_=rs16), reads=["rs16"], writes=["rs16"])
        if LVL < 1.2:
            continue
        P.op("vector", lambda e: e.tensor_tensor(out=v3(qn32, 8), in0=v3(pf[0], 8), in1=rs16[:, 0:8].unsqueeze(2).to_broadcast([128, 8, 64]), op=ALU.mult),
             reads=["pf0", "rs16"], writes=["qn32"])
        P.op("vector", lambda e: e.tensor_tensor(out=v3(qb, 8), in0=v3(qn32, 8), in1=gq.unsqueeze(1).to_broadcast([128, 8, 64]), op=ALU.mult),
             reads=["qn32", "gq"], writes=["qb"])
        P.op("vector", lambda e: e.tensor_tensor(out=v3(kb, 8), in0=v3(pf[1], 8), in1=rs16[:, 8:16].unsqueeze(2).to_broadcast([128, 8, 64]), op=ALU.mult),
             reads=["pf1", "rs16"], writes=["kb"])
        if LVL < 1.4:
            continue
        pt = pbf(k, 0)
        for c in range(4):
            P.op("tensor", lambda e, c=c: e.transpose(out=pt[:, c * 128:(c + 1) * 128], in_=qb[:, c * 128:(c + 1) * 128], identity=idb),
                 reads=["qb", "cm_b"], writes=["pf0"])
        for c in range(4):
            P.op("tensor", lambda e, c=c: e.transpose(out=pt[:, 512 + c * 128:512 + (c + 1) * 128], in_=kb[:, c * 128:(c + 1) * 128], identity=idb),
                 reads=["kb", "cm_b"], writes=["pf0"])
        if LVL < 1.6:
            continue
        Q = "qTm%d" % b
        q4 = qTm[b].rearrange("p (c u t) -> p c u t", c=4, u=2)
        for u in range(2):
            P.op("scalar", lambda e, u=u, q4=q4: e.copy(out=q4[u * 64:(u + 1) * 64, :, u, :], in_=v3(pt[u * 64:(u + 1) * 64, 0:512], 4)),
                 reads=["pf0"], writes=[Q])
        if LVL < 1.8:
            continue
        P.op("scalar", lambda e, slot=slot: e.copy(out=kT[:, :, slot, :], in_=v3(pt[:, 512:1024], 4)), reads=["pf0"], writes=["kT"])
        if LVL < 1.9:
            continue
        P.op("scalar", lambda e, slot=slot: e.copy(out=Vr[:, slot, :, 0:64], in_=v3(pf[2], 8)), reads=["pf2"], writes=["Vr"])
        if LVL < 1.95:
            continue
        VAR = os.environ.get("VAR", "")
        if VAR == "copy3":
            P.op("scalar", lambda e: e.copy(out=sg, in_=pf[3]), reads=["pf3"], writes=["sg"])
        elif VAR == "sig2":
            P.op("scalar", lambda e: e.activation(out=sg, in_=pf[2], func=AF.Sigmoid), reads=["pf2"], writes=["sg"])
        elif VAR == "dve3":
            P.op("vector", lambda e: e.tensor_copy(out=sg, in_=pf[3]), reads=["pf3"], writes=["sg"])
        else:
            P.op("scalar", lambda e: e.activation(out=sg, in_=pf[3], func=AF.Silu), reads=["pf3"], writes=["sg"])
        if LVL < 3:
            continue
        blocks = [(h, r) for h in range(8) for r in range(5) if n - 4 + r >= 0]
        groups = [blocks[i:i + 4] for i in range(0, len(blocks), 4)]
        first_r = max(0, 4 - n)
        def emit_pv(grp, pb):
            PTK = "PT%d" % pb
            for j, (h, r) in enumerate(grp):
                kslot = (n - 4 + r) % 8
                ob = 6 + h // 4
                P.op("tensor", lambda e: e.matmul(pf[ob][:, (h % 4) * 65:(h % 4) * 65 + 65], lhsT=PTb[pb][:, j * 128:(j + 1) * 128],
                                                  rhs=Vr[:, kslot, h, 0:65], start=(r == first_r), stop=(r == 4)), reads=[PTK, "Vr"], writes=["pf%d" % ob])

        prev = None
        for gi, grp in enumerate(groups):
            bank = 4 + gi % 2
            BK = "pf%d" % bank
            for j, (h, r) in enumerate(grp):
                kslot = (n - 4 + r) % 8
                P.op("tensor", lambda e: e.matmul(pf[bank][:, j * 128:(j + 1) * 128], lhsT=kT[:, h // 2, kslot, :],
                                                  rhs=qTm[b][:, h * 128:(h + 1) * 128], start=True, stop=False), reads=["kT", Q], writes=[BK])
                P.op("tensor", lambda e: e.matmul(pf[bank][:, j * 128:(j + 1) * 128], lhsT=idb, rhs=bias4[:, h, r, :], start=False, stop=True),
                     reads=["biasT", "cm_b"], writes=[BK])
            if prev is not None:
                emit_pv(*prev)
            pb = gi % 3
            ncol = len(grp) * 128
            P.op("scalar", lambda e: e.activation(out=PTb[pb][:, 0:ncol], in_=pf[bank][:, 0:ncol], func=AF.Exp), reads=[BK], writes=["PT%d" % pb])
            prev = (grp, pb)
        if prev is not None:
            emit_pv(*prev)
        if LVL < 4:
            continue
        for hb_ in range(2):
            o3 = v3(pf[6 + hb_][:, 0:260], 4)
            P.op("vector", lambda e, o3=o3, hb_=hb_: e.reciprocal(out=rinv[:, hb_ * 4:(hb_ + 1) * 4].unsqueeze(2), in_=o3[:, :, 64:65]),
                 reads=["pf%d" % (6 + hb_)], writes=["rinv"])
            P.op("vector", lambda e, o3=o3, hb_=hb_: e.tensor_tensor(out=v3(y32[:, hb_ * 256:(hb_ + 1) * 256], 4), in0=o3[:, :, 0:64],
                                                                    in1=rinv[:, hb_ * 4:(hb_ + 1) * 4].unsqueeze(2).to_broadcast([128, 4, 64]), op=ALU.mult),
                 reads=["pf%d" % (6 + hb_), "rinv"], writes=["y32"])
        if k.debug:
            P.dma("sync", lambda e, n=n: e.dma_start(out=k.dbg["b"][n * 128:(n + 1) * 128, :], in_=y32), reads=["y32"])
        P.op("vector", lambda e: e.tensor_tensor(out=ygb, in0=y32, in1=sg, op=ALU.mult), reads=["y32", "sg"], writes=["ygb"])
        store_yT(k, "b", n, "ygb", ygb, 3, l)


def phaseM(k, l, xin, xout):
    P, nc = k.P, k.nc
    new_phase(k)
    fa, ba = k.fa, k.ba
    brs = [b for b in "abc" if b in k.branches]
    if not brs:
        return phaseCopy(k, xin, xout)
    projs = {"a": k.proj_a, "b": k.proj_b, "c": k.proj_c}
    Wz, Wp = {}, {}
    for bi, br in enumerate("abc"):
        if br not in brs:
            continue
        Wz[br] = v3(ba.alloc(8 * 1024), 8)
        load_w_cast(k, Wz[br], k.w_in[l, :, 6272 + bi * 1024:6272 + (bi + 1) * 1024], "Wz" + br)
        Wp[br] = v3(ba.alloc(4 * 1024), 4)
        load_w_cast(k, Wp[br], projs[br][l, :, :], "Wp" + br)
    Wo = v3(ba.alloc(8 * 1024), 8)
    load_w_cast(k, Wo, k.w_out[l, :, :], "Wo")
    TB = 512
    NB = k.T // TB
    hTb = [v3(ba.alloc(8 * TB), 8) for _ in range(2)]
    yTb = {br: [v3(ba.alloc(4 * TB), 4) for _ in range(2)] for br in brs}
    mT = v3(ba.alloc(8 * TB), 8)
    gsb = {br: fa.alloc(TB) for br in brs}
    acc = fa.alloc(TB)
    tmp = fa.alloc(TB)
    xt = [fa.alloc(1024) for _ in range(2)]
    ot = [fa.alloc(1024) for _ in range(2)]
    pf = k.pf
    for tb in range(NB):
        b = tb % 2
        H = "hTb%d" % b
        def load_blk(t_):
            b_ = t_ % 2
            P.dma("sync", lambda e: e.dma_start(out=hTb[b_], in_=k.hT_d[:, :, 16 + t_ * TB:16 + (t_ + 1) * TB]), writes=["hTb%d" % b_])
            for br in brs:
                P.dma("sync", lambda e: e.dma_start(out=yTb[br][b_], in_=k.yT_d[br][:, :, t_ * TB:(t_ + 1) * TB]), writes=["yTb%s%d" % (br, b_)])
        if tb == 0:
            load_blk(0)
        if tb + 1 < NB:
            load_blk(tb + 1)
        for fc in range(8):
            for bi, br in enumerate(brs):
                zb, pb_ = (fc * len(brs) + bi) % 4, 4 + (fc * len(brs) + bi) % 4
                for c in range(8):
                    P.op("tensor", lambda e, c=c, br=br, zb=zb: e.matmul(pf[zb], lhsT=Wz[br][:, c, fc * 128:(fc + 1) * 128], rhs=hTb[b][:, c, :],
                                                                        start=(c == 0), stop=(c == 7)), reads=["Wz" + br, H], writes=["pf%d" % zb])
                P.op("scalar", lambda e, br=br, zb=zb: e.activation(out=gsb[br], in_=pf[zb], func=AF.Sigmoid), reads=["pf%d" % zb], writes=["gsb" + br])
                for c in range(4):
                    P.op("tensor", lambda e, c=c, br=br, pb_=pb_: e.matmul(pf[pb_], lhsT=Wp[br][:, c, fc * 128:(fc + 1) * 128], rhs=yTb[br][b][:, c, :],
                                                                          start=(c == 0), stop=(c == 3)),
                         reads=["Wp" + br, "yTb%s%d" % (br, b)], writes=["pf%d" % pb_])
            for bi, br in enumerate(brs):
                pb_ = 4 + (fc * len(brs) + bi) % 4
                last = bi == len(brs) - 1
                if bi == 0:
                    dst = mT[:, fc, :] if last else acc
                    P.op("vector", lambda e, br=br, pb_=pb_, dst=dst: e.tensor_tensor(out=dst, in0=pf[pb_], in1=gsb[br], op=ALU.mult),
                         reads=["pf%d" % pb_, "gsb" + br], writes=["mT" if last else "acc"])
                else:
                    P.op("vector", lambda e, br=br, pb_=pb_: e.tensor_tensor(out=tmp, in0=pf[pb_], in1=gsb[br], op=ALU.mult),
                         reads=["pf%d" % pb_, "gsb" + br], writes=["tmp"])
                    dst = mT[:, fc, :] if last else acc
                    P.op("vector", lambda e, dst=dst: e.tensor_tensor(out=dst, in0=acc, in1=tmp, op=ALU.add),
                         reads=["acc", "tmp"], writes=["mT" if last else "acc"])
        for tt in range(TB // 128):
            n = tb * (TB // 128) + tt
            xb = n % 2
            X, O = "xm%d" % xb, "om%d" % xb
            if n == 0:
                P.dma("sync", lambda e: e.dma_start(out=xt[0], in_=xin[0:128, :]), writes=["xm0"])
            if n + 1 < k.NT:
                P.dma("sync", lambda e: e.dma_start(out=xt[(n + 1) % 2], in_=xin[(n + 1) * 128:(n + 2) * 128, :]), writes=["xm%d" % ((n + 1) % 2)])
            for cb in range(2):
                bank = cb
                for c in range(8):
                    P.op("tensor", lambda e, c=c, cb=cb, bank=bank, tt=tt: e.matmul(pf[bank], lhsT=mT[:, c, tt * 128:(tt + 1) * 128],
                                                                                   rhs=Wo[:, c, cb * 512:(cb + 1) * 512], start=(c == 0), stop=(c == 7)),
                         reads=["mT", "Wo"], writes=["pf%d" % bank])
                P.op("vector", lambda e, cb=cb, bank=bank, xb=xb: e.tensor_tensor(out=ot[xb][:, cb * 512:(cb + 1) * 512], in0=pf[bank],
                                                                                 in1=xt[xb][:, cb * 512:(cb + 1) * 512], op=ALU.add),
                     reads=["pf%d" % bank, X], writes=[O])
            P.dma("gpsimd", lambda e, n=n, xb=xb: e.dma_start(out=xout[n * 128:(n + 1) * 128, :], in_=ot[xb]), reads=[O], writes=["xout"])


_NC_CACHE = {}


def make_in_maps(inputs, T, L, nb):
    f = lambda a: np.ascontiguousarray(np.asarray(a, dtype=np.float32))
    shared = {
        "norm_g": f(inputs["norm_g"])[:L], "w_in": f(inputs["w_in"])[:L], "rwkv_mu": f(inputs["rwkv_mu"])[:L],
        "rwkv_w0": f(inputs["rwkv_w0"])[:L], "rwkv_w2": f(inputs["rwkv_w2"])[:L], "rwkv_a0": f(inputs["rwkv_a0"])[:L],
        "rwkv_a2": f(inputs["rwkv_a2"])[:L], "rwkv_k_k": f(inputs["rwkv_k_k"])[:L], "rwkv_k_a": f(inputs["rwkv_k_a"])[:L],
        "rwkv_r_k": f(inputs["rwkv_r_k"])[:L].reshape(L, 512), "rwkv_ln_w": f(inputs["rwkv_ln_w"])[:L],
        "rwkv_ln_b": f(inputs["rwkv_ln_b"])[:L], "attn_q_norm": f(inputs["attn_q_norm"])[:L],
        "attn_k_norm": f(inputs["attn_k_norm"])[:L],
        "attn_bias": _bias_gather(f(inputs["attn_rel_bias"])[:L]).reshape(L, 128, 8 * 5 * 128),
        "hgrn_lb": f(inputs["hgrn_lb"])[:L], "hgrn_norm": f(inputs["hgrn_norm"])[:L],
        "proj_a": f(inputs["proj_a"])[:L], "proj_b": f(inputs["proj_b"])[:L], "proj_c": f(inputs["proj_c"])[:L],
        "w_out": f(inputs["w_out"])[:L],
        "cmat": np.ascontiguousarray(CARR.reshape(128, 7 * 128)), "chsel": CHSEL,
        "negm": np.ascontiguousarray(NEGM.reshape(128, 640)),
    }
    x = f(inputs["x"])
    maps = []
    for b in range(nb):
        m = dict(shared)
        m["x"] = np.ascontiguousarray(x[b, :T])
        maps.append(m)
    return maps


def kernel(**inputs):
    T, L = 4096, 2
    key = (T, L)
    if key not in _NC_CACHE:
        _NC_CACHE[key] = build_nc(T, L, branches=tuple(os.environ.get("KBR", "abc")))
    nc = _NC_CACHE[key]
    maps = make_in_maps(inputs, T, L, 8)
    res = run_bass_kernel_spmd(nc, maps, core_ids=list(range(8)))
    return np.stack([r["out"] for r in res.results], axis=0).astype(np.float32)


def phaseC(k, l):
    P, nc = k.P, k.nc
    new_phase(k)
    fa, ba = k.fa, k.ba
    NT, L = k.NT, k.L
    pf = k.pf
    idb = cview(k, "ident")
    V_ = lambda fn, r, w: P.op("vector", fn, r, w)
    S_ = lambda fn, r, w: P.op("scalar", fn, r, w)
    G_ = lambda fn, r, w: P.op("gpsimd", fn, r, w)
    T_ = lambda fn, r, w: P.op("tensor", fn, r, w)
    W = v3(ba.alloc(8 * 2048), 8)
    load_w_cast(k, W, k.w_in[l, :, A_COLS + B_COLS:A_COLS + B_COLS + C_COLS], "WC")
    lbb = fa.alloc(512)
    oml = fa.alloc(512)
    if l == 0:
        V_(lambda e: e.memset(lbb, 0.0), [], ["lbb"])
    else:
        er = fa.alloc(L * 512)
        P.dma("sync", lambda e: e.dma_start(out=er, in_=k.hgrn_lb.rearrange("l c -> (l c)").unsqueeze(0).partition_broadcast(128).squeeze(1)
                                            if False else k.hgrn_lb.rearrange("(o l) c -> o (l c)", o=1).partition_broadcast(128)), writes=["er"])
        S_(lambda e: e.activation(out=er, in_=er, func=AF.Exp), ["er"], ["er"])
        ssum = fa.alloc(512)
        V_(lambda e: e.tensor_tensor(out=ssum, in0=er[:, 0:512], in1=er[:, 512:1024], op=ALU.add), ["er"], ["ssum"])
        for j in range(2, L):
            V_(lambda e: e.tensor_tensor(out=ssum, in0=ssum, in1=er[:, j * 512:(j + 1) * 512], op=ALU.add), ["er", "ssum"], ["ssum"])
        V_(lambda e: e.tensor_copy(out=lbb, in_=er[:, 512:1024]), ["er"], ["lbb"])
        for j in range(2, l + 1):
            V_(lambda e: e.tensor_tensor(out=lbb, in0=lbb, in1=er[:, j * 512:(j + 1) * 512], op=ALU.add), ["er", "lbb"], ["lbb"])
        V_(lambda e: e.reciprocal(out=ssum, in_=ssum), ["ssum"], ["ssum"])
        V_(lambda e: e.tensor_tensor(out=lbb, in0=lbb, in1=ssum, op=ALU.mult), ["lbb", "ssum"], ["lbb"])
    V_(lambda e: e.tensor_scalar(out=oml, in0=lbb, scalar1=-1.0, scalar2=1.0, op0=ALU.mult, op1=ALU.add), ["lbb"], ["oml"])
    gn = fa.alloc(64)
    P.dma("sync", lambda e: e.dma_start(out=gn, in_=k.hgrn_norm[l:l + 1, :].partition_broadcast(128)), writes=["gn"])
    tri, chm, midm, miu = cview(k, "tri", False), cview(k, "ch", False), cview(k, "midm", False), cview(k, "miu", False)
    Sf = fa.alloc(256)
    V_(lambda e: e.memset(Sf, 0.0), [], ["Sf"])
    Sb = [ba.alloc(256) for _ in range(3)]
    G_(lambda e: e.memset(Sb[0], 0.0), [], ["Sb0"])
    QpTm = [ba.alloc(2048) for _ in range(2)]
    QTm = [ba.alloc(1024) for _ in range(2)]
    Kh = [ba.alloc(2048) for _ in range(2)]
    for b in range(2):
        G_(lambda e: e.memset(QpTm[b], 0.0), [], ["QpTm%d" % b])
        G_(lambda e: e.memset(QTm[b], 0.0), [], ["QTm%d" % b])
        G_(lambda e: e.memset(Kh[b], 0.0), [], ["Kh%d" % b])
    hTt = [v3(ba.alloc(1024), 8) for _ in range(2)]
    sgm, sgn, fg, key, logf, qh, eb = [fa.alloc(512) for _ in range(7)]
    bm, be, d1, e1 = [fa.alloc(512) for _ in range(4)]
    Qp, Qt, Kt, Vb = [ba.alloc(512) for _ in range(4)]
    KT = ba.alloc(512)
    attm = ba.alloc(1024)
    gS = fa.alloc(8)
    sq = fa.alloc(512)
    ss8 = fa.alloc(8)
    rs8 = fa.alloc(8)
    o32 = fa.alloc(512)
    gate = fa.alloc(512)
    ygb = ba.alloc(512)
    k.ystage = [ba.alloc(512) for _ in range(2)]
    sbi = 0
    for n in range(NT):
        b = n % 2
        H = "hTt%d" % b
        if n == 0:
            P.dma("sync", lambda e: e.dma_start(out=hTt[0], in_=k.hT_d[:, :, 16:16 + 128]), writes=["hTt0"])
        if n + 1 < NT:
            P.dma("sync", lambda e: e.dma_start(out=hTt[(n + 1) % 2], in_=k.hT_d[:, :, 16 + (n + 1) * 128:16 + (n + 2) * 128]), writes=["hTt%d" % ((n + 1) % 2)])
        for blk in range(4):
            proj_block(k, pf[blk], "pf%d" % blk, hTt[b], H, W, "WC", blk * 512, 512)
        S_(lambda e: e.activation(out=sgm, in_=pf[1], func=AF.Sigmoid), ["pf1"], ["sgm"])
        S_(lambda e: e.activation(out=sgn, in_=pf[1], func=AF.Sigmoid, scale=-1.0), ["pf1"], ["sgn"])
        V_(lambda e: e.tensor_tensor(out=fg, in0=sgm, in1=oml, op=ALU.mult), ["sgm", "oml"], ["fg"])
        V_(lambda e: e.tensor_tensor(out=fg, in0=fg, in1=lbb, op=ALU.add), ["fg", "lbb"], ["fg"])
        G_(lambda e: e.tensor_tensor(out=key, in0=sgn, in1=oml, op=ALU.mult), ["sgn", "oml"], ["key"])
        S_(lambda e: e.activation(out=logf, in_=fg, func=AF.Ln), ["fg"], ["logf"])
        for bank, m in ((4, tri), (5, midm), (6, chm)):
            T_(lambda e: e.matmul(pf[bank], lhsT=m, rhs=logf, start=True, stop=True), ["logf", "cm_f"], ["pf%d" % bank])
        S_(lambda e: e.copy(out=Vb, in_=pf[2]), ["pf2"], ["Vb"])
        S_(lambda e: e.activation(out=qh, in_=pf[0], func=AF.Silu), ["pf0"], ["qh"])
        S_(lambda e: e.activation(out=gate, in_=pf[3], func=AF.Silu), ["pf3"], ["gate"])
        S_(lambda e: e.activation(out=eb, in_=pf[4], func=AF.Exp), ["pf4"], ["eb"])
        S_(lambda e: e.copy(out=bm, in_=pf[5]), ["pf5"], ["bm"])
        S_(lambda e: e.copy(out=be, in_=pf[6]), ["pf6"], ["be"])
        for p in range(4):
            T_(lambda e: e.matmul(pf[5][:, 256 + p * 2:256 + p * 2 + 2], lhsT=logf[:, p * 128:(p + 1) * 128], rhs=k.chs, start=True, stop=True),
               ["logf", "chs"], ["pf5"])
        S_(lambda e: e.activation(out=gS, in_=pf[5][:, 256:264], func=AF.Exp), ["pf5"], ["gS"])
        V_(lambda e: e.tensor_tensor(out=Qp, in0=qh, in1=eb, op=ALU.mult), ["qh", "eb"], ["Qp"])
        V_(lambda e: e.tensor_tensor(out=d1, in0=pf[4], in1=bm, op=ALU.subtract), ["pf4", "bm"], ["d1"])
        S_(lambda e: e.activation(out=e1, in_=d1, func=AF.Exp), ["d1"], ["e1"])
        V_(lambda e: e.tensor_tensor(out=Qt, in0=qh, in1=e1, op=ALU.mult), ["qh", "e1"], ["Qt"])
        S_(lambda e: e.activation(out=e1, in_=d1, func=AF.Exp, scale=-1.0), ["d1", "Qt"], ["e1"])
        V_(lambda e: e.tensor_tensor(out=Kt, in0=key, in1=e1, op=ALU.mult), ["key", "e1"], ["Kt"])
        V_(lambda e: e.tensor_tensor(out=d1, in0=be, in1=pf[4], op=ALU.subtract), ["pf4", "be", "e1"], ["d1"])
        S_(lambda e: e.activation(out=e1, in_=d1, func=AF.Exp), ["d1", "Kt"], ["e1"])
        KH = "Kh%d" % b
        kh5 = Kh[b].rearrange("q (c u p d) -> q c u p d", c=2, u=2, p=4)
        for c in range(2):
            for u in range(2):
                rows = slice(c * 64, (c + 1) * 64)
                V_(lambda e: e.tensor_tensor(out=kh5[rows, c, u, :, u * 64:(u + 1) * 64] if False else
                                             Kh[b].rearrange("q (c u p w) -> q c u p w", c=2, u=2, p=4)[rows, c, u, :, u * 64:(u + 1) * 64],
                                             in0=v3(key, 4)[rows, :, u * 64:(u + 1) * 64], in1=v3(e1, 4)[rows, :, u * 64:(u + 1) * 64], op=ALU.mult),
                   ["key", "e1"], [KH])
        p7 = pbf(k, 7)
        p5 = pbf(k, 5)
        for c in range(4):
            T_(lambda e: e.transpose(out=p7[:, c * 128:(c + 1) * 128], in_=Qp[:, c * 128:(c + 1) * 128], identity=idb), ["Qp", "cm_b"], ["pf7"])
        for c in range(4):
            T_(lambda e: e.transpose(out=p7[:, 512 + c * 128:512 + (c + 1) * 128], in_=Qt[:, c * 128:(c + 1) * 128], identity=idb), ["Qt", "cm_b"], ["pf7"])
        for c in range(4):
            T_(lambda e: e.transpose(out=p5[:, c * 128:(c + 1) * 128], in_=Kt[:, c * 128:(c + 1) * 128], identity=idb), ["Kt", "cm_b"], ["pf5"])
        QP, QT = "QpTm%d" % b, "QTm%d" % b
        qp6 = QpTm[b].rearrange("q (p u c t) -> q p u c t", p=4, u=2, c=2)
        qt4 = QTm[b].rearrange("q (p u t) -> q p u t", p=4, u=2)
        for u in range(2):
            rows = slice(u * 64, (u + 1) * 64)
            for c in range(2):
                S_(lambda e: e.copy(out=qp6[rows, :, u, c, c * 64:(c + 1) * 64], in_=v3(p7[rows, 0:512], 4)[:, :, c * 64:(c + 1) * 64]), ["pf7"], [QP])
            S_(lambda e: e.copy(out=qt4[rows, :, u, :], in_=v3(p7[rows, 512:1024], 4)), ["pf7"], [QT])
        S_(lambda e: e.copy(out=KT, in_=p5[:, 0:512]), ["pf5"], ["KT"])
        for h in range(8):
            bank = 4 if h < 4 else 6
            T_(lambda e: e.matmul(pf[bank][:, (h % 4) * 128:(h % 4 + 1) * 128], lhsT=KT[:, (h // 2) * 128:(h // 2 + 1) * 128],
                                  rhs=QTm[b][:, h * 128:(h + 1) * 128], start=True, stop=True), ["KT", QT], ["pf%d" % bank])
        for hb_ in range(2):
            bank = 4 if hb_ == 0 else 6
            V_(lambda e: e.tensor_tensor(out=v3(attm[:, hb_ * 512:(hb_ + 1) * 512], 4), in0=v3(pf[bank], 4),
                                         in1=miu.unsqueeze(1).to_broadcast([128, 4, 128]), op=ALU.mult), ["pf%d" % bank, "cm_f"], ["attm"])
        for c in range(2):
            for p in range(4):
                for u in range(2):
                    h = 2 * p + u
                    T_(lambda e: e.matmul(pf[1][:, c * 256 + p * 64:c * 256 + (p + 1) * 64],
                                          lhsT=Kh[b][:, (c * 2 + u) * 512 + p * 128:(c * 2 + u) * 512 + (p + 1) * 128],
                                          rhs=Vb[:, h * 64:(h + 1) * 64], start=(u == 0), stop=(u == 1)), [KH, "Vb"], ["pf1"])
        sb_in = sbi
        for c in range(2):
            V_(lambda e: e.tensor_tensor(out=v3(Sf, 4), in0=v3(Sf, 4), in1=gS[:, c:8:2].unsqueeze(2).to_broadcast([128, 4, 64]), op=ALU.mult),
               ["Sf", "gS"], ["Sf"])
            V_(lambda e: e.tensor_tensor(out=Sf, in0=Sf, in1=pf[1][:, c * 256:(c + 1) * 256], op=ALU.add), ["Sf", "pf1"], ["Sf"])
            sbi = (sbi + 1) % 3
            S_(lambda e: e.copy(out=Sb[sbi], in_=Sf), ["Sf"], ["Sb%d" % sbi])
        sb0, sb1 = sb_in, (sb_in + 1) % 3
        for h in range(8):
            p = h // 2
            for c, sbx in ((0, sb0), (1, sb1)):
                T_(lambda e: e.matmul(pf[0][:, h * 64:(h + 1) * 64], lhsT=QpTm[b][:, ((p * 2 + h % 2) * 2 + c) * 128:((p * 2 + h % 2) * 2 + c + 1) * 128],
                                      rhs=Sb[sbx][:, p * 64:(p + 1) * 64], start=(c == 0), stop=False), [QP, "Sb%d" % sbx], ["pf0"])
            T_(lambda e: e.matmul(pf[0][:, h * 64:(h + 1) * 64], lhsT=attm[:, h * 128:(h + 1) * 128], rhs=Vb[:, h * 64:(h + 1) * 64],
                                  start=False, stop=True), ["attm", "Vb"], ["pf0"])
        S_(lambda e: e.activation(out=sq, in_=pf[0], func=AF.Square), ["pf0"], ["sq"])
        V_(lambda e: e.tensor_reduce(out=ss8, in_=v3(sq, 8), op=ALU.add, axis=AX.X), ["sq"], ["ss8"])
        S_(lambda e: e.activation(out=rs8, in_=ss8, func=AF.Sqrt, scale=1.0 / 64, bias=1e-6), ["ss8"], ["rs8"])
        V_(lambda e: e.reciprocal(out=rs8, in_=rs8), ["rs8"], ["rs8"])
        V_(lambda e: e.tensor_tensor(out=v3(o32, 8), in0=v3(pf[0], 8), in1=rs8.unsqueeze(2).to_broadcast([128, 8, 64]), op=ALU.mult), ["pf0", "rs8"], ["o32"])
        V_(lambda e: e.tensor_tensor(out=v3(o32, 8), in0=v3(o32, 8), in1=gn.unsqueeze(1).to_broadcast([128, 8, 64]), op=ALU.mult), ["o32", "gn"], ["o32"])
        if k.debug:
            P.dma("sync", lambda e: e.dma_start(out=k.dbg["c"][n * 128:(n + 1) * 128, :], in_=o32), reads=["o32"])
        V_(lambda e: e.tensor_tensor(out=ygb, in0=o32, in1=gate, op=ALU.mult), ["o32", "gate"], ["ygb"])
        store_yT(k, "c", n, "ygb", ygb, 3, l)


def phaseA(k, l):
    P, nc = k.P, k.nc
    new_phase(k)
    fa, ba = k.fa, k.ba
    NT = k.NT
    pf = k.pf
    idb = cview(k, "ident")
    C0 = float(np.exp(-0.5))
    V_ = lambda fn, r, w: P.op("vector", fn, r, w)
    S_ = lambda fn, r, w: P.op("scalar", fn, r, w)
    G_ = lambda fn, r, w: P.op("gpsimd", fn, r, w)
    T_ = lambda fn, r, w: P.op("tensor", fn, r, w)
    bc = lambda src: src.partition_broadcast(128)
    W1 = v3(ba.alloc(8 * A_COLS), 8)
    W2 = v3(ba.alloc(8 * A_COLS), 8)
    w2a2 = ba.alloc(1024)
    vecs = {}
    for nm, src in (("w0b", k.rwkv_w0), ("a0b", k.rwkv_a0), ("kkb", k.rwkv_k_k), ("kab", k.rwkv_k_a), ("rkb", k.rwkv_r_k),
                    ("lnw", k.rwkv_ln_w), ("lnb", k.rwkv_ln_b)):
        vecs[nm] = fa.alloc(512)
        P.dma("sync", lambda e: e.dma_start(out=vecs[nm], in_=bc(src[l:l + 1, :])), writes=[nm])
    G_(lambda e: e.memset(w2a2, 0.0), [], ["w2a2"])
    P.dma("gpsimd", lambda e: e.dma_start(out=w2a2[0:64, 0:512], in_=k.rwkv_w2[l, :, :]), writes=["w2a2"])
    P.dma("gpsimd", lambda e: e.dma_start(out=w2a2[64:128, 512:1024], in_=k.rwkv_a2[l, :, :]), writes=["w2a2"])
    keep = k.fa.a.off
    mu_b = fa.alloc(A_COLS)
    omu = fa.alloc(A_COLS)
    stage = fa.alloc(A_COLS)
    P.dma("sync", lambda e: e.dma_start(out=mu_b, in_=bc(k.rwkv_mu[l:l + 1, :])), writes=["mu_b"])
    V_(lambda e: e.tensor_scalar(out=omu, in0=mu_b, scalar1=-1.0, scalar2=1.0, op0=ALU.mult, op1=ALU.add), ["mu_b"], ["omu"])
    for c in range(8):
        P.dma("sync", lambda e: e.dma_start(out=stage, in_=k.w_in[l, c * 128:(c + 1) * 128, 0:A_COLS]), writes=["stage"])
        V_(lambda e: e.tensor_tensor(out=W1[:, c, :], in0=stage, in1=omu, op=ALU.mult), ["stage", "omu"], ["W1"])
        G_(lambda e: e.tensor_tensor(out=W2[:, c, :], in0=stage, in1=mu_b, op=ALU.mult), ["stage", "mu_b"], ["W2"])
    P.barrier()
    k.fa.a.off = keep
    tri, chm = cview(k, "tri", False), cview(k, "ch", False)
    msu, msl, miu = cview(k, "msu", False), cview(k, "msl", False), cview(k, "miu", False)
    idf = cview(k, "ident", False)
    STf = fa.alloc(256)
    V_(lambda e: e.memset(STf, 0.0), [], ["STf"])
    STb = [ba.alloc(256) for _ in range(3)]
    G_(lambda e: e.memset(STb[0], 0.0), [], ["STb0"])
    Uall = ba.alloc(512)
    G_(lambda e: e.memset(Uall, 0.0), [], ["Uall"])
    msk = {}
    for nm, sz in (("ATm", 1024), ("BTm", 1024), ("KTm", 1024), ("RTc", 2048), ("M1m", 2048), ("Atm", 1024), ("Bh", 2048), ("Kh", 2048)):
        msk[nm] = ba.alloc(sz)
        G_(lambda e: e.memset(msk[nm], 0.0), [], [nm])
    hTc = v3(ba.alloc(1024), 8)
    hTp = v3(ba.alloc(1024), 8)
    F = [fa.alloc(512) for _ in range(11)]
    FK = ["F%d" % i for i in range(11)]
    lor = ba.alloc(128)
    lorT = ba.alloc(128)
    Rt, At, Bt, Kt, Vb = [ba.alloc(512) for _ in range(5)]
    PA, QA, PB, QB, XI, Aak, ArbT, ArkT = [ba.alloc(1024) for _ in range(8)]
    gS, ss8, rk8, m8, q8, r8 = [fa.alloc(8) for _ in range(6)]
    k.ystage = [ba.alloc(512)] * 2
    cols = {"r": 0, "wl": 512, "k": 576, "v": 1088, "al": 1600, "g": 1664}
    h3 = lambda ap: v3(ap, 8)
    b864 = lambda ap: ap.unsqueeze(2).to_broadcast([128, 8, 64])
    sbi = 0

    def evac_masked(dst, dkey, src_bf, skey, chunked):
        if chunked:
            d6 = dst.rearrange("q (p u c t) -> q p u c t", p=4, u=2, c=2)
        else:
            d4 = dst.rearrange("q (p u t) -> q p u t", p=4, u=2)
        for u in range(2):
            rows = slice(u * 64, (u + 1) * 64)
            if chunked:
                for c in range(2):
                    S_(lambda e: e.copy(out=d6[rows, :, u, c, c * 64:(c + 1) * 64], in_=v3(src_bf[rows, :], 4)[:, :, c * 64:(c + 1) * 64]), [skey], [dkey])
            else:
                S_(lambda e: e.copy(out=d4[rows, :, u, :], in_=v3(src_bf[rows, :], 4)), [skey], [dkey + "0", dkey + "1"])

    def headmm(bankA, bankB, lhs, lkey, rhs, rkey, halves=(0, 1)):
        for h in range(8):
            if h // 4 not in halves:
                continue
            bank = bankA if h < 4 else bankB
            hk = str(h // 4)
            T_(lambda e: e.matmul(pf[bank][:, (h % 4) * 128:(h % 4 + 1) * 128], lhsT=lhs[:, h * 128:(h + 1) * 128], rhs=rhs[:, h * 128:(h + 1) * 128],
                                  start=True, stop=True), [lkey + hk, rkey + hk], ["pf%d" % bank])

    def headmm_rc(bankA, bankB, lhs, lkey):
        for h in range(8):
            bank = bankA if h < 4 else bankB
            for c in range(2):
                o0 = (h % 4) * 128 + c * 64
                r0 = (h * 2 + c) * 128 + c * 64
                T_(lambda e: e.matmul(pf[bank][:, o0:o0 + 64], lhsT=lhs[:, h * 128:(h + 1) * 128], rhs=msk["RTc"][:, r0:r0 + 64],
                                      start=True, stop=True), [lkey + str(h // 4), "RTc"], ["pf%d" % bank])

    def evac_mask(dst, dkey, bankA, bankB, mask, halves=(0, 1), engs=("scalar", "vector")):
        for i, bank in enumerate((bankA, bankB)):
            if i not in halves:
                continue
            if mask is None:
                if engs[i] == "scalar":
                    S_(lambda e: e.copy(out=dst[:, i * 512:(i + 1) * 512], in_=pf[bank]), ["pf%d" % bank], [dkey + str(i)])
                else:
                    V_(lambda e: e.tensor_copy(out=dst[:, i * 512:(i + 1) * 512], in_=pf[bank]), ["pf%d" % bank], [dkey + str(i)])
            else:
                V_(lambda e: e.tensor_tensor(out=v3(dst[:, i * 512:(i + 1) * 512], 4), in0=v3(pf[bank], 4), in1=mask.unsqueeze(1).to_broadcast([128, 4, 128]),
                                             op=ALU.mult), ["pf%d" % bank, "cm_f"], [dkey + str(i)])

    def load_h(n):
        P.dma("sync", lambda e: e.dma_start(out=hTc, in_=k.hT_d[:, :, 16 + n * 128:16 + (n + 1) * 128]), writes=["hTc"])
        P.dma("sync", lambda e: e.dma_start(out=hTp, in_=k.hT_d[:, :, 15 + n * 128:15 + (n + 1) * 128]), writes=["hTp"])

    def proj(out_ap, okey, c0, ncols):
        for c in range(8):
            T_(lambda e: e.matmul(out_ap, lhsT=hTc[:, c, :], rhs=W1[:, c, c0:c0 + ncols], start=(c == 0), stop=False), ["hTc", "W1"], [okey])
            T_(lambda e: e.matmul(out_ap, lhsT=hTp[:, c, :], rhs=W2[:, c, c0:c0 + ncols], start=False, stop=(c == 7)), ["hTp", "W2"], [okey])

    def proj_tile():
        proj(pf[0], "pf0", cols["r"], 512)
        proj(pf[1], "pf1", cols["k"], 512)
        proj(pf[2], "pf2", cols["v"], 512)
        proj(pf[3], "pf3", cols["g"], 512)
        proj(pf[7][:, 256:320], "pf7", cols["wl"], 64)
        proj(pf[7][:, 320:384], "pf7", cols["al"], 64)

    load_h(0)
    proj_tile()
    for n in range(NT):
        if n + 1 < NT:
            load_h(n + 1)
        p5b, p6b = pbf(k, 5), pbf(k, 6)
        S_(lambda e: e.activation(out=lor[:, 0:64], in_=pf[7][:, 256:320], func=AF.Tanh), ["pf7"], ["lor"])
        S_(lambda e: e.copy(out=lor[:, 64:128], in_=pf[7][:, 320:384]), ["pf7"], ["lor"])
        T_(lambda e: e.transpose(out=p5b[:, 0:128], in_=lor, identity=idb), ["lor", "cm_b"], ["pf5"])
        S_(lambda e: e.copy(out=lorT, in_=p5b[:, 0:128]), ["pf5"], ["lorT"])
        T_(lambda e: e.matmul(pf[5], lhsT=lorT, rhs=w2a2[:, 0:512], start=True, stop=True), ["lorT", "w2a2"], ["pf5"])
        T_(lambda e: e.matmul(pf[6], lhsT=lorT, rhs=w2a2[:, 512:1024], start=True, stop=True), ["lorT", "w2a2"], ["pf6"])
        V_(lambda e: e.tensor_tensor(out=F[1], in0=pf[5], in1=vecs["w0b"], op=ALU.add), ["pf5", "w0b"], ["F1"])
        S_(lambda e: e.activation(out=F[1], in_=F[1], func=AF.Sigmoid), ["F1"], ["F1"])
        V_(lambda e: e.tensor_tensor(out=F[4], in0=pf[6], in1=vecs["a0b"], op=ALU.add), ["pf6", "a0b"], ["F4"])
        S_(lambda e: e.activation(out=F[2], in_=F[4], func=AF.Sigmoid), ["F4"], ["F2"])
        S_(lambda e: e.copy(out=F[0], in_=pf[2]), ["pf2"], ["F0"])
        S_(lambda e: e.copy(out=Vb, in_=pf[2]), ["pf2"], ["Vb"])
        S_(lambda e: e.activation(out=F[10], in_=pf[3], func=AF.Silu), ["pf3"], ["F10"])
        T_(lambda e: e.matmul(pf[5], lhsT=tri, rhs=F[1], start=True, stop=True), ["F1", "cm_f"], ["pf5"])
        T_(lambda e: e.matmul(pf[6], lhsT=chm, rhs=F[1], start=True, stop=True), ["F1", "cm_f"], ["pf6"])
        for p in range(4):
            T_(lambda e: e.matmul(pf[7][:, p * 2:p * 2 + 2], lhsT=F[1][:, p * 128:(p + 1) * 128], rhs=k.chs, start=True, stop=True), ["F1", "chs"], ["pf7"])
        S_(lambda e: e.activation(out=gS, in_=pf[7][:, 0:8], func=AF.Exp, scale=-C0), ["pf7"], ["gS"])
        S_(lambda e: e.copy(out=F[3], in_=pf[5]), ["pf5"], ["F3"])
        V_(lambda e: e.tensor_tensor(out=F[4], in0=pf[5], in1=F[1], op=ALU.subtract), ["pf5", "F1"], ["F4"])
        S_(lambda e: e.activation(out=F[5], in_=F[3], func=AF.Exp, scale=-C0), ["F3"], ["F5"])
        V_(lambda e: e.tensor_tensor(out=Rt, in0=pf[0], in1=F[5], op=ALU.mult), ["pf0", "F5"], ["Rt"])
        S_(lambda e: e.activation(out=F[6], in_=F[4], func=AF.Exp, scale=-C0), ["F4"], ["F6"])
        V_(lambda e: e.tensor_tensor(out=F[7], in0=pf[1], in1=vecs["kkb"], op=ALU.mult), ["pf1", "kkb"], ["F7"])
        S_(lambda e: e.activation(out=F[9], in_=F[7], func=AF.Square), ["F7"], ["F9"])
        V_(lambda e: e.tensor_reduce(out=ss8, in_=h3(F[9]), op=ALU.add, axis=AX.X), ["F9"], ["ss8"])
        S_(lambda e: e.activation(out=ss8, in_=ss8, func=AF.Sqrt), ["ss8"], ["ss8"])
        V_(lambda e: e.tensor_scalar(out=ss8, in0=ss8, scalar1=1e-12, scalar2=0.0, op0=ALU.max, op1=ALU.add), ["ss8"], ["ss8"])
        V_(lambda e: e.reciprocal(out=ss8, in_=ss8), ["ss8"], ["ss8"])
        V_(lambda e: e.tensor_tensor(out=h3(F[7]), in0=h3(F[7]), in1=b864(ss8), op=ALU.mult), ["F7", "ss8"], ["F7"])
        V_(lambda e: e.scalar_tensor_tensor(out=At, in0=F[7], scalar=-1.0, in1=F[6], op0=ALU.mult, op1=ALU.mult), ["F7", "F6"], ["At"])
        atm4 = msk["Atm"].rearrange("q (u p w) -> q u p w", u=2, p=4)
        for u in range(2):
            G_(lambda e: e.tensor_copy(out=atm4[:, u, :, u * 64:(u + 1) * 64], in_=v3(At, 4)[:, :, u * 64:(u + 1) * 64]), ["At"], ["Atm"])
        V_(lambda e: e.tensor_tensor(out=F[4], in0=pf[6], in1=F[3], op=ALU.subtract), ["pf6", "F3", "F6"], ["F4"])
        S_(lambda e: e.activation(out=F[5], in_=F[3], func=AF.Exp, scale=C0), ["F3", "Rt"], ["F5"])
        S_(lambda e: e.activation(out=F[6], in_=F[4], func=AF.Exp, scale=-C0), ["F4", "At"], ["F6"])
        V_(lambda e: e.scalar_tensor_tensor(out=F[9], in0=F[2], scalar=-1.0, in1=vecs["kab"], op0=ALU.add, op1=ALU.mult), ["F2", "kab"], ["F9"])
        V_(lambda e: e.scalar_tensor_tensor(out=F[8], in0=F[9], scalar=1.0, in1=pf[1], op0=ALU.add, op1=ALU.mult), ["F9", "pf1"], ["F8"])
        V_(lambda e: e.tensor_tensor(out=F[9], in0=F[7], in1=F[2], op=ALU.mult), ["F7", "F2", "F8"], ["F9"])
        G_(lambda e: e.tensor_tensor(out=Bt, in0=F[9], in1=F[5], op=ALU.mult), ["F9", "F5"], ["Bt"])
        G_(lambda e: e.tensor_tensor(out=Kt, in0=F[8], in1=F[5], op=ALU.mult), ["F8", "F5"], ["Kt"])
        for nm, src, skey in (("Bh", F[9], "F9"), ("Kh", F[8], "F8")):
            d5 = msk[nm].rearrange("q (c u p w) -> q c u p w", c=2, u=2, p=4)
            for c in range(2):
                for u in range(2):
                    rows = slice(c * 64, (c + 1) * 64)
                    V_(lambda e: e.tensor_tensor(out=d5[rows, c, u, :, u * 64:(u + 1) * 64], in0=v3(src, 4)[rows, :, u * 64:(u + 1) * 64],
                                                 in1=v3(F[6], 4)[rows, :, u * 64:(u + 1) * 64], op=ALU.mult), [skey, "F6"], [nm])
        V_(lambda e: e.tensor_tensor(out=F[4], in0=pf[0], in1=F[8], op=ALU.mult), ["pf0", "F8", "F6"], ["F4"])
        V_(lambda e: e.tensor_tensor(out=F[4], in0=F[4], in1=vecs["rkb"], op=ALU.mult), ["F4", "rkb"], ["F4"])
        V_(lambda e: e.tensor_reduce(out=rk8, in_=h3(F[4]), op=ALU.add, axis=AX.X), ["F4"], ["rk8"])
        for j, (src, skey) in enumerate(((Rt, "Rt"), (At, "At"))):
            for c in range(4):
                T_(lambda e: e.transpose(out=p5b[:, j * 512 + c * 128:j * 512 + (c + 1) * 128], in_=src[:, c * 128:(c + 1) * 128], identity=idb),
                   [skey, "cm_b"], ["pf5"])
        for j, (src, skey) in enumerate(((Bt, "Bt"), (Kt, "Kt"))):
            for c in range(4):
                T_(lambda e: e.transpose(out=p6b[:, j * 512 + c * 128:j * 512 + (c + 1) * 128], in_=src[:, c * 128:(c + 1) * 128], identity=idb),
                   [skey, "cm_b"], ["pf6"])
        evac_masked(msk["RTc"], "RTc", p5b[:, 0:512], "pf5", True)
        evac_masked(msk["ATm"], "ATm", p5b[:, 512:1024], "pf5", False)
        evac_masked(msk["BTm"], "BTm", p6b[:, 0:512], "pf6", False)
        evac_masked(msk["KTm"], "KTm", p6b[:, 512:1024], "pf6", False)
        headmm(0, 1, msk["BTm"], "BTm", msk["ATm"], "ATm"); evac_mask(PA, "PA", 0, 1, msu)
        headmm(2, 3, msk["ATm"], "ATm", msk["BTm"], "BTm"); evac_mask(QA, "QA", 2, 3, msl)
        headmm(4, 7, msk["ATm"], "ATm", msk["KTm"], "KTm"); evac_mask(Aak, "Aak", 4, 7, msl)
        headmm_rc(5, 6, msk["BTm"], "BTm"); evac_mask(ArbT, "ArbT", 5, 6, miu)
        headmm_rc(0, 1, msk["KTm"], "KTm"); evac_mask(ArkT, "ArkT", 0, 1, miu)
        for i in range(2):
            V_(lambda e: e.tensor_tensor(out=v3(XI[:, i * 512:(i + 1) * 512], 4), in0=v3(PA[:, i * 512:(i + 1) * 512], 4),
                                         in1=idf.unsqueeze(1).to_broadcast([128, 4, 128]), op=ALU.add), ["PA%d" % i, "cm_f"], ["XI%d" % i])
        cur, nxt = (PA, QA, "PA", "QA"), (PB, QB, "PB", "QB")
        for lev in range(5):
            Pc, Qc, Pk, Qk = cur
            Pn, Qn, Pnk, Qnk = nxt
            for g in range(2):
                if lev < 4:
                    headmm(2, 3, Qc, Qk, Pc, Pk, halves=(g,))
                headmm(4, 7, Pc, Pk, Qc, Qk, halves=(g,))
            for g in range(2):
                evac_mask(Qn, Qnk, 4, 7, None, halves=(g,), engs=("vector", "vector"))
                if lev < 4:
                    evac_mask(Pn, Pnk, 2, 3, None, halves=(g,), engs=("scalar", "scalar"))
            for g in range(2):
                for h in range(4 * g, 4 * g + 4):
                    bank = 5 if h < 4 else 6
                    o = pf[bank][:, (h % 4) * 128:(h % 4 + 1) * 128]
                    T_(lambda e: e.matmul(o, lhsT=idb, rhs=XI[:, h * 128:(h + 1) * 128], start=True, stop=False), ["XI%d" % g, "cm_b"], ["pf%d" % bank])
                    T_(lambda e: e.matmul(o, lhsT=Qn[:, h * 128:(h + 1) * 128], rhs=XI[:, h * 128:(h + 1) * 128], start=False, stop=True),
                       ["XI%d" % g, Qnk + str(g)], ["pf%d" % bank])
            evac_mask(XI, "XI", 5, 6, None, engs=("scalar", "vector"))
            cur, nxt = nxt, cur
        for p in range(4):
            for u in range(2):
                h = 2 * p + u
                T_(lambda e: e.matmul(pf[2][:, p * 128:(p + 1) * 128], lhsT=msk["Atm"][:, u * 512 + p * 128:u * 512 + (p + 1) * 128],
                                      rhs=XI[:, h * 128:(h + 1) * 128], start=(u == 0), stop=(u == 1)), ["Atm", "XI%d" % (h // 4)], ["pf2"])
        S_(lambda e: e.copy(out=Bt, in_=pf[2]), ["pf2"], ["Bt"])
        m6 = msk["M1m"].rearrange("q (p u c t) -> q p u c t", p=4, u=2, c=2)
        for u in range(2):
            rows = slice(u * 64, (u + 1) * 64)
            for c in range(2):
                G_(lambda e: e.tensor_copy(out=m6[rows, :, u, c, c * 64:(c + 1) * 64], in_=v3(Bt[rows, :], 4)[:, :, c * 64:(c + 1) * 64]), ["Bt"], ["M1m"])
        M2 = PA
        headmm(3, 4, Aak, "Aak", XI, "XI"); evac_mask(M2, "PA", 3, 4, None)
        sb_in = sbi
        for c in range(2):
            ub = 5 if c == 0 else 6
            UB = "pf%d" % ub
            sbc = (sb_in + c) % 3
            for h in range(8):
                p = h // 2
                o = pf[ub][:, h * 64:(h + 1) * 64]
                T_(lambda e: e.matmul(o, lhsT=msk["M1m"][:, (h * 2 + c) * 128:(h * 2 + c + 1) * 128], rhs=STb[sbc][:, p * 64:(p + 1) * 64],
                                      start=True, stop=False), ["M1m", "STb%d" % sbc], [UB])
                T_(lambda e: e.matmul(o, lhsT=M2[:, h * 128:(h + 1) * 128], rhs=Vb[:, h * 64:(h + 1) * 64], start=False, stop=True), ["PA%d" % (h // 4), "Vb"], [UB])
            rows = slice(c * 64, (c + 1) * 64)
            S_(lambda e: e.copy(out=Uall[rows, :], in_=pf[ub][rows, :]), [UB], ["Uall"])
            for p in range(4):
                for u in range(2):
                    h = 2 * p + u
                    o = pf[7][:, p * 64:(p + 1) * 64]
                    T_(lambda e: e.matmul(o, lhsT=msk["Bh"][:, (c * 2 + u) * 512 + p * 128:(c * 2 + u) * 512 + (p + 1) * 128],
                                          rhs=Uall[:, h * 64:(h + 1) * 64], start=(u == 0), stop=False), ["Bh", "Uall"], ["pf7"])
                    T_(lambda e: e.matmul(o, lhsT=msk["Kh"][:, (c * 2 + u) * 512 + p * 128:(c * 2 + u) * 512 + (p + 1) * 128],
                                          rhs=Vb[:, h * 64:(h + 1) * 64], start=False, stop=(u == 1)), ["Kh", "Vb"], ["pf7"])
            V_(lambda e: e.tensor_tensor(out=v3(STf, 4), in0=v3(STf, 4), in1=gS[:, c:8:2].unsqueeze(2).to_broadcast([128, 4, 64]), op=ALU.mult),
               ["STf", "gS"], ["STf"])
            V_(lambda e: e.tensor_tensor(out=STf, in0=STf, in1=pf[7][:, 0:256], op=ALU.add), ["STf", "pf7"], ["STf"])
            sbi = (sbi + 1) % 3
            S_(lambda e: e.copy(out=STb[sbi], in_=STf), ["STf"], ["STb%d" % sbi])
        sb0, sb1 = sb_in, (sb_in + 1) % 3
        for h in range(8):
            p = h // 2
            o = pf[4][:, h * 64:(h + 1) * 64]
            for c, sbx in ((0, sb0), (1, sb1)):
                T_(lambda e: e.matmul(o, lhsT=msk["RTc"][:, (h * 2 + c) * 128:(h * 2 + c + 1) * 128], rhs=STb[sbx][:, p * 64:(p + 1) * 64],
                                      start=(c == 0), stop=False), ["RTc", "STb%d" % sbx], ["pf4"])
            T_(lambda e: e.matmul(o, lhsT=ArbT[:, h * 128:(h + 1) * 128], rhs=Uall[:, h * 64:(h + 1) * 64], start=False, stop=False), ["ArbT%d" % (h // 4), "Uall"], ["pf4"])
            T_(lambda e: e.matmul(o, lhsT=ArkT[:, h * 128:(h + 1) * 128], rhs=Vb[:, h * 64:(h + 1) * 64], start=False, stop=True), ["ArkT%d" % (h // 4), "Vb"], ["pf4"])
        if n + 1 < NT:
            proj_tile()
        S_(lambda e: e.copy(out=F[4], in_=pf[4]), ["pf4"], ["F4"])
        V_(lambda e: e.tensor_reduce(out=m8, in_=h3(F[4]), op=ALU.add, axis=AX.X), ["F4"], ["m8"])
        S_(lambda e: e.activation(out=F[9], in_=F[4], func=AF.Square), ["F4"], ["F9"])
        V_(lambda e: e.tensor_reduce(out=q8, in_=h3(F[9]), op=ALU.add, axis=AX.X), ["F9"], ["q8"])
        V_(lambda e: e.tensor_scalar(out=m8, in0=m8, scalar1=1.0 / 64, scalar2=0.0, op0=ALU.mult, op1=ALU.add), ["m8"], ["m8"])
        V_(lambda e: e.tensor_tensor(out=r8, in0=m8, in1=m8, op=ALU.mult), ["m8"], ["r8"])
        V_(lambda e: e.scalar_tensor_tensor(out=q8, in0=q8, scalar=1.0 / 64, in1=r8, op0=ALU.mult, op1=ALU.subtract), ["q8", "r8"], ["q8"])
        S_(lambda e: e.activation(out=r8, in_=q8, func=AF.Sqrt, bias=64e-5), ["q8"], ["r8"])
        V_(lambda e: e.reciprocal(out=r8, in_=r8), ["r8"], ["r8"])
        V_(lambda e: e.tensor_tensor(out=h3(F[4]), in0=h3(F[4]), in1=b864(m8), op=ALU.subtract), ["F4", "m8"], ["F4"])
        V_(lambda e: e.tensor_tensor(out=h3(F[4]), in0=h3(F[4]), in1=b864(r8), op=ALU.mult), ["F4", "r8"], ["F4"])
        V_(lambda e: e.tensor_tensor(out=F[4], in0=F[4], in1=vecs["lnw"], op=ALU.mult), ["F4", "lnw"], ["F4"])
        V_(lambda e: e.tensor_tensor(out=F[4], in0=F[4], in1=vecs["lnb"], op=ALU.add), ["F4", "lnb"], ["F4"])
        V_(lambda e: e.tensor_tensor(out=h3(F[9]), in0=h3(F[0]), in1=b864(rk8), op=ALU.mult), ["F0", "rk8"], ["F9"])
        V_(lambda e: e.tensor_tensor(out=F[4], in0=F[4], in1=F[9], op=ALU.add), ["F4", "F9"], ["F4"])
        if k.debug:
            P.dma("sync", lambda e: e.dma_start(out=k.dbg["a"][n * 128:(n + 1) * 128, :], in_=F[4]), reads=["F4"])
        V_(lambda e: e.tensor_tensor(out=Rt, in0=F[4], in1=F[10], op=ALU.mult), ["F4", "F10"], ["Rt"])
        store_yT(k, "a", n, "Rt", Rt, 4, l)
```

```python
import numpy as np
from contextlib import ExitStack
import concourse.bass as bass
import concourse.mybir as mybir
from concourse.bass_utils import run_bass_kernel_spmd

F32 = mybir.dt.float32
BF16 = mybir.dt.bfloat16
AF = mybir.ActivationFunctionType
ALU = mybir.AluOpType
AX = mybir.AxisListType

D_MODEL = 1024
A_COLS = 2176
B_COLS = 2048
C_COLS = 2048
IN_COLS = 9344
N_REL = 320


class _Rec:
    def __init__(self):
        self.call = None

    def __getattr__(self, name):
        def f(*a, **kw):
            self.call = (name, a, kw)
            return self
        return f


def _bind(fn):
    rec = _Rec()
    fn(rec)
    name, a, kw = rec.call
    return lambda eng: getattr(eng, name)(*a, **kw)


class Op:
    __slots__ = ("eng", "fn", "deps", "signal", "sigval", "dma_sem", "dma_val", "is_dma", "pre_wait")

    def __init__(self, eng, fn):
        self.eng = eng
        self.fn = fn
        self.deps = []
        self.signal = False
        self.sigval = 0
        self.is_dma = False
        self.dma_sem = None
        self.dma_val = 0
        self.pre_wait = None


class Prog:
    ENGS = ("tensor", "vector", "scalar", "gpsimd", "sync")
    NDMA = 12

    def __init__(self, nc, es):
        self.nc = nc
        self.ops = {e: [] for e in self.ENGS}
        self.last_write = {}
        self.readers = {}
        self.sem = {e: es.enter_context(nc.semaphore("s_" + e)) for e in ("tensor", "vector", "scalar", "gpsimd")}
        self.dma_sems = {q: [es.enter_context(nc.semaphore("d_%s_%d" % (q, i))) for i in range(self.NDMA)]
                         for q in ("sync", "gpsimd")}
        self.dma_count = {"sync": 0, "gpsimd": 0}
        self.dma_hist = {"sync": [], "gpsimd": []}

    def _deps(self, op, reads, writes):
        deps = []
        for k in reads:
            w = self.last_write.get(k)
            if w is not None:
                deps.append(w)
        for k in writes:
            w = self.last_write.get(k)
            if w is not None:
                deps.append(w)
            deps.extend(self.readers.get(k, ()))
        seen = set()
        for d in deps:
            if id(d) in seen or d is op:
                continue
            seen.add(id(d))
            if d.eng == op.eng and op.eng == "tensor" and not d.is_dma:
                continue
            op.deps.append(d)
            if not d.is_dma:
                d.signal = True
        for k in reads:
            self.readers.setdefault(k, []).append(op)
        for k in writes:
            self.last_write[k] = op
            self.readers[k] = []

    def op(self, eng, fn, reads=(), writes=()):
        o = Op(eng, _bind(fn))
        self._deps(o, reads, writes)
        self._apply_bar(o)
        self.ops[eng].append(o)
        return o

    def dma(self, q, fn, reads=(), writes=()):
        o = Op(q, _bind(fn))
        o.is_dma = True
        i = self.dma_count[q]
        self.dma_count[q] += 1
        o.dma_sem = self.dma_sems[q][i % self.NDMA]
        o.dma_val = 16 * (i // self.NDMA + 1)
        if i >= self.NDMA:
            o.pre_wait = self.dma_hist[q][i - self.NDMA]
        self.dma_hist[q].append(o)
        self._deps(o, reads, writes)
        self._apply_bar(o)
        self.ops[q].append(o)
        return o

    def emit(self, block):
        for e in ("tensor", "vector", "scalar", "gpsimd"):
            c = 0
            for o in self.ops[e]:
                if o.is_dma:
                    continue
                if o.signal:
                    c += 1
                o.sigval = c
        all_dmas = self.dma_hist["sync"] + self.dma_hist["gpsimd"]

        def run(eng_name):
            def body(eng):
                water = {}

                def wait(sem, val):
                    key = id(sem)
                    if water.get(key, 0) >= val:
                        return
                    water[key] = val
                    eng.wait_ge(sem, val)

                for o in self.ops[eng_name]:
                    if o.pre_wait is not None:
                        wait(o.pre_wait.dma_sem, o.pre_wait.dma_val)
                    for d in o.deps:
                        if d.is_dma:
                            wait(d.dma_sem, d.dma_val)
                        else:
                            wait(self.sem[d.eng], d.sigval)
                    ins = o.fn(eng)
                    if o.is_dma:
                        ins.then_inc(o.dma_sem, 16)
                    elif o.signal:
                        ins.then_inc(self.sem[eng_name], 1)
                if eng_name in ("sync", "gpsimd"):
                    for o in self.dma_hist[eng_name][-self.NDMA:]:
                        wait(o.dma_sem, o.dma_val)
            return body

        block.tensor(run("tensor"))
        block.vector(run("vector"))
        block.scalar(run("scalar"))
        block.gpsimd(run("gpsimd"))
        block.sync(run("sync"))

    def barrier(self):
        lasts = []
        for e in ("tensor", "vector", "scalar", "gpsimd"):
            cs = [o for o in self.ops[e] if not o.is_dma]
            if cs:
                lasts.append(cs[-1])
        dmas = self.dma_hist["sync"][-self.NDMA:] + self.dma_hist["gpsimd"][-self.NDMA:]
        self.last_write = {"__bar__": None}
        self.readers = {}
        self._bar = lasts + dmas
        self._bar_pending = set(self.ENGS)

    def _apply_bar(self, o):
        if getattr(self, "_bar_pending", None) and o.eng in self._bar_pending:
            self._bar_pending.discard(o.eng)
            for d in self._bar:
                if d is o:
                    continue
                if d.eng == o.eng and not d.is_dma and o.eng == "tensor":
                    continue
                o.deps.append(d)
                if not d.is_dma:
                    d.signal = True


class Arena:
    def __init__(self, tens, size):
        self.t = tens
        self.size = size
        self.off = 0
        self.mark = 0

    def reset(self):
        self.off = self.mark

    def alloc(self, n, dt):
        nf = n if dt == F32 else (n + 1) // 2
        nf_al = (nf + 7) // 8 * 8
        assert self.off + nf_al <= self.size, ("arena overflow", self.off, nf_al, self.size)
        ap = self.t[:, self.off:self.off + nf]
        self.off += nf_al
        if dt != F32:
            ap = ap.bitcast(dt)[:, 0:n]
        return ap


class _AView:
    def __init__(self, arena, dt):
        self.a, self.dt = arena, dt

    def alloc(self, n):
        return self.a.alloc(n, self.dt)

    def reset(self):
        self.a.reset()

    @property
    def off(self):
        return self.a.off

    @property
    def mark(self):
        return self.a.mark

    @mark.setter
    def mark(self, v):
        self.a.mark = v


def _consts():
    s = np.arange(128)[:, None]
    t = np.arange(128)[None, :]
    same = (s // 64) == (t // 64)
    c = {}
    c["ident"] = np.eye(128)
    c["tri"] = same & (s <= t)
    c["ch"] = same
    c["midm"] = same & ((s % 64) <= 31)
    c["msu"] = same & (s < t)
    c["msl"] = same & (s > t)
    c["miu"] = same & (s <= t)
    names = ["ident", "tri", "ch", "midm", "msu", "msl", "miu"]
    arr = np.stack([c[k].astype(np.float32) for k in names], axis=1)
    chsel = np.zeros((128, 2), np.float32)
    chsel[:64, 0] = 1
    chsel[64:, 1] = 1
    negm = np.zeros((128, 5, 128), np.float32)
    negm[:64, 0, 64:] = -30000.0
    negm[64:, 4, :64] = -30000.0
    return names, arr, chsel, negm


CNAMES, CARR, CHSEL, NEGM = _consts()


def _bias_gather(rel_bias):
    k = np.arange(128)[:, None, None]
    r = np.arange(5)[None, :, None]
    q = np.arange(128)[None, None, :]
    idx = np.clip(512 + q - (r * 128 + k), -63, 256) + 63
    g = rel_bias[:, :, idx]
    return np.ascontiguousarray(np.transpose(g, (0, 2, 1, 3, 4)))


class K:
    pass


def build_nc(T=4096, L=2, branches=("a", "b", "c"), debug=False):
    NT = T // 128
    nc = bass.Bass("TRN2", target_bir_lowering=False)
    k = K()
    k.nc, k.T, k.L, k.NT, k.branches, k.debug = nc, T, L, NT, branches, debug

    def din(name, shape, dt=F32):
        return nc.dram_tensor(name, list(shape), dt, kind="ExternalInput").ap()

    k.x = din("x", [T, 1024])
    k.norm_g = din("norm_g", [L, 1024])
    k.w_in = din("w_in", [L, 1024, IN_COLS])
    k.rwkv_mu = din("rwkv_mu", [L, A_COLS])
    k.rwkv_w0 = din("rwkv_w0", [L, 512])
    k.rwkv_w2 = din("rwkv_w2", [L, 64, 512])
    k.rwkv_a0 = din("rwkv_a0", [L, 512])
    k.rwkv_a2 = din("rwkv_a2", [L, 64, 512])
    k.rwkv_k_k = din("rwkv_k_k", [L, 512])
    k.rwkv_k_a = din("rwkv_k_a", [L, 512])
    k.rwkv_r_k = din("rwkv_r_k", [L, 512])
    k.rwkv_ln_w = din("rwkv_ln_w", [L, 512])
    k.rwkv_ln_b = din("rwkv_ln_b", [L, 512])
    k.attn_q_norm = din("attn_q_norm", [L, 64])
    k.attn_k_norm = din("attn_k_norm", [L, 64])
    k.attn_bias = din("attn_bias", [L, 128, 8 * 5 * 128])
    k.hgrn_lb = din("hgrn_lb", [L, 512])
    k.hgrn_norm = din("hgrn_norm", [L, 64])
    k.proj_a = din("proj_a", [L, 512, 1024])
    k.proj_b = din("proj_b", [L, 512, 1024])
    k.proj_c = din("proj_c", [L, 512, 1024])
    k.w_out = din("w_out", [L, 1024, 1024])
    k.cmat = din("cmat", [128, 7 * 128])
    k.chsel = din("chsel", [128, 2])
    k.negm = din("negm", [128, 5 * 128])
    k.out = nc.dram_tensor("out", [T, 1024], F32, kind="ExternalOutput").ap()
    k.x1 = nc.dram_tensor("x1", [T, 1024], F32).ap()
    k.hT_d = nc.dram_tensor("hT_d", [128, 8, T + 16], BF16).ap()
    k.yT_d = {b: nc.dram_tensor("yT_" + b, [128, 4, T], BF16).ap() for b in "abc"}
    if debug:
        k.dbg = {b: nc.dram_tensor("dbg_" + b, [T, 512], F32, kind="ExternalOutput").ap() for b in "abc"}

    with ExitStack() as es:
        FA = 42 * 1024
        arena = Arena(es.enter_context(nc.sbuf_tensor("arena", [128, FA], F32))[:], FA)
        k.fa = _AView(arena, F32)
        k.ba = _AView(arena, BF16)
        k.pf = [es.enter_context(nc.psum_tensor("pf%d" % i, [128, 512], F32))[:] for i in range(8)]
        k.P = Prog(nc, es)
        block = es.enter_context(nc.Block())
        P = k.P
        k.cm_f = k.fa.alloc(7 * 128)
        k.cm_b = k.ba.alloc(7 * 128)
        k.chs = k.fa.alloc(2)
        P.dma("sync", lambda e: e.dma_start(out=k.cm_f, in_=k.cmat[:, :]), writes=["cm_f"])
        P.dma("gpsimd", lambda e: e.dma_start(out=k.cm_b, in_=k.cmat[:, :]), writes=["cm_b"])
        P.dma("sync", lambda e: e.dma_start(out=k.chs, in_=k.chsel[:, :]), writes=["chs"])
        k.fa.mark = k.fa.off
        k.ba.mark = k.ba.off
        for l in range(L):
            xin = k.x if l == 0 else k.x1
            xout = k.out if l == L - 1 else k.x1
            phase0(k, l, xin)
            if STOP == "0":
                phaseCopy(k, xin, xout)
                continue
            if "a" in branches:
                phaseA(k, l)
            if "b" in branches:
                phaseB(k, l)
            if "c" in branches:
                phaseC(k, l)
            if STOP == "B":
                phaseCopy(k, xin, xout)
                continue
            phaseM(k, l, xin, xout)
        P.emit(block)
    return nc


import os
STOP = os.environ.get("STOP", "")
LVL = float(os.environ.get("LVL", "9"))


def phaseCopy(k, xin, xout):
    P = k.P
    new_phase(k)
    t = k.fa.alloc(1024)
    for n in range(k.NT):
        P.dma("sync", lambda e: e.dma_start(out=t, in_=xin[n * 128:(n + 1) * 128, :]), writes=["t"])
        P.dma("sync", lambda e: e.dma_start(out=xout[n * 128:(n + 1) * 128, :], in_=t), reads=["t"])


def cview(k, name, bf=True):
    i = CNAMES.index(name)
    t = k.cm_b if bf else k.cm_f
    return t[:, i * 128:(i + 1) * 128]


def new_phase(k):
    k.P.barrier()
    k.fa.reset()
    k.ba.reset()


def phase0(k, l, xin):
    P, nc = k.P, k.nc
    new_phase(k)
    fa, ba = k.fa, k.ba
    gb = fa.alloc(1024)
    xt = [fa.alloc(1024) for _ in range(2)]
    junk = fa.alloc(1024)
    ss = [fa.alloc(1) for _ in range(2)]
    rs = [fa.alloc(1) for _ in range(2)]
    hb = [ba.alloc(1024) for _ in range(2)]
    hs = [ba.alloc(1024) for _ in range(2)]
    zc = ba.alloc(8 * 16)
    idb = cview(k, "ident")
    P.dma("sync", lambda e: e.dma_start(out=gb, in_=k.norm_g[l:l + 1, :].partition_broadcast(128)), writes=["gb"])
    if l == 0:
        P.op("gpsimd", lambda e: e.memset(zc, 0.0), writes=["zc"])
        P.dma("sync", lambda e: e.dma_start(out=k.hT_d[:, :, 0:16], in_=zc.rearrange("p (c t) -> p c t", c=8)), reads=["zc"])
    for n in range(k.NT):
        b = n % 2
        X, HB, HS, SS, RS = "xt%d" % b, "hb%d" % b, "hs%d" % b, "ss%d" % b, "rs%d" % b
        if n == 0:
            P.dma("sync", lambda e: e.dma_start(out=xt[0], in_=xin[0:128, :]), writes=["xt0"])
        if n + 1 < k.NT:
            P.dma("sync", lambda e: e.dma_start(out=xt[(n + 1) % 2], in_=xin[(n + 1) * 128:(n + 2) * 128, :]), writes=["xt%d" % ((n + 1) % 2)])
        P.op("scalar", lambda e, b=b: e.activation(out=junk, in_=xt[b], func=AF.Square, accum_out=ss[b]), reads=[X], writes=["junk", SS])
        P.op("scalar", lambda e, b=b: e.activation(out=rs[b], in_=ss[b], func=AF.Sqrt, scale=1.0 / 1024, bias=1e-6), reads=[SS], writes=[RS])
        P.op("vector", lambda e, b=b: e.reciprocal(out=rs[b], in_=rs[b]), reads=[RS], writes=[RS])
        P.op("vector", lambda e, b=b: e.scalar_tensor_tensor(out=hb[b], in0=xt[b], scalar=rs[b][:, 0:1], in1=gb, op0=ALU.mult, op1=ALU.mult),
             reads=[X, RS, "gb"], writes=[HB])
        pt = k.pf[n % 2].bitcast(BF16)
        PT = "pf%d" % (n % 2)
        for c in range(8):
            P.op("tensor", lambda e, c=c, b=b, pt=pt: e.transpose(out=pt[:, c * 128:(c + 1) * 128], in_=hb[b][:, c * 128:(c + 1) * 128], identity=idb),
                 reads=[HB, "cm_b"], writes=[PT])
        P.op("scalar", lambda e, b=b, pt=pt: e.copy(out=hs[b], in_=pt[:, 0:1024]), reads=[PT], writes=[HS])
        P.dma("gpsimd", lambda e, n=n, b=b: e.dma_start(out=k.hT_d[:, :, 16 + n * 128:16 + (n + 1) * 128], in_=hs[b].rearrange("p (c t) -> p c t", c=8)),
              reads=[HS], writes=["hT_d"])


def pbf(k, i):
    return k.pf[i].bitcast(BF16)


def load_w_cast(k, dst3, src2d, key, nsplit=8):
    C = dst3.shape[1]
    N = dst3.shape[2]
    step = max(1, 2048 // 1)
    for c in range(C):
        for n0 in range(0, N, 2048):
            n1 = min(N, n0 + 2048)
            k.P.dma("gpsimd", lambda e, c=c, n0=n0, n1=n1: e.dma_start(out=dst3[:, c, n0:n1], in_=src2d[c * 128:(c + 1) * 128, n0:n1]),
                    writes=[key])


def proj_block(k, pbank, pkey, hT, hkey, W, wkey, c0, ncols, shift=0):
    for c in range(8):
        k.P.op("tensor", lambda e, c=c: e.matmul(pbank[:, 0:ncols], lhsT=hT[:, c, shift:shift + 128], rhs=W[:, c, c0:c0 + ncols],
                                                start=(c == 0), stop=(c == 7)),
               reads=[hkey, wkey], writes=[pkey])


def store_yT(k, br, n, ysrc_key, ysrc_bf, tp_bank, l):
    P = k.P
    b = n % 2
    pt = pbf(k, tp_bank)
    PT = "pf%d" % tp_bank
    idb = cview(k, "ident")
    for c in range(4):
        P.op("tensor", lambda e, c=c: e.transpose(out=pt[:, c * 128:(c + 1) * 128], in_=ysrc_bf[:, c * 128:(c + 1) * 128], identity=idb),
             reads=[ysrc_key, "cm_b"], writes=[PT])
    ys = k.ystage[b]
    YS = "ystage%d" % (b if k.ystage[0] is not k.ystage[1] else 0)
    P.op("scalar", lambda e: e.copy(out=ys, in_=pt[:, 0:512]), reads=[PT], writes=[YS])
    P.dma("gpsimd", lambda e: e.dma_start(out=k.yT_d[br][:, :, n * 128:(n + 1) * 128], in_=ys.rearrange("p (c t) -> p c t", c=4)),
          reads=[YS], writes=["yT_d" + br])


def v3(ap, a):
    return ap.rearrange("p (a b) -> p a b", a=a)


def phaseB(k, l):
    P, nc = k.P, k.nc
    new_phase(k)
    fa, ba = k.fa, k.ba
    NT = k.NT
    idb = cview(k, "ident")
    W = v3(ba.alloc(8 * 2048), 8)
    load_w_cast(k, W, k.w_in[l, :, A_COLS:A_COLS + B_COLS], "WB")
    gq = fa.alloc(64)
    gk = fa.alloc(64)
    P.dma("sync", lambda e: e.dma_start(out=gq, in_=k.attn_q_norm[l:l + 1, :].partition_broadcast(128)), writes=["gq"])
    P.dma("sync", lambda e: e.dma_start(out=gk, in_=k.attn_k_norm[l:l + 1, :].partition_broadcast(128)), writes=["gk"])
    P.op("vector", lambda e: e.scalar_tensor_tensor(out=gq, in0=gq, scalar=0.125, in1=gk, op0=ALU.mult, op1=ALU.mult), reads=["gq", "gk"], writes=["gq"])
    bstage = fa.alloc(5120)
    nm = fa.alloc(640)
    biasT = ba.alloc(5120)
    P.dma("sync", lambda e: e.dma_start(out=bstage, in_=k.attn_bias[l, :, :]), writes=["bstage"])
    P.dma("sync", lambda e: e.dma_start(out=nm, in_=k.negm[:, :]), writes=["nm"])
    P.op("vector", lambda e: e.tensor_tensor(out=v3(biasT, 8), in0=v3(bstage, 8), in1=nm.unsqueeze(1).to_broadcast([128, 8, 640]), op=ALU.add),
         reads=["bstage", "nm"], writes=["biasT"])
    bias4 = biasT.rearrange("p (h r q) -> p h r q", h=8, r=5)
    Vr = ba.alloc(8 * 8 * 80).rearrange("p (s h d) -> p s h d", s=8, h=8)
    P.op("gpsimd", lambda e: e.memset(Vr, 1.0), writes=["Vr"])
    kT = ba.alloc(4 * 8 * 128).rearrange("p (c s t) -> p c s t", c=4, s=8)
    qTm = [ba.alloc(8 * 128) for _ in range(2)]
    for b in range(2):
        P.op("gpsimd", lambda e, b=b: e.memset(qTm[b], 0.0), writes=["qTm%d" % b])
    hTt = [v3(ba.alloc(1024), 8) for _ in range(2)]
    sqt = fa.alloc(1024)
    ss16 = fa.alloc(16)
    rs16 = fa.alloc(16)
    qn32 = fa.alloc(512)
    qb = ba.alloc(512)
    kb = ba.alloc(512)
    sg = fa.alloc(512)
    PTb = [ba.alloc(512) for _ in range(3)]
    rinv = fa.alloc(8)
    y32 = fa.alloc(512)
    ygb = ba.alloc(512)
    k.ystage = [ba.alloc(512) for _ in range(2)]
    pf = k.pf
    for n in range(NT):
        b = n % 2
        slot = n % 8
        H = "hTt%d" % b
        if n == 0:
            P.dma("sync", lambda e: e.dma_start(out=hTt[0], in_=k.hT_d[:, :, 16:16 + 128]), writes=["hTt0"])
        if n + 1 < NT:
            P.dma("sync", lambda e: e.dma_start(out=hTt[(n + 1) % 2], in_=k.hT_d[:, :, 16 + (n + 1) * 128:16 + (n + 2) * 128]), writes=["hTt%d" % ((n + 1) % 2)])
        for blk in range(4):
            proj_block(k, pf[blk], "pf%d" % blk, hTt[b], H, W, "WB", blk * 512, 512)
        if LVL < 1.1:
            continue
        P.op("scalar", lambda e: e.activation(out=sqt[:, 0:512], in_=pf[0], func=AF.Square), reads=["pf0"], writes=["sqt"])
        P.op("scalar", lambda e: e.activation(out=sqt[:, 512:1024], in_=pf[1], func=AF.Square), reads=["pf1"], writes=["sqt"])
        P.op("vector", lambda e: e.tensor_reduce(out=ss16, in_=v3(sqt, 16), op=ALU.add, axis=AX.X), reads=["sqt"], writes=["ss16"])
        P.op("scalar", lambda e: e.activation(out=rs16, in_=ss16, func=AF.Sqrt, scale=1.0 / 64, bias=1e-6), reads=["ss16"], writes=["rs16"])
        P.op("vector", lambda e: e.reciprocal(out=rs16, in_=rs16), reads=["rs16"], writes=["rs16"])
        if LVL < 1.2:
            continue
        P.op("vector", lambda e: e.tensor_tensor(out=v3(qn32, 8), in0=v3(pf[0], 8), in1=rs16[:, 0:8].unsqueeze(2).to_broadcast([128, 8, 64]), op=ALU.mult),
             reads=["pf0", "rs16"], writes=["qn32"])
        P.op("vector", lambda e: e.tensor_tensor(out=v3(qb, 8), in0=v3(qn32, 8), in1=gq.unsqueeze(1).to_broadcast([128, 8, 64]), op=ALU.mult),
             reads=["qn32", "gq"], writes=["qb"])
        P.op("vector", lambda e: e.tensor_tensor(out=v3(kb, 8), in0=v3(pf[1], 8), in1=rs16[:, 8:16].unsqueeze(2).to_broadcast([128, 8, 64]), op=ALU.mult),
             reads=["pf1", "rs16"], writes=["kb"])
        if LVL < 1.4:
            continue
        pt = pbf(k, 0)
        for c in range(4):
            P.op("tensor", lambda e, c=c: e.transpose(out=pt[:, c * 128:(c + 1) * 128], in_=qb[:, c * 128:(c + 1) * 128], identity=idb),
                 reads=["qb", "cm_b"], writes=["pf0"])
        for c in range(4):
            P.op("tensor", lambda e, c=c: e.transpose(out=pt[:, 512 + c * 128:512 + (c + 1) * 128], in_=kb[:, c * 128:(c + 1) * 128], identity=idb),
                 reads=["kb", "cm_b"], writes=["pf0"])
        if LVL < 1.6:
            continue
        Q = "qTm%d" % b
        q4 = qTm[b].rearrange("p (c u t) -> p c u t", c=4, u=2)
        for u in range(2):
            P.op("scalar", lambda e, u=u, q4=q4: e.copy(out=q4[u * 64:(u + 1) * 64, :, u, :], in_=v3(pt[u * 64:(u + 1) * 64, 0:512], 4)),
                 reads=["pf0"], writes=[Q])
        if LVL < 1.8:
            continue
        P.op("scalar", lambda e, slot=slot: e.copy(out=kT[:, :, slot, :], in_=v3(pt[:, 512:1024], 4)), reads=["pf0"], writes=["kT"])
        if LVL < 1.9:
            continue
        P.op("scalar", lambda e, slot=slot: e.copy(out=Vr[:, slot, :, 0:64], in_=v3(pf[2], 8)), reads=["pf2"], writes=["Vr"])
        if LVL < 1.95:
            continue
        VAR = os.environ.get("VAR", "")
        if VAR == "copy3":
            P.op("scalar", lambda e: e.copy(out=sg, in_=pf[3]), reads=["pf3"], writes=["sg"])
        elif VAR == "sig2":
            P.op("scalar", lambda e: e.activation(out=sg, in_=pf[2], func=AF.Sigmoid), reads=["pf2"], writes=["sg"])
        elif VAR == "dve3":
            P.op("vector", lambda e: e.tensor_copy(out=sg, in_=pf[3]), reads=["pf3"], writes=["sg"])
        else:
            P.op("scalar", lambda e: e.activation(out=sg, in_=pf[3], func=AF.Silu), reads=["pf3"], writes=["sg"])
        if LVL < 3:
            continue
        blocks = [(h, r) for h in range(8) for r in range(5) if n - 4 + r >= 0]
        groups = [blocks[i:i + 4] for i in range(0, len(blocks), 4)]
        first_r = max(0, 4 - n)
        def emit_pv(grp, pb):
            PTK = "PT%d" % pb
            for j, (h, r) in enumerate(grp):
                kslot = (n - 4 + r) % 8
                ob = 6 + h // 4
                P.op("tensor", lambda e: e.matmul(pf[ob][:, (h % 4) * 65:(h % 4) * 65 + 65], lhsT=PTb[pb][:, j * 128:(j + 1) * 128],
                                                  rhs=Vr[:, kslot, h, 0:65], start=(r == first_r), stop=(r == 4)), reads=[PTK, "Vr"], writes=["pf%d" % ob])

        prev = None
        for gi, grp in enumerate(groups):
            bank = 4 + gi % 2
            BK = "pf%d" % bank
            for j, (h, r) in enumerate(grp):
                kslot = (n - 4 + r) % 8
                P.op("tensor", lambda e: e.matmul(pf[bank][:, j * 128:(j + 1) * 128], lhsT=kT[:, h // 2, kslot, :],
                                                  rhs=qTm[b][:, h * 128:(h + 1) * 128], start=True, stop=False), reads=["kT", Q], writes=[BK])
                P.op("tensor", lambda e: e.matmul(pf[bank][:, j * 128:(j + 1) * 128], lhsT=idb, rhs=bias4[:, h, r, :], start=False, stop=True),
                     reads=["biasT", "cm_b"], writes=[BK])
            if prev is not None:
                emit_pv(*prev)
            pb = gi % 3
            ncol = len(grp) * 128
            P.op("scalar", lambda e: e.activation(out=PTb[pb][:, 0:ncol], in_=pf[bank][:, 0:ncol], func=AF.Exp), reads=[BK], writes=["PT%d" % pb])
            prev = (grp, pb)
        if prev is not None:
            emit_pv(*prev)
        if LVL < 4:
            continue
        for hb_ in range(2):
            o3 = v3(pf[6 + hb_][:, 0:260], 4)
            P.op("vector", lambda e, o3=o3, hb_=hb_: e.reciprocal(out=rinv[:, hb_ * 4:(hb_ + 1) * 4].unsqueeze(2), in_=o3[:, :, 64:65]),
                 reads=["pf%d" % (6 + hb_)], writes=["rinv"])
            P.op("vector", lambda e, o3=o3, hb_=hb_: e.tensor_tensor(out=v3(y32[:, hb_ * 256:(hb_ + 1) * 256], 4), in0=o3[:, :, 0:64],
                                                                    in1=rinv[:, hb_ * 4:(hb_ + 1) * 4].unsqueeze(2).to_broadcast([128, 4, 64]), op=ALU.mult),
                 reads=["pf%d" % (6 + hb_), "rinv"], writes=["y32"])
        if k.debug:
            P.dma("sync", lambda e, n=n: e.dma_start(out=k.dbg["b"][n * 128:(n + 1) * 128, :], in_=y32), reads=["y32"])
        P.op("vector", lambda e: e.tensor_tensor(out=ygb, in0=y32, in1=sg, op=ALU.mult), reads=["y32", "sg"], writes=["ygb"])
        store_yT(k, "b", n, "ygb", ygb, 3, l)


def phaseM(k, l, xin, xout):
    P, nc = k.P, k.nc
    new_phase(k)
    fa, ba = k.fa, k.ba
    brs = [b for b in "abc" if b in k.branches]
    if not brs:
        return phaseCopy(k, xin, xout)
    projs = {"a": k.proj_a, "b": k.proj_b, "c": k.proj_c}
    Wz, Wp = {}, {}
    for bi, br in enumerate("abc"):
        if br not in brs:
            continue
        Wz[br] = v3(ba.alloc(8 * 1024), 8)
        load_w_cast(k, Wz[br], k.w_in[l, :, 6272 + bi * 1024:6272 + (bi + 1) * 1024], "Wz" + br)
        Wp[br] = v3(ba.alloc(4 * 1024), 4)
        load_w_cast(k, Wp[br], projs[br][l, :, :], "Wp" + br)
    Wo = v3(ba.alloc(8 * 1024), 8)
    load_w_cast(k, Wo, k.w_out[l, :, :], "Wo")
    TB = 512
    NB = k.T // TB
    hTb = [v3(ba.alloc(8 * TB), 8) for _ in range(2)]
    yTb = {br: [v3(ba.alloc(4 * TB), 4) for _ in range(2)] for br in brs}
    mT = v3(ba.alloc(8 * TB), 8)
    gsb = {br: fa.alloc(TB) for br in brs}
    acc = fa.alloc(TB)
    tmp = fa.alloc(TB)
    xt = [fa.alloc(1024) for _ in range(2)]
    ot = [fa.alloc(1024) for _ in range(2)]
    pf = k.pf
    for tb in range(NB):
        b = tb % 2
        H = "hTb%d" % b
        def load_blk(t_):
            b_ = t_ % 2
            P.dma("sync", lambda e: e.dma_start(out=hTb[b_], in_=k.hT_d[:, :, 16 + t_ * TB:16 + (t_ + 1) * TB]), writes=["hTb%d" % b_])
            for br in brs:
                P.dma("sync", lambda e: e.dma_start(out=yTb[br][b_], in_=k.yT_d[br][:, :, t_ * TB:(t_ + 1) * TB]), writes=["yTb%s%d" % (br, b_)])
        if tb == 0:
            load_blk(0)
        if tb + 1 < NB:
            load_blk(tb + 1)
        for fc in range(8):
            for bi, br in enumerate(brs):
                zb, pb_ = (fc * len(brs) + bi) % 4, 4 + (fc * len(brs) + bi) % 4
                for c in range(8):
                    P.op("tensor", lambda e, c=c, br=br, zb=zb: e.matmul(pf[zb], lhsT=Wz[br][:, c, fc * 128:(fc + 1) * 128], rhs=hTb[b][:, c, :],
                                                                        start=(c == 0), stop=(c == 7)), reads=["Wz" + br, H], writes=["pf%d" % zb])
                P.op("scalar", lambda e, br=br, zb=zb: e.activation(out=gsb[br], in_=pf[zb], func=AF.Sigmoid), reads=["pf%d" % zb], writes=["gsb" + br])
                for c in range(4):
                    P.op("tensor", lambda e, c=c, br=br, pb_=pb_: e.matmul(pf[pb_], lhsT=Wp[br][:, c, fc * 128:(fc + 1) * 128], rhs=yTb[br][b][:, c, :],
                                                                          start=(c == 0), stop=(c == 3)),
                         reads=["Wp" + br, "yTb%s%d" % (br, b)], writes=["pf%d" % pb_])
            for bi, br in enumerate(brs):
                pb_ = 4 + (fc * len(brs) + bi) % 4
                last = bi == len(brs) - 1
                if bi == 0:
                    dst = mT[:, fc, :] if last else acc
                    P.op("vector", lambda e, br=br, pb_=pb_, dst=dst: e.tensor_tensor(out=dst, in0=pf[pb_], in1=gsb[br], op=ALU.mult),
                         reads=["pf%d" % pb_, "gsb" + br], writes=["mT" if last else "acc"])
                else:
                    P.op("vector", lambda e, br=br, pb_=pb_: e.tensor_tensor(out=tmp, in0=pf[pb_], in1=gsb[br], op=ALU.mult),
                         reads=["pf%d" % pb_, "gsb" + br], writes=["tmp"])
                    dst = mT[:, fc, :] if last else acc
                    P.op("vector", lambda e, dst=dst: e.tensor_tensor(out=dst, in0=acc, in1=tmp, op=ALU.add),
                         reads=["acc", "tmp"], writes=["mT" if last else "acc"])
        for tt in range(TB // 128):
            n = tb * (TB // 128) + tt
            xb = n % 2
            X, O = "xm%d" % xb, "om%d" % xb
            if n == 0:
                P.dma("sync", lambda e: e.dma_start(out=xt[0], in_=xin[0:128, :]), writes=["xm0"])
            if n + 1 < k.NT:
                P.dma("sync", lambda e: e.dma_start(out=xt[(n + 1) % 2], in_=xin[(n + 1) * 128:(n + 2) * 128, :]), writes=["xm%d" % ((n + 1) % 2)])
            for cb in range(2):
                bank = cb
                for c in range(8):
                    P.op("tensor", lambda e, c=c, cb=cb, bank=bank, tt=tt: e.matmul(pf[bank], lhsT=mT[:, c, tt * 128:(tt + 1) * 128],
                                                                                   rhs=Wo[:, c, cb * 512:(cb + 1) * 512], start=(c == 0), stop=(c == 7)),
                         reads=["mT", "Wo"], writes=["pf%d" % bank])
                P.op("vector", lambda e, cb=cb, bank=bank, xb=xb: e.tensor_tensor(out=ot[xb][:, cb * 512:(cb + 1) * 512], in0=pf[bank],
                                                                                 in1=xt[xb][:, cb * 512:(cb + 1) * 512], op=ALU.add),
                     reads=["pf%d" % bank, X], writes=[O])
            P.dma("gpsimd", lambda e, n=n, xb=xb: e.dma_start(out=xout[n * 128:(n + 1) * 128, :], in_=ot[xb]), reads=[O], writes=["xout"])


_NC_CACHE = {}


def make_in_maps(inputs, T, L, nb):
    f = lambda a: np.ascontiguousarray(np.asarray(a, dtype=np.float32))
    shared = {
        "norm_g": f(inputs["norm_g"])[:L], "w_in": f(inputs["w_in"])[:L], "rwkv_mu": f(inputs["rwkv_mu"])[:L],
        "rwkv_w0": f(inputs["rwkv_w0"])[:L], "rwkv_w2": f(inputs["rwkv_w2"])[:L], "rwkv_a0": f(inputs["rwkv_a0"])[:L],
        "rwkv_a2": f(inputs["rwkv_a2"])[:L], "rwkv_k_k": f(inputs["rwkv_k_k"])[:L], "rwkv_k_a": f(inputs["rwkv_k_a"])[:L],
        "rwkv_r_k": f(inputs["rwkv_r_k"])[:L].reshape(L, 512), "rwkv_ln_w": f(inputs["rwkv_ln_w"])[:L],
        "rwkv_ln_b": f(inputs["rwkv_ln_b"])[:L], "attn_q_norm": f(inputs["attn_q_norm"])[:L],
        "attn_k_norm": f(inputs["attn_k_norm"])[:L],
        "attn_bias": _bias_gather(f(inputs["attn_rel_bias"])[:L]).reshape(L, 128, 8 * 5 * 128),
        "hgrn_lb": f(inputs["hgrn_lb"])[:L], "hgrn_norm": f(inputs["hgrn_norm"])[:L],
        "proj_a": f(inputs["proj_a"])[:L], "proj_b": f(inputs["proj_b"])[:L], "proj_c": f(inputs["proj_c"])[:L],
        "w_out": f(inputs["w_out"])[:L],
        "cmat": np.ascontiguousarray(CARR.reshape(128, 7 * 128)), "chsel": CHSEL,
        "negm": np.ascontiguousarray(NEGM.reshape(128, 640)),
    }
    x = f(inputs["x"])
    maps = []
    for b in range(nb):
        m = dict(shared)
        m["x"] = np.ascontiguousarray(x[b, :T])
        maps.append(m)
    return maps


def kernel(**inputs):
    T, L = 4096, 2
    key = (T, L)
    if key not in _NC_CACHE:
        _NC_CACHE[key] = build_nc(T, L, branches=tuple(os.environ.get("KBR", "abc")))
    nc = _NC_CACHE[key]
    maps = make_in_maps(inputs, T, L, 8)
    res = run_bass_kernel_spmd(nc, maps, core_ids=list(range(8)))
    return np.stack([r["out"] for r in res.results], axis=0).astype(np.float32)


def phaseC(k, l):
    P, nc = k.P, k.nc
    new_phase(k)
    fa, ba = k.fa, k.ba
    NT, L = k.NT, k.L
    pf = k.pf
    idb = cview(k, "ident")
    V_ = lambda fn, r, w: P.op("vector", fn, r, w)
    S_ = lambda fn, r, w: P.op("scalar", fn, r, w)
    G_ = lambda fn, r, w: P.op("gpsimd", fn, r, w)
    T_ = lambda fn, r, w: P.op("tensor", fn, r, w)
    W = v3(ba.alloc(8 * 2048), 8)
    load_w_cast(k, W, k.w_in[l, :, A_COLS + B_COLS:A_COLS + B_COLS + C_COLS], "WC")
    lbb = fa.alloc(512)
    oml = fa.alloc(512)
    if l == 0:
        V_(lambda e: e.memset(lbb, 0.0), [], ["lbb"])
    else:
        er = fa.alloc(L * 512)
        P.dma("sync", lambda e: e.dma_start(out=er, in_=k.hgrn_lb.rearrange("l c -> (l c)").unsqueeze(0).partition_broadcast(128).squeeze(1)
                                            if False else k.hgrn_lb.rearrange("(o l) c -> o (l c)", o=1).partition_broadcast(128)), writes=["er"])
        S_(lambda e: e.activation(out=er, in_=er, func=AF.Exp), ["er"], ["er"])
        ssum = fa.alloc(512)
        V_(lambda e: e.tensor_tensor(out=ssum, in0=er[:, 0:512], in1=er[:, 512:1024], op=ALU.add), ["er"], ["ssum"])
        for j in range(2, L):
            V_(lambda e: e.tensor_tensor(out=ssum, in0=ssum, in1=er[:, j * 512:(j + 1) * 512], op=ALU.add), ["er", "ssum"], ["ssum"])
        V_(lambda e: e.tensor_copy(out=lbb, in_=er[:, 512:1024]), ["er"], ["lbb"])
        for j in range(2, l + 1):
            V_(lambda e: e.tensor_tensor(out=lbb, in0=lbb, in1=er[:, j * 512:(j + 1) * 512], op=ALU.add), ["er", "lbb"], ["lbb"])
        V_(lambda e: e.reciprocal(out=ssum, in_=ssum), ["ssum"], ["ssum"])
        V_(lambda e: e.tensor_tensor(out=lbb, in0=lbb, in1=ssum, op=ALU.mult), ["lbb", "ssum"], ["lbb"])
    V_(lambda e: e.tensor_scalar(out=oml, in0=lbb, scalar1=-1.0, scalar2=1.0, op0=ALU.mult, op1=ALU.add), ["lbb"], ["oml"])
    gn = fa.alloc(64)
    P.dma("sync", lambda e: e.dma_start(out=gn, in_=k.hgrn_norm[l:l + 1, :].partition_broadcast(128)), writes=["gn"])
    tri, chm, midm, miu = cview(k, "tri", False), cview(k, "ch", False), cview(k, "midm", False), cview(k, "miu", False)
    Sf = fa.alloc(256)
    V_(lambda e: e.memset(Sf, 0.0), [], ["Sf"])
    Sb = [ba.alloc(256) for _ in range(3)]
    G_(lambda e: e.memset(Sb[0], 0.0), [], ["Sb0"])
    QpTm = [ba.alloc(2048) for _ in range(2)]
    QTm = [ba.alloc(1024) for _ in range(2)]
    Kh = [ba.alloc(2048) for _ in range(2)]
    for b in range(2):
        G_(lambda e: e.memset(QpTm[b], 0.0), [], ["QpTm%d" % b])
        G_(lambda e: e.memset(QTm[b], 0.0), [], ["QTm%d" % b])
        G_(lambda e: e.memset(Kh[b], 0.0), [], ["Kh%d" % b])
    hTt = [v3(ba.alloc(1024), 8) for _ in range(2)]
    sgm, sgn, fg, key, logf, qh, eb = [fa.alloc(512) for _ in range(7)]
    bm, be, d1, e1 = [fa.alloc(512) for _ in range(4)]
    Qp, Qt, Kt, Vb = [ba.alloc(512) for _ in range(4)]
    KT = ba.alloc(512)
    attm = ba.alloc(1024)
    gS = fa.alloc(8)
    sq = fa.alloc(512)
    ss8 = fa.alloc(8)
    rs8 = fa.alloc(8)
    o32 = fa.alloc(512)
    gate = fa.alloc(512)
    ygb = ba.alloc(512)
    k.ystage = [ba.alloc(512) for _ in range(2)]
    sbi = 0
    for n in range(NT):
        b = n % 2
        H = "hTt%d" % b
        if n == 0:
            P.dma("sync", lambda e: e.dma_start(out=hTt[0], in_=k.hT_d[:, :, 16:16 + 128]), writes=["hTt0"])
        if n + 1 < NT:
            P.dma("sync", lambda e: e.dma_start(out=hTt[(n + 1) % 2], in_=k.hT_d[:, :, 16 + (n + 1) * 128:16 + (n + 2) * 128]), writes=["hTt%d" % ((n + 1) % 2)])
        for blk in range(4):
            proj_block(k, pf[blk], "pf%d" % blk, hTt[b], H, W, "WC", blk * 512, 512)
        S_(lambda e: e.activation(out=sgm, in_=pf[1], func=AF.Sigmoid), ["pf1"], ["sgm"])
        S_(lambda e: e.activation(out=sgn, in_=pf[1], func=AF.Sigmoid, scale=-1.0), ["pf1"], ["sgn"])
        V_(lambda e: e.tensor_tensor(out=fg, in0=sgm, in1=oml, op=ALU.mult), ["sgm", "oml"], ["fg"])
        V_(lambda e: e.tensor_tensor(out=fg, in0=fg, in1=lbb, op=ALU.add), ["fg", "lbb"], ["fg"])
        G_(lambda e: e.tensor_tensor(out=key, in0=sgn, in1=oml, op=ALU.mult), ["sgn", "oml"], ["key"])
        S_(lambda e: e.activation(out=logf, in_=fg, func=AF.Ln), ["fg"], ["logf"])
        for bank, m in ((4, tri), (5, midm), (6, chm)):
            T_(lambda e: e.matmul(pf[bank], lhsT=m, rhs=logf, start=True, stop=True), ["logf", "cm_f"], ["pf%d" % bank])
        S_(lambda e: e.copy(out=Vb, in_=pf[2]), ["pf2"], ["Vb"])
        S_(lambda e: e.activation(out=qh, in_=pf[0], func=AF.Silu), ["pf0"], ["qh"])
        S_(lambda e: e.activation(out=gate, in_=pf[3], func=AF.Silu), ["pf3"], ["gate"])
        S_(lambda e: e.activation(out=eb, in_=pf[4], func=AF.Exp), ["pf4"], ["eb"])
        S_(lambda e: e.copy(out=bm, in_=pf[5]), ["pf5"], ["bm"])
        S_(lambda e: e.copy(out=be, in_=pf[6]), ["pf6"], ["be"])
        for p in range(4):
            T_(lambda e: e.matmul(pf[5][:, 256 + p * 2:256 + p * 2 + 2], lhsT=logf[:, p * 128:(p + 1) * 128], rhs=k.chs, start=True, stop=True),
               ["logf", "chs"], ["pf5"])
        S_(lambda e: e.activation(out=gS, in_=pf[5][:, 256:264], func=AF.Exp), ["pf5"], ["gS"])
        V_(lambda e: e.tensor_tensor(out=Qp, in0=qh, in1=eb, op=ALU.mult), ["qh", "eb"], ["Qp"])
        V_(lambda e: e.tensor_tensor(out=d1, in0=pf[4], in1=bm, op=ALU.subtract), ["pf4", "bm"], ["d1"])
        S_(lambda e: e.activation(out=e1, in_=d1, func=AF.Exp), ["d1"], ["e1"])
        V_(lambda e: e.tensor_tensor(out=Qt, in0=qh, in1=e1, op=ALU.mult), ["qh", "e1"], ["Qt"])
        S_(lambda e: e.activation(out=e1, in_=d1, func=AF.Exp, scale=-1.0), ["d1", "Qt"], ["e1"])
        V_(lambda e: e.tensor_tensor(out=Kt, in0=key, in1=e1, op=ALU.mult), ["key", "e1"], ["Kt"])
        V_(lambda e: e.tensor_tensor(out=d1, in0=be, in1=pf[4], op=ALU.subtract), ["pf4", "be", "e1"], ["d1"])
        S_(lambda e: e.activation(out=e1, in_=d1, func=AF.Exp), ["d1", "Kt"], ["e1"])
        KH = "Kh%d" % b
        kh5 = Kh[b].rearrange("q (c u p d) -> q c u p d", c=2, u=2, p=4)
        for c in range(2):
            for u in range(2):
                rows = slice(c * 64, (c + 1) * 64)
                V_(lambda e: e.tensor_tensor(out=kh5[rows, c, u, :, u * 64:(u + 1) * 64] if False else
                                             Kh[b].rearrange("q (c u p w) -> q c u p w", c=2, u=2, p=4)[rows, c, u, :, u * 64:(u + 1) * 64],
                                             in0=v3(key, 4)[rows, :, u * 64:(u + 1) * 64], in1=v3(e1, 4)[rows, :, u * 64:(u + 1) * 64], op=ALU.mult),
                   ["key", "e1"], [KH])
        p7 = pbf(k, 7)
        p5 = pbf(k, 5)
        for c in range(4):
            T_(lambda e: e.transpose(out=p7[:, c * 128:(c + 1) * 128], in_=Qp[:, c * 128:(c + 1) * 128], identity=idb), ["Qp", "cm_b"], ["pf7"])
        for c in range(4):
            T_(lambda e: e.transpose(out=p7[:, 512 + c * 128:512 + (c + 1) * 128], in_=Qt[:, c * 128:(c + 1) * 128], identity=idb), ["Qt", "cm_b"], ["pf7"])
        for c in range(4):
            T_(lambda e: e.transpose(out=p5[:, c * 128:(c + 1) * 128], in_=Kt[:, c * 128:(c + 1) * 128], identity=idb), ["Kt", "cm_b"], ["pf5"])
        QP, QT = "QpTm%d" % b, "QTm%d" % b
        qp6 = QpTm[b].rearrange("q (p u c t) -> q p u c t", p=4, u=2, c=2)
        qt4 = QTm[b].rearrange("q (p u t) -> q p u t", p=4, u=2)
        for u in range(2):
            rows = slice(u * 64, (u + 1) * 64)
            for c in range(2):
                S_(lambda e: e.copy(out=qp6[rows, :, u, c, c * 64:(c + 1) * 64], in_=v3(p7[rows, 0:512], 4)[:, :, c * 64:(c + 1) * 64]), ["pf7"], [QP])
            S_(lambda e: e.copy(out=qt4[rows, :, u, :], in_=v3(p7[rows, 512:1024], 4)), ["pf7"], [QT])
        S_(lambda e: e.copy(out=KT, in_=p5[:, 0:512]), ["pf5"], ["KT"])
        for h in range(8):
            bank = 4 if h < 4 else 6
            T_(lambda e: e.matmul(pf[bank][:, (h % 4) * 128:(h % 4 + 1) * 128], lhsT=KT[:, (h // 2) * 128:(h // 2 + 1) * 128],
                                  rhs=QTm[b][:, h * 128:(h + 1) * 128], start=True, stop=True), ["KT", QT], ["pf%d" % bank])
        for hb_ in range(2):
            bank = 4 if hb_ == 0 else 6
            V_(lambda e: e.tensor_tensor(out=v3(attm[:, hb_ * 512:(hb_ + 1) * 512], 4), in0=v3(pf[bank], 4),
                                         in1=miu.unsqueeze(1).to_broadcast([128, 4, 128]), op=ALU.mult), ["pf%d" % bank, "cm_f"], ["attm"])
        for c in range(2):
            for p in range(4):
                for u in range(2):
                    h = 2 * p + u
                    T_(lambda e: e.matmul(pf[1][:, c * 256 + p * 64:c * 256 + (p + 1) * 64],
                                          lhsT=Kh[b][:, (c * 2 + u) * 512 + p * 128:(c * 2 + u) * 512 + (p + 1) * 128],
                                          rhs=Vb[:, h * 64:(h + 1) * 64], start=(u == 0), stop=(u == 1)), [KH, "Vb"], ["pf1"])
        sb_in = sbi
        for c in range(2):
            V_(lambda e: e.tensor_tensor(out=v3(Sf, 4), in0=v3(Sf, 4), in1=gS[:, c:8:2].unsqueeze(2).to_broadcast([128, 4, 64]), op=ALU.mult),
               ["Sf", "gS"], ["Sf"])
            V_(lambda e: e.tensor_tensor(out=Sf, in0=Sf, in1=pf[1][:, c * 256:(c + 1) * 256], op=ALU.add), ["Sf", "pf1"], ["Sf"])
            sbi = (sbi + 1) % 3
            S_(lambda e: e.copy(out=Sb[sbi], in_=Sf), ["Sf"], ["Sb%d" % sbi])
        sb0, sb1 = sb_in, (sb_in + 1) % 3
        for h in range(8):
            p = h // 2
            for c, sbx in ((0, sb0), (1, sb1)):
                T_(lambda e: e.matmul(pf[0][:, h * 64:(h + 1) * 64], lhsT=QpTm[b][:, ((p * 2 + h % 2) * 2 + c) * 128:((p * 2 + h % 2) * 2 + c + 1) * 128],
                                      rhs=Sb[sbx][:, p * 64:(p + 1) * 64], start=(c == 0), stop=False), [QP, "Sb%d" % sbx], ["pf0"])
            T_(lambda e: e.matmul(pf[0][:, h * 64:(h + 1) * 64], lhsT=attm[:, h * 128:(h + 1) * 128], rhs=Vb[:, h * 64:(h + 1) * 64],
                                  start=False, stop=True), ["attm", "Vb"], ["pf0"])
        S_(lambda e: e.activation(out=sq, in_=pf[0], func=AF.Square), ["pf0"], ["sq"])
        V_(lambda e: e.tensor_reduce(out=ss8, in_=v3(sq, 8), op=ALU.add, axis=AX.X), ["sq"], ["ss8"])
        S_(lambda e: e.activation(out=rs8, in_=ss8, func=AF.Sqrt, scale=1.0 / 64, bias=1e-6), ["ss8"], ["rs8"])
        V_(lambda e: e.reciprocal(out=rs8, in_=rs8), ["rs8"], ["rs8"])
        V_(lambda e: e.tensor_tensor(out=v3(o32, 8), in0=v3(pf[0], 8), in1=rs8.unsqueeze(2).to_broadcast([128, 8, 64]), op=ALU.mult), ["pf0", "rs8"], ["o32"])
        V_(lambda e: e.tensor_tensor(out=v3(o32, 8), in0=v3(o32, 8), in1=gn.unsqueeze(1).to_broadcast([128, 8, 64]), op=ALU.mult), ["o32", "gn"], ["o32"])
        if k.debug:
            P.dma("sync", lambda e: e.dma_start(out=k.dbg["c"][n * 128:(n + 1) * 128, :], in_=o32), reads=["o32"])
        V_(lambda e: e.tensor_tensor(out=ygb, in0=o32, in1=gate, op=ALU.mult), ["o32", "gate"], ["ygb"])
        store_yT(k, "c", n, "ygb", ygb, 3, l)


def phaseA(k, l):
    P, nc = k.P, k.nc
    new_phase(k)
    fa, ba = k.fa, k.ba
    NT = k.NT
    pf = k.pf
    idb = cview(k, "ident")
    C0 = float(np.exp(-0.5))
    V_ = lambda fn, r, w: P.op("vector", fn, r, w)
    S_ = lambda fn, r, w: P.op("scalar", fn, r, w)
    G_ = lambda fn, r, w: P.op("gpsimd", fn, r, w)
    T_ = lambda fn, r, w: P.op("tensor", fn, r, w)
    bc = lambda src: src.partition_broadcast(128)
    W1 = v3(ba.alloc(8 * A_COLS), 8)
    W2 = v3(ba.alloc(8 * A_COLS), 8)
    w2a2 = ba.alloc(1024)
    vecs = {}
    for nm, src in (("w0b", k.rwkv_w0), ("a0b", k.rwkv_a0), ("kkb", k.rwkv_k_k), ("kab", k.rwkv_k_a), ("rkb", k.rwkv_r_k),
                    ("lnw", k.rwkv_ln_w), ("lnb", k.rwkv_ln_b)):
        vecs[nm] = fa.alloc(512)
        P.dma("sync", lambda e: e.dma_start(out=vecs[nm], in_=bc(src[l:l + 1, :])), writes=[nm])
    G_(lambda e: e.memset(w2a2, 0.0), [], ["w2a2"])
    P.dma("gpsimd", lambda e: e.dma_start(out=w2a2[0:64, 0:512], in_=k.rwkv_w2[l, :, :]), writes=["w2a2"])
    P.dma("gpsimd", lambda e: e.dma_start(out=w2a2[64:128, 512:1024], in_=k.rwkv_a2[l, :, :]), writes=["w2a2"])
    keep = k.fa.a.off
    mu_b = fa.alloc(A_COLS)
    omu = fa.alloc(A_COLS)
    stage = fa.alloc(A_COLS)
    P.dma("sync", lambda e: e.dma_start(out=mu_b, in_=bc(k.rwkv_mu[l:l + 1, :])), writes=["mu_b"])
    V_(lambda e: e.tensor_scalar(out=omu, in0=mu_b, scalar1=-1.0, scalar2=1.0, op0=ALU.mult, op1=ALU.add), ["mu_b"], ["omu"])
    for c in range(8):
        P.dma("sync", lambda e: e.dma_start(out=stage, in_=k.w_in[l, c * 128:(c + 1) * 128, 0:A_COLS]), writes=["stage"])
        V_(lambda e: e.tensor_tensor(out=W1[:, c, :], in0=stage, in1=omu, op=ALU.mult), ["stage", "omu"], ["W1"])
        G_(lambda e: e.tensor_tensor(out=W2[:, c, :], in0=stage, in1=mu_b, op=ALU.mult), ["stage", "mu_b"], ["W2"])
    P.barrier()
    k.fa.a.off = keep
    tri, chm = cview(k, "tri", False), cview(k, "ch", False)
    msu, msl, miu = cview(k, "msu", False), cview(k, "msl", False), cview(k, "miu", False)
    idf = cview(k, "ident", False)
    STf = fa.alloc(256)
    V_(lambda e: e.memset(STf, 0.0), [], ["STf"])
    STb = [ba.alloc(256) for _ in range(3)]
    G_(lambda e: e.memset(STb[0], 0.0), [], ["STb0"])
    Uall = ba.alloc(512)
    G_(lambda e: e.memset(Uall, 0.0), [], ["Uall"])
    msk = {}
    for nm, sz in (("ATm", 1024), ("BTm", 1024), ("KTm", 1024), ("RTc", 2048), ("M1m", 2048), ("Atm", 1024), ("Bh", 2048), ("Kh", 2048)):
        msk[nm] = ba.alloc(sz)
        G_(lambda e: e.memset(msk[nm], 0.0), [], [nm])
    hTc = v3(ba.alloc(1024), 8)
    hTp = v3(ba.alloc(1024), 8)
    F = [fa.alloc(512) for _ in range(11)]
    FK = ["F%d" % i for i in range(11)]
    lor = ba.alloc(128)
    lorT = ba.alloc(128)
    Rt, At, Bt, Kt, Vb = [ba.alloc(512) for _ in range(5)]
    PA, QA, PB, QB, XI, Aak, ArbT, ArkT = [ba.alloc(1024) for _ in range(8)]
    gS, ss8, rk8, m8, q8, r8 = [fa.alloc(8) for _ in range(6)]
    k.ystage = [ba.alloc(512)] * 2
    cols = {"r": 0, "wl": 512, "k": 576, "v": 1088, "al": 1600, "g": 1664}
    h3 = lambda ap: v3(ap, 8)
    b864 = lambda ap: ap.unsqueeze(2).to_broadcast([128, 8, 64])
    sbi = 0

    def evac_masked(dst, dkey, src_bf, skey, chunked):
        if chunked:
            d6 = dst.rearrange("q (p u c t) -> q p u c t", p=4, u=2, c=2)
        else:
            d4 = dst.rearrange("q (p u t) -> q p u t", p=4, u=2)
        for u in range(2):
            rows = slice(u * 64, (u + 1) * 64)
            if chunked:
                for c in range(2):
                    S_(lambda e: e.copy(out=d6[rows, :, u, c, c * 64:(c + 1) * 64], in_=v3(src_bf[rows, :], 4)[:, :, c * 64:(c + 1) * 64]), [skey], [dkey])
            else:
                S_(lambda e: e.copy(out=d4[rows, :, u, :], in_=v3(src_bf[rows, :], 4)), [skey], [dkey + "0", dkey + "1"])

    def headmm(bankA, bankB, lhs, lkey, rhs, rkey, halves=(0, 1)):
        for h in range(8):
            if h // 4 not in halves:
                continue
            bank = bankA if h < 4 else bankB
            hk = str(h // 4)
            T_(lambda e: e.matmul(pf[bank][:, (h % 4) * 128:(h % 4 + 1) * 128], lhsT=lhs[:, h * 128:(h + 1) * 128], rhs=rhs[:, h * 128:(h + 1) * 128],
                                  start=True, stop=True), [lkey + hk, rkey + hk], ["pf%d" % bank])

    def headmm_rc(bankA, bankB, lhs, lkey):
        for h in range(8):
            bank = bankA if h < 4 else bankB
            for c in range(2):
                o0 = (h % 4) * 128 + c * 64
                r0 = (h * 2 + c) * 128 + c * 64
                T_(lambda e: e.matmul(pf[bank][:, o0:o0 + 64], lhsT=lhs[:, h * 128:(h + 1) * 128], rhs=msk["RTc"][:, r0:r0 + 64],
                                      start=True, stop=True), [lkey + str(h // 4), "RTc"], ["pf%d" % bank])

    def evac_mask(dst, dkey, bankA, bankB, mask, halves=(0, 1), engs=("scalar", "vector")):
        for i, bank in enumerate((bankA, bankB)):
            if i not in halves:
                continue
            if mask is None:
                if engs[i] == "scalar":
                    S_(lambda e: e.copy(out=dst[:, i * 512:(i + 1) * 512], in_=pf[bank]), ["pf%d" % bank], [dkey + str(i)])
                else:
                    V_(lambda e: e.tensor_copy(out=dst[:, i * 512:(i + 1) * 512], in_=pf[bank]), ["pf%d" % bank], [dkey + str(i)])
            else:
                V_(lambda e: e.tensor_tensor(out=v3(dst[:, i * 512:(i + 1) * 512], 4), in0=v3(pf[bank], 4), in1=mask.unsqueeze(1).to_broadcast([128, 4, 128]),
                                             op=ALU.mult), ["pf%d" % bank, "cm_f"], [dkey + str(i)])

    def load_h(n):
        P.dma("sync", lambda e: e.dma_start(out=hTc, in_=k.hT_d[:, :, 16 + n * 128:16 + (n + 1) * 128]), writes=["hTc"])
        P.dma("sync", lambda e: e.dma_start(out=hTp, in_=k.hT_d[:, :, 15 + n * 128:15 + (n + 1) * 128]), writes=["hTp"])

    def proj(out_ap, okey, c0, ncols):
        for c in range(8):
            T_(lambda e: e.matmul(out_ap, lhsT=hTc[:, c, :], rhs=W1[:, c, c0:c0 + ncols], start=(c == 0), stop=False), ["hTc", "W1"], [okey])
            T_(lambda e: e.matmul(out_ap, lhsT=hTp[:, c, :], rhs=W2[:, c, c0:c0 + ncols], start=False, stop=(c == 7)), ["hTp", "W2"], [okey])

    def proj_tile():
        proj(pf[0], "pf0", cols["r"], 512)
        proj(pf[1], "pf1", cols["k"], 512)
        proj(pf[2], "pf2", cols["v"], 512)
        proj(pf[3], "pf3", cols["g"], 512)
        proj(pf[7][:, 256:320], "pf7", cols["wl"], 64)
        proj(pf[7][:, 320:384], "pf7", cols["al"], 64)

    load_h(0)
    proj_tile()
    for n in range(NT):
        if n + 1 < NT:
            load_h(n + 1)
        p5b, p6b = pbf(k, 5), pbf(k, 6)
        S_(lambda e: e.activation(out=lor[:, 0:64], in_=pf[7][:, 256:320], func=AF.Tanh), ["pf7"], ["lor"])
        S_(lambda e: e.copy(out=lor[:, 64:128], in_=pf[7][:, 320:384]), ["pf7"], ["lor"])
        T_(lambda e: e.transpose(out=p5b[:, 0:128], in_=lor, identity=idb), ["lor", "cm_b"], ["pf5"])
        S_(lambda e: e.copy(out=lorT, in_=p5b[:, 0:128]), ["pf5"], ["lorT"])
        T_(lambda e: e.matmul(pf[5], lhsT=lorT, rhs=w2a2[:, 0:512], start=True, stop=True), ["lorT", "w2a2"], ["pf5"])
        T_(lambda e: e.matmul(pf[6], lhsT=lorT, rhs=w2a2[:, 512:1024], start=True, stop=True), ["lorT", "w2a2"], ["pf6"])
        V_(lambda e: e.tensor_tensor(out=F[7], in0=pf[1], in1=vecs["kkb"], op=ALU.mult), ["pf1", "kkb"], ["F7"])
        S_(lambda e: e.activation(out=F[9], in_=F[7], func=AF.Square), ["F7"], ["F9"])
        V_(lambda e: e.tensor_reduce(out=ss8, in_=h3(F[9]), op=ALU.add, axis=AX.X), ["F9"], ["ss8"])
        S_(lambda e: e.activation(out=ss8, in_=ss8, func=AF.Sqrt), ["ss8"], ["ss8"])
        V_(lambda e: e.tensor_scalar(out=ss8, in0=ss8, scalar1=1e-12, scalar2=0.0, op0=ALU.max, op1=ALU.add), ["ss8"], ["ss8"])
        V_(lambda e: e.reciprocal(out=ss8, in_=ss8), ["ss8"], ["ss8"])
        V_(lambda e: e.tensor_tensor(out=h3(F[7]), in0=h3(F[7]), in1=b864(ss8), op=ALU.mult), ["F7", "ss8"], ["F7"])
        V_(lambda e: e.tensor_tensor(out=F[1], in0=pf[5], in1=vecs["w0b"], op=ALU.add), ["pf5", "w0b"], ["F1"])
        S_(lambda e: e.activation(out=F[1], in_=F[1], func=AF.Sigmoid), ["F1"], ["F1"])
        V_(lambda e: e.tensor_tensor(out=F[4], in0=pf[6], in1=vecs["a0b"], op=ALU.add), ["pf6", "a0b"], ["F4"])
        S_(lambda e: e.activation(out=F[2], in_=F[4], func=AF.Sigmoid), ["F4"], ["F2"])
        T_(lambda e: e.matmul(pf[5], lhsT=tri, rhs=F[1], start=True, stop=True), ["F1", "cm_f"], ["pf5"])
        T_(lambda e: e.matmul(pf[6], lhsT=chm, rhs=F[1], start=True, stop=True), ["F1", "cm_f"], ["pf6"])
        for p in range(4):
            T_(lambda e: e.matmul(pf[7][:, p * 2:p * 2 + 2], lhsT=F[1][:, p * 128:(p + 1) * 128], rhs=k.chs, start=True, stop=True), ["F1", "chs"], ["pf7"])
        S_(lambda e: e.copy(out=F[0], in_=pf[2]), ["pf2"], ["F0"])
        S_(lambda e: e.copy(out=Vb, in_=pf[2]), ["pf2"], ["Vb"])
        S_(lambda e: e.activation(out=F[10], in_=pf[3], func=AF.Silu), ["pf3"], ["F10"])
        V_(lambda e: e.scalar_tensor_tensor(out=F[9], in0=F[2], scalar=-1.0, in1=vecs["kab"], op0=ALU.add, op1=ALU.mult), ["F2", "kab"], ["F9"])
        V_(lambda e: e.scalar_tensor_tensor(out=F[8], in0=F[9], scalar=1.0, in1=pf[1], op0=ALU.add, op1=ALU.mult), ["F9", "pf1"], ["F8"])
        V_(lambda e: e.tensor_tensor(out=F[9], in0=F[7], in1=F[2], op=ALU.mult), ["F7", "F2", "F8"], ["F9"])
        V_(lambda e: e.tensor_tensor(out=F[5], in0=pf[0], in1=F[8], op=ALU.mult), ["pf0", "F8"], ["F5"])
        V_(lambda e: e.tensor_tensor(out=F[5], in0=F[5], in1=vecs["rkb"], op=ALU.mult), ["F5", "rkb"], ["F5"])
        V_(lambda e: e.tensor_reduce(out=rk8, in_=h3(F[5]), op=ALU.add, axis=AX.X), ["F5"], ["rk8"])
        S_(lambda e: e.activation(out=gS, in_=pf[7][:, 0:8], func=AF.Exp, scale=-C0), ["pf7"], ["gS"])
        S_(lambda e: e.copy(out=F[3], in_=pf[5]), ["pf5"], ["F3"])
        V_(lambda e: e.tensor_tensor(out=F[4], in0=pf[5], in1=F[1], op=ALU.subtract), ["pf5", "F1"], ["F4"])
        S_(lambda e: e.activation(out=F[5], in_=F[3], func=AF.Exp, scale=-C0), ["F3", "rk8"], ["F5"])
        V_(lambda e: e.tensor_tensor(out=Rt, in0=pf[0], in1=F[5], op=ALU.mult), ["pf0", "F5"], ["Rt"])
        S_(lambda e: e.activation(out=F[6], in_=F[4], func=AF.Exp, scale=-C0), ["F4"], ["F6"])
        V_(lambda e: e.scalar_tensor_tensor(out=At, in0=F[7], scalar=-1.0, in1=F[6], op0=ALU.mult, op1=ALU.mult), ["F7", "F6"], ["At"])
        atm4 = msk["Atm"].rearrange("q (u p w) -> q u p w", u=2, p=4)
        for u in range(2):
            G_(lambda e: e.tensor_copy(out=atm4[:, u, :, u * 64:(u + 1) * 64], in_=v3(At, 4)[:, :, u * 64:(u + 1) * 64]), ["At"], ["Atm"])
        V_(lambda e: e.tensor_tensor(out=F[4], in0=pf[6], in1=F[3], op=ALU.subtract), ["pf6", "F3", "F6"], ["F4"])
        S_(lambda e: e.activation(out=F[5], in_=F[3], func=AF.Exp, scale=C0), ["F3", "Rt"], ["F5"])
        S_(lambda e: e.activation(out=F[6], in_=F[4], func=AF.Exp, scale=-C0), ["F4", "At"], ["F6"])
        G_(lambda e: e.tensor_tensor(out=Bt, in0=F[9], in1=F[5], op=ALU.mult), ["F9", "F5"], ["Bt"])
        V_(lambda e: e.tensor_tensor(out=Kt, in0=F[8], in1=F[5], op=ALU.mult), ["F8", "F5"], ["Kt"])
        for nm, src, skey in (("Bh", F[9], "F9"), ("Kh", F[8], "F8")):
            d5 = msk[nm].rearrange("q (c u p w) -> q c u p w", c=2, u=2, p=4)
            for c in range(2):
                for u in range(2):
                    rows = slice(c * 64, (c + 1) * 64)
                    eng = V_
                    eng(lambda e: e.tensor_tensor(out=d5[rows, c, u, :, u * 64:(u + 1) * 64], in0=v3(src, 4)[rows, :, u * 64:(u + 1) * 64],
                                                  in1=v3(F[6], 4)[rows, :, u * 64:(u + 1) * 64], op=ALU.mult), [skey, "F6"], [nm])
        for j, (src, skey) in enumerate(((Rt, "Rt"), (At, "At"))):
            for c in range(4):
                T_(lambda e: e.transpose(out=p5b[:, j * 512 + c * 128:j * 512 + (c + 1) * 128], in_=src[:, c * 128:(c + 1) * 128], identity=idb),
                   [skey, "cm_b"], ["pf5"])
        for j, (src, skey) in enumerate(((Bt, "Bt"), (Kt, "Kt"))):
            for c in range(4):
                T_(lambda e: e.transpose(out=p6b[:, j * 512 + c * 128:j * 512 + (c + 1) * 128], in_=src[:, c * 128:(c + 1) * 128], identity=idb),
                   [skey, "cm_b"], ["pf6"])
        evac_masked(msk["RTc"], "RTc", p5b[:, 0:512], "pf5", True)
        evac_masked(msk["ATm"], "ATm", p5b[:, 512:1024], "pf5", False)
        evac_masked(msk["BTm"], "BTm", p6b[:, 0:512], "pf6", False)
        evac_masked(msk["KTm"], "KTm", p6b[:, 512:1024], "pf6", False)
        headmm(0, 1, msk["BTm"], "BTm", msk["ATm"], "ATm"); evac_mask(PA, "PA", 0, 1, msu)
        headmm(2, 3, msk["ATm"], "ATm", msk["BTm"], "BTm"); evac_mask(QA, "QA", 2, 3, msl)
        headmm(4, 7, msk["ATm"], "ATm", msk["KTm"], "KTm"); evac_mask(Aak, "Aak", 4, 7, msl)
        headmm_rc(5, 6, msk["BTm"], "BTm"); evac_mask(ArbT, "ArbT", 5, 6, miu)
        headmm_rc(0, 1, msk["KTm"], "KTm"); evac_mask(ArkT, "ArkT", 0, 1, miu)
        for i in range(2):
            V_(lambda e: e.tensor_tensor(out=v3(XI[:, i * 512:(i + 1) * 512], 4), in0=v3(PA[:, i * 512:(i + 1) * 512], 4),
                                         in1=idf.unsqueeze(1).to_broadcast([128, 4, 128]), op=ALU.add), ["PA%d" % i, "cm_f"], ["XI%d" % i])
        cur, nxt = (PA, QA, "PA", "QA"), (PB, QB, "PB", "QB")
        for lev in range(5):
            Pc, Qc, Pk, Qk = cur
            Pn, Qn, Pnk, Qnk = nxt
            for g in range(2):
                if lev < 4:
                    headmm(2, 3, Qc, Qk, Pc, Pk, halves=(g,))
                headmm(4, 7, Pc, Pk, Qc, Qk, halves=(g,))
            for g in range(2):
                evac_mask(Qn, Qnk, 4, 7, None, halves=(g,), engs=("vector", "vector"))
                if lev < 4:
                    evac_mask(Pn, Pnk, 2, 3, None, halves=(g,), engs=("scalar", "scalar"))
            for g in range(2):
                for h in range(4 * g, 4 * g + 4):
                    bank = 5 if h < 4 else 6
                    o = pf[bank][:, (h % 4) * 128:(h % 4 + 1) * 128]
                    T_(lambda e: e.matmul(o, lhsT=idb, rhs=XI[:, h * 128:(h + 1) * 128], start=True, stop=False), ["XI%d" % g, "cm_b"], ["pf%d" % bank])
                    T_(lambda e: e.matmul(o, lhsT=Qn[:, h * 128:(h + 1) * 128], rhs=XI[:, h * 128:(h + 1) * 128], start=False, stop=True),
                       ["XI%d" % g, Qnk + str(g)], ["pf%d" % bank])
            evac_mask(XI, "XI", 5, 6, None, engs=("scalar", "vector"))
            cur, nxt = nxt, cur
        for p in range(4):
            for u in range(2):
                h = 2 * p + u
                T_(lambda e: e.matmul(pf[2][:, p * 128:(p + 1) * 128], lhsT=msk["Atm"][:, u * 512 + p * 128:u * 512 + (p + 1) * 128],
                                      rhs=XI[:, h * 128:(h + 1) * 128], start=(u == 0), stop=(u == 1)), ["Atm", "XI%d" % (h // 4)], ["pf2"])
        m6 = msk["M1m"].rearrange("q (p u c t) -> q p u c t", p=4, u=2, c=2)
        for u in range(2):
            rows = slice(u * 64, (u + 1) * 64)
            for c in range(2):
                S_(lambda e: e.copy(out=m6[rows, :, u, c, c * 64:(c + 1) * 64], in_=v3(pf[2][rows, :], 4)[:, :, c * 64:(c + 1) * 64]), ["pf2"], ["M1m"])
        M2 = PA
        headmm(3, 4, Aak, "Aak", XI, "XI"); evac_mask(M2, "PA", 3, 4, None)
        sb_in = sbi
        for c in range(2):
            ub = 5 if c == 0 else 6
            UB = "pf%d" % ub
            sbc = (sb_in + c) % 3
            for h in range(8):
                p = h // 2
                o = pf[ub][:, h * 64:(h + 1) * 64]
                T_(lambda e: e.matmul(o, lhsT=msk["M1m"][:, (h * 2 + c) * 128:(h * 2 + c + 1) * 128], rhs=STb[sbc][:, p * 64:(p + 1) * 64],
                                      start=True, stop=False), ["M1m", "STb%d" % sbc], [UB])
                T_(lambda e: e.matmul(o, lhsT=M2[:, h * 128:(h + 1) * 128], rhs=Vb[:, h * 64:(h + 1) * 64], start=False, stop=True), ["PA%d" % (h // 4), "Vb"], [UB])
            rows = slice(c * 64, (c + 1) * 64)
            S_(lambda e: e.copy(out=Uall[rows, :], in_=pf[ub][rows, :]), [UB], ["Uall"])
            for p in range(4):
                for u in range(2):
                    h = 2 * p + u
                    o = pf[7][:, p * 64:(p + 1) * 64]
                    T_(lambda e: e.matmul(o, lhsT=msk["Bh"][:, (c * 2 + u) * 512 + p * 128:(c * 2 + u) * 512 + (p + 1) * 128],
                                          rhs=Uall[:, h * 64:(h + 1) * 64], start=(u == 0), stop=False), ["Bh", "Uall"], ["pf7"])
                    T_(lambda e: e.matmul(o, lhsT=msk["Kh"][:, (c * 2 + u) * 512 + p * 128:(c * 2 + u) * 512 + (p + 1) * 128],
                                          rhs=Vb[:, h * 64:(h + 1) * 64], start=False, stop=(u == 1)), ["Kh", "Vb"], ["pf7"])
            V_(lambda e: e.tensor_tensor(out=v3(STf, 4), in0=v3(STf, 4), in1=gS[:, c:8:2].unsqueeze(2).to_broadcast([128, 4, 64]), op=ALU.mult),
               ["STf", "gS"], ["STf"])
            V_(lambda e: e.tensor_tensor(out=STf, in0=STf, in1=pf[7][:, 0:256], op=ALU.add), ["STf", "pf7"], ["STf"])
            sbi = (sbi + 1) % 3
            S_(lambda e: e.copy(out=STb[sbi], in_=STf), ["STf"], ["STb%d" % sbi])
        sb0, sb1 = sb_in, (sb_in + 1) % 3
        for h in range(8):
            p = h // 2
            o = pf[4][:, h * 64:(h + 1) * 64]
            for c, sbx in ((0, sb0), (1, sb1)):
                T_(lambda e: e.matmul(o, lhsT=msk["RTc"][:, (h * 2 + c) * 128:(h * 2 + c + 1) * 128], rhs=STb[sbx][:, p * 64:(p + 1) * 64],
                                      start=(c == 0), stop=False), ["RTc", "STb%d" % sbx], ["pf4"])
            T_(lambda e: e.matmul(o, lhsT=ArbT[:, h * 128:(h + 1) * 128], rhs=Uall[:, h * 64:(h + 1) * 64], start=False, stop=False), ["ArbT%d" % (h // 4), "Uall"], ["pf4"])
            T_(lambda e: e.matmul(o, lhsT=ArkT[:, h * 128:(h + 1) * 128], rhs=Vb[:, h * 64:(h + 1) * 64], start=False, stop=True), ["ArkT%d" % (h // 4), "Vb"], ["pf4"])
        if n + 1 < NT:
            proj_tile()
        S_(lambda e: e.copy(out=F[4], in_=pf[4]), ["pf4"], ["F4"])
        V_(lambda e: e.tensor_reduce(out=m8, in_=h3(F[4]), op=ALU.add, axis=AX.X), ["F4"], ["m8"])
        S_(lambda e: e.activation(out=F[9], in_=F[4], func=AF.Square), ["F4"], ["F9"])
        V_(lambda e: e.tensor_reduce(out=q8, in_=h3(F[9]), op=ALU.add, axis=AX.X), ["F9"], ["q8"])
        V_(lambda e: e.tensor_scalar(out=m8, in0=m8, scalar1=1.0 / 64, scalar2=0.0, op0=ALU.mult, op1=ALU.add), ["m8"], ["m8"])
        V_(lambda e: e.tensor_tensor(out=r8, in0=m8, in1=m8, op=ALU.mult), ["m8"], ["r8"])
        V_(lambda e: e.scalar_tensor_tensor(out=q8, in0=q8, scalar=1.0 / 64, in1=r8, op0=ALU.mult, op1=ALU.subtract), ["q8", "r8"], ["q8"])
        S_(lambda e: e.activation(out=r8, in_=q8, func=AF.Sqrt, bias=64e-5), ["q8"], ["r8"])
        V_(lambda e: e.reciprocal(out=r8, in_=r8), ["r8"], ["r8"])
        V_(lambda e: e.tensor_tensor(out=h3(F[4]), in0=h3(F[4]), in1=b864(m8), op=ALU.subtract), ["F4", "m8"], ["F4"])
        V_(lambda e: e.tensor_tensor(out=h3(F[4]), in0=h3(F[4]), in1=b864(r8), op=ALU.mult), ["F4", "r8"], ["F4"])
        V_(lambda e: e.tensor_tensor(out=F[4], in0=F[4], in1=vecs["lnw"], op=ALU.mult), ["F4", "lnw"], ["F4"])
        V_(lambda e: e.tensor_tensor(out=F[4], in0=F[4], in1=vecs["lnb"], op=ALU.add), ["F4", "lnb"], ["F4"])
        V_(lambda e: e.tensor_tensor(out=h3(F[9]), in0=h3(F[0]), in1=b864(rk8), op=ALU.mult), ["F0", "rk8"], ["F9"])
        V_(lambda e: e.tensor_tensor(out=F[4], in0=F[4], in1=F[9], op=ALU.add), ["F4", "F9"], ["F4"])
        if k.debug:
            P.dma("sync", lambda e: e.dma_start(out=k.dbg["a"][n * 128:(n + 1) * 128, :], in_=F[4]), reads=["F4"])
        V_(lambda e: e.tensor_tensor(out=Rt, in0=F[4], in1=F[10], op=ALU.mult), ["F4", "F10"], ["Rt"])
        store_yT(k, "a", n, "Rt", Rt, 4, l)
```

```python
import numpy as np
from contextlib import ExitStack
import concourse.bass as bass
import concourse.mybir as mybir
from concourse.bass_utils import run_bass_kernel_spmd

F32 = mybir.dt.float32
BF16 = mybir.dt.bfloat16
AF = mybir.ActivationFunctionType
ALU = mybir.AluOpType
AX = mybir.AxisListType

D_MODEL = 1024
A_COLS = 2176
B_COLS = 2048
C_COLS = 2048
IN_COLS = 9344
N_REL = 320


class _Rec:
    def __init__(self):
        self.call = None

    def __getattr__(self, name):
        def f(*a, **kw):
            self.call = (name, a, kw)
            return self
        return f


def _bind(fn):
    rec = _Rec()
    fn(rec)
    name, a, kw = rec.call
    return lambda eng: getattr(eng, name)(*a, **kw)


class Op:
    __slots__ = ("eng", "fn", "deps", "signal", "sigval", "dma_sem", "dma_val", "is_dma", "pre_wait")

    def __init__(self, eng, fn):
        self.eng = eng
        self.fn = fn
        self.deps = []
        self.signal = False
        self.sigval = 0
        self.is_dma = False
        self.dma_sem = None
        self.dma_val = 0
        self.pre_wait = None


class Prog:
    ENGS = ("tensor", "vector", "scalar", "gpsimd", "sync")
    NDMA = 12

    def __init__(self, nc, es):
        self.nc = nc
        self.ops = {e: [] for e in self.ENGS}
        self.last_write = {}
        self.readers = {}
        self.sem = {e: es.enter_context(nc.semaphore("s_" + e)) for e in ("tensor", "vector", "scalar", "gpsimd")}
        self.dma_sems = {q: [es.enter_context(nc.semaphore("d_%s_%d" % (q, i))) for i in range(self.NDMA)]
                         for q in ("sync", "gpsimd")}
        self.dma_count = {"sync": 0, "gpsimd": 0}
        self.dma_hist = {"sync": [], "gpsimd": []}

    def _deps(self, op, reads, writes):
        deps = []
        for k in reads:
            w = self.last_write.get(k)
            if w is not None:
                deps.append(w)
        for k in writes:
            w = self.last_write.get(k)
            if w is not None:
                deps.append(w)
            deps.extend(self.readers.get(k, ()))
        seen = set()
        for d in deps:
            if id(d) in seen or d is op:
                continue
            seen.add(id(d))
            if d.eng == op.eng and op.eng == "tensor" and not d.is_dma:
                continue
            op.deps.append(d)
            if not d.is_dma:
                d.signal = True
        for k in reads:
            self.readers.setdefault(k, []).append(op)
        for k in writes:
            self.last_write[k] = op
            self.readers[k] = []

    def op(self, eng, fn, reads=(), writes=()):
        o = Op(eng, _bind(fn))
        self._deps(o, reads, writes)
        self._apply_bar(o)
        self.ops[eng].append(o)
        return o

    def dma(self, q, fn, reads=(), writes=()):
        o = Op(q, _bind(fn))
        o.is_dma = True
        i = self.dma_count[q]
        self.dma_count[q] += 1
        o.dma_sem = self.dma_sems[q][i % self.NDMA]
        o.dma_val = 16 * (i // self.NDMA + 1)
        if i >= self.NDMA:
            o.pre_wait = self.dma_hist[q][i - self.NDMA]
        self.dma_hist[q].append(o)
        self._deps(o, reads, writes)
        self._apply_bar(o)
        self.ops[q].append(o)
        return o

    def emit(self, block):
        for e in ("tensor", "vector", "scalar", "gpsimd"):
            c = 0
            for o in self.ops[e]:
                if o.is_dma:
                    continue
                if o.signal:
                    c += 1
                o.sigval = c
        all_dmas = self.dma_hist["sync"] + self.dma_hist["gpsimd"]

        def run(eng_name):
            def body(eng):
                water = {}

                def wait(sem, val):
                    key = id(sem)
                    if water.get(key, 0) >= val:
                        return
                    water[key] = val
                    eng.wait_ge(sem, val)

                for o in self.ops[eng_name]:
                    if o.pre_wait is not None:
                        wait(o.pre_wait.dma_sem, o.pre_wait.dma_val)
                    for d in o.deps:
                        if d.is_dma:
                            wait(d.dma_sem, d.dma_val)
                        else:
                            wait(self.sem[d.eng], d.sigval)
                    ins = o.fn(eng)
                    if o.is_dma:
                        ins.then_inc(o.dma_sem, 16)
                    elif o.signal:
                        ins.then_inc(self.sem[eng_name], 1)
                if eng_name in ("sync", "gpsimd"):
                    for o in self.dma_hist[eng_name][-self.NDMA:]:
                        wait(o.dma_sem, o.dma_val)
            return body

        block.tensor(run("tensor"))
        block.vector(run("vector"))
        block.scalar(run("scalar"))
        block.gpsimd(run("gpsimd"))
        block.sync(run("sync"))

    def barrier(self):
        lasts = []
        for e in ("tensor", "vector", "scalar", "gpsimd"):
            cs = [o for o in self.ops[e] if not o.is_dma]
            if cs:
                lasts.append(cs[-1])
        dmas = self.dma_hist["sync"][-self.NDMA:] + self.dma_hist["gpsimd"][-self.NDMA:]
        self.last_write = {"__bar__": None}
        self.readers = {}
        self._bar = lasts + dmas
        self._bar_pending = set(self.ENGS)

    def _apply_bar(self, o):
        if getattr(self, "_bar_pending", None) and o.eng in self._bar_pending:
            self._bar_pending.discard(o.eng)
            for d in self._bar:
                if d is o:
                    continue
                if d.eng == o.eng and not d.is_dma and o.eng == "tensor":
                    continue
                o.deps.append(d)
                if not d.is_dma:
                    d.signal = True


class Arena:
    def __init__(self, tens, size):
        self.t = tens
        self.size = size
        self.off = 0
        self.mark = 0

    def reset(self):
        self.off = self.mark

    def alloc(self, n, dt):
        nf = n if dt == F32 else (n + 1) // 2
        nf_al = (nf + 7) // 8 * 8
        assert self.off + nf_al <= self.size, ("arena overflow", self.off, nf_al, self.size)
        ap = self.t[:, self.off:self.off + nf]
        self.off += nf_al
        if dt != F32:
            ap = ap.bitcast(dt)[:, 0:n]
        return ap


class _AView:
    def __init__(self, arena, dt):
        self.a, self.dt = arena, dt

    def alloc(self, n):
        return self.a.alloc(n, self.dt)

    def reset(self):
        self.a.reset()

    @property
    def off(self):
        return self.a.off

    @property
    def mark(self):
        return self.a.mark

    @mark.setter
    def mark(self, v):
        self.a.mark = v


def _consts():
    s = np.arange(128)[:, None]
    t = np.arange(128)[None, :]
    same = (s // 64) == (t // 64)
    c = {}
    c["ident"] = np.eye(128)
    c["tri"] = same & (s <= t)
    c["ch"] = same
    c["midm"] = same & ((s % 64) <= 31)
    c["msu"] = same & (s < t)
    c["msl"] = same & (s > t)
    c["miu"] = same & (s <= t)
    names = ["ident", "tri", "ch", "midm", "msu", "msl", "miu"]
    arr = np.stack([c[k].astype(np.float32) for k in names], axis=1)
    chsel = np.zeros((128, 2), np.float32)
    chsel[:64, 0] = 1
    chsel[64:, 1] = 1
    negm = np.zeros((128, 5, 128), np.float32)
    negm[:64, 0, 64:] = -30000.0
    negm[64:, 4, :64] = -30000.0
    return names, arr, chsel, negm


CNAMES, CARR, CHSEL, NEGM = _consts()


def _bias_gather(rel_bias):
    k = np.arange(128)[:, None, None]
    r = np.arange(5)[None, :, None]
    q = np.arange(128)[None, None, :]
    idx = np.clip(512 + q - (r * 128 + k), -63, 256) + 63
    g = rel_bias[:, :, idx]
    return np.ascontiguousarray(np.transpose(g, (0, 2, 1, 3, 4)))


class K:
    pass


def build_nc(T=4096, L=2, branches=("a", "b", "c"), debug=False):
    NT = T // 128
    nc = bass.Bass("TRN2", target_bir_lowering=False)
    k = K()
    k.nc, k.T, k.L, k.NT, k.branches, k.debug = nc, T, L, NT, branches, debug

    def din(name, shape, dt=F32):
        return nc.dram_tensor(name, list(shape), dt, kind="ExternalInput").ap()

    k.x = din("x", [T, 1024])
    k.norm_g = din("norm_g", [L, 1024])
    k.w_in = din("w_in", [L, 1024, IN_COLS])
    k.rwkv_mu = din("rwkv_mu", [L, A_COLS])
    k.rwkv_w0 = din("rwkv_w0", [L, 512])
    k.rwkv_w2 = din("rwkv_w2", [L, 64, 512])
    k.rwkv_a0 = din("rwkv_a0", [L, 512])
    k.rwkv_a2 = din("rwkv_a2", [L, 64, 512])
    k.rwkv_k_k = din("rwkv_k_k", [L, 512])
    k.rwkv_k_a = din("rwkv_k_a", [L, 512])
    k.rwkv_r_k = din("rwkv_r_k", [L, 512])
    k.rwkv_ln_w = din("rwkv_ln_w", [L, 512])
    k.rwkv_ln_b = din("rwkv_ln_b", [L, 512])
    k.attn_q_norm = din("attn_q_norm", [L, 64])
    k.attn_k_norm = din("attn_k_norm", [L, 64])
    k.attn_bias = din("attn_bias", [L, 128, 8 * 5 * 128])
    k.hgrn_lb = din("hgrn_lb", [L, 512])
    k.hgrn_norm = din("hgrn_norm", [L, 64])
    k.proj_a = din("proj_a", [L, 512, 1024])
    k.proj_b = din("proj_b", [L, 512, 1024])
    k.proj_c = din("proj_c", [L, 512, 1024])
    k.w_out = din("w_out", [L, 1024, 1024])
    k.cmat = din("cmat", [128, 7 * 128])
    k.chsel = din("chsel", [128, 2])
    k.negm = din("negm", [128, 5 * 128])
    k.out = nc.dram_tensor("out", [T, 1024], F32, kind="ExternalOutput").ap()
    k.x1 = nc.dram_tensor("x1", [T, 1024], F32).ap()
    k.hT_d = nc.dram_tensor("hT_d", [128, 8, T + 16], BF16).ap()
    k.yT_d = {b: nc.dram_tensor("yT_" + b, [128, 4, T], BF16).ap() for b in "abc"}
    if debug:
        k.dbg = {b: nc.dram_tensor("dbg_" + b, [T, 512], F32, kind="ExternalOutput").ap() for b in "abc"}

    with ExitStack() as es:
        FA = 42 * 1024
        arena = Arena(es.enter_context(nc.sbuf_tensor("arena", [128, FA], F32))[:], FA)
        k.fa = _AView(arena, F32)
        k.ba = _AView(arena, BF16)
        k.pf = [es.enter_context(nc.psum_tensor("pf%d" % i, [128, 512], F32))[:] for i in range(8)]
        k.P = Prog(nc, es)
        block = es.enter_context(nc.Block())
        P = k.P
        k.cm_f = k.fa.alloc(7 * 128)
        k.cm_b = k.ba.alloc(7 * 128)
        k.chs = k.fa.alloc(2)
        P.dma("sync", lambda e: e.dma_start(out=k.cm_f, in_=k.cmat[:, :]), writes=["cm_f"])
        P.dma("gpsimd", lambda e: e.dma_start(out=k.cm_b, in_=k.cmat[:, :]), writes=["cm_b"])
        P.dma("sync", lambda e: e.dma_start(out=k.chs, in_=k.chsel[:, :]), writes=["chs"])
        k.fa.mark = k.fa.off
        k.ba.mark = k.ba.off
        for l in range(L):
            xin = k.x if l == 0 else k.x1
            xout = k.out if l == L - 1 else k.x1
            phase0(k, l, xin)
            if STOP == "0":
                phaseCopy(k, xin, xout)
                continue
            if "a" in branches:
                phaseA(k, l)
            if "b" in branches:
                phaseB(k, l)
            if "c" in branches:
                phaseC(k, l)
            if STOP == "B":
                phaseCopy(k, xin, xout)
                continue
            phaseM(k, l, xin, xout)
        P.emit(block)
    return nc


import os
STOP = os.environ.get("STOP", "")
LVL = float(os.environ.get("LVL", "9"))


def phaseCopy(k, xin, xout):
    P = k.P
    new_phase(k)
    t = k.fa.alloc(1024)
    for n in range(k.NT):
        P.dma("sync", lambda e: e.dma_start(out=t, in_=xin[n * 128:(n + 1) * 128, :]), writes=["t"])
        P.dma("sync", lambda e: e.dma_start(out=xout[n * 128:(n + 1) * 128, :], in_=t), reads=["t"])


def cview(k, name, bf=True):
    i = CNAMES.index(name)
    t = k.cm_b if bf else k.cm_f
    return t[:, i * 128:(i + 1) * 128]


def new_phase(k):
    k.P.barrier()
    k.fa.reset()
    k.ba.reset()


def phase0(k, l, xin):
    P, nc = k.P, k.nc
    new_phase(k)
    fa, ba = k.fa, k.ba
    gb = fa.alloc(1024)
    xt = [fa.alloc(1024) for _ in range(2)]
    junk = fa.alloc(1024)
    ss = [fa.alloc(1) for _ in range(2)]
    rs = [fa.alloc(1) for _ in range(2)]
    hb = [ba.alloc(1024) for _ in range(2)]
    hs = [ba.alloc(1024) for _ in range(2)]
    zc = ba.alloc(8 * 16)
    idb = cview(k, "ident")
    P.dma("sync", lambda e: e.dma_start(out=gb, in_=k.norm_g[l:l + 1, :].partition_broadcast(128)), writes=["gb"])
    if l == 0:
        P.op("gpsimd", lambda e: e.memset(zc, 0.0), writes=["zc"])
        P.dma("sync", lambda e: e.dma_start(out=k.hT_d[:, :, 0:16], in_=zc.rearrange("p (c t) -> p c t", c=8)), reads=["zc"])
    for n in range(k.NT):
        b = n % 2
        X, HB, HS, SS, RS = "xt%d" % b, "hb%d" % b, "hs%d" % b, "ss%d" % b, "rs%d" % b
        if n == 0:
            P.dma("sync", lambda e: e.dma_start(out=xt[0], in_=xin[0:128, :]), writes=["xt0"])
        if n + 1 < k.NT:
            P.dma("sync", lambda e: e.dma_start(out=xt[(n + 1) % 2], in_=xin[(n + 1) * 128:(n + 2) * 128, :]), writes=["xt%d" % ((n + 1) % 2)])
        P.op("scalar", lambda e, b=b: e.activation(out=junk, in_=xt[b], func=AF.Square, accum_out=ss[b]), reads=[X], writes=["junk", SS])
        P.op("scalar", lambda e, b=b: e.activation(out=rs[b], in_=ss[b], func=AF.Sqrt, scale=1.0 / 1024, bias=1e-6), reads=[SS], writes=[RS])
        P.op("vector", lambda e, b=b: e.reciprocal(out=rs[b], in_=rs[b]), reads=[RS], writes=[RS])
        P.op("vector", lambda e, b=b: e.scalar_tensor_tensor(out=hb[b], in0=xt[b], scalar=rs[b][:, 0:1], in1=gb, op0=ALU.mult, op1=ALU.mult),
             reads=[X, RS, "gb"], writes=[HB])
        pt = k.pf[n % 2].bitcast(BF16)
        PT = "pf%d" % (n % 2)
        for c in range(8):
            P.op("tensor", lambda e, c=c, b=b, pt=pt: e.transpose(out=pt[:, c * 128:(c + 1) * 128], in_=hb[b][:, c * 128:(c + 1) * 128], identity=idb),
                 reads=[HB, "cm_b"], writes=[PT])
        P.op("scalar", lambda e, b=b, pt=pt: e.copy(out=hs[b], in_=pt[:, 0:1024]), reads=[PT], writes=[HS])
        P.dma("gpsimd", lambda e, n=n, b=b: e.dma_start(out=k.hT_d[:, :, 16 + n * 128:16 + (n + 1) * 128], in_=hs[b].rearrange("p (c t) -> p c t", c=8)),
              reads=[HS], writes=["hT_d"])


def pbf(k, i):
    return k.pf[i].bitcast(BF16)


def load_w_cast(k, dst3, src2d, key, nsplit=8):
    C = dst3.shape[1]
    N = dst3.shape[2]
    step = max(1, 2048 // 1)
    for c in range(C):
        for n0 in range(0, N, 2048):
            n1 = min(N, n0 + 2048)
            k.P.dma("gpsimd", lambda e, c=c, n0=n0, n1=n1: e.dma_start(out=dst3[:, c, n0:n1], in_=src2d[c * 128:(c + 1) * 128, n0:n1]),
                    writes=[key])


def proj_block(k, pbank, pkey, hT, hkey, W, wkey, c0, ncols, shift=0):
    for c in range(8):
        k.P.op("tensor", lambda e, c=c: e.matmul(pbank[:, 0:ncols], lhsT=hT[:, c, shift:shift + 128], rhs=W[:, c, c0:c0 + ncols],
                                                start=(c == 0), stop=(c == 7)),
               reads=[hkey, wkey], writes=[pkey])


def store_yT(k, br, n, ysrc_key, ysrc_bf, tp_bank, l):
    P = k.P
    b = n % 2
    pt = pbf(k, tp_bank)
    PT = "pf%d" % tp_bank
    idb = cview(k, "ident")
    for c in range(4):
        P.op("tensor", lambda e, c=c: e.transpose(out=pt[:, c * 128:(c + 1) * 128], in_=ysrc_bf[:, c * 128:(c + 1) * 128], identity=idb),
             reads=[ysrc_key, "cm_b"], writes=[PT])
    ys = k.ystage[b]
    YS = "ystage%d" % (b if k.ystage[0] is not k.ystage[1] else 0)
    P.op("scalar", lambda e: e.copy(out=ys, in_=pt[:, 0:512]), reads=[PT], writes=[YS])
    P.dma("gpsimd", lambda e: e.dma_start(out=k.yT_d[br][:, :, n * 128:(n + 1) * 128], in_=ys.rearrange("p (c t) -> p c t", c=4)),
          reads=[YS], writes=["yT_d" + br])


def v3(ap, a):
    return ap.rearrange("p (a b) -> p a b", a=a)


def phaseB(k, l):
    P, nc = k.P, k.nc
    new_phase(k)
    fa, ba = k.fa, k.ba
    NT = k.NT
    idb = cview(k, "ident")
    W = v3(ba.alloc(8 * 2048), 8)
    load_w_cast(k, W, k.w_in[l, :, A_COLS:A_COLS + B_COLS], "WB")
    gq = fa.alloc(64)
    gk = fa.alloc(64)
    P.dma("sync", lambda e: e.dma_start(out=gq, in_=k.attn_q_norm[l:l + 1, :].partition_broadcast(128)), writes=["gq"])
    P.dma("sync", lambda e: e.dma_start(out=gk, in_=k.attn_k_norm[l:l + 1, :].partition_broadcast(128)), writes=["gk"])
    P.op("vector", lambda e: e.scalar_tensor_tensor(out=gq, in0=gq, scalar=0.125, in1=gk, op0=ALU.mult, op1=ALU.mult), reads=["gq", "gk"], writes=["gq"])
    bstage = fa.alloc(5120)
    nm = fa.alloc(640)
    biasT = ba.alloc(5120)
    P.dma("sync", lambda e: e.dma_start(out=bstage, in_=k.attn_bias[l, :, :]), writes=["bstage"])
    P.dma("sync", lambda e: e.dma_start(out=nm, in_=k.negm[:, :]), writes=["nm"])
    P.op("vector", lambda e: e.tensor_tensor(out=v3(biasT, 8), in0=v3(bstage, 8), in1=nm.unsqueeze(1).to_broadcast([128, 8, 640]), op=ALU.add),
         reads=["bstage", "nm"], writes=["biasT"])
    bias4 = biasT.rearrange("p (h r q) -> p h r q", h=8, r=5)
    Vr = ba.alloc(8 * 8 * 80).rearrange("p (s h d) -> p s h d", s=8, h=8)
    P.op("gpsimd", lambda e: e.memset(Vr, 1.0), writes=["Vr"])
    kT = ba.alloc(4 * 8 * 128).rearrange("p (c s t) -> p c s t", c=4, s=8)
    qTm = [ba.alloc(8 * 128) for _ in range(2)]
    for b in range(2):
        P.op("gpsimd", lambda e, b=b: e.memset(qTm[b], 0.0), writes=["qTm%d" % b])
    hTt = [v3(ba.alloc(1024), 8) for _ in range(2)]
    sqt = fa.alloc(1024)
    ss16 = fa.alloc(16)
    rs16 = fa.alloc(16)
    qn32 = fa.alloc(512)
    qb = ba.alloc(512)
    kb = ba.alloc(512)
    sg = fa.alloc(512)
    PTb = [ba.alloc(512) for _ in range(3)]
    rinv = fa.alloc(8)
    y32 = fa.alloc(512)
    ygb = ba.alloc(512)
    k.ystage = [ba.alloc(512) for _ in range(2)]
    pf = k.pf
    sg2 = [sg, fa.alloc(512)]

    def s1a(n):
        b, slot = n % 2, n % 8
        H = "hTt%d" % b
        for blk in range(4):
            proj_block(k, pf[blk], "pf%d" % blk, hTt[b], H, W, "WB", blk * 512, 512)
        P.op("scalar", lambda e: e.activation(out=sqt[:, 0:512], in_=pf[0], func=AF.Square), reads=["pf0"], writes=["sqt"])
        P.op("scalar", lambda e: e.activation(out=sqt[:, 512:1024], in_=pf[1], func=AF.Square), reads=["pf1"], writes=["sqt"])
        P.op("vector", lambda e: e.tensor_reduce(out=ss16, in_=v3(sqt, 16), op=ALU.add, axis=AX.X), reads=["sqt"], writes=["ss16"])
        P.op("scalar", lambda e: e.activation(out=rs16, in_=ss16, func=AF.Sqrt, scale=1.0 / 64, bias=1e-6), reads=["ss16"], writes=["rs16"])
        P.op("vector", lambda e: e.reciprocal(out=rs16, in_=rs16), reads=["rs16"], writes=["rs16"])
        P.op("vector", lambda e: e.tensor_tensor(out=v3(qn32, 8), in0=v3(pf[0], 8), in1=rs16[:, 0:8].unsqueeze(2).to_broadcast([128, 8, 64]), op=ALU.mult),
             reads=["pf0", "rs16"], writes=["qn32"])
        P.op("vector", lambda e: e.tensor_tensor(out=v3(qb, 8), in0=v3(qn32, 8), in1=gq.unsqueeze(1).to_broadcast([128, 8, 64]), op=ALU.mult),
             reads=["qn32", "gq"], writes=["qb"])
        P.op("vector", lambda e: e.tensor_tensor(out=v3(kb, 8), in0=v3(pf[1], 8), in1=rs16[:, 8:16].unsqueeze(2).to_broadcast([128, 8, 64]), op=ALU.mult),
             reads=["pf1", "rs16"], writes=["kb"])
        P.op("scalar", lambda e: e.copy(out=Vr[:, slot, :, 0:64], in_=v3(pf[2], 8)), reads=["pf2"], writes=["Vr"])
        P.op("scalar", lambda e: e.activation(out=sg2[b], in_=pf[3], func=AF.Silu), reads=["pf3"], writes=["sg%d" % b])

    def s1b(n):
        b, slot = n % 2, n % 8
        pt = pbf(k, 0)
        for c in range(4):
            P.op("tensor", lambda e: e.transpose(out=pt[:, c * 128:(c + 1) * 128], in_=qb[:, c * 128:(c + 1) * 128], identity=idb),
                 reads=["qb", "cm_b"], writes=["pf0"])
        for c in range(4):
            P.op("tensor", lambda e: e.transpose(out=pt[:, 512 + c * 128:512 + (c + 1) * 128], in_=kb[:, c * 128:(c + 1) * 128], identity=idb),
                 reads=["kb", "cm_b"], writes=["pf0"])
        Q = "qTm%d" % b
        q4 = qTm[b].rearrange("p (c u t) -> p c u t", c=4, u=2)
        for u in range(2):
            P.op("scalar", lambda e: e.copy(out=q4[u * 64:(u + 1) * 64, :, u, :], in_=v3(pt[u * 64:(u + 1) * 64, 0:512], 4)), reads=["pf0"], writes=[Q])
        P.op("scalar", lambda e: e.copy(out=kT[:, :, slot, :], in_=v3(pt[:, 512:1024], 4)), reads=["pf0"], writes=["kT"])

    def att(n):
        b = n % 2
        Q = "qTm%d" % b
        blocks = [(h, r) for h in range(8) for r in range(5) if n - 4 + r >= 0]
        groups = [blocks[i:i + 4] for i in range(0, len(blocks), 4)]
        first_r = max(0, 4 - n)

        def emit_pv(grp, pb):
            PTK = "PT%d" % pb
            for j, (h, r) in enumerate(grp):
                kslot = (n - 4 + r) % 8
                ob = 6 + h // 4
                P.op("tensor", lambda e: e.matmul(pf[ob][:, (h % 4) * 65:(h % 4) * 65 + 65], lhsT=PTb[pb][:, j * 128:(j + 1) * 128],
                                                  rhs=Vr[:, kslot, h, 0:65], start=(r == first_r), stop=(r == 4)), reads=[PTK, "Vr"], writes=["pf%d" % ob])

        prev = None
        for gi, grp in enumerate(groups):
            bank = 4 + gi % 2
            BK = "pf%d" % bank
            for j, (h, r) in enumerate(grp):
                kslot = (n - 4 + r) % 8
                P.op("tensor", lambda e: e.matmul(pf[bank][:, j * 128:(j + 1) * 128], lhsT=kT[:, h // 2, kslot, :],
                                                  rhs=qTm[b][:, h * 128:(h + 1) * 128], start=True, stop=False), reads=["kT", Q], writes=[BK])
                P.op("tensor", lambda e: e.matmul(pf[bank][:, j * 128:(j + 1) * 128], lhsT=idb, rhs=bias4[:, h, r, :], start=False, stop=True),
                     reads=["biasT", "cm_b"], writes=[BK])
            if prev is not None:
                emit_pv(*prev)
            pb = gi % 3
            ncol = len(grp) * 128
            P.op("scalar", lambda e: e.activation(out=PTb[pb][:, 0:ncol], in_=pf[bank][:, 0:ncol], func=AF.Exp), reads=[BK], writes=["PT%d" % pb])
            prev = (grp, pb)
        if prev is not None:
            emit_pv(*prev)

    def post(n):
        b = n % 2
        for hb_ in range(2):
            o3 = v3(pf[6 + hb_][:, 0:260], 4)
            P.op("vector", lambda e: e.reciprocal(out=rinv[:, hb_ * 4:(hb_ + 1) * 4].unsqueeze(2), in_=o3[:, :, 64:65]),
                 reads=["pf%d" % (6 + hb_)], writes=["rinv"])
            P.op("vector", lambda e: e.tensor_tensor(out=v3(y32[:, hb_ * 256:(hb_ + 1) * 256], 4), in0=o3[:, :, 0:64],
                                                     in1=rinv[:, hb_ * 4:(hb_ + 1) * 4].unsqueeze(2).to_broadcast([128, 4, 64]), op=ALU.mult),
                 reads=["pf%d" % (6 + hb_), "rinv"], writes=["y32"])
        if k.debug:
            P.dma("sync", lambda e: e.dma_start(out=k.dbg["b"][n * 128:(n + 1) * 128, :], in_=y32), reads=["y32"])
        P.op("vector", lambda e: e.tensor_tensor(out=ygb, in0=y32, in1=sg2[b], op=ALU.mult), reads=["y32", "sg%d" % b], writes=["ygb"])
        store_yT(k, "b", n, "ygb", ygb, 6, l)

    def load(n):
        P.dma("sync", lambda e: e.dma_start(out=hTt[n % 2], in_=k.hT_d[:, :, 16 + n * 128:16 + (n + 1) * 128]), writes=["hTt%d" % (n % 2)])

    load(0)
    if NT > 1:
        load(1)
    s1a(0)
    s1b(0)
    for n in range(NT):
        if n + 1 < NT:
            s1a(n + 1)
        if n + 2 < NT:
            load(n + 2)
        att(n)
        if n + 1 < NT:
            s1b(n + 1)
        post(n)


def phaseM(k, l, xin, xout):
    P, nc = k.P, k.nc
    new_phase(k)
    fa, ba = k.fa, k.ba
    brs = [b for b in "abc" if b in k.branches]
    if not brs:
        return phaseCopy(k, xin, xout)
    projs = {"a": k.proj_a, "b": k.proj_b, "c": k.proj_c}
    Wz, Wp = {}, {}
    for bi, br in enumerate("abc"):
        if br not in brs:
            continue
        Wz[br] = v3(ba.alloc(8 * 1024), 8)
        load_w_cast(k, Wz[br], k.w_in[l, :, 6272 + bi * 1024:6272 + (bi + 1) * 1024], "Wz" + br)
        Wp[br] = v3(ba.alloc(4 * 1024), 4)
        load_w_cast(k, Wp[br], projs[br][l, :, :], "Wp" + br)
    Wo = v3(ba.alloc(8 * 1024), 8)
    load_w_cast(k, Wo, k.w_out[l, :, :], "Wo")
    TB = 512
    NB = k.T // TB
    hTb = [v3(ba.alloc(8 * TB), 8) for _ in range(2)]
    yTb = {br: [v3(ba.alloc(4 * TB), 4) for _ in range(2)] for br in brs}
    mT = v3(ba.alloc(8 * TB), 8)
    gsb = {br: fa.alloc(TB) for br in brs}
    acc = fa.alloc(TB)
    tmp = fa.alloc(TB)
    xt = [fa.alloc(1024) for _ in range(2)]
    ot = [fa.alloc(1024) for _ in range(2)]
    pf = k.pf
    for tb in range(NB):
        b = tb % 2
        H = "hTb%d" % b
        def load_blk(t_):
            b_ = t_ % 2
            P.dma("sync", lambda e: e.dma_start(out=hTb[b_], in_=k.hT_d[:, :, 16 + t_ * TB:16 + (t_ + 1) * TB]), writes=["hTb%d" % b_])
            for br in brs:
                P.dma("sync", lambda e: e.dma_start(out=yTb[br][b_], in_=k.yT_d[br][:, :, t_ * TB:(t_ + 1) * TB]), writes=["yTb%s%d" % (br, b_)])
        if tb == 0:
            load_blk(0)
        if tb + 1 < NB:
            load_blk(tb + 1)
        for fc in range(8):
            for bi, br in enumerate(brs):
                zb, pb_ = (fc * len(brs) + bi) % 4, 4 + (fc * len(brs) + bi) % 4
                for c in range(8):
                    P.op("tensor", lambda e, c=c, br=br, zb=zb: e.matmul(pf[zb], lhsT=Wz[br][:, c, fc * 128:(fc + 1) * 128], rhs=hTb[b][:, c, :],
                                                                        start=(c == 0), stop=(c == 7)), reads=["Wz" + br, H], writes=["pf%d" % zb])
                P.op("scalar", lambda e, br=br, zb=zb: e.activation(out=gsb[br], in_=pf[zb], func=AF.Sigmoid), reads=["pf%d" % zb], writes=["gsb" + br])
                for c in range(4):
                    P.op("tensor", lambda e, c=c, br=br, pb_=pb_: e.matmul(pf[pb_], lhsT=Wp[br][:, c, fc * 128:(fc + 1) * 128], rhs=yTb[br][b][:, c, :],
                                                                          start=(c == 0), stop=(c == 3)),
                         reads=["Wp" + br, "yTb%s%d" % (br, b)], writes=["pf%d" % pb_])
            for bi, br in enumerate(brs):
                pb_ = 4 + (fc * len(brs) + bi) % 4
                last = bi == len(brs) - 1
                if bi == 0:
                    dst = mT[:, fc, :] if last else acc
                    P.op("vector", lambda e, br=br, pb_=pb_, dst=dst: e.tensor_tensor(out=dst, in0=pf[pb_], in1=gsb[br], op=ALU.mult),
                         reads=["pf%d" % pb_, "gsb" + br], writes=["mT" if last else "acc"])
                else:
                    P.op("vector", lambda e, br=br, pb_=pb_: e.tensor_tensor(out=tmp, in0=pf[pb_], in1=gsb[br], op=ALU.mult),
                         reads=["pf%d" % pb_, "gsb" + br], writes=["tmp"])
                    dst = mT[:, fc, :] if last else acc
                    P.op("vector", lambda e, dst=dst: e.tensor_tensor(out=dst, in0=acc, in1=tmp, op=ALU.add),
                         reads=["acc", "tmp"], writes=["mT" if last else "acc"])
        for tt in range(TB // 128):
            n = tb * (TB // 128) + tt
            xb = n % 2
            X, O = "xm%d" % xb, "om%d" % xb
            if n == 0:
                P.dma("sync", lambda e: e.dma_start(out=xt[0], in_=xin[0:128, :]), writes=["xm0"])
            if n + 1 < k.NT:
                P.dma("sync", lambda e: e.dma_start(out=xt[(n + 1) % 2], in_=xin[(n + 1) * 128:(n + 2) * 128, :]), writes=["xm%d" % ((n + 1) % 2)])
            for cb in range(2):
                bank = cb
                for c in range(8):
                    P.op("tensor", lambda e, c=c, cb=cb, bank=bank, tt=tt: e.matmul(pf[bank], lhsT=mT[:, c, tt * 128:(tt + 1) * 128],
                                                                                   rhs=Wo[:, c, cb * 512:(cb + 1) * 512], start=(c == 0), stop=(c == 7)),
                         reads=["mT", "Wo"], writes=["pf%d" % bank])
                P.op("vector", lambda e, cb=cb, bank=bank, xb=xb: e.tensor_tensor(out=ot[xb][:, cb * 512:(cb + 1) * 512], in0=pf[bank],
                                                                                 in1=xt[xb][:, cb * 512:(cb + 1) * 512], op=ALU.add),
                     reads=["pf%d" % bank, X], writes=[O])
            P.dma("gpsimd", lambda e, n=n, xb=xb: e.dma_start(out=xout[n * 128:(n + 1) * 128, :], in_=ot[xb]), reads=[O], writes=["xout"])


_NC_CACHE = {}


def make_in_maps(inputs, T, L, nb):
    f = lambda a: np.ascontiguousarray(np.asarray(a, dtype=np.float32))
    shared = {
        "norm_g": f(inputs["norm_g"])[:L], "w_in": f(inputs["w_in"])[:L], "rwkv_mu": f(inputs["rwkv_mu"])[:L],
        "rwkv_w0": f(inputs["rwkv_w0"])[:L], "rwkv_w2": f(inputs["rwkv_w2"])[:L], "rwkv_a0": f(inputs["rwkv_a0"])[:L],
        "rwkv_a2": f(inputs["rwkv_a2"])[:L], "rwkv_k_k": f(inputs["rwkv_k_k"])[:L], "rwkv_k_a": f(inputs["rwkv_k_a"])[:L],
        "rwkv_r_k": f(inputs["rwkv_r_k"])[:L].reshape(L, 512), "rwkv_ln_w": f(inputs["rwkv_ln_w"])[:L],
        "rwkv_ln_b": f(inputs["rwkv_ln_b"])[:L], "attn_q_norm": f(inputs["attn_q_norm"])[:L],
        "attn_k_norm": f(inputs["attn_k_norm"])[:L],
        "attn_bias": _bias_gather(f(inputs["attn_rel_bias"])[:L]).reshape(L, 128, 8 * 5 * 128),
        "hgrn_lb": f(inputs["hgrn_lb"])[:L], "hgrn_norm": f(inputs["hgrn_norm"])[:L],
        "proj_a": f(inputs["proj_a"])[:L], "proj_b": f(inputs["proj_b"])[:L], "proj_c": f(inputs["proj_c"])[:L],
        "w_out": f(inputs["w_out"])[:L],
        "cmat": np.ascontiguousarray(CARR.reshape(128, 7 * 128)), "chsel": CHSEL,
        "negm": np.ascontiguousarray(NEGM.reshape(128, 640)),
    }
    x = f(inputs["x"])
    maps = []
    for b in range(nb):
        m = dict(shared)
        m["x"] = np.ascontiguousarray(x[b, :T])
        maps.append(m)
    return maps


def kernel(**inputs):
    T, L = 4096, 2
    key = (T, L)
    if key not in _NC_CACHE:
        _NC_CACHE[key] = build_nc(T, L, branches=tuple(os.environ.get("KBR", "abc")))
    nc = _NC_CACHE[key]
    maps = make_in_maps(inputs, T, L, 8)
    res = run_bass_kernel_spmd(nc, maps, core_ids=list(range(8)))
    return np.stack([r["out"] for r in res.results], axis=0).astype(np.float32)


def phaseC(k, l):
    P, nc = k.P, k.nc
    new_phase(k)
    fa, ba = k.fa, k.ba
    NT, L = k.NT, k.L
    pf = k.pf
    idb = cview(k, "ident")
    V_ = lambda fn, r, w: P.op("vector", fn, r, w)
    S_ = lambda fn, r, w: P.op("scalar", fn, r, w)
    G_ = lambda fn, r, w: P.op("gpsimd", fn, r, w)
    T_ = lambda fn, r, w: P.op("tensor", fn, r, w)
    W = v3(ba.alloc(8 * 2048), 8)
    load_w_cast(k, W, k.w_in[l, :, A_COLS + B_COLS:A_COLS + B_COLS + C_COLS], "WC")
    lbb = fa.alloc(512)
    oml = fa.alloc(512)
    if l == 0:
        V_(lambda e: e.memset(lbb, 0.0), [], ["lbb"])
    else:
        er = fa.alloc(L * 512)
        P.dma("sync", lambda e: e.dma_start(out=er, in_=k.hgrn_lb.rearrange("l c -> (l c)").unsqueeze(0).partition_broadcast(128).squeeze(1)
                                            if False else k.hgrn_lb.rearrange("(o l) c -> o (l c)", o=1).partition_broadcast(128)), writes=["er"])
        S_(lambda e: e.activation(out=er, in_=er, func=AF.Exp), ["er"], ["er"])
        ssum = fa.alloc(512)
        V_(lambda e: e.tensor_tensor(out=ssum, in0=er[:, 0:512], in1=er[:, 512:1024], op=ALU.add), ["er"], ["ssum"])
        for j in range(2, L):
            V_(lambda e: e.tensor_tensor(out=ssum, in0=ssum, in1=er[:, j * 512:(j + 1) * 512], op=ALU.add), ["er", "ssum"], ["ssum"])
        V_(lambda e: e.tensor_copy(out=lbb, in_=er[:, 512:1024]), ["er"], ["lbb"])
        for j in range(2, l + 1):
            V_(lambda e: e.tensor_tensor(out=lbb, in0=lbb, in1=er[:, j * 512:(j + 1) * 512], op=ALU.add), ["er", "lbb"], ["lbb"])
        V_(lambda e: e.reciprocal(out=ssum, in_=ssum), ["ssum"], ["ssum"])
        V_(lambda e: e.tensor_tensor(out=lbb, in0=lbb, in1=ssum, op=ALU.mult), ["lbb", "ssum"], ["lbb"])
    V_(lambda e: e.tensor_scalar(out=oml, in0=lbb, scalar1=-1.0, scalar2=1.0, op0=ALU.mult, op1=ALU.add), ["lbb"], ["oml"])
    gn = fa.alloc(64)
    P.dma("sync", lambda e: e.dma_start(out=gn, in_=k.hgrn_norm[l:l + 1, :].partition_broadcast(128)), writes=["gn"])
    tri, chm, midm, miu = cview(k, "tri", False), cview(k, "ch", False), cview(k, "midm", False), cview(k, "miu", False)
    Sf = fa.alloc(256)
    V_(lambda e: e.memset(Sf, 0.0), [], ["Sf"])
    Sb = [ba.alloc(256) for _ in range(3)]
    G_(lambda e: e.memset(Sb[0], 0.0), [], ["Sb0"])
    QpTm = [ba.alloc(2048) for _ in range(2)]
    QTm = [ba.alloc(1024) for _ in range(2)]
    Kh = [ba.alloc(2048) for _ in range(2)]
    for b in range(2):
        G_(lambda e: e.memset(QpTm[b], 0.0), [], ["QpTm%d" % b])
        G_(lambda e: e.memset(QTm[b], 0.0), [], ["QTm%d" % b])
        G_(lambda e: e.memset(Kh[b], 0.0), [], ["Kh%d" % b])
    hTt = [v3(ba.alloc(1024), 8) for _ in range(2)]
    sgm, sgn, fg, key, logf, qh, eb = [fa.alloc(512) for _ in range(7)]
    bm, be, d1, e1 = [fa.alloc(512) for _ in range(4)]
    Qp, Qt, Kt, Vb = [ba.alloc(512) for _ in range(4)]
    KT = ba.alloc(512)
    attm = ba.alloc(1024)
    gS = fa.alloc(8)
    sq = fa.alloc(512)
    ss8 = fa.alloc(8)
    rs8 = fa.alloc(8)
    o32 = fa.alloc(512)
    gate = fa.alloc(512)
    ygb = ba.alloc(512)
    k.ystage = [ba.alloc(512) for _ in range(2)]
    sbi = 0

    def loadc(n_):
        P.dma("sync", lambda e: e.dma_start(out=hTt[n_ % 2], in_=k.hT_d[:, :, 16 + n_ * 128:16 + (n_ + 1) * 128]), writes=["hTt%d" % (n_ % 2)])

    def projc(n_):
        for blk in range(4):
            proj_block(k, pf[blk], "pf%d" % blk, hTt[n_ % 2], "hTt%d" % (n_ % 2), W, "WC", blk * 512, 512)

    for n in range(NT):
        b = n % 2
        H = "hTt%d" % b
        if n == 0:
            loadc(0)
            if NT > 1:
                loadc(1)
            projc(0)
        S_(lambda e: e.activation(out=sgm, in_=pf[1], func=AF.Sigmoid), ["pf1"], ["sgm"])
        S_(lambda e: e.activation(out=sgn, in_=pf[1], func=AF.Sigmoid, scale=-1.0), ["pf1"], ["sgn"])
        V_(lambda e: e.tensor_tensor(out=fg, in0=sgm, in1=oml, op=ALU.mult), ["sgm", "oml"], ["fg"])
        V_(lambda e: e.tensor_tensor(out=fg, in0=fg, in1=lbb, op=ALU.add), ["fg", "lbb"], ["fg"])
        G_(lambda e: e.tensor_tensor(out=key, in0=sgn, in1=oml, op=ALU.mult), ["sgn", "oml"], ["key"])
        S_(lambda e: e.activation(out=logf, in_=fg, func=AF.Ln), ["fg"], ["logf"])
        for bank, m in ((4, tri), (5, midm), (6, chm)):
            T_(lambda e: e.matmul(pf[bank], lhsT=m, rhs=logf, start=True, stop=True), ["logf", "cm_f"], ["pf%d" % bank])
        S_(lambda e: e.copy(out=Vb, in_=pf[2]), ["pf2"], ["Vb"])
        S_(lambda e: e.activation(out=qh, in_=pf[0], func=AF.Silu), ["pf0"], ["qh"])
        S_(lambda e: e.activation(out=gate, in_=pf[3], func=AF.Silu), ["pf3"], ["gate"])
        S_(lambda e: e.activation(out=eb, in_=pf[4], func=AF.Exp), ["pf4"], ["eb"])
        S_(lambda e: e.copy(out=bm, in_=pf[5]), ["pf5"], ["bm"])
        S_(lambda e: e.copy(out=be, in_=pf[6]), ["pf6"], ["be"])
        for p in range(4):
            T_(lambda e: e.matmul(pf[5][:, 256 + p * 2:256 + p * 2 + 2], lhsT=logf[:, p * 128:(p + 1) * 128], rhs=k.chs, start=True, stop=True),
               ["logf", "chs"], ["pf5"])
        S_(lambda e: e.activation(out=gS, in_=pf[5][:, 256:264], func=AF.Exp), ["pf5"], ["gS"])
        V_(lambda e: e.tensor_tensor(out=Qp, in0=qh, in1=eb, op=ALU.mult), ["qh", "eb"], ["Qp"])
        V_(lambda e: e.tensor_tensor(out=d1, in0=pf[4], in1=bm, op=ALU.subtract), ["pf4", "bm"], ["d1"])
        S_(lambda e: e.activation(out=e1, in_=d1, func=AF.Exp), ["d1"], ["e1"])
        V_(lambda e: e.tensor_tensor(out=Qt, in0=qh, in1=e1, op=ALU.mult), ["qh", "e1"], ["Qt"])
        S_(lambda e: e.activation(out=e1, in_=d1, func=AF.Exp, scale=-1.0), ["d1", "Qt"], ["e1"])
        V_(lambda e: e.tensor_tensor(out=Kt, in0=key, in1=e1, op=ALU.mult), ["key", "e1"], ["Kt"])
        V_(lambda e: e.tensor_tensor(out=d1, in0=be, in1=pf[4], op=ALU.subtract), ["pf4", "be", "e1"], ["d1"])
        S_(lambda e: e.activation(out=e1, in_=d1, func=AF.Exp), ["d1", "Kt"], ["e1"])
        KH = "Kh%d" % b
        kh5 = Kh[b].rearrange("q (c u p d) -> q c u p d", c=2, u=2, p=4)
        for c in range(2):
            for u in range(2):
                rows = slice(c * 64, (c + 1) * 64)
                V_(lambda e: e.tensor_tensor(out=kh5[rows, c, u, :, u * 64:(u + 1) * 64] if False else
                                             Kh[b].rearrange("q (c u p w) -> q c u p w", c=2, u=2, p=4)[rows, c, u, :, u * 64:(u + 1) * 64],
                                             in0=v3(key, 4)[rows, :, u * 64:(u + 1) * 64], in1=v3(e1, 4)[rows, :, u * 64:(u + 1) * 64], op=ALU.mult),
                   ["key", "e1"], [KH])
        p7 = pbf(k, 7)
        p5 = pbf(k, 5)
        for c in range(4):
            T_(lambda e: e.transpose(out=p7[:, c * 128:(c + 1) * 128], in_=Qp[:, c * 128:(c + 1) * 128], identity=idb), ["Qp", "cm_b"], ["pf7"])
        for c in range(4):
            T_(lambda e: e.transpose(out=p7[:, 512 + c * 128:512 + (c + 1) * 128], in_=Qt[:, c * 128:(c + 1) * 128], identity=idb), ["Qt", "cm_b"], ["pf7"])
        for c in range(4):
            T_(lambda e: e.transpose(out=p5[:, c * 128:(c + 1) * 128], in_=Kt[:, c * 128:(c + 1) * 128], identity=idb), ["Kt", "cm_b"], ["pf5"])
        QP, QT = "QpTm%d" % b, "QTm%d" % b
        qp6 = QpTm[b].rearrange("q (p u c t) -> q p u c t", p=4, u=2, c=2)
        qt4 = QTm[b].rearrange("q (p u t) -> q p u t", p=4, u=2)
        for u in range(2):
            rows = slice(u * 64, (u + 1) * 64)
            for c in range(2):
                S_(lambda e: e.copy(out=qp6[rows, :, u, c, c * 64:(c + 1) * 64], in_=v3(p7[rows, 0:512], 4)[:, :, c * 64:(c + 1) * 64]), ["pf7"], [QP])
            S_(lambda e: e.copy(out=qt4[rows, :, u, :], in_=v3(p7[rows, 512:1024], 4)), ["pf7"], [QT])
        S_(lambda e: e.copy(out=KT, in_=p5[:, 0:512]), ["pf5"], ["KT"])
        for h in range(8):
            bank = 4 if h < 4 else 6
            T_(lambda e: e.matmul(pf[bank][:, (h % 4) * 128:(h % 4 + 1) * 128], lhsT=KT[:, (h // 2) * 128:(h // 2 + 1) * 128],
                                  rhs=QTm[b][:, h * 128:(h + 1) * 128], start=True, stop=True), ["KT", QT], ["pf%d" % bank])
        for hb_ in range(2):
            bank = 4 if hb_ == 0 else 6
            V_(lambda e: e.tensor_tensor(out=v3(attm[:, hb_ * 512:(hb_ + 1) * 512], 4), in0=v3(pf[bank], 4),
                                         in1=miu.unsqueeze(1).to_broadcast([128, 4, 128]), op=ALU.mult), ["pf%d" % bank, "cm_f"], ["attm"])
        for c in range(2):
            for p in range(4):
                for u in range(2):
                    h = 2 * p + u
                    T_(lambda e: e.matmul(pf[5][:, c * 256 + p * 64:c * 256 + (p + 1) * 64],
                                          lhsT=Kh[b][:, (c * 2 + u) * 512 + p * 128:(c * 2 + u) * 512 + (p + 1) * 128],
                                          rhs=Vb[:, h * 64:(h + 1) * 64], start=(u == 0), stop=(u == 1)), [KH, "Vb"], ["pf5"])
        sb_in = sbi
        for c in range(2):
            V_(lambda e: e.tensor_tensor(out=v3(Sf, 4), in0=v3(Sf, 4), in1=gS[:, c:8:2].unsqueeze(2).to_broadcast([128, 4, 64]), op=ALU.mult),
               ["Sf", "gS"], ["Sf"])
            V_(lambda e: e.tensor_tensor(out=Sf, in0=Sf, in1=pf[5][:, c * 256:(c + 1) * 256], op=ALU.add), ["Sf", "pf5"], ["Sf"])
            sbi = (sbi + 1) % 3
            S_(lambda e: e.copy(out=Sb[sbi], in_=Sf), ["Sf"], ["Sb%d" % sbi])
        sb0, sb1 = sb_in, (sb_in + 1) % 3
        for h in range(8):
            p = h // 2
            for c, sbx in ((0, sb0), (1, sb1)):
                T_(lambda e: e.matmul(pf[7][:, h * 64:(h + 1) * 64], lhsT=QpTm[b][:, ((p * 2 + h % 2) * 2 + c) * 128:((p * 2 + h % 2) * 2 + c + 1) * 128],
                                      rhs=Sb[sbx][:, p * 64:(p + 1) * 64], start=(c == 0), stop=False), [QP, "Sb%d" % sbx], ["pf7"])
            T_(lambda e: e.matmul(pf[7][:, h * 64:(h + 1) * 64], lhsT=attm[:, h * 128:(h + 1) * 128], rhs=Vb[:, h * 64:(h + 1) * 64],
                                  start=False, stop=True), ["attm", "Vb"], ["pf7"])
        if n + 1 < NT:
            projc(n + 1)
        if n + 2 < NT:
            loadc(n + 2)
        S_(lambda e: e.activation(out=sq, in_=pf[7], func=AF.Square), ["pf7"], ["sq"])
        V_(lambda e: e.tensor_reduce(out=ss8, in_=v3(sq, 8), op=ALU.add, axis=AX.X), ["sq"], ["ss8"])
        S_(lambda e: e.activation(out=rs8, in_=ss8, func=AF.Sqrt, scale=1.0 / 64, bias=1e-6), ["ss8"], ["rs8"])
        V_(lambda e: e.reciprocal(out=rs8, in_=rs8), ["rs8"], ["rs8"])
        V_(lambda e: e.tensor_tensor(out=v3(o32, 8), in0=v3(pf[7], 8), in1=rs8.unsqueeze(2).to_broadcast([128, 8, 64]), op=ALU.mult), ["pf7", "rs8"], ["o32"])
        V_(lambda e: e.tensor_tensor(out=v3(o32, 8), in0=v3(o32, 8), in1=gn.unsqueeze(1).to_broadcast([128, 8, 64]), op=ALU.mult), ["o32", "gn"], ["o32"])
        if k.debug:
            P.dma("sync", lambda e: e.dma_start(out=k.dbg["c"][n * 128:(n + 1) * 128, :], in_=o32), reads=["o32"])
        V_(lambda e: e.tensor_tensor(out=ygb, in0=o32, in1=gate, op=ALU.mult), ["o32", "gate"], ["ygb"])
        store_yT(k, "c", n, "ygb", ygb, 7, l)


def phaseA(k, l):
    P, nc = k.P, k.nc
    new_phase(k)
    fa, ba = k.fa, k.ba
    NT = k.NT
    pf = k.pf
    idb = cview(k, "ident")
    C0 = float(np.exp(-0.5))
    V_ = lambda fn, r, w: P.op("vector", fn, r, w)
    S_ = lambda fn, r, w: P.op("scalar", fn, r, w)
    G_ = lambda fn, r, w: P.op("gpsimd", fn, r, w)
    T_ = lambda fn, r, w: P.op("tensor", fn, r, w)
    bc = lambda src: src.partition_broadcast(128)
    W1 = v3(ba.alloc(8 * A_COLS), 8)
    W2 = v3(ba.alloc(8 * A_COLS), 8)
    w2a2 = ba.alloc(1024)
    vecs = {}
    for nm, src in (("w0b", k.rwkv_w0), ("a0b", k.rwkv_a0), ("kkb", k.rwkv_k_k), ("kab", k.rwkv_k_a), ("rkb", k.rwkv_r_k),
                    ("lnw", k.rwkv_ln_w), ("lnb", k.rwkv_ln_b)):
        vecs[nm] = fa.alloc(512)
        P.dma("sync", lambda e: e.dma_start(out=vecs[nm], in_=bc(src[l:l + 1, :])), writes=[nm])
    G_(lambda e: e.memset(w2a2, 0.0), [], ["w2a2"])
    P.dma("gpsimd", lambda e: e.dma_start(out=w2a2[0:64, 0:512], in_=k.rwkv_w2[l, :, :]), writes=["w2a2"])
    P.dma("gpsimd", lambda e: e.dma_start(out=w2a2[64:128, 512:1024], in_=k.rwkv_a2[l, :, :]), writes=["w2a2"])
    keep = k.fa.a.off
    mu_b = fa.alloc(A_COLS)
    omu = fa.alloc(A_COLS)
    stage = fa.alloc(A_COLS)
    P.dma("sync", lambda e: e.dma_start(out=mu_b, in_=bc(k.rwkv_mu[l:l + 1, :])), writes=["mu_b"])
    V_(lambda e: e.tensor_scalar(out=omu, in0=mu_b, scalar1=-1.0, scalar2=1.0, op0=ALU.mult, op1=ALU.add), ["mu_b"], ["omu"])
    for c in range(8):
        P.dma("sync", lambda e: e.dma_start(out=stage, in_=k.w_in[l, c * 128:(c + 1) * 128, 0:A_COLS]), writes=["stage"])
        V_(lambda e: e.tensor_tensor(out=W1[:, c, :], in0=stage, in1=omu, op=ALU.mult), ["stage", "omu"], ["W1"])
        G_(lambda e: e.tensor_tensor(out=W2[:, c, :], in0=stage, in1=mu_b, op=ALU.mult), ["stage", "mu_b"], ["W2"])
    P.barrier()
    k.fa.a.off = keep
    tri, chm = cview(k, "tri", False), cview(k, "ch", False)
    msu, msl, miu = cview(k, "msu", False), cview(k, "msl", False), cview(k, "miu", False)
    idf = cview(k, "ident", False)
    STf = fa.alloc(256)
    V_(lambda e: e.memset(STf, 0.0), [], ["STf"])
    STb = [ba.alloc(256) for _ in range(3)]
    G_(lambda e: e.memset(STb[0], 0.0), [], ["STb0"])
    Uall = ba.alloc(512)
    G_(lambda e: e.memset(Uall, 0.0), [], ["Uall"])
    msk = {}
    for nm, sz in (("ATm", 1024), ("BTm", 1024), ("KTm", 1024), ("RTc", 2048), ("M1m", 2048), ("Atm", 1024), ("Bh", 2048), ("Kh", 2048)):
        msk[nm] = ba.alloc(sz)
        G_(lambda e: e.memset(msk[nm], 0.0), [], [nm])
    hTc = v3(ba.alloc(1024), 8)
    hTp = v3(ba.alloc(1024), 8)
    F = [fa.alloc(512) for _ in range(11)]
    FK = ["F%d" % i for i in range(11)]
    lor = ba.alloc(128)
    lorT = ba.alloc(128)
    Rt, At, Bt, Kt, Vb = [ba.alloc(512) for _ in range(5)]
    PA, QA, PB, QB, XI, Aak, ArbT, ArkT = [ba.alloc(1024) for _ in range(8)]
    gS, ss8, rk8, m8, q8, r8 = [fa.alloc(8) for _ in range(6)]
    k.ystage = [ba.alloc(512)] * 2
    cols = {"r": 0, "wl": 512, "k": 576, "v": 1088, "al": 1600, "g": 1664}
    h3 = lambda ap: v3(ap, 8)
    b864 = lambda ap: ap.unsqueeze(2).to_broadcast([128, 8, 64])
    sbi = 0

    def evac_masked(dst, dkey, src_bf, skey, chunked):
        if chunked:
            d6 = dst.rearrange("q (p u c t) -> q p u c t", p=4, u=2, c=2)
        else:
            d4 = dst.rearrange("q (p u t) -> q p u t", p=4, u=2)
        for u in range(2):
            rows = slice(u * 64, (u + 1) * 64)
            if chunked:
                for c in range(2):
                    S_(lambda e: e.copy(out=d6[rows, :, u, c, c * 64:(c + 1) * 64], in_=v3(src_bf[rows, :], 4)[:, :, c * 64:(c + 1) * 64]), [skey], [dkey])
            else:
                S_(lambda e: e.copy(out=d4[rows, :, u, :], in_=v3(src_bf[rows, :], 4)), [skey], [dkey + "0", dkey + "1"])

    def headmm(bankA, bankB, lhs, lkey, rhs, rkey, halves=(0, 1)):
        for h in range(8):
            if h // 4 not in halves:
                continue
            bank = bankA if h < 4 else bankB
            hk = str(h // 4)
            T_(lambda e: e.matmul(pf[bank][:, (h % 4) * 128:(h % 4 + 1) * 128], lhsT=lhs[:, h * 128:(h + 1) * 128], rhs=rhs[:, h * 128:(h + 1) * 128],
                                  start=True, stop=True), [lkey + hk, rkey + hk], ["pf%d" % bank])

    def headmm_rc(bankA, bankB, lhs, lkey):
        for h in range(8):
            bank = bankA if h < 4 else bankB
            for c in range(2):
                o0 = (h % 4) * 128 + c * 64
                r0 = (h * 2 + c) * 128 + c * 64
                T_(lambda e: e.matmul(pf[bank][:, o0:o0 + 64], lhsT=lhs[:, h * 128:(h + 1) * 128], rhs=msk["RTc"][:, r0:r0 + 64],
                                      start=True, stop=True), [lkey + str(h // 4), "RTc"], ["pf%d" % bank])

    def evac_mask(dst, dkey, bankA, bankB, mask, halves=(0, 1), engs=("scalar", "vector")):
        for i, bank in enumerate((bankA, bankB)):
            if i not in halves:
                continue
            if mask is None:
                if engs[i] == "scalar":
                    S_(lambda e: e.copy(out=dst[:, i * 512:(i + 1) * 512], in_=pf[bank]), ["pf%d" % bank], [dkey + str(i)])
                else:
                    V_(lambda e: e.tensor_copy(out=dst[:, i * 512:(i + 1) * 512], in_=pf[bank]), ["pf%d" % bank], [dkey + str(i)])
            else:
                V_(lambda e: e.tensor_tensor(out=v3(dst[:, i * 512:(i + 1) * 512], 4), in0=v3(pf[bank], 4), in1=mask.unsqueeze(1).to_broadcast([128, 4, 128]),
                                             op=ALU.mult), ["pf%d" % bank, "cm_f"], [dkey + str(i)])

    def load_h(n):
        P.dma("sync", lambda e: e.dma_start(out=hTc, in_=k.hT_d[:, :, 16 + n * 128:16 + (n + 1) * 128]), writes=["hTc"])
        P.dma("sync", lambda e: e.dma_start(out=hTp, in_=k.hT_d[:, :, 15 + n * 128:15 + (n + 1) * 128]), writes=["hTp"])

    def proj(out_ap, okey, c0, ncols):
        for c in range(8):
            T_(lambda e: e.matmul(out_ap, lhsT=hTc[:, c, :], rhs=W1[:, c, c0:c0 + ncols], start=(c == 0), stop=False), ["hTc", "W1"], [okey])
            T_(lambda e: e.matmul(out_ap, lhsT=hTp[:, c, :], rhs=W2[:, c, c0:c0 + ncols], start=False, stop=(c == 7)), ["hTp", "W2"], [okey])

    def proj_tile():
        proj(pf[0], "pf0", cols["r"], 512)
        proj(pf[1], "pf1", cols["k"], 512)
        proj(pf[2], "pf2", cols["v"], 512)
        proj(pf[3], "pf3", cols["g"], 512)
        proj(pf[7][:, 256:320], "pf7", cols["wl"], 64)
        proj(pf[7][:, 320:384], "pf7", cols["al"], 64)

    load_h(0)
    proj_tile()
    for n in range(NT):
        if n + 1 < NT:
            load_h(n + 1)
        p5b, p6b = pbf(k, 5), pbf(k, 6)
        S_(lambda e: e.activation(out=lor[:, 0:64], in_=pf[7][:, 256:320], func=AF.Tanh), ["pf7"], ["lor"])
        S_(lambda e: e.copy(out=lor[:, 64:128], in_=pf[7][:, 320:384]), ["pf7"], ["lor"])
        T_(lambda e: e.transpose(out=p5b[:, 0:128], in_=lor, identity=idb), ["lor", "cm_b"], ["pf5"])
        S_(lambda e: e.copy(out=lorT, in_=p5b[:, 0:128]), ["pf5"], ["lorT"])
        T_(lambda e: e.matmul(pf[5], lhsT=lorT, rhs=w2a2[:, 0:512], start=True, stop=True), ["lorT", "w2a2"], ["pf5"])
        T_(lambda e: e.matmul(pf[6], lhsT=lorT, rhs=w2a2[:, 512:1024], start=True, stop=True), ["lorT", "w2a2"], ["pf6"])
        V_(lambda e: e.tensor_tensor(out=F[7], in0=pf[1], in1=vecs["kkb"], op=ALU.mult), ["pf1", "kkb"], ["F7"])
        S_(lambda e: e.activation(out=F[9], in_=F[7], func=AF.Square), ["F7"], ["F9"])
        V_(lambda e: e.tensor_reduce(out=ss8, in_=h3(F[9]), op=ALU.add, axis=AX.X), ["F9"], ["ss8"])
        S_(lambda e: e.activation(out=ss8, in_=ss8, func=AF.Sqrt), ["ss8"], ["ss8"])
        V_(lambda e: e.tensor_scalar(out=ss8, in0=ss8, scalar1=1e-12, scalar2=0.0, op0=ALU.max, op1=ALU.add), ["ss8"], ["ss8"])
        V_(lambda e: e.reciprocal(out=ss8, in_=ss8), ["ss8"], ["ss8"])
        V_(lambda e: e.tensor_tensor(out=h3(F[7]), in0=h3(F[7]), in1=b864(ss8), op=ALU.mult), ["F7", "ss8"], ["F7"])
        V_(lambda e: e.tensor_tensor(out=F[1], in0=pf[5], in1=vecs["w0b"], op=ALU.add), ["pf5", "w0b"], ["F1"])
        S_(lambda e: e.activation(out=F[1], in_=F[1], func=AF.Sigmoid), ["F1"], ["F1"])
        V_(lambda e: e.tensor_tensor(out=F[4], in0=pf[6], in1=vecs["a0b"], op=ALU.add), ["pf6", "a0b"], ["F4"])
        S_(lambda e: e.activation(out=F[2], in_=F[4], func=AF.Sigmoid), ["F4"], ["F2"])
        T_(lambda e: e.matmul(pf[5], lhsT=tri, rhs=F[1], start=True, stop=True), ["F1", "cm_f"], ["pf5"])
        T_(lambda e: e.matmul(pf[6], lhsT=chm, rhs=F[1], start=True, stop=True), ["F1", "cm_f"], ["pf6"])
        for p in range(4):
            T_(lambda e: e.matmul(pf[7][:, p * 2:p * 2 + 2], lhsT=F[1][:, p * 128:(p + 1) * 128], rhs=k.chs, start=True, stop=True), ["F1", "chs"], ["pf7"])
        S_(lambda e: e.copy(out=F[0], in_=pf[2]), ["pf2"], ["F0"])
        S_(lambda e: e.copy(out=Vb, in_=pf[2]), ["pf2"], ["Vb"])
        S_(lambda e: e.activation(out=F[10], in_=pf[3], func=AF.Silu), ["pf3"], ["F10"])
        V_(lambda e: e.scalar_tensor_tensor(out=F[9], in0=F[2], scalar=-1.0, in1=vecs["kab"], op0=ALU.add, op1=ALU.mult), ["F2", "kab"], ["F9"])
        V_(lambda e: e.scalar_tensor_tensor(out=F[8], in0=F[9], scalar=1.0, in1=pf[1], op0=ALU.add, op1=ALU.mult), ["F9", "pf1"], ["F8"])
        V_(lambda e: e.tensor_tensor(out=F[9], in0=F[7], in1=F[2], op=ALU.mult), ["F7", "F2", "F8"], ["F9"])
        V_(lambda e: e.tensor_tensor(out=F[5], in0=pf[0], in1=F[8], op=ALU.mult), ["pf0", "F8"], ["F5"])
        V_(lambda e: e.tensor_tensor(out=F[5], in0=F[5], in1=vecs["rkb"], op=ALU.mult), ["F5", "rkb"], ["F5"])
        V_(lambda e: e.tensor_reduce(out=rk8, in_=h3(F[5]), op=ALU.add, axis=AX.X), ["F5"], ["rk8"])
        S_(lambda e: e.activation(out=gS, in_=pf[7][:, 0:8], func=AF.Exp, scale=-C0), ["pf7"], ["gS"])
        S_(lambda e: e.copy(out=F[3], in_=pf[5]), ["pf5"], ["F3"])
        V_(lambda e: e.tensor_tensor(out=F[4], in0=pf[5], in1=F[1], op=ALU.subtract), ["pf5", "F1"], ["F4"])
        S_(lambda e: e.activation(out=F[5], in_=F[3], func=AF.Exp, scale=-C0), ["F3", "rk8"], ["F5"])
        V_(lambda e: e.tensor_tensor(out=Rt, in0=pf[0], in1=F[5], op=ALU.mult), ["pf0", "F5"], ["Rt"])
        S_(lambda e: e.activation(out=F[6], in_=F[4], func=AF.Exp, scale=-C0), ["F4"], ["F6"])
        V_(lambda e: e.scalar_tensor_tensor(out=At, in0=F[7], scalar=-1.0, in1=F[6], op0=ALU.mult, op1=ALU.mult), ["F7", "F6"], ["At"])
        atm4 = msk["Atm"].rearrange("q (u p w) -> q u p w", u=2, p=4)
        for u in range(2):
            G_(lambda e: e.tensor_copy(out=atm4[:, u, :, u * 64:(u + 1) * 64], in_=v3(At, 4)[:, :, u * 64:(u + 1) * 64]), ["At"], ["Atm"])
        V_(lambda e: e.tensor_tensor(out=F[4], in0=pf[6], in1=F[3], op=ALU.subtract), ["pf6", "F3", "F6"], ["F4"])
        S_(lambda e: e.activation(out=F[5], in_=F[3], func=AF.Exp, scale=C0), ["F3", "Rt"], ["F5"])
        S_(lambda e: e.activation(out=F[6], in_=F[4], func=AF.Exp, scale=-C0), ["F4", "At"], ["F6"])
        G_(lambda e: e.tensor_tensor(out=Bt, in0=F[9], in1=F[5], op=ALU.mult), ["F9", "F5"], ["Bt"])
        V_(lambda e: e.tensor_tensor(out=Kt, in0=F[8], in1=F[5], op=ALU.mult), ["F8", "F5"], ["Kt"])
        for nm, src, skey in (("Bh", F[9], "F9"), ("Kh", F[8], "F8")):
            d5 = msk[nm].rearrange("q (c u p w) -> q c u p w", c=2, u=2, p=4)
            for c in range(2):
                for u in range(2):
                    rows = slice(c * 64, (c + 1) * 64)
                    eng = V_
                    eng(lambda e: e.tensor_tensor(out=d5[rows, c, u, :, u * 64:(u + 1) * 64], in0=v3(src, 4)[rows, :, u * 64:(u + 1) * 64],
                                                  in1=v3(F[6], 4)[rows, :, u * 64:(u + 1) * 64], op=ALU.mult), [skey, "F6"], [nm])
        for j, (src, skey) in enumerate(((Rt, "Rt"), (At, "At"))):
            for c in range(4):
                T_(lambda e: e.transpose(out=p5b[:, j * 512 + c * 128:j * 512 + (c + 1) * 128], in_=src[:, c * 128:(c + 1) * 128], identity=idb),
                   [skey, "cm_b"], ["pf5"])
        for j, (src, skey) in enumerate(((Bt, "Bt"), (Kt, "Kt"))):
            for c in range(4):
                T_(lambda e: e.transpose(out=p6b[:, j * 512 + c * 128:j * 512 + (c + 1) * 128], in_=src[:, c * 128:(c + 1) * 128], identity=idb),
                   [skey, "cm_b"], ["pf6"])
        evac_masked(msk["RTc"], "RTc", p5b[:, 0:512], "pf5", True)
        evac_masked(msk["ATm"], "ATm", p5b[:, 512:1024], "pf5", False)
        evac_masked(msk["BTm"], "BTm", p6b[:, 0:512], "pf6", False)
        evac_masked(msk["KTm"], "KTm", p6b[:, 512:1024], "pf6", False)
        headmm(0, 1, msk["BTm"], "BTm", msk["ATm"], "ATm"); evac_mask(PA, "PA", 0, 1, msu)
        headmm(2, 3, msk["ATm"], "ATm", msk["BTm"], "BTm"); evac_mask(QA, "QA", 2, 3, msl)
        headmm(4, 7, msk["ATm"], "ATm", msk["KTm"], "KTm"); evac_mask(Aak, "Aak", 4, 7, msl)
        headmm_rc(5, 6, msk["BTm"], "BTm"); evac_mask(ArbT, "ArbT", 5, 6, miu)
        headmm_rc(0, 1, msk["KTm"], "KTm"); evac_mask(ArkT, "ArkT", 0, 1, miu)
        for i in range(2):
            V_(lambda e: e.tensor_tensor(out=v3(XI[:, i * 512:(i + 1) * 512], 4), in0=v3(PA[:, i * 512:(i + 1) * 512], 4),
                                         in1=idf.unsqueeze(1).to_broadcast([128, 4, 128]), op=ALU.add), ["PA%d" % i, "cm_f"], ["XI%d" % i])
        cur, nxt = (PA, QA, "PA", "QA"), (PB, QB, "PB", "QB")
        for lev in range(5):
            Pc, Qc, Pk, Qk = cur
            Pn, Qn, Pnk, Qnk = nxt
            for g in range(2):
                if lev < 4:
                    headmm(2, 3, Qc, Qk, Pc, Pk, halves=(g,))
                headmm(4, 7, Pc, Pk, Qc, Qk, halves=(g,))
            for g in range(2):
                evac_mask(Qn, Qnk, 4, 7, None, halves=(g,), engs=("vector", "vector"))
                if lev < 4:
                    evac_mask(Pn, Pnk, 2, 3, None, halves=(g,), engs=("scalar", "scalar"))
            for g in range(2):
                for h in range(4 * g, 4 * g + 4):
                    bank = 5 if h < 4 else 6
                    o = pf[bank][:, (h % 4) * 128:(h % 4 + 1) * 128]
                    T_(lambda e: e.matmul(o, lhsT=idb, rhs=XI[:, h * 128:(h + 1) * 128], start=True, stop=False), ["XI%d" % g, "cm_b"], ["pf%d" % bank])
                    T_(lambda e: e.matmul(o, lhsT=Qn[:, h * 128:(h + 1) * 128], rhs=XI[:, h * 128:(h + 1) * 128], start=False, stop=True),
                       ["XI%d" % g, Qnk + str(g)], ["pf%d" % bank])
            evac_mask(XI, "XI", 5, 6, None, engs=("scalar", "vector"))
            cur, nxt = nxt, cur
        for p in range(4):
            for u in range(2):
                h = 2 * p + u
                T_(lambda e: e.matmul(pf[2][:, p * 128:(p + 1) * 128], lhsT=msk["Atm"][:, u * 512 + p * 128:u * 512 + (p + 1) * 128],
                                      rhs=XI[:, h * 128:(h + 1) * 128], start=(u == 0), stop=(u == 1)), ["Atm", "XI%d" % (h // 4)], ["pf2"])
        m6 = msk["M1m"].rearrange("q (p u c t) -> q p u c t", p=4, u=2, c=2)
        for u in range(2):
            rows = slice(u * 64, (u + 1) * 64)
            for c in range(2):
                S_(lambda e: e.copy(out=m6[rows, :, u, c, c * 64:(c + 1) * 64], in_=v3(pf[2][rows, :], 4)[:, :, c * 64:(c + 1) * 64]), ["pf2"], ["M1m"])
        M2 = PA
        headmm(3, 4, Aak, "Aak", XI, "XI"); evac_mask(M2, "PA", 3, 4, None)
        sb_in = sbi
        for c in range(2):
            ub = 5 if c == 0 else 6
            UB = "pf%d" % ub
            sbc = (sb_in + c) % 3
            for h in range(8):
                p = h // 2
                o = pf[ub][:, h * 64:(h + 1) * 64]
                T_(lambda e: e.matmul(o, lhsT=msk["M1m"][:, (h * 2 + c) * 128:(h * 2 + c + 1) * 128], rhs=STb[sbc][:, p * 64:(p + 1) * 64],
                                      start=True, stop=False), ["M1m", "STb%d" % sbc], [UB])
                T_(lambda e: e.matmul(o, lhsT=M2[:, h * 128:(h + 1) * 128], rhs=Vb[:, h * 64:(h + 1) * 64], start=False, stop=True), ["PA%d" % (h // 4), "Vb"], [UB])
            rows = slice(c * 64, (c + 1) * 64)
            S_(lambda e: e.copy(out=Uall[rows, :], in_=pf[ub][rows, :]), [UB], ["Uall"])
            for p in range(4):
                for u in range(2):
                    h = 2 * p + u
                    o = pf[7][:, p * 64:(p + 1) * 64]
                    T_(lambda e: e.matmul(o, lhsT=msk["Bh"][:, (c * 2 + u) * 512 + p * 128:(c * 2 + u) * 512 + (p + 1) * 128],
                                          rhs=Uall[:, h * 64:(h + 1) * 64], start=(u == 0), stop=False), ["Bh", "Uall"], ["pf7"])
                    T_(lambda e: e.matmul(o, lhsT=msk["Kh"][:, (c * 2 + u) * 512 + p * 128:(c * 2 + u) * 512 + (p + 1) * 128],
                                          rhs=Vb[:, h * 64:(h + 1) * 64], start=False, stop=(u == 1)), ["Kh", "Vb"], ["pf7"])
            V_(lambda e: e.tensor_tensor(out=v3(STf, 4), in0=v3(STf, 4), in1=gS[:, c:8:2].unsqueeze(2).to_broadcast([128, 4, 64]), op=ALU.mult),
               ["STf", "gS"], ["STf"])
            V_(lambda e: e.tensor_tensor(out=STf, in0=STf, in1=pf[7][:, 0:256], op=ALU.add), ["STf", "pf7"], ["STf"])
            sbi = (sbi + 1) % 3
            S_(lambda e: e.copy(out=STb[sbi], in_=STf), ["STf"], ["STb%d" % sbi])
        sb0, sb1 = sb_in, (sb_in + 1) % 3
        for h in range(8):
            p = h // 2
            o = pf[4][:, h * 64:(h + 1) * 64]
            for c, sbx in ((0, sb0), (1, sb1)):
                T_(lambda e: e.matmul(o, lhsT=msk["RTc"][:, (h * 2 + c) * 128:(h * 2 + c + 1) * 128], rhs=STb[sbx][:, p * 64:(p + 1) * 64],
                                      start=(c == 0), stop=False), ["RTc", "STb%d" % sbx], ["pf4"])
            T_(lambda e: e.matmul(o, lhsT=ArbT[:, h * 128:(h + 1) * 128], rhs=Uall[:, h * 64:(h + 1) * 64], start=False, stop=False), ["ArbT%d" % (h // 4), "Uall"], ["pf4"])
            T_(lambda e: e.matmul(o, lhsT=ArkT[:, h * 128:(h + 1) * 128], rhs=Vb[:, h * 64:(h + 1) * 64], start=False, stop=True), ["ArkT%d" % (h // 4), "Vb"], ["pf4"])
        if n + 1 < NT:
            proj_tile()
        S_(lambda e: e.copy(out=F[4], in_=pf[4]), ["pf4"], ["F4"])
        V_(lambda e: e.tensor_reduce(out=m8, in_=h3(F[4]), op=ALU.add, axis=AX.X), ["F4"], ["m8"])
        S_(lambda e: e.activation(out=F[9], in_=F[4], func=AF.Square), ["F4"], ["F9"])
        V_(lambda e: e.tensor_reduce(out=q8, in_=h3(F[9]), op=ALU.add, axis=AX.X), ["F9"], ["q8"])
        V_(lambda e: e.tensor_scalar(out=m8, in0=m8, scalar1=1.0 / 64, scalar2=0.0, op0=ALU.mult, op1=ALU.add), ["m8"], ["m8"])
        V_(lambda e: e.tensor_tensor(out=r8, in0=m8, in1=m8, op=ALU.mult), ["m8"], ["r8"])
        V_(lambda e: e.scalar_tensor_tensor(out=q8, in0=q8, scalar=1.0 / 64, in1=r8, op0=ALU.mult, op1=ALU.subtract), ["q8", "r8"], ["q8"])
        S_(lambda e: e.activation(out=r8, in_=q8, func=AF.Sqrt, bias=64e-5), ["q8"], ["r8"])
        V_(lambda e: e.reciprocal(out=r8, in_=r8), ["r8"], ["r8"])
        V_(lambda e: e.tensor_tensor(out=h3(F[4]), in0=h3(F[4]), in1=b864(m8), op=ALU.subtract), ["F4", "m8"], ["F4"])
        V_(lambda e: e.tensor_tensor(out=h3(F[4]), in0=h3(F[4]), in1=b864(r8), op=ALU.mult), ["F4", "r8"], ["F4"])
        V_(lambda e: e.tensor_tensor(out=F[4], in0=F[4], in1=vecs["lnw"], op=ALU.mult), ["F4", "lnw"], ["F4"])
        V_(lambda e: e.tensor_tensor(out=F[4], in0=F[4], in1=vecs["lnb"], op=ALU.add), ["F4", "lnb"], ["F4"])
        V_(lambda e: e.tensor_tensor(out=h3(F[9]), in0=h3(F[0]), in1=b864(rk8), op=ALU.mult), ["F0", "rk8"], ["F9"])
        V_(lambda e: e.tensor_tensor(out=F[4], in0=F[4], in1=F[9], op=ALU.add), ["F4", "F9"], ["F4"])
        if k.debug:
            P.dma("sync", lambda e: e.dma_start(out=k.dbg["a"][n * 128:(n + 1) * 128, :], in_=F[4]), reads=["F4"])
        V_(lambda e: e.tensor_tensor(out=Rt, in0=F[4], in1=F[10], op=ALU.mult), ["F4", "F10"], ["Rt"])
        store_yT(k, "a", n, "Rt", Rt, 4, l)
```
